# Optimizing a Trainium2 kernel written in Bass

```python
import jax, jax.numpy as jnp
from jax import lax
import numpy as np

D_MODEL = 1024
BATCH = 8
SEQ = 4096
DEPTH = 2

HEAD_DIM = 64
ROPE_THETA = 10000.0
EPS = 1e-6
NEG = -1e30
D_FF = 4 * D_MODEL
N_MEM = 256
MEM_HEADS = 4
NSA_HEADS = 12
NSA_KV_HEADS = 3
NSA_GQA = NSA_HEADS // NSA_KV_HEADS
CMP_BLOCK = 32
CMP_STRIDE = 16
CMP_HIDDEN = 256
SEL_BLOCK = 64
SEL_TOPK = 16
WINDOW = 512
NSA_Q_BLOCK = 32
SEL_FORCE = 1e9
DIL_PATTERNS = ((128, 1), (512, 4), (2048, 16))
N_DIL_GROUPS = 3
DIL_SLOTS = 8
DIL_Q_BLOCK = 64
N_A_LAYERS = DEPTH // 2
N_B_LAYERS = DEPTH - N_A_LAYERS
MEM_W = MEM_HEADS * HEAD_DIM
A_Q = NSA_HEADS * HEAD_DIM
A_KV = NSA_KV_HEADS * HEAD_DIM
A_GATES = 3 * NSA_HEADS
A_IN = A_Q + 6 * A_KV + MEM_W + A_GATES
A_OUT = A_Q + MEM_W
A_SPLITS = (A_Q, A_Q + A_KV, A_Q + 2 * A_KV, A_Q + 3 * A_KV, A_Q + 4 * A_KV, A_Q + 5 * A_KV, A_Q + 6 * A_KV, A_Q + 6 * A_KV + MEM_W)
B_Q = N_DIL_GROUPS * DIL_SLOTS * HEAD_DIM
B_IN = B_Q + MEM_W
B_OUT = DIL_SLOTS * HEAD_DIM + MEM_W
SHARED_KV = 2 * DIL_SLOTS * HEAD_DIM

kernel_name = "yoco_nsa_dilated_hybrid"


def rmsnorm(x, g):
    x32 = x.astype(jnp.float32)
    y = x32 * lax.rsqrt(jnp.mean(x32 * x32, axis=-1, keepdims=True) + EPS)
    return (y * g.astype(jnp.float32)).astype(x.dtype)


def rope(x, pos):
    half = x.shape[-1] // 2
    inv_freq = ROPE_THETA ** (-jnp.arange(half, dtype=jnp.float32) / half)
    ang = jnp.asarray(pos, jnp.float32)[:, None] * inv_freq[None, :]
    cos = jnp.cos(ang).astype(x.dtype)
    sin = jnp.sin(ang).astype(x.dtype)
    x1, x2 = x[..., :half], x[..., half:]
    return jnp.concatenate([x1 * cos - x2 * sin, x1 * sin + x2 * cos], axis=-1)


def split_heads(x, n):
    b, s, _ = x.shape
    return x.reshape(b, s, n, HEAD_DIM).transpose(0, 2, 1, 3)


def merge_heads(x):
    b, h, s, d = x.shape
    return x.transpose(0, 2, 1, 3).reshape(b, s, h * d)


def chunk_seq(x, axis, size):
    shp = x.shape
    x = x.reshape(shp[:axis] + (shp[axis] // size, size) + shp[axis + 1:])
    return jnp.moveaxis(x, axis, 0)


def unchunk_seq(y, axis):
    y = jnp.moveaxis(y, 0, axis)
    shp = y.shape
    return y.reshape(shp[:axis] + (shp[axis] * shp[axis + 1],) + shp[axis + 2:])


def masked_softmax(s, mask):
    return jax.nn.softmax(jnp.where(mask, s.astype(jnp.float32), NEG), axis=-1)


def nsa_attention(q, k_cmp, v_cmp, k_sel, v_sel, k_win, v_win, gate_logits,
                  q_norm, k_norm, cmp_pos, cmp_w1, cmp_b1, cmp_w2, cmp_b2):
    B, S, _ = q.shape
    G, R, D, QB = NSA_KV_HEADS, NSA_GQA, HEAD_DIM, NSA_Q_BLOCK
    scale = D ** -0.5
    pos = jnp.arange(S)
    q = rope(rmsnorm(split_heads(q, NSA_HEADS), q_norm), pos).reshape(B, G, R, S, D)
    k_sel = rope(rmsnorm(split_heads(k_sel, G), k_norm[1]), pos)
    v_sel = split_heads(v_sel, G)
    k_win = rope(rmsnorm(split_heads(k_win, G), k_norm[2]), pos)
    v_win = split_heads(v_win, G)

    n_cmp = (S - CMP_BLOCK) // CMP_STRIDE + 1
    blk_tok = np.arange(n_cmp)[:, None] * CMP_STRIDE + np.arange(CMP_BLOCK)[None, :]
    cmp_end = jnp.asarray(blk_tok[:, -1], jnp.int32)

    def compress(t, i):
        tb = (t[:, :, blk_tok] + cmp_pos[i]).reshape(B, G, n_cmp, CMP_BLOCK * D)
        return jax.nn.gelu(tb @ cmp_w1[i] + cmp_b1[i]) @ cmp_w2[i] + cmp_b2[i]

    k_c = rope(rmsnorm(compress(split_heads(k_cmp, G), 0), k_norm[0]), cmp_end)
    v_c = compress(split_heads(v_cmp, G), 1)

    n_blk = S // SEL_BLOCK
    n_sel = min(SEL_TOPK, n_blk)
    c_start = np.arange(n_cmp)[:, None] * CMP_STRIDE
    b_start = np.arange(n_blk)[None, :] * SEL_BLOCK
    overlap = jnp.asarray((c_start < b_start + SEL_BLOCK) & (c_start + CMP_BLOCK > b_start), jnp.float32)
    k_sel_blk = k_sel.reshape(B, G, n_blk, SEL_BLOCK, D)
    v_sel_blk = v_sel.reshape(B, G, n_blk, SEL_BLOCK, D)
    k_win_pad = jnp.pad(k_win, ((0, 0), (0, 0), (WINDOW, 0), (0, 0)))
    v_win_pad = jnp.pad(v_win, ((0, 0), (0, 0), (WINDOW, 0), (0, 0)))
    b_ix = jnp.arange(B)[:, None, None, None]
    g_ix = jnp.arange(G)[None, :, None, None]
    blk_ids = jnp.arange(n_blk)

    gates = jax.nn.sigmoid(gate_logits).reshape(B, S, NSA_HEADS, 3)
    gates = gates.transpose(0, 2, 1, 3).reshape(B, G, R, S, 3)

    def block(args):
        c, q_b, g_b = args
        t = c * QB + jnp.arange(QB)
        m_c = cmp_end[None, :] <= t[:, None]
        p_c = masked_softmax(jnp.einsum('bgrqd,bgcd->bgrqc', q_b, k_c) * scale, m_c)
        o_c = jnp.einsum('bgrqc,bgcd->bgrqd', p_c.astype(v_c.dtype), v_c)
        o_c = jnp.where(m_c.any(-1)[:, None], o_c, 0)
        imp = jnp.einsum('bgqc,cj->bgqj', p_c.sum(axis=2), overlap)
        cur = (t // SEL_BLOCK)[:, None]
        forced = (blk_ids == 0) | (blk_ids == cur) | (blk_ids == cur - 1)
        imp = jnp.where(forced, SEL_FORCE, jnp.where(blk_ids <= cur, imp, NEG))
        top_v, top_i = lax.top_k(imp, n_sel)
        k_g = k_sel_blk[b_ix, g_ix, top_i].reshape(B, G, QB, n_sel * SEL_BLOCK, D)
        v_g = v_sel_blk[b_ix, g_ix, top_i].reshape(B, G, QB, n_sel * SEL_BLOCK, D)
        key_pos = top_i[..., None] * SEL_BLOCK + jnp.arange(SEL_BLOCK)
        m_s = (key_pos <= t[:, None, None]) & (top_v > NEG / 2)[..., None]
        m_s = m_s.reshape(B, G, 1, QB, n_sel * SEL_BLOCK)
        p_s = masked_softmax(jnp.einsum('bgrqd,bgqkd->bgrqk', q_b, k_g) * scale, m_s)
        o_s = jnp.einsum('bgrqk,bgqkd->bgrqd', p_s.astype(v_g.dtype), v_g)
        k_b = lax.dynamic_slice_in_dim(k_win_pad, c * QB, QB + WINDOW, axis=2)
        v_b = lax.dynamic_slice_in_dim(v_win_pad, c * QB, QB + WINDOW, axis=2)
        kpos = c * QB - WINDOW + jnp.arange(QB + WINDOW)
        dist = t[:, None] - kpos[None, :]
        m_w = (kpos[None, :] >= 0) & (dist >= 0) & (dist < WINDOW)
        p_w = masked_softmax(jnp.einsum('bgrqd,bgkd->bgrqk', q_b, k_b) * scale, m_w)
        o_w = jnp.einsum('bgrqk,bgkd->bgrqd', p_w.astype(v_b.dtype), v_b)
        return g_b[..., 0:1] * o_c + g_b[..., 1:2] * o_s + g_b[..., 2:3] * o_w

    n_chunks = S // QB
    out = lax.map(block, (jnp.arange(n_chunks), chunk_seq(q, 3, QB), chunk_seq(gates, 3, QB)))
    out = unchunk_seq(out, 3).reshape(B, NSA_HEADS, S, D)
    return merge_heads(out)


def dilated_attention(q, k, v, q_norm):
    B, S, _ = q.shape
    H, D, QB = DIL_SLOTS, HEAD_DIM, DIL_Q_BLOCK
    scale = D ** -0.5
    q = q.reshape(B, S, N_DIL_GROUPS, H, D).transpose(0, 2, 3, 1, 4)
    q = rope(rmsnorm(q, q_norm[:, None, None, :]), jnp.arange(S))

    def block(args):
        c, q_b = args
        t = c * QB + jnp.arange(QB)
        outs, lses = [], []
        for gi, (window, dil) in enumerate(DIL_PATTERNS):
            n_k = window // dil + 1
            idx = t[:, None] - dil * jnp.arange(n_k)[None, :]
            valid = idx >= 0
            idx = jnp.maximum(idx, 0)
            k_g = jnp.take(k, idx, axis=2)
            v_g = jnp.take(v, idx, axis=2)
            s = jnp.einsum('bhqd,bhqkd->bhqk', q_b[:, gi], k_g).astype(jnp.float32) * scale
            s = jnp.where(valid, s, NEG)
            lse = jax.nn.logsumexp(s, axis=-1, keepdims=True)
            p = jnp.exp(s - lse)
            outs.append(jnp.einsum('bhqk,bhqkd->bhqd', p.astype(v.dtype), v_g))
            lses.append(lse)
        wts = jax.nn.softmax(jnp.stack(lses), axis=0).astype(v.dtype)
        return jnp.sum(jnp.stack(outs) * wts, axis=0)

    out = lax.map(block, (jnp.arange(S // QB), chunk_seq(q, 3, QB)))
    return merge_heads(unchunk_seq(out, 2))


def memory_attention(q, mem_kv, q_norm, k_norm):
    scale = HEAD_DIM ** -0.5
    q = rmsnorm(split_heads(q, MEM_HEADS), q_norm)
    k, v = jnp.split(mem_kv, 2, axis=-1)
    k = rmsnorm(split_heads(k, MEM_HEADS), k_norm)
    v = split_heads(v, MEM_HEADS)
    p = jax.nn.softmax(jnp.einsum('bhqd,bhmd->bhqm', q, k).astype(jnp.float32) * scale, axis=-1)
    return merge_heads(jnp.einsum('bhqm,bhmd->bhqd', p.astype(v.dtype), v))


def squared_relu_mlp(h, w_up, w_down):
    return jnp.square(jax.nn.relu(h @ w_up)) @ w_down


def setup_inputs(seed: int = 0) -> dict:
    key = jax.random.key(seed)
    keys = jax.random.split(key, 25)
    cnt = [0]

    def nk():
        k = keys[cnt[0]]
        cnt[0] += 1
        return k

    def normal(shape, scale):
        return scale * jax.random.normal(nk(), shape, jnp.float32)

    def gain(shape):
        return 1.0 + 0.05 * jax.random.normal(nk(), shape, jnp.float32)

    NA, NB = N_A_LAYERS, N_B_LAYERS
    return {
        "x": normal((BATCH, SEQ, D_MODEL), 1.0),
        "mem": normal((BATCH, N_MEM, D_MODEL), 1.0),
        "attn_norm": gain((DEPTH, D_MODEL)),
        "mlp_norm": gain((DEPTH, D_MODEL)),
        "w_up": normal((DEPTH, D_MODEL, D_FF), D_MODEL ** -0.5),
        "w_down": normal((DEPTH, D_FF, D_MODEL), D_FF ** -0.5),
        "mem_norm": gain((DEPTH, D_MODEL)),
        "w_mem_kv": normal((DEPTH, D_MODEL, 2 * MEM_W), D_MODEL ** -0.5),
        "mem_q_norm": gain((DEPTH, HEAD_DIM)),
        "mem_k_norm": gain((DEPTH, HEAD_DIM)),
        "a_w_in": normal((NA, D_MODEL, A_IN), D_MODEL ** -0.5),
        "a_w_out": normal((NA, A_OUT, D_MODEL), A_OUT ** -0.5),
        "a_q_norm": gain((NA, HEAD_DIM)),
        "a_k_norm": gain((NA, 3, HEAD_DIM)),
        "a_cmp_pos": normal((NA, 2, CMP_BLOCK, HEAD_DIM), 0.1),
        "a_cmp_w1": normal((NA, 2, CMP_BLOCK * HEAD_DIM, CMP_HIDDEN), (CMP_BLOCK * HEAD_DIM) ** -0.5),
        "a_cmp_b1": normal((NA, 2, CMP_HIDDEN), 0.01),
        "a_cmp_w2": normal((NA, 2, CMP_HIDDEN, HEAD_DIM), CMP_HIDDEN ** -0.5),
        "a_cmp_b2": normal((NA, 2, HEAD_DIM), 0.01),
        "kv_norm": gain((D_MODEL,)),
        "w_kv_shared": normal((D_MODEL, SHARED_KV), D_MODEL ** -0.5),
        "kv_k_norm": gain((HEAD_DIM,)),
        "b_w_in": normal((NB, D_MODEL, B_IN), D_MODEL ** -0.5),
        "b_w_out": normal((NB, B_OUT, D_MODEL), B_OUT ** -0.5),
        "b_q_norm": gain((NB, N_DIL_GROUPS, HEAD_DIM)),
    }


def reference(x, mem, attn_norm, mlp_norm, w_up, w_down, mem_norm, w_mem_kv, mem_q_norm, mem_k_norm,
              a_w_in, a_w_out, a_q_norm, a_k_norm, a_cmp_pos, a_cmp_w1, a_cmp_b1, a_cmp_w2, a_cmp_b2,
              kv_norm, w_kv_shared, kv_k_norm, b_w_in, b_w_out, b_q_norm):
    pos = jnp.arange(x.shape[1])
    k_shared = None
    v_shared = None
    for layer in range(DEPTH):
        h = rmsnorm(x, attn_norm[layer])
        mem_kv = rmsnorm(mem, mem_norm[layer]) @ w_mem_kv[layer]
        if layer < N_A_LAYERS:
            i = layer
            q, kc, vc, ks, vs, kw, vw, q_mem, gl = jnp.split(h @ a_w_in[i], A_SPLITS, axis=-1)
            o_main = nsa_attention(q, kc, vc, ks, vs, kw, vw, gl, a_q_norm[i], a_k_norm[i],
                                   a_cmp_pos[i], a_cmp_w1[i], a_cmp_b1[i], a_cmp_w2[i], a_cmp_b2[i])
            o_mem = memory_attention(q_mem, mem_kv, mem_q_norm[layer], mem_k_norm[layer])
            x = x + jnp.concatenate([o_main, o_mem], axis=-1) @ a_w_out[i]
        else:
            i = layer - N_A_LAYERS
            if i == 0:
                k_s, v_s = jnp.split(rmsnorm(x, kv_norm) @ w_kv_shared, 2, axis=-1)
                k_shared = rope(rmsnorm(split_heads(k_s, DIL_SLOTS), kv_k_norm), pos)
                v_shared = split_heads(v_s, DIL_SLOTS)
            q, q_mem = jnp.split(h @ b_w_in[i], [B_Q], axis=-1)
            o_main = dilated_attention(q, k_shared, v_shared, b_q_norm[i])
            o_mem = memory_attention(q_mem, mem_kv, mem_q_norm[layer], mem_k_norm[layer])
            x = x + jnp.concatenate([o_main, o_mem], axis=-1) @ b_w_out[i]
        x = x + squared_relu_mlp(rmsnorm(x, mlp_norm[layer]), w_up[layer], w_down[layer])
    return x
```

```python
import numpy as np
import ml_dtypes
import concourse.bass as bass
import concourse.mybir as mybir
from concourse.bass_utils import run_bass_kernel_spmd
from contextlib import ExitStack

F32 = mybir.dt.float32
BF16 = mybir.dt.bfloat16
AF = mybir.ActivationFunctionType
ALU = mybir.AluOpType
AX = mybir.AxisListType
SEM_ROT = 30000
SEQ = 4096
DM = 1024
EPS = 1e-6
NBIG = -30000.0


class Tile:
    __slots__ = ("name", "w", "r", "dsem", "ap", "base")

    def __init__(self, name, ap=None, base=None):
        self.name = name
        self.w = None
        self.r = []
        self.dsem = None
        self.ap = ap
        self.base = base

    def view(self, ap):
        return Tile(self.name + "_v", ap, base=(self.base or self))

    def __getitem__(self, k):
        return self.ap[k]


class DramT:
    def __init__(self, ap, name="d"):
        self.ap = ap
        self.name = name
        self.pend = set()

    def __getitem__(self, k):
        return self.ap[k]


class Sched:
    def __init__(self, nc):
        self.nc = nc
        self.eng = {"pe": nc.tensor, "act": nc.scalar, "dve": nc.vector,
                    "pool": nc.gpsimd, "sp": nc.sync}
        self.esem = {}
        self.ecnt = {}
        self.seen = {e: {} for e in self.eng}
        self.latest = {}
        self.nsem = 0
        self.free_dsems = []
        for e in ("pe", "act", "dve", "pool"):
            self._new_esem(e)

    def _alloc(self, name):
        self.nsem += 1
        return self.nc.alloc_semaphore(name=f"{name}_{self.nsem}")

    def _new_esem(self, e):
        self.esem[e] = self._alloc("e" + e)
        self.ecnt[e] = 0

    def tile(self, ap, name="t"):
        return Tile(name, ap)

    def _need(self, engine, reads, writes):
        need = {}
        reads = [t.base or t for t in reads]
        writes = [t.base or t for t in writes]

        def add(ev, raw):
            if ev is None:
                return
            sem, val, kind, eng = ev
            if kind == "c" and eng == engine:
                if engine == "pe" or not raw:
                    return
            if kind == "d":
                val = self.latest[sem]
            if need.get(sem, 0) < val:
                need[sem] = val

        for t in reads:
            add(t.w, True)
        for t in writes:
            add(t.w, False)
            for ev in t.r:
                add(ev, False)
        seen = self.seen[engine]
        out = []
        for sem, val in need.items():
            if seen.get(sem, 0) < val:
                seen[sem] = val
                out.append((sem, val))
        return out

    def _record(self, ev, reads, writes):
        reads = [t.base or t for t in reads]
        writes = [t.base or t for t in writes]
        for t in reads:
            t.r = [x for x in t.r if x[0] is not ev[0]]
            t.r.append(ev)
        for t in writes:
            t.w = ev
            t.r = []

    def op(self, engine, fn, reads=(), writes=()):
        e = self.eng[engine]
        for sem, val in self._need(engine, reads, writes):
            e.wait_ge(sem, val)
        if self.ecnt[engine] >= SEM_ROT:
            self._new_esem(engine)
        inst = fn(e)
        self.ecnt[engine] += 1
        inst.then_inc(self.esem[engine], 1)
        ev = (self.esem[engine], self.ecnt[engine], "c", engine)
        self._record(ev, reads, writes)
        return ev

    def free_dsem(self, tiles):
        for t in tiles:
            if t.dsem is not None:
                self.free_dsems.append(t.dsem)
                t.dsem = None

    def dma(self, queue, out, in_, reads=(), writes=(), dram_r=(), dram_w=(), **kw):
        e = self.eng[queue]
        stile = writes[0] if writes else reads[0]
        stile = stile.base or stile
        if stile.dsem is None or self.latest[stile.dsem] >= SEM_ROT:
            stile.dsem = None
            while self.free_dsems and stile.dsem is None:
                c = self.free_dsems.pop()
                if self.latest[c] < SEM_ROT:
                    stile.dsem = c
            if stile.dsem is None:
                stile.dsem = self._alloc("d")
                self.latest[stile.dsem] = 0
        waits = self._need(queue, reads, writes)
        seen = self.seen[queue]
        for d in dram_r:
            for sem in d.pend:
                val = self.latest[sem]
                if seen.get(sem, 0) < val:
                    seen[sem] = val
                    waits.append((sem, val))
        for sem, val in waits:
            e.wait_ge(sem, val)
        inst = e.dma_start(out=out, in_=in_, **kw)
        sem = stile.dsem
        self.latest[sem] += 16
        inst.then_inc(sem, 16)
        ev = (sem, self.latest[sem], "d", None)
        self._record(ev, reads, writes)
        for d in dram_w:
            d.pend.add(sem)
        return ev

    def barrier(self):
        for en, e in self.eng.items():
            seen = self.seen[en]
            for f in ("pe", "act", "dve", "pool"):
                if f == en:
                    continue
                sem, val = self.esem[f], self.ecnt[f]
                if val > 0 and seen.get(sem, 0) < val:
                    seen[sem] = val
                    e.wait_ge(sem, val)
            for sem, val in self.latest.items():
                if val > 0 and seen.get(sem, 0) < val:
                    seen[sem] = val
                    e.wait_ge(sem, val)


def host_consts():
    bf = ml_dtypes.bfloat16
    c = {}
    half = 32
    inv = (10000.0 ** (-np.arange(half, dtype=np.float32) / half)).astype(np.float32)
    ang = np.arange(SEQ + 32, dtype=np.float32)[:, None] * inv[None, :]
    c["c_cos"] = np.cos(ang).astype(np.float32)
    c["c_sin"] = np.sin(ang).astype(np.float32)
    c["c_ident"] = np.eye(128, dtype=np.float32).astype(bf)
    kk = np.arange(128)[:, None]

    def toep(width, f):
        j = np.arange(width)[None, :]
        return f(j - 384 - kk).astype(np.float32).astype(bf)

    c["m_sel"] = toep(896, lambda d: d >= 0)
    c["m_win"] = toep(1408, lambda d: (d >= 0) & (d < 512))
    c["m_d0"] = toep(1024, lambda d: (d >= 0) & (d <= 128))
    c["m_d1"] = toep(1408, lambda d: (d >= 0) & (d <= 512) & (d % 4 == 0))
    c["m_d2"] = toep(2944, lambda d: (d >= 0) & (d <= 2048) & (d % 16 == 0))
    u = np.arange(512)[None, :]
    c["m_cmp"] = np.where(16 * (u - 248) + 31 <= kk, 0.0, NBIG).astype(np.float32)
    u = np.arange(128)[None, :]
    rel = u - 62 - (kk // 64)
    keep = (rel < -1).astype(np.float32)
    add = np.where(rel == 0, 2e9, np.where(rel == -1, 1e9, np.where(rel > 0, -1e30, 0.0)))
    c["m_tkeep"] = keep.astype(np.float32)
    c["m_tadd"] = add.astype(np.float32)
    n_cmp = SEQ // 16 - 1
    cs = np.arange(256)[:, None] * 16
    bs = np.arange(64)[None, :] * 64
    ov = ((cs < bs + 64) & (cs + 32 > bs)).astype(np.float32)
    ov[n_cmp:] = 0
    c["c_ovl"] = ov.astype(bf)
    c["c_onehot"] = (np.arange(SEQ)[None, :] // 64 == np.arange(64)[:, None]).astype(np.float32).astype(bf)
    return c


CONST_SHAPES = {
    "c_cos": ([SEQ + 32, 32], F32), "c_sin": ([SEQ + 32, 32], F32), "c_ident": ([128, 128], BF16),
    "m_sel": ([128, 896], BF16), "m_win": ([128, 1408], BF16), "m_d0": ([128, 1024], BF16),
    "m_d1": ([128, 1408], BF16), "m_d2": ([128, 2944], BF16), "m_cmp": ([128, 512], F32),
    "m_tkeep": ([128, 128], F32), "m_tadd": ([128, 128], F32), "c_ovl": ([256, 64], BF16),
    "c_onehot": ([64, SEQ], BF16),
}

IN_SHAPES = {
    "x": [SEQ, DM], "mem": [256, DM], "attn_norm": [2, DM], "mlp_norm": [2, DM],
    "w_up": [2, DM, 4096], "w_down": [2, 4096, DM], "mem_norm": [2, DM], "w_mem_kv": [2, DM, 512],
    "mem_q_norm": [2, 64], "mem_k_norm": [2, 64], "a_w_in": [1, DM, 2212], "a_w_out": [1, DM, DM],
    "a_q_norm": [1, 64], "a_k_norm": [1, 3, 64], "a_cmp_pos": [1, 2, 32, 64],
    "a_cmp_w1": [1, 2, 2048, 256], "a_cmp_b1": [1, 2, 256], "a_cmp_w2": [1, 2, 256, 64],
    "a_cmp_b2": [1, 2, 64], "kv_norm": [1, DM], "w_kv_shared": [DM, DM], "kv_k_norm": [1, 64],
    "b_w_in": [1, DM, 1792], "b_w_out": [1, 768, DM], "b_q_norm": [1, 3, 64],
}


def build_program(debug=()):
    nc = bass.Bass("TRN2", target_bir_lowering=False)
    S = Sched(nc)
    I = {}
    for k, shp in IN_SHAPES.items():
        I[k] = nc.dram_tensor(k, shp, F32, kind="ExternalInput").ap()
    C = {}
    for k, (shp, dt) in CONST_SHAPES.items():
        C[k] = nc.dram_tensor(k, shp, dt, kind="ExternalInput").ap()
    out_ap = nc.dram_tensor("out", [SEQ, DM], F32, kind="ExternalOutput").ap()

    def scratch(name, shape, dt):
        kind = "ExternalOutput" if name in debug else "Internal"
        return DramT(nc.dram_tensor(name, shape, dt, kind=kind).ap(), name)

    featT0 = scratch("featT0", [28, 64, SEQ], BF16)
    vtok0 = scratch("vtok0", [SEQ, 6, 65], BF16)
    gates = scratch("gates", [SEQ, 36], F32)
    x1 = scratch("x1", [SEQ, DM], F32)
    x2 = scratch("x2", [SEQ, DM], F32)
    x3 = scratch("x3", [SEQ, DM], F32)
    featT1 = scratch("featT1", [36, 64, SEQ], BF16)
    vtok1 = scratch("vtok1", [SEQ, 8, 65], BF16)
    outT = DramT(out_ap, "out")
    xin = DramT(I["x"], "x")

    gst = ExitStack()
    cur = {"st": gst}

    uid = {"i": 0}
    local_tiles = []

    def sb(name, shape, dt=F32, persist=False):
        st = gst if persist else cur["st"]
        uid["i"] += 1
        t = S.tile(st.enter_context(nc.sbuf_tensor(f"{name}_{uid['i']}", list(shape), dt)), name)
        if not persist:
            local_tiles.append(t)
        return t

    def ps(name, shape, dt=F32):
        uid["i"] += 1
        full = [128, 512] if dt == F32 else [128, 1024]
        h = cur["st"].enter_context(nc.psum_tensor(f"{name}_{uid['i']}", full, dt))
        n = 1
        for d in shape[1:]:
            n *= d
        v = h[:, 0:n]
        if len(shape) == 3:
            v = v.rearrange("p (a b) -> p a b", b=shape[2])
        return S.tile(v, name)

    class Phase:
        def __enter__(self):
            self.st = ExitStack()
            cur["st"] = self.st
            return self

        def __exit__(self, *a):
            S.barrier()
            S.free_dsem(local_tiles)
            del local_tiles[:]
            self.st.close()
            cur["st"] = gst
            return False

    rr = {"i": 0}

    def alt(engs=("dve", "pool")):
        rr["i"] += 1
        return engs[rr["i"] % len(engs)]

    def ring(name, shape, dt, n):
        tiles = [sb(f"{name}{i}", shape, dt) for i in range(n)]
        st = {"i": -1}

        def nxt():
            st["i"] = (st["i"] + 1) % n
            return tiles[st["i"]]
        nxt.tiles = tiles
        return nxt

    ident = sb("ident", [128, 128], BF16, persist=True)
    S.dma("sp", ident[:], C["c_ident"][:, :], writes=[ident])
    ones_bf = sb("ones_bf", [128, 128], BF16, persist=True)
    S.op("dve", lambda e: e.memset(ones_bf[:], 1.0), writes=[ones_bf])

    def bcast_row(dst_ap, src_row_ap, tile):
        S.dma("sp", dst_ap, src_row_ap.partition_broadcast(128), writes=[tile])

    def load_gT(name, src_row):
        t = sb(name, [128, 8], F32, persist=True)
        S.dma("sp", t[:], src_row.rearrange("o (kc p) -> p (o kc)", p=128), writes=[t],
              allow_slow_non_contiguous=True)
        return t

    gT_attn = [load_gT(f"gT_attn{l}", I["attn_norm"][l:l + 1, :]) for l in range(2)]
    gT_mlp = [load_gT(f"gT_mlp{l}", I["mlp_norm"][l:l + 1, :]) for l in range(2)]
    gT_mem = [load_gT(f"gT_mem{l}", I["mem_norm"][l:l + 1, :]) for l in range(2)]
    gT_kv = load_gT("gT_kv", I["kv_norm"][0:1, :])

    def load_weight(wt, src, nk, ncols, stg_ring, col_chunk=2048):
        for kc in range(nk):
            eng = ("dve", "act", "pool")[kc % 3]
            for c0 in range(0, ncols, col_chunk):
                w = min(col_chunk, ncols - c0)
                stg = stg_ring()
                S.dma("sp", stg[:, 0:w], src[kc * 128:(kc + 1) * 128, c0:c0 + w], writes=[stg])
                if eng == "act":
                    S.op(eng, lambda e: e.copy(out=wt.ap[:, kc, c0:c0 + w], in_=stg[:, 0:w]), reads=[stg], writes=[wt.k[kc]])
                else:
                    S.op(eng, lambda e: e.tensor_copy(out=wt.ap[:, kc, c0:c0 + w], in_=stg[:, 0:w]),
                         reads=[stg], writes=[wt.k[kc]])

    class WT:
        def __init__(self, name, nk, ncols):
            self.h = sb(name, [128, nk, ncols], BF16)
            self.ap = self.h.ap
            self.k = [Tile(f"{name}_{i}") for i in range(nk)]

    def run(gen):
        for _ in gen:
            pass

    def interleave(gens):
        state = [[g, 0, float(tot)] for g, tot in gens]
        while state:
            st = min(state, key=lambda z: z[1] / z[2])
            try:
                next(st[0])
                st[1] += 1
            except StopIteration:
                state.remove(st)

    def norm_tile(xt_ap, xt_tile, gTs, hTs, tph, tmp):
        junk, ssq, rs, rs2, xn = tmp
        S.op("pool", lambda e: e.memset(ssq[:], 0.0), writes=[ssq])
        yield
        S.op("act", lambda e: e.activation(out=junk[:], in_=xt_ap, func=AF.Square, accum_out=ssq[:, 0:1]),
             reads=[xt_tile, ssq], writes=[junk, ssq])
        yield
        S.op("act", lambda e: e.activation(out=rs[:], in_=ssq[:], func=AF.Sqrt, scale=1.0 / DM, bias=EPS),
             reads=[ssq], writes=[rs])
        yield
        S.op("dve", lambda e: e.reciprocal(out=rs2[:], in_=rs[:]), reads=[rs], writes=[rs2])
        yield
        S.op("dve", lambda e: e.tensor_scalar(out=xn[:], in0=xt_ap, scalar1=rs2[:, 0:1], scalar2=None, op0=ALU.mult),
             reads=[xt_tile, rs2], writes=[xn])
        yield
        for kc in range(8):
            S.op("pe", lambda e: e.transpose(out=tph[:, kc, :], in_=xn[:, kc * 128:(kc + 1) * 128], identity=ident[:]),
                 reads=[xn, ident], writes=[tph])
            yield
        for gT, hT in zip(gTs, hTs):
            S.op("dve", lambda e: e.tensor_tensor(out=hT[:], in0=tph[:], in1=gT[:].unsqueeze(2).to_broadcast([128, 8, 128]), op=ALU.mult),
                 reads=[tph, gT], writes=[hT])
            yield

    def head_norm(YA, nh, nrope, GA, CS, YB, tmp):
        SQ, SS, RS, RS2, YN, T1, T2 = tmp
        S.op("act", lambda e: e.activation(out=SQ[:, 0:nh, :], in_=YA[:, 0:nh, :], func=AF.Square),
             reads=[YA], writes=[SQ])
        yield
        S.op("dve", lambda e: e.tensor_reduce(out=SS[:, 0:nh], in_=SQ[:, 0:nh, :], axis=AX.X, op=ALU.add),
             reads=[SQ], writes=[SS])
        yield
        S.op("act", lambda e: e.activation(out=RS[:, 0:nh], in_=SS[:, 0:nh], func=AF.Sqrt, scale=1.0 / 64, bias=EPS),
             reads=[SS], writes=[RS])
        yield
        S.op("dve", lambda e: e.reciprocal(out=RS2[:, 0:nh], in_=RS[:, 0:nh]), reads=[RS], writes=[RS2])
        yield
        S.op("dve", lambda e: e.tensor_tensor(out=YN[:, 0:nh, :], in0=YA[:, 0:nh, :],
                                              in1=RS2[:, 0:nh].unsqueeze(2).to_broadcast([128, nh, 64]), op=ALU.mult),
             reads=[YA, RS2], writes=[YN])
        yield
        S.op("pool", lambda e: e.tensor_tensor(out=YN[:, 0:nh, :], in0=YN[:, 0:nh, :], in1=GA[:, 0:nh, :], op=ALU.mult),
             reads=[YN, GA], writes=[YN])
        yield
        if nrope:
            n = nrope
            cosb = CS[:, 0:1, :].to_broadcast([128, n, 32])
            sinb = CS[:, 1:2, :].to_broadcast([128, n, 32])
            x1v = YN[:, 0:n, 0:32]
            x2v = YN[:, 0:n, 32:64]
            S.op("dve", lambda e: e.tensor_tensor(out=T1[:, 0:n, 0:32], in0=x1v, in1=cosb, op=ALU.mult), reads=[YN, CS], writes=[T1])
            yield
            S.op("pool", lambda e: e.tensor_tensor(out=T2[:, 0:n, 0:32], in0=x2v, in1=sinb, op=ALU.mult), reads=[YN, CS], writes=[T2])
            yield
            S.op("dve", lambda e: e.tensor_tensor(out=YB[:, 0:n, 0:32], in0=T1[:, 0:n, 0:32], in1=T2[:, 0:n, 0:32], op=ALU.subtract),
                 reads=[T1, T2], writes=[YB])
            yield
            S.op("pool", lambda e: e.tensor_tensor(out=T1[:, 0:n, 32:64], in0=x1v, in1=sinb, op=ALU.mult), reads=[YN, CS], writes=[T1])
            yield
            S.op("dve", lambda e: e.tensor_tensor(out=T2[:, 0:n, 32:64], in0=x2v, in1=cosb, op=ALU.mult), reads=[YN, CS], writes=[T2])
            yield
            S.op("pool", lambda e: e.tensor_tensor(out=YB[:, 0:n, 32:64], in0=T1[:, 0:n, 32:64], in1=T2[:, 0:n, 32:64], op=ALU.add),
                 reads=[T1, T2], writes=[YB])
            yield
        if nh > nrope:
            S.op("act", lambda e: e.copy(out=YB[:, nrope:nh, :], in_=YN[:, nrope:nh, :]), reads=[YN], writes=[YB])
            yield

    def feat_transposes(YB, npairs, tpf, TT):
        YBf = YB[:].rearrange("p h d -> p (h d)")
        for b0 in range(0, npairs, 8):
            nb = min(8, npairs - b0)
            for j in range(nb):
                S.op("pe", lambda e: e.transpose(out=tpf[:, j, :], in_=YBf[:, (b0 + j) * 128:(b0 + j + 1) * 128], identity=ident[:]),
                     reads=[YB, ident], writes=[tpf])
                yield
            S.op("act", lambda e: e.copy(out=TT[:, b0:b0 + nb, :], in_=tpf[:, 0:nb, :]), reads=[tpf], writes=[TT])
            yield

    def build_gains(name, nh, specs):
        GA = sb(name, [128, nh, 64], F32)
        G1 = sb(name + "_raw", [128, len(specs), 64], F32)
        for i, (src, h0, n, scale) in enumerate(specs):
            bcast_row(G1[:, i, :], src, G1)
        for i, (src, h0, n, scale) in enumerate(specs):
            S.op("dve", lambda e: e.tensor_scalar(out=GA[:, h0:h0 + n, :], in0=G1[:, i:i + 1, :].to_broadcast([128, n, 64]),
                                                  scalar1=float(scale), scalar2=None, op0=ALU.mult),
                 reads=[G1], writes=[GA])
        return GA

    def proj_phase(x_src, ntiles, projs, nh, nrope, GA, raw_specs, featT, vtok, vtok_n, gate_cols=None,
                   seg_map=None, pos_tab=True):
        nraw_t = sum((c1 - c0) // 64 for (_, c0, c1, k, _) in seg_map if k == "T")
        ntot = nh + nraw_t
        npairs = ntot // 2
        xr = ring("xt", [128, DM], F32, 3)
        junkr = ring("junk", [128, DM], BF16, 2)
        ssqr = ring("ssq", [128, 1], F32, 2); rsr_ = ring("rs", [128, 1], F32, 2); rs2r = ring("rs2", [128, 1], F32, 2)
        xnr = ring("xn", [128, DM], BF16, 2)
        tph = ps("tph", [128, 8, 128], BF16)
        tpf = ps("tpf", [128, 8, 128], BF16)
        hTr = [ring(f"hT{i}", [128, 8, 128], BF16, 2) for i in range(len(projs))]
        banks = []
        for pi, (gT, W, ncols) in enumerate(projs):
            for c0 in range(0, ncols, 512):
                banks.append((pi, c0, min(512, ncols - c0), ps(f"pb{pi}_{c0}", [128, 512], F32)))
        YAr = ring("YA", [128, ntot, 64], F32, 2)
        YBr = ring("YB", [128, ntot, 64], BF16, 2)
        SQ = sb("SQ", [128, nh, 64], F32); SS = sb("SS", [128, nh]); RS = sb("RS", [128, nh]); RS2 = sb("RS2", [128, nh])
        YN = sb("YN", [128, nh, 64], F32); T1 = sb("T1", [128, max(nrope, 1), 64], F32); T2 = sb("T2", [128, max(nrope, 1), 64], F32)
        CSr = ring("CS", [128, 2, 32], F32, 6)
        TTr = ring("TT", [128, npairs, 128], BF16, 2)
        YVr = ring("YV", [128, vtok_n, 65], BF16, 2) if vtok is not None else None
        if YVr is not None:
            for t in YVr.tiles:
                S.op("pool", lambda e: e.memset(t[:], 1.0), writes=[t])
        YGr = ring("YG", [128, 36], F32, 2) if gate_cols else None
        ctx = {}

        def s0(ti):
            t0 = ti * 128
            c = ctx[ti] = {}
            xt = c["xt"] = xr()
            S.dma("sp", xt[:], x_src[t0:t0 + 128, :], writes=[xt], dram_r=[x_src])
            CS = CSr()
            c["CS"] = CS
            if nrope:
                S.dma("sp", CS[:, 0, :], C["c_cos"][t0:t0 + 128, :], writes=[CS])
                S.dma("sp", CS[:, 1, :], C["c_sin"][t0:t0 + 128, :], writes=[CS])

        def s1(ti):
            c = ctx[ti]
            xt = c["xt"]
            c["hT"] = [r() for r in hTr]
            yield from norm_tile(xt[:], xt, [p[0] for p in projs], c["hT"], tph, (junkr(), ssqr(), rsr_(), rs2r(), xnr()))

        def s2(ti):
            c = ctx[ti]
            hTs = c["hT"]
            for (pi, c0, w, pb) in banks:
                W = projs[pi][1]
                for kc in range(8):
                    S.op("pe", lambda e: e.matmul(pb[:, 0:w], lhsT=hTs[pi][:, kc, :], rhs=W.ap[:, kc, c0:c0 + w],
                                                  start=(kc == 0), stop=(kc == 7)),
                         reads=[hTs[pi], W.k[kc]], writes=[pb])
                    yield
            YA = c["YA"] = YAr()
            YB = c["YB"] = YBr()
            YV = c["YV"] = YVr() if YVr is not None else None
            YG = c["YG"] = YGr() if YGr is not None else None
            for (pi, c0, c1, kind, di) in seg_map:
                cc = c0
                while cc < c1:
                    bk = [b for b in banks if b[0] == pi and b[1] <= cc < b[1] + b[2]][0]
                    ce = min(c1, bk[1] + bk[2])
                    src = bk[3][:, cc - bk[1]:ce - bk[1]]
                    off = cc - c0
                    n = ce - cc
                    if kind in ("A", "T"):
                        dst = YA if kind == "A" else YB
                        dflat = dst[:].rearrange("p h d -> p (h d)")
                        S.op("act", lambda e: e.copy(out=dflat[:, di * 64 + off: di * 64 + off + n], in_=src),
                             reads=[bk[3]], writes=[dst])
                        yield
                    elif kind == "V":
                        h0 = di + off // 64
                        S.op("act", lambda e: e.copy(out=YV[:, h0:h0 + n // 64, 0:64],
                                                     in_=src.rearrange("p (h d) -> p h d", d=64)),
                             reads=[bk[3]], writes=[YV])
                        yield
                    elif kind == "G":
                        S.op("act", lambda e: e.copy(out=YG[:, off:off + n], in_=src), reads=[bk[3]], writes=[YG])
                        yield
                    cc = ce

        def s3(ti):
            t0 = ti * 128
            c = ctx.pop(ti)
            yield from head_norm(c["YA"], nh, nrope, GA, c["CS"], c["YB"], (SQ, SS, RS, RS2, YN, T1, T2))
            TT = TTr()
            yield from feat_transposes(c["YB"], npairs, tpf, TT)
            S.dma("sp", featT.ap.rearrange("(j h2) d t -> (h2 d) j t", h2=2)[:, :, t0:t0 + 128], TT[:],
                  reads=[TT], dram_w=[featT])
            yield
            if vtok is not None:
                S.dma("sp", vtok[t0:t0 + 128, :, :], c["YV"][:], reads=[c["YV"]], dram_w=[vtok])
                yield
            if c["YG"] is not None:
                S.dma("sp", gates[t0:t0 + 128, :], c["YG"][:], reads=[c["YG"]], dram_w=[gates])
                yield

        s0(0)
        s0(1)
        for step in range(ntiles + 2):
            if step + 2 < ntiles:
                s0(step + 2)
            gens = []
            if step < ntiles:
                gens.append((s1(step), 16))
            if 0 <= step - 1 < ntiles:
                gens.append((s2(step - 1), 56))
            if 0 <= step - 2 < ntiles:
                gens.append((s3(step - 2), 40))
            interleave(gens)

    def mem_phase(layer, kmT, VM):
        with Phase():
            stg = ring("stg", [128, 2048], F32, 2)
            W = WT("wmem", 8, 512)
            load_weight(W, I["w_mem_kv"][layer], 8, 512, stg)
            GA = build_gains(f"GAm{layer}", 4, [(I["mem_k_norm"][layer:layer + 1, :], 0, 4, 1.0)])
            xr = ring("xt", [128, DM], F32, 2)
            junk = sb("junk", [128, DM], BF16)
            ssq = sb("ssq", [128, 1]); rs = sb("rs", [128, 1]); rs2 = sb("rs2", [128, 1])
            xn = sb("xn", [128, DM], BF16)
            tph = ps("tph", [128, 8, 128], BF16)
            tpf = ps("tpf", [128, 8, 128], BF16)
            hT = sb("hT", [128, 8, 128], BF16)
            pb = ps("pb", [128, 512], F32)
            YA = sb("YA", [128, 4, 64], F32); YB = sb("YB", [128, 4, 64], BF16)
            SQ = sb("SQ", [128, 4, 64], F32); SS = sb("SS", [128, 4]); RS = sb("RS", [128, 4]); RS2 = sb("RS2", [128, 4])
            YN = sb("YN", [128, 4, 64], F32); T1 = sb("T1", [128, 1, 64], F32); T2 = sb("T2", [128, 1, 64], F32)
            TT = sb("TT", [128, 2, 128], BF16)
            S.op("pool", lambda e: e.memset(VM[:], 1.0), writes=[VM])
            memT = DramT(I["mem"], "mem")
            for mt in range(2):
                xt = xr()
                S.dma("sp", xt[:], memT[mt * 128:(mt + 1) * 128, :], writes=[xt])
                run(norm_tile(xt[:], xt, [gT_mem[layer]], [hT], tph, (junk, ssq, rs, rs2, xn)))
                for kc in range(8):
                    S.op("pe", lambda e: e.matmul(pb[:, :], lhsT=hT[:, kc, :], rhs=W.ap[:, kc, :], start=(kc == 0), stop=(kc == 7)),
                         reads=[hT, W.k[kc]], writes=[pb])
                S.op("act", lambda e: e.copy(out=YA[:].rearrange("p h d -> p (h d)"), in_=pb[:, 0:256]), reads=[pb], writes=[YA])
                S.op("act", lambda e: e.copy(out=VM[:, mt, :, 0:64], in_=pb[:, 256:512].rearrange("p (h d) -> p h d", d=64)),
                     reads=[pb], writes=[VM])
                run(head_norm(YA, 4, 0, GA, None, YB, (SQ, SS, RS, RS2, YN, T1, T2)))
                run(feat_transposes(YB, 2, tpf, TT))
                for mh in range(4):
                    S.op("dve", lambda e: e.tensor_copy(out=kmT[:, mh, mt * 128:(mt + 1) * 128],
                                                        in_=TT[(mh % 2) * 64:(mh % 2) * 64 + 64, mh // 2, :]),
                         reads=[TT], writes=[kmT])

    LOOK = 2

    def tile_job(lhsT_ap, rhs_fn, reads, mask_tile, mask_off, acc, v_ap, v_tile, subs, first):
        return ["tile", dict(lhsT=lhsT_ap, rhs_fn=rhs_fn, reads=reads, mt=mask_tile, mo=mask_off, acc=acc, v=v_ap,
                             vt=v_tile, subs=subs, first=first)]

    def run_jobs(jobs, ST, PTr):
        tiles = [j[1] for j in jobs if j[0] == "tile"]
        pos = {"fi": 0, "ti": 0}

        def front(j):
            c0 = min(j["subs"]) * 128
            c1 = (max(j["subs"]) + 1) * 128
            st = ST()
            pt = PTr()
            j["pt"] = pt
            S.op("pe", lambda e: e.matmul(st[:, c0:c1], lhsT=j["lhsT"], rhs=j["rhs_fn"](c0, c1), start=True, stop=True),
                 reads=j["reads"], writes=[st])
            S.op("act", lambda e: e.activation(out=pt[:, c0:c1], in_=st[:, c0:c1], func=AF.Exp), reads=[st], writes=[pt])
            if j["mt"] is not None:
                mo = j["mo"]
                S.op("dve", lambda e: e.tensor_tensor(out=pt[:, c0:c1], in0=pt[:, c0:c1], in1=j["mt"][:, mo + c0:mo + c1], op=ALU.mult),
                     reads=[pt, j["mt"]], writes=[pt])

        def back(j):
            pt = j["pt"]
            for i, sidx in enumerate(j["subs"]):
                S.op("pe", lambda e: e.matmul(j["acc"][:, sidx, :], lhsT=pt[:, sidx * 128:(sidx + 1) * 128], rhs=j["v"],
                                              start=(j["first"] and i == 0), stop=False, skip_group_check=True),
                     reads=[pt, j["vt"]], writes=[j["acc"]])

        for j in jobs:
            if j[0] == "tile":
                while pos["fi"] < len(tiles) and pos["fi"] <= pos["ti"] + LOOK:
                    front(tiles[pos["fi"]])
                    pos["fi"] += 1
                back(j[1])
                pos["ti"] += 1
            else:
                j[1]()

    def finalize(acc, OM, col0, fac_fn, tmp, accumulate):
        R, F, T = tmp
        S.op("dve", lambda e: e.reciprocal(out=R[:], in_=acc[:, :, 64]), reads=[acc], writes=[R])
        fac = R
        if fac_fn is not None:
            gap, gtile = fac_fn
            S.op("dve", lambda e: e.tensor_tensor(out=F[:], in0=R[:], in1=gap, op=ALU.mult), reads=[R, gtile], writes=[F])
            fac = F
        facb = fac[:].unsqueeze(2).to_broadcast([128, 4, 64])
        if not accumulate:
            S.op("dve", lambda e: e.tensor_tensor(out=OM[:, :, col0:col0 + 64], in0=acc[:, :, 0:64], in1=facb, op=ALU.mult),
                 reads=[acc, fac], writes=[OM])
        else:
            S.op("dve", lambda e: e.tensor_tensor(out=T[:], in0=acc[:, :, 0:64], in1=facb, op=ALU.mult), reads=[acc, fac], writes=[T])
            S.op("pool", lambda e: e.tensor_tensor(out=OM[:, :, col0:col0 + 64], in0=OM[:, :, col0:col0 + 64], in1=T[:], op=ALU.add),
                 reads=[OM, T], writes=[OM])

    def out_proj(OM, ncols_in, WO, Xres, dst, Q0, tp, po, OBr, OTr, XOr):
        nk = ncols_in // 128
        for s in range(4):
            OB = OBr()
            S.op("dve", lambda e: e.tensor_copy(out=OB[:, 0:ncols_in], in_=OM[:, s, :]), reads=[OM], writes=[OB])
            for kc in range(nk):
                S.op("pe", lambda e: e.transpose(out=tp[:, kc, :], in_=OB[:, kc * 128:(kc + 1) * 128], identity=ident[:]),
                     reads=[OB, ident], writes=[tp])
            OT = OTr()
            S.op("act", lambda e: e.copy(out=OT[:, 0:nk, :], in_=tp[:, 0:nk, :]), reads=[tp], writes=[OT])
            XO = XOr()
            for half in range(2):
                for kc in range(nk):
                    S.op("pe", lambda e: e.matmul(po[:, :], lhsT=OT[:, kc, :], rhs=WO.ap[:, kc, half * 512:(half + 1) * 512],
                                                  start=(kc == 0), stop=(kc == nk - 1)),
                         reads=[OT, WO.k[kc]], writes=[po])
                S.op("dve", lambda e: e.tensor_tensor(out=XO[:, half * 512:(half + 1) * 512], in0=po[:, :],
                                                      in1=Xres[:, s, half * 512:(half + 1) * 512], op=ALU.add),
                     reads=[po, Xres], writes=[XO])
            S.dma("sp", dst[Q0 + s * 128:Q0 + (s + 1) * 128, :], XO[:], reads=[XO], dram_w=[dst])

    def fin_job(acc, OM, col0, fac_fn, tmp, accumulate):
        return ["fin", lambda: finalize(acc, OM, col0, fac_fn, tmp, accumulate)]

    def mem_jobs(QM, qm_tile, kmT, VM, ACC, OM, col0, tmp):
        jobs = []
        for mh in range(4):
            acc = ACC()
            for mt in range(2):
                jobs.append(tile_job(kmT[:, mh, mt * 128:(mt + 1) * 128], (lambda c0, c1, mh=mh: QM[0:64, mh, c0:c1]), [kmT, qm_tile],
                                     None, None, acc, VM[:, mt, mh, :], VM, [0, 1, 2, 3], mt == 0))
            jobs.append(fin_job(acc, OM, col0 + mh * 64, None, tmp, False))
        return jobs

    def bank_ring(tiles):
        st = {"i": -1}

        def nxt():
            st["i"] = (st["i"] + 1) % len(tiles)
            return tiles[st["i"]]
        return nxt

    def acc_view(b):
        return b.view(b.ap[:, 0:260].rearrange("p (s c) -> p s c", c=65))

    def mlp_phase(layer, src, dst):
        with Phase():
            stg = ring("stg", [128, 1024], F32, 2)
            WU = WT("WU", 8, 4096)
            WD = WT("WD", 32, 1024)
            load_weight(WU, I["w_up"][layer], 8, 4096, stg, col_chunk=1024)
            load_weight(WD, I["w_down"][layer], 32, 1024, stg, col_chunk=1024)
            xr = ring("xt", [128, 2, DM], F32, 2)
            junk = sb("junk", [128, DM], BF16)
            ssq = sb("ssq", [128, 1]); rs = sb("rs", [128, 1]); rs2 = sb("rs2", [128, 1])
            xn = sb("xn", [128, DM], BF16)
            tph = ps("tph", [128, 8, 128], BF16)
            hT = sb("hT", [128, 8, 256], BF16)
            hTa = sb("hTa", [128, 8, 128], BF16)
            uT = sb("uT", [128, 32, 256], BF16)
            rl = ring("rl", [128, 256], BF16, 3)
            XO = ring("XO", [128, DM], F32, 2)
            pu = [ps(f"pu{i}", [128, 512], F32) for i in range(3)]
            pd = [ps(f"pd{i}", [128, 512], F32) for i in range(3)]
            cnt = 0
            xts = {}

            def ldx(tb):
                xts[tb] = xr()
                S.dma("sp", xts[tb][:], src.ap[tb * 256:tb * 256 + 256, :].rearrange("(s p) d -> p s d", p=128),
                      writes=[xts[tb]], dram_r=[src])
            ldx(0)
            for tb in range(SEQ // 256):
                t0 = tb * 256
                if tb + 1 < SEQ // 256:
                    ldx(tb + 1)
                xt = xts.pop(tb)
                for s in range(2):
                    run(norm_tile(xt[:, s, :], xt, [gT_mlp[layer]], [hTa], tph, (junk, ssq, rs, rs2, xn)))
                    S.op(alt(), lambda e: e.tensor_copy(out=hT[:, :, s * 128:(s + 1) * 128], in_=hTa[:]), reads=[hTa], writes=[hT])
                for fc in range(32):
                    p = pu[fc % 3]
                    for kc in range(8):
                        S.op("pe", lambda e: e.matmul(p[:, 0:256], lhsT=WU.ap[:, kc, fc * 128:(fc + 1) * 128], rhs=hT[:, kc, :],
                                                      start=(kc == 0), stop=(kc == 7)),
                             reads=[WU.k[kc], hT], writes=[p])
                    r = rl()
                    S.op("act", lambda e: e.activation(out=r[:], in_=p[:, 0:256], func=AF.Relu), reads=[p], writes=[r])
                    S.op(alt(), lambda e: e.tensor_tensor(out=uT[:, fc, :], in0=r[:], in1=r[:], op=ALU.mult), reads=[r], writes=[uT])
                for s in range(2):
                    xo = XO()
                    for half in range(2):
                        p = pd[cnt % 3]
                        cnt += 1
                        for kc in range(32):
                            S.op("pe", lambda e: e.matmul(p[:, :], lhsT=uT[:, kc, s * 128:(s + 1) * 128],
                                                          rhs=WD.ap[:, kc, half * 512:(half + 1) * 512],
                                                          start=(kc == 0), stop=(kc == 31)),
                                 reads=[uT, WD.k[kc]], writes=[p])
                        S.op("dve", lambda e: e.tensor_tensor(out=xo[:, half * 512:(half + 1) * 512], in0=p[:, :],
                                                              in1=xt[:, s, half * 512:(half + 1) * 512], op=ALU.add),
                             reads=[p, xt], writes=[xo])
                    S.dma("sp", dst[t0 + s * 128:t0 + (s + 1) * 128, :], xo[:], reads=[xo], dram_w=[dst])

    with Phase():
        stg = ring("stg", [128, 2212], F32, 2)
        WI = WT("WI", 8, 2212)
        load_weight(WI, I["a_w_in"][0], 8, 2212, stg, col_chunk=2212)
        GA0 = build_gains("GA0", 22, [(I["a_q_norm"][0:1, :], 0, 12, 0.125), (I["a_k_norm"][0, 1:2, :], 12, 3, 1.0),
                                     (I["a_k_norm"][0, 2:3, :], 15, 3, 1.0), (I["mem_q_norm"][0:1, :], 18, 4, 0.125)])
        seg0 = [(0, 0, 768, "A", 0), (0, 1152, 1344, "A", 12), (0, 1536, 1728, "A", 15), (0, 1920, 2176, "A", 18),
                (0, 768, 960, "T", 22), (0, 960, 1152, "T", 25),
                (0, 1344, 1536, "V", 0), (0, 1728, 1920, "V", 3), (0, 2176, 2212, "G", 0)]
        proj_phase(xin, SEQ // 128, [(gT_attn[0], WI, 2212)], 22, 18, GA0, None, featT0, vtok0, 6, gate_cols=True, seg_map=seg0)

    kcT = sb("kcT", [64, 3, 256], BF16, persist=True)
    VCO = sb("VCO", [128, 2, 3, 128], BF16, persist=True)
    with Phase():
        GAc = build_gains("GAc", 1, [(I["a_k_norm"][0, 0:1, :], 0, 1, 1.0)])
        for ct in range(2):
            for g in range(3):
                S.dma("sp", VCO[:, ct, g, 64:128], C["c_ovl"][ct * 128:(ct + 1) * 128, :], writes=[VCO])
        XCr = ring("XC", [64, SEQ], BF16, 2)
        W1s = sb("W1s", [64, 32, 256], F32)
        W1 = sb("W1", [64, 32, 256], BF16)
        W2s = sb("W2s", [128, 2, 64], F32)
        W2 = sb("W2", [128, 2, 64], BF16)
        posS = sb("posS", [64, 32], F32); posT = sb("posT", [64, 32], BF16)
        b1c = sb("b1c", [128, 2], F32); BT = sb("BT", [128, 2], F32)
        b2s = sb("b2s", [1, 64], F32); b2r = sb("b2r", [1, 64], BF16)
        HG = sb("HG", [128, 2, 256], BF16)
        S.op("pool", lambda e: e.memset(HG[:], 0.0), writes=[HG])
        ph = [ps(f"ph{i}", [128, 512], F32) for i in range(2)]
        pbias = ps("pbias", [128, 2], F32)
        pout = ps("pout", [128, 64], F32)
        tpf = ps("tpf", [128, 8, 128], BF16)
        CSc = sb("CSc", [128, 2, 2, 32], F32)
        S.op("pool", lambda e: e.memset(CSc[:], 0.0), writes=[CSc])
        for ct in range(2):
            n = 128 if ct == 0 else 127
            S.dma("sp", CSc[0:n, ct, 0, :], C["c_cos"].rearrange("(c s) f -> c s f", s=16)[ct * 128 + 1:ct * 128 + 1 + n, 15, :], writes=[CSc])
            S.dma("sp", CSc[0:n, ct, 1, :], C["c_sin"].rearrange("(c s) f -> c s f", s=16)[ct * 128 + 1:ct * 128 + 1 + n, 15, :], writes=[CSc])
        YA = sb("YAc", [128, 1, 64], F32); YB = sb("YBc", [128, 2, 64], BF16)
        S.op("pool", lambda e: e.memset(YB[:], 0.0), writes=[YB])
        SQ = sb("SQ", [128, 1, 64], F32); SS = sb("SS", [128, 1]); RS = sb("RS", [128, 1]); RS2 = sb("RS2", [128, 1])
        YN = sb("YN", [128, 1, 64], F32); T1 = sb("T1", [128, 1, 64], F32); T2 = sb("T2", [128, 1, 64], F32)
        CS1 = sb("CS1", [128, 2, 32], F32)
        TTc = sb("TTc", [128, 1, 128], BF16)
        for i in range(2):
            S.dma("sp", W1s[:], I["a_cmp_w1"][0, i].rearrange("(j d) h -> d j h", d=64), writes=[W1s])
            S.op("dve", lambda e: e.tensor_copy(out=W1[:, 0:16, :], in_=W1s[:, 0:16, :]), reads=[W1s], writes=[W1])
            S.op("pool", lambda e: e.tensor_copy(out=W1[:, 16:32, :], in_=W1s[:, 16:32, :]), reads=[W1s], writes=[W1])
            S.dma("sp", W2s[:], I["a_cmp_w2"][0, i].rearrange("(hf p) d -> p hf d", p=128), writes=[W2s])
            S.op("dve", lambda e: e.tensor_copy(out=W2[:], in_=W2s[:]), reads=[W2s], writes=[W2])
            S.dma("sp", posS[:], I["a_cmp_pos"][0, i].rearrange("j d -> d j"), writes=[posS], allow_slow_non_contiguous=True)
            S.op("dve", lambda e: e.tensor_copy(out=posT[:], in_=posS[:]), reads=[posS], writes=[posT])
            S.dma("sp", b1c[:], I["a_cmp_b1"][0, i:i + 1, :].rearrange("o (hf p) -> p (o hf)", p=128), writes=[b1c],
                  allow_slow_non_contiguous=True)
            S.dma("sp", b2s[:], I["a_cmp_b2"][0, i:i + 1, :], writes=[b2s])
            S.op("dve", lambda e: e.tensor_copy(out=b2r[:], in_=b2s[:]), reads=[b2s], writes=[b2r])
            for hf in range(2):
                for j in range(32):
                    S.op("pe", lambda e: e.matmul(pbias[:, hf:hf + 1], lhsT=W1[:, j, hf * 128:(hf + 1) * 128], rhs=posT[:, j:j + 1],
                                                  start=(j == 0 and hf == 0), stop=(j == 31), skip_group_check=True),
                         reads=[W1, posT], writes=[pbias])
            S.op("dve", lambda e: e.tensor_tensor(out=BT[:], in0=pbias[:], in1=b1c[:], op=ALU.add), reads=[pbias, b1c], writes=[BT])
            for g in range(3):
                XC = XCr()
                S.dma("sp", XC[:], featT0[22 + 3 * i + g, :, :], writes=[XC], dram_r=[featT0])
                XCv = XC[:].rearrange("d (c s) -> d c s", s=16)
                for hf in range(2):
                    for j in range(32):
                        S.op("pe", lambda e: e.matmul(ph[hf][:, 0:255], lhsT=W1[:, j, hf * 128:(hf + 1) * 128],
                                                      rhs=XCv[:, (j // 16):(j // 16) + 255, j % 16], start=(j == 0), stop=(j == 31)),
                             reads=[W1, XC], writes=[ph[hf]])
                    S.op("act", lambda e: e.activation(out=HG[:, hf, 0:255], in_=ph[hf][:, 0:255], func=AF.Gelu_apprx_tanh,
                                                       bias=BT[:, hf:hf + 1]),
                         reads=[ph[hf], BT], writes=[HG])
                for ct in range(2):
                    for hf in range(2):
                        S.op("pe", lambda e: e.matmul(pout[:, :], lhsT=HG[:, hf, ct * 128:(ct + 1) * 128], rhs=W2[:, hf, :],
                                                      start=(hf == 0), stop=False),
                             reads=[HG, W2], writes=[pout])
                    S.op("pe", lambda e: e.matmul(pout[:, :], lhsT=ones_bf[0:1, :], rhs=b2r[0:1, :], start=False, stop=True),
                         reads=[ones_bf, b2r], writes=[pout])
                    if i == 0:
                        S.op("act", lambda e: e.copy(out=YA[:, 0, :], in_=pout[:, :]), reads=[pout], writes=[YA])
                        S.op("dve", lambda e: e.tensor_copy(out=CS1[:], in_=CSc[:, ct, :, :]), reads=[CSc], writes=[CS1])
                        run(head_norm(YA, 1, 1, GAc, CS1, YB, (SQ, SS, RS, RS2, YN, T1, T2)))
                        run(feat_transposes(YB, 1, tpf, TTc))
                        S.op("dve", lambda e: e.tensor_copy(out=kcT[:, g, ct * 128:(ct + 1) * 128], in_=TTc[0:64, 0, :]),
                             reads=[TTc], writes=[kcT])
                    else:
                        S.op("act", lambda e: e.copy(out=VCO[:, ct, g, 0:64], in_=pout[:, :]), reads=[pout], writes=[VCO])

    kmT0 = sb("kmT0", [64, 4, 256], BF16, persist=True)
    VM0 = sb("VM0", [128, 2, 4, 65], BF16, persist=True)
    mem_phase(0, kmT0, VM0)

    with Phase():
        stg = ring("stg", [128, 1024], F32, 1)
        WO = WT("WO", 8, 1024)
        load_weight(WO, I["a_w_out"][0], 8, 1024, stg)
        KA = sb("KA", [128, 3, SEQ], BF16)
        KW = sb("KW", [64, 3, SEQ], BF16)
        VSW = sb("VSW", [128, 32, 6, 65], BF16)
        S.dma("sp", KA[0:64, :, :], featT0.ap[12:15, :, :].rearrange("h d t -> d h t"), writes=[KA], dram_r=[featT0])
        for g in range(3):
            S.dma("sp", KA[64:128, g, :], C["c_onehot"][:, :], writes=[KA])
        S.dma("sp", KW[:, :, :], featT0.ap[15:18, :, :].rearrange("h d t -> d h t"), writes=[KW], dram_r=[featT0])
        for kt0 in range(0, 32, 8):
            S.dma("sp", VSW[:, kt0:kt0 + 8, :, :], vtok0.ap[kt0 * 128:(kt0 + 8) * 128].rearrange("(kt p) h d -> p kt h d", p=128),
                  writes=[VSW], dram_r=[vtok0])
        msel = sb("msel", [128, 896], BF16); mwin = sb("mwin", [128, 1408], BF16)
        mcmp = sb("mcmp", [128, 512], F32); tkeep = sb("tkeep", [128, 128], F32); tadd = sb("tadd", [128, 128], F32)
        for t, k in ((msel, "m_sel"), (mwin, "m_win"), (mcmp, "m_cmp"), (tkeep, "m_tkeep"), (tadd, "m_tadd")):
            S.dma("sp", t[:], C[k][:, :], writes=[t])
        QAr = ring("QA", [128, 16, 512], BF16, 1)
        GTr = ring("GT", [128, 4, 36], F32, 1)
        XRr = ring("XR", [128, 4, DM], F32, 1)
        OM = sb("OM", [128, 4, DM], F32)
        Bk = [ps(f"bk{i}", [128, 512], F32) for i in range(7)]
        TP = ps("tp", [128, 8, 128], BF16)
        ST = bank_ring(Bk[0:3])
        ACC = bank_ring([acc_view(b) for b in Bk[3:7]])
        PO = Bk[6]
        SCsets = [(Bk[0], Bk[1]), (Bk[2], Bk[3])]
        OCI = [b.view(b.ap[:, 0:512].rearrange("p (r d) -> p r d", d=64)) for b in Bk[4:7]]
        PTr = ring("PT", [128, 512], BF16, LOOK + 2)
        SCm = ring("SCm", [128, 256], F32, 4)
        PCr = ring("PC", [128, 256], F32, 4)
        for t in PCr.tiles:
            S.op("pool", lambda e: e.memset(t[:], 0.0), writes=[t])
        PNr = ring("PN", [128, 256], BF16, 24)
        PTc = ring("PTc", [128, 8, 128], BF16, 3)
        rsr = ring("rsr", [128, 4, 2], F32, 2)
        TK = [dict(I1=sb("I1", [128, 64], F32), I2=sb("I2", [128, 64], F32), I3=sb("I3", [128, 64], F32),
                   M8=sb("M8", [128, 16], F32), SEL=sb("SEL", [128, 64], F32), VAL=sb("VAL", [128, 64], F32)) for _ in range(3)]
        MTr = ring("MT", [128, 128], BF16, 4)
        for t in MTr.tiles:
            S.op("pool", lambda e: e.memset(t[:], 0.0), writes=[t])
        Rt = sb("Rt", [128, 4], F32); Ft = sb("Ft", [128, 4], F32); Tt = sb("Tt", [128, 4, 64], F32)
        OBr = ring("OB", [128, DM], BF16, 1); OTr = ring("OT", [128, 8, 128], BF16, 1); XOr = ring("XO", [128, DM], F32, 1)
        for qi in range(SEQ // 512):
            Q0 = qi * 512
            QA = QAr(); GT = GTr(); XR = XRr()
            S.dma("sp", QA[0:64, 0:12, :], featT0.ap[0:12, :, Q0:Q0 + 512].rearrange("h d t -> d h t"), writes=[QA], dram_r=[featT0])
            S.dma("sp", QA[0:64, 12:16, :], featT0.ap[18:22, :, Q0:Q0 + 512].rearrange("h d t -> d h t"), writes=[QA], dram_r=[featT0])
            S.dma("sp", GT[:], gates.ap[Q0:Q0 + 512, :].rearrange("(s p) c -> p s c", p=128), writes=[GT], dram_r=[gates])
            S.op("act", lambda e: e.activation(out=GT[:], in_=GT[:], func=AF.Sigmoid), reads=[GT], writes=[GT])
            S.dma("sp", XR[:], xin.ap[Q0:Q0 + 512, :].rearrange("(s p) d -> p s d", p=128), writes=[XR])

            def stepA_front(n, s, g):
                i128 = qi * 4 + s
                bA, bB = SCsets[n % 2]
                scv = [(bA if r < 2 else bB) for r in range(4)]
                off = 248 - 8 * i128
                pns = []
                for r in range(4):
                    h = g * 4 + r
                    c0 = (r % 2) * 256
                    S.op("pe", lambda e: e.matmul(scv[r][:, c0:c0 + 255], lhsT=QA[0:64, h, s * 128:(s + 1) * 128], rhs=kcT[:, g, 0:255],
                                                  start=True, stop=True), reads=[QA, kcT], writes=[scv[r]])
                    yield
                scms = []
                for r in range(4):
                    c0 = (r % 2) * 256
                    scm = SCm()
                    scms.append(scm)
                    S.op("dve", lambda e: e.tensor_tensor(out=scm[:, 0:255], in0=scv[r][:, c0:c0 + 255], in1=mcmp[:, off:off + 255], op=ALU.add),
                         reads=[scv[r], mcmp], writes=[scm])
                    yield
                rs4 = rsr()
                S.op("pool", lambda e: e.memset(rs4[:], 0.0), writes=[rs4])
                yield
                pcs = []
                for r in range(4):
                    pc = PCr()
                    pcs.append(pc)
                    S.op("act", lambda e: e.activation(out=pc[:, 0:255], in_=scms[r][:, 0:255], func=AF.Exp, accum_out=rs4[:, r, 0:1]),
                         reads=[scms[r], rs4], writes=[pc, rs4])
                    yield
                S.op("dve", lambda e: e.tensor_scalar(out=rs4[:, :, 1], in0=rs4[:, :, 0], scalar1=1e-30, scalar2=None, op0=ALU.add),
                     reads=[rs4], writes=[rs4])
                yield
                S.op("dve", lambda e: e.reciprocal(out=rs4[:, :, 0], in_=rs4[:, :, 1]), reads=[rs4], writes=[rs4])
                yield
                for r in range(4):
                    pn = PNr()
                    pns.append(pn)
                    S.op("dve" if r % 2 == 0 else "pool",
                         lambda e: e.tensor_scalar(out=pn[:], in0=pcs[r][:], scalar1=rs4[:, r, 0:1], scalar2=None, op0=ALU.mult),
                         reads=[pcs[r], rs4], writes=[pn])
                    yield
                ctxs[(s, g)] = pns

            def stepA_back(ci, s, g):
                pns = ctxs.pop((s, g))
                i128 = qi * 4 + s
                oci = OCI[ci]
                tk = TK[ci]
                I1, I2, I3, M8, SEL, VAL = tk["I1"], tk["I2"], tk["I3"], tk["M8"], tk["SEL"], tk["VAL"]
                for r in range(4):
                    for ct in range(2):
                        S.op("pe", lambda e: e.transpose(out=TP[:, r * 2 + ct, :], in_=pns[r][:, ct * 128:(ct + 1) * 128], identity=ident[:]),
                             reads=[pns[r], ident], writes=[TP])
                ptc = PTc()
                S.op("act", lambda e: e.copy(out=ptc[:], in_=TP[:, 0:8, :]), reads=[TP], writes=[ptc])
                yield
                for r in range(4):
                    for ct in range(2):
                        S.op("pe", lambda e: e.matmul(oci[:, r, :], lhsT=ptc[:, r * 2 + ct, :], rhs=VCO[:, ct, g, 0:64], start=(ct == 0), stop=(ct == 1),
                                                      skip_group_check=True),
                             reads=[ptc, VCO], writes=[oci])
                    for ct in range(2):
                        S.op("pe", lambda e: e.matmul(oci[:, 4 + r, :], lhsT=ptc[:, r * 2 + ct, :], rhs=VCO[:, ct, g, 64:128], start=(ct == 0), stop=(ct == 1),
                                                      skip_group_check=True),
                             reads=[ptc, VCO], writes=[oci])
                    yield
                gc = GT[:, s, 12 * g:12 * g + 12].rearrange("p (r b) -> p r b", b=3)[:, :, 0]
                S.op("dve", lambda e: e.tensor_tensor(out=OM[:, s, 256 * g:256 * g + 256].rearrange("p (r d) -> p r d", d=64), in0=oci[:, 0:4, :],
                                                      in1=gc.unsqueeze(2).to_broadcast([128, 4, 64]), op=ALU.mult),
                     reads=[oci, GT], writes=[OM])
                yield
                toff = 62 - 2 * i128
                S.op("dve", lambda e: e.tensor_reduce(out=I3[:], in_=oci[:, 4:8, :].rearrange("p r j -> p j r"), axis=AX.X, op=ALU.add),
                     reads=[oci], writes=[I3])
                yield
                S.op("dve", lambda e: e.tensor_tensor(out=I1[:], in0=I3[:], in1=tkeep[:, toff:toff + 64], op=ALU.mult),
                     reads=[I3, tkeep], writes=[I1])
                yield
                S.op("dve", lambda e: e.tensor_tensor(out=I2[:], in0=I1[:], in1=tadd[:, toff:toff + 64], op=ALU.add),
                     reads=[I1, tadd], writes=[I2])
                yield
                S.op("dve", lambda e: e.memset(I2[:, 0:1], 3e9), reads=[I2], writes=[I2])
                yield
                S.op("dve", lambda e: e.max(out=M8[:, 0:8], in_=I2[:]), reads=[I2], writes=[M8])
                yield
                S.op("dve", lambda e: e.match_replace(out=I3[:], in_to_replace=M8[:, 0:8], in_values=I2[:], imm_value=-1e30),
                     reads=[I2, M8], writes=[I3])
                yield
                S.op("dve", lambda e: e.max(out=M8[:, 8:16], in_=I3[:]), reads=[I3], writes=[M8])
                yield
                S.op("dve", lambda e: e.tensor_scalar(out=SEL[:], in0=I2[:], scalar1=M8[:, 15:16], scalar2=None, op0=ALU.is_ge),
                     reads=[I2, M8], writes=[SEL])
                yield
                S.op("dve", lambda e: e.tensor_scalar(out=VAL[:], in0=I2[:], scalar1=-1e29, scalar2=None, op0=ALU.is_gt),
                     reads=[I2], writes=[VAL])
                yield
                S.op("dve", lambda e: e.tensor_tensor(out=SEL[:], in0=SEL[:], in1=VAL[:], op=ALU.mult), reads=[SEL, VAL], writes=[SEL])
                yield
                MT = MTr()
                S.op("dve", lambda e: e.tensor_scalar(out=MT[:, 64:128], in0=SEL[:], scalar1=-1.0, scalar2=-NBIG, op0=ALU.add, op1=ALU.mult),
                     reads=[SEL], writes=[MT])
                yield
                S.op("pe", lambda e: e.transpose(out=TP[:, 0, :], in_=MT[:], identity=ident[:]), reads=[MT, ident], writes=[TP])
                S.op("act", lambda e: e.copy(out=QA[64:128, 4 * g:4 * g + 4, s * 128:(s + 1) * 128],
                                             in_=TP[64:128, 0:1, :].to_broadcast([64, 4, 128])), reads=[TP], writes=[QA])
                yield

            def fronts(s):
                for g in range(3):
                    yield from stepA_front(s * 3 + g, s, g)

            ctxs = {}
            run(fronts(0))
            for s in range(4):
                gens = [(stepA_back(g, s, g), 24) for g in range(3)]
                if s + 1 < 4:
                    gens.append((fronts(s + 1), 60))
                interleave(gens)

            nkt = qi * 4 + 4
            jobs = []
            for h in range(12):
                g = h // 4
                accS = ACC(); accW = ACC()
                for kt in range(nkt):
                    D = Q0 - kt * 128
                    subs = [sx for sx in range(4) if D + 128 * sx >= 0]
                    mt_, mo_ = (msel, D + 384) if D < 128 else (None, None)
                    jobs.append(tile_job(KA[:, g, kt * 128:(kt + 1) * 128], (lambda c0, c1, h=h: QA[:, h, c0:c1]), [KA, QA], mt_, mo_,
                                         accS, VSW[:, kt, g, :], VSW, subs, kt == 0))
                kts = list(range(max(0, qi * 4 - 4), nkt))
                for kt in kts:
                    D = Q0 - kt * 128
                    subs = [sx for sx in range(4) if 0 <= D + 128 * sx <= 512]
                    jobs.append(tile_job(KW[:, g, kt * 128:(kt + 1) * 128], (lambda c0, c1, h=h: QA[0:64, h, c0:c1]), [KW, QA], mwin, D + 384,
                                         accW, VSW[:, kt, 3 + g, :], VSW, subs, kt == kts[0]))
                jobs.append(fin_job(accS, OM, h * 64, (GT[:, :, 3 * h + 1], GT), (Rt, Ft, Tt), True))
                jobs.append(fin_job(accW, OM, h * 64, (GT[:, :, 3 * h + 2], GT), (Rt, Ft, Tt), True))
            jobs += mem_jobs(QA[:, 12:16, :], QA, kmT0, VM0, ACC, OM, 768, (Rt, Ft, Tt))
            run_jobs(jobs, ST, PTr)
            out_proj(OM, 1024, WO, XR, x1, Q0, TP, PO, OBr, OTr, XOr)

    mlp_phase(0, x1, x2)

    with Phase():
        stg = ring("stg", [128, 1792], F32, 2)
        WB = WT("WB", 8, 1792)
        WK = WT("WK", 8, 1024)
        load_weight(WB, I["b_w_in"][0], 8, 1792, stg, col_chunk=1792)
        load_weight(WK, I["w_kv_shared"], 8, 1024, stg, col_chunk=1024)
        GA1 = build_gains("GA1", 36, [(I["b_q_norm"][0, 0:1, :], 0, 8, 0.125), (I["b_q_norm"][0, 1:2, :], 8, 8, 0.125),
                                     (I["b_q_norm"][0, 2:3, :], 16, 8, 0.125), (I["kv_k_norm"][0:1, :], 24, 8, 1.0),
                                     (I["mem_q_norm"][1:2, :], 32, 4, 0.125)])
        seg1 = [(0, 0, 1536, "A", 0), (1, 0, 512, "A", 24), (0, 1536, 1792, "A", 32), (1, 512, 1024, "V", 0)]
        proj_phase(x2, SEQ // 128, [(gT_attn[1], WB, 1792), (gT_kv, WK, 1024)], 36, 32, GA1, None, featT1, vtok1, 8, seg_map=seg1)

    kmT1 = sb("kmT1", [64, 4, 256], BF16, persist=True)
    VM1 = sb("VM1", [128, 2, 4, 65], BF16, persist=True)
    mem_phase(1, kmT1, VM1)

    with Phase():
        stg = ring("stg", [128, 1024], F32, 2)
        WO1 = WT("WO1", 6, 1024)
        load_weight(WO1, I["b_w_out"][0], 6, 1024, stg)
        KT = sb("KT", [128, 4, SEQ], BF16)
        V1 = sb("V1", [128, 32, 8, 65], BF16)
        S.dma("sp", KT[:, :, :], featT1.ap[24:32, :, :].rearrange("(j h2) d t -> (h2 d) j t", h2=2), writes=[KT], dram_r=[featT1])
        for kt0 in range(0, 32, 8):
            S.dma("sp", V1[:, kt0:kt0 + 8, :, :], vtok1.ap[kt0 * 128:(kt0 + 8) * 128].rearrange("(kt p) h d -> p kt h d", p=128),
                  writes=[V1], dram_r=[vtok1])
        md = []
        for k, w in (("m_d0", 1024), ("m_d1", 1408), ("m_d2", 2944)):
            t = sb(k, [128, w], BF16)
            S.dma("sp", t[:], C[k][:, :], writes=[t])
            md.append(t)
        QTr = ring("QT", [128, 12, 512], BF16, 2)
        QMr = ring("QM", [64, 4, 512], BF16, 2)
        XRr = ring("XR", [128, 4, DM], F32, 1)
        OM = sb("OM", [128, 4, 768], F32)
        Bk = [ps(f"bk{i}", [128, 512], F32) for i in range(7)]
        TP = ps("tp", [128, 8, 128], BF16)
        ST = bank_ring(Bk[0:3])
        ACC = bank_ring([acc_view(b) for b in Bk[3:6]])
        PO = Bk[6]
        PTr = ring("PT", [128, 512], BF16, LOOK + 2)
        Rt = sb("Rt", [128, 4], F32); Ft = sb("Ft", [128, 4], F32); Tt = sb("Tt", [128, 4, 64], F32)
        OBr = ring("OB", [128, DM], BF16, 2); OTr = ring("OT", [128, 8, 128], BF16, 2); XOr = ring("XO", [128, DM], F32, 2)
        pats = ((128, 1), (512, 4), (2048, 16))
        for qi in range(SEQ // 512):
            Q0 = qi * 512
            QT = QTr(); QM = QMr(); XR = XRr()
            S.dma("sp", QT[:, :, :], featT1.ap[0:24, :, Q0:Q0 + 512].rearrange("(j h2) d t -> (h2 d) j t", h2=2), writes=[QT], dram_r=[featT1])
            S.dma("sp", QM[:, :, :], featT1.ap[32:36, :, Q0:Q0 + 512].rearrange("h d t -> d h t"), writes=[QM], dram_r=[featT1])
            S.dma("sp", XR[:], x2.ap[Q0:Q0 + 512, :].rearrange("(s p) d -> p s d", p=128), writes=[XR], dram_r=[x2])
            jobs = []
            for hh in range(8):
                acc = ACC()
                p0 = (hh % 2) * 64
                first = True
                for gi, (window, dil) in enumerate(pats):
                    qh = gi * 8 + hh
                    for kt in range(max(0, (Q0 - window) // 128), qi * 4 + 4):
                        D = Q0 - kt * 128
                        subs = [sx for sx in range(4) if 0 <= D + 128 * sx <= window]
                        jobs.append(tile_job(KT[p0:p0 + 64, hh // 2, kt * 128:(kt + 1) * 128],
                                             (lambda c0, c1, p0=p0, qh=qh: QT[p0:p0 + 64, qh // 2, c0:c1]), [KT, QT],
                                             md[gi], D + 384, acc, V1[:, kt, hh, :], V1, subs, first))
                        first = False
                jobs.append(fin_job(acc, OM, hh * 64, None, (Rt, Ft, Tt), False))
            jobs += mem_jobs(QM, QM, kmT1, VM1, ACC, OM, 512, (Rt, Ft, Tt))
            run_jobs(jobs, ST, PTr)
            out_proj(OM, 768, WO1, XR, x3, Q0, TP, PO, OBr, OTr, XOr)

    mlp_phase(1, x3, outT)
    S.barrier()
    gst.close()
    return nc


_CACHE = {}


def kernel(**inputs):
    n = 8
    consts = host_consts()
    if "nc" not in _CACHE:
        _CACHE["nc"] = build_program()
    nc = _CACHE["nc"]
    in_maps = []
    for b in range(n):
        m = {}
        for k, shp in IN_SHAPES.items():
            a = np.asarray(inputs[k], dtype=np.float32)
            if k in ("x", "mem"):
                a = a[b]
            m[k] = np.ascontiguousarray(a.reshape(shp))
        m.update(consts)
        in_maps.append(m)
    res = run_bass_kernel_spmd(nc, in_maps, core_ids=list(range(n)))
    return np.stack([np.asarray(r["out"], dtype=np.float32) for r in res.results], axis=0)
```

```python
import numpy as np
import ml_dtypes
import concourse.bass as bass
import concourse.mybir as mybir
from concourse.bass_utils import run_bass_kernel_spmd
from contextlib import ExitStack

F32 = mybir.dt.float32
BF16 = mybir.dt.bfloat16
AF = mybir.ActivationFunctionType
ALU = mybir.AluOpType
AX = mybir.AxisListType
SEM_ROT = 30000
SEQ = 4096
DM = 1024
EPS = 1e-6
NBIG = -30000.0


class Tile:
    __slots__ = ("name", "w", "r", "dsem", "ap", "base")

    def __init__(self, name, ap=None, base=None):
        self.name = name
        self.w = None
        self.r = []
        self.dsem = None
        self.ap = ap
        self.base = base

    def view(self, ap):
        return Tile(self.name + "_v", ap, base=(self.base or self))

    def __getitem__(self, k):
        return self.ap[k]


class DramT:
    def __init__(self, ap, name="d"):
        self.ap = ap
        self.name = name
        self.pend = set()

    def __getitem__(self, k):
        return self.ap[k]


class Sched:
    def __init__(self, nc):
        self.nc = nc
        self.eng = {"pe": nc.tensor, "act": nc.scalar, "dve": nc.vector,
                    "pool": nc.gpsimd, "sp": nc.sync}
        self.esem = {}
        self.ecnt = {}
        self.seen = {e: {} for e in self.eng}
        self.latest = {}
        self.nsem = 0
        self.free_dsems = []
        for e in ("pe", "act", "dve", "pool"):
            self._new_esem(e)

    def _alloc(self, name):
        self.nsem += 1
        return self.nc.alloc_semaphore(name=f"{name}_{self.nsem}")

    def _new_esem(self, e):
        self.esem[e] = self._alloc("e" + e)
        self.ecnt[e] = 0

    def tile(self, ap, name="t"):
        return Tile(name, ap)

    def _need(self, engine, reads, writes):
        need = {}
        reads = [t.base or t for t in reads]
        writes = [t.base or t for t in writes]

        def add(ev, raw):
            if ev is None:
                return
            sem, val, kind, eng = ev
            if kind == "c" and eng == engine:
                if engine == "pe":
                    return
            if kind == "d":
                val = self.latest[sem]
            if need.get(sem, 0) < val:
                need[sem] = val

        for t in reads:
            add(t.w, True)
        for t in writes:
            add(t.w, False)
            for ev in t.r:
                add(ev, False)
        seen = self.seen[engine]
        out = []
        for sem, val in need.items():
            if seen.get(sem, 0) < val:
                seen[sem] = val
                out.append((sem, val))
        return out

    def _record(self, ev, reads, writes):
        reads = [t.base or t for t in reads]
        writes = [t.base or t for t in writes]
        for t in reads:
            t.r = [x for x in t.r if x[0] is not ev[0]]
            t.r.append(ev)
        for t in writes:
            t.w = ev
            t.r = []

    def op(self, engine, fn, reads=(), writes=()):
        e = self.eng[engine]
        for sem, val in self._need(engine, reads, writes):
            e.wait_ge(sem, val)
        if self.ecnt[engine] >= SEM_ROT:
            self._new_esem(engine)
        inst = fn(e)
        self.ecnt[engine] += 1
        inst.then_inc(self.esem[engine], 1)
        ev = (self.esem[engine], self.ecnt[engine], "c", engine)
        self._record(ev, reads, writes)
        return ev

    def free_dsem(self, tiles):
        for t in tiles:
            if t.dsem is not None:
                self.free_dsems.append(t.dsem)
                t.dsem = None

    def dma(self, queue, out, in_, reads=(), writes=(), dram_r=(), dram_w=(), **kw):
        e = self.eng[queue]
        stile = writes[0] if writes else reads[0]
        stile = stile.base or stile
        if stile.dsem is None or self.latest[stile.dsem] >= SEM_ROT:
            stile.dsem = None
            while self.free_dsems and stile.dsem is None:
                c = self.free_dsems.pop()
                if self.latest[c] < SEM_ROT:
                    stile.dsem = c
            if stile.dsem is None:
                stile.dsem = self._alloc("d")
                self.latest[stile.dsem] = 0
        waits = self._need(queue, reads, writes)
        seen = self.seen[queue]
        for d in dram_r:
            for sem in d.pend:
                val = self.latest[sem]
                if seen.get(sem, 0) < val:
                    seen[sem] = val
                    waits.append((sem, val))
        for sem, val in waits:
            e.wait_ge(sem, val)
        inst = e.dma_start(out=out, in_=in_, **kw)
        sem = stile.dsem
        self.latest[sem] += 16
        inst.then_inc(sem, 16)
        ev = (sem, self.latest[sem], "d", None)
        self._record(ev, reads, writes)
        for d in dram_w:
            d.pend.add(sem)
        return ev

    def barrier(self):
        for en, e in self.eng.items():
            seen = self.seen[en]
            for f in ("pe", "act", "dve", "pool"):
                if f == en:
                    continue
                sem, val = self.esem[f], self.ecnt[f]
                if val > 0 and seen.get(sem, 0) < val:
                    seen[sem] = val
                    e.wait_ge(sem, val)
            for sem, val in self.latest.items():
                if val > 0 and seen.get(sem, 0) < val:
                    seen[sem] = val
                    e.wait_ge(sem, val)


def host_consts():
    bf = ml_dtypes.bfloat16
    c = {}
    half = 32
    inv = (10000.0 ** (-np.arange(half, dtype=np.float32) / half)).astype(np.float32)
    ang = np.arange(SEQ + 32, dtype=np.float32)[:, None] * inv[None, :]
    c["c_cos"] = np.cos(ang).astype(np.float32)
    c["c_sin"] = np.sin(ang).astype(np.float32)
    c["c_ident"] = np.eye(128, dtype=np.float32).astype(bf)
    kk = np.arange(128)[:, None]

    def toep(width, f):
        j = np.arange(width)[None, :]
        return f(j - 384 - kk).astype(np.float32).astype(bf)

    c["m_sel"] = toep(896, lambda d: d >= 0)
    c["m_win"] = toep(1408, lambda d: (d >= 0) & (d < 512))
    c["m_d0"] = toep(1024, lambda d: (d >= 0) & (d <= 128))
    c["m_d1"] = toep(1408, lambda d: (d >= 0) & (d <= 512) & (d % 4 == 0))
    c["m_d2"] = toep(2944, lambda d: (d >= 0) & (d <= 2048) & (d % 16 == 0))
    u = np.arange(512)[None, :]
    c["m_cmp"] = np.where(16 * (u - 248) + 31 <= kk, 0.0, NBIG).astype(np.float32)
    u = np.arange(128)[None, :]
    rel = u - 62 - (kk // 64)
    keep = (rel < -1).astype(np.float32)
    add = np.where(rel == 0, 2e9, np.where(rel == -1, 1e9, np.where(rel > 0, -1e30, 0.0)))
    c["m_tkeep"] = keep.astype(np.float32)
    c["m_tadd"] = add.astype(np.float32)
    n_cmp = SEQ // 16 - 1
    cs = np.arange(256)[:, None] * 16
    bs = np.arange(64)[None, :] * 64
    ov = ((cs < bs + 64) & (cs + 32 > bs)).astype(np.float32)
    ov[n_cmp:] = 0
    c["c_ovl"] = ov.astype(bf)
    c["c_onehot"] = (np.arange(SEQ)[None, :] // 64 == np.arange(64)[:, None]).astype(np.float32).astype(bf)
    return c


CONST_SHAPES = {
    "c_cos": ([SEQ + 32, 32], F32), "c_sin": ([SEQ + 32, 32], F32), "c_ident": ([128, 128], BF16),
    "m_sel": ([128, 896], BF16), "m_win": ([128, 1408], BF16), "m_d0": ([128, 1024], BF16),
    "m_d1": ([128, 1408], BF16), "m_d2": ([128, 2944], BF16), "m_cmp": ([128, 512], F32),
    "m_tkeep": ([128, 128], F32), "m_tadd": ([128, 128], F32), "c_ovl": ([256, 64], BF16),
    "c_onehot": ([64, SEQ], BF16),
}

IN_SHAPES = {
    "x": [SEQ, DM], "mem": [256, DM], "attn_norm": [2, DM], "mlp_norm": [2, DM],
    "w_up": [2, DM, 4096], "w_down": [2, 4096, DM], "mem_norm": [2, DM], "w_mem_kv": [2, DM, 512],
    "mem_q_norm": [2, 64], "mem_k_norm": [2, 64], "a_w_in": [1, DM, 2212], "a_w_out": [1, DM, DM],
    "a_q_norm": [1, 64], "a_k_norm": [1, 3, 64], "a_cmp_pos": [1, 2, 32, 64],
    "a_cmp_w1": [1, 2, 2048, 256], "a_cmp_b1": [1, 2, 256], "a_cmp_w2": [1, 2, 256, 64],
    "a_cmp_b2": [1, 2, 64], "kv_norm": [1, DM], "w_kv_shared": [DM, DM], "kv_k_norm": [1, 64],
    "b_w_in": [1, DM, 1792], "b_w_out": [1, 768, DM], "b_q_norm": [1, 3, 64],
}


def build_program(debug=()):
    nc = bass.Bass("TRN2", target_bir_lowering=False)
    S = Sched(nc)
    I = {}
    for k, shp in IN_SHAPES.items():
        I[k] = nc.dram_tensor(k, shp, F32, kind="ExternalInput").ap()
    C = {}
    for k, (shp, dt) in CONST_SHAPES.items():
        C[k] = nc.dram_tensor(k, shp, dt, kind="ExternalInput").ap()
    out_ap = nc.dram_tensor("out", [SEQ, DM], F32, kind="ExternalOutput").ap()

    def scratch(name, shape, dt):
        kind = "ExternalOutput" if name in debug else "Internal"
        return DramT(nc.dram_tensor(name, shape, dt, kind=kind).ap(), name)

    featT0 = scratch("featT0", [28, 64, SEQ], BF16)
    vtok0 = scratch("vtok0", [SEQ, 6, 65], BF16)
    gates = scratch("gates", [SEQ, 36], F32)
    x1 = scratch("x1", [SEQ, DM], F32)
    x2 = scratch("x2", [SEQ, DM], F32)
    x3 = scratch("x3", [SEQ, DM], F32)
    featT1 = scratch("featT1", [36, 64, SEQ], BF16)
    vtok1 = scratch("vtok1", [SEQ, 8, 65], BF16)
    outT = DramT(out_ap, "out")
    xin = DramT(I["x"], "x")

    gst = ExitStack()
    cur = {"st": gst}

    uid = {"i": 0}
    local_tiles = []

    def sb(name, shape, dt=F32, persist=False):
        st = gst if persist else cur["st"]
        uid["i"] += 1
        t = S.tile(st.enter_context(nc.sbuf_tensor(f"{name}_{uid['i']}", list(shape), dt)), name)
        if not persist:
            local_tiles.append(t)
        return t

    def ps(name, shape, dt=F32):
        uid["i"] += 1
        full = [128, 512] if dt == F32 else [128, 1024]
        h = cur["st"].enter_context(nc.psum_tensor(f"{name}_{uid['i']}", full, dt))
        n = 1
        for d in shape[1:]:
            n *= d
        v = h[:, 0:n]
        if len(shape) == 3:
            v = v.rearrange("p (a b) -> p a b", b=shape[2])
        return S.tile(v, name)

    class Phase:
        def __enter__(self):
            self.st = ExitStack()
            cur["st"] = self.st
            return self

        def __exit__(self, *a):
            S.barrier()
            S.free_dsem(local_tiles)
            del local_tiles[:]
            self.st.close()
            cur["st"] = gst
            return False

    rr = {"i": 0}

    def alt(engs=("dve", "pool")):
        rr["i"] += 1
        return engs[rr["i"] % len(engs)]

    def ring(name, shape, dt, n):
        tiles = [sb(f"{name}{i}", shape, dt) for i in range(n)]
        st = {"i": -1}

        def nxt():
            st["i"] = (st["i"] + 1) % n
            return tiles[st["i"]]
        nxt.tiles = tiles
        return nxt

    ident = sb("ident", [128, 128], BF16, persist=True)
    S.dma("sp", ident[:], C["c_ident"][:, :], writes=[ident])
    ones_bf = sb("ones_bf", [128, 128], BF16, persist=True)
    S.op("dve", lambda e: e.memset(ones_bf[:], 1.0), writes=[ones_bf])

    def bcast_row(dst_ap, src_row_ap, tile):
        S.dma("sp", dst_ap, src_row_ap.partition_broadcast(128), writes=[tile])

    def load_gT(name, src_row):
        t = sb(name, [128, 8], F32, persist=True)
        S.dma("sp", t[:], src_row.rearrange("o (kc p) -> p (o kc)", p=128), writes=[t],
              allow_slow_non_contiguous=True)
        return t

    gT_attn = [load_gT(f"gT_attn{l}", I["attn_norm"][l:l + 1, :]) for l in range(2)]
    gT_mlp = [load_gT(f"gT_mlp{l}", I["mlp_norm"][l:l + 1, :]) for l in range(2)]
    gT_mem = [load_gT(f"gT_mem{l}", I["mem_norm"][l:l + 1, :]) for l in range(2)]
    gT_kv = load_gT("gT_kv", I["kv_norm"][0:1, :])

    def load_weight(wt, src, nk, ncols, stg_ring, col_chunk=2048, engs=("dve", "act", "pool")):
        for kc in range(nk):
            eng = engs[kc % len(engs)]
            for c0 in range(0, ncols, col_chunk):
                w = min(col_chunk, ncols - c0)
                stg = stg_ring()
                S.dma("sp", stg[:, 0:w], src[kc * 128:(kc + 1) * 128, c0:c0 + w], writes=[stg])
                if eng == "act":
                    S.op(eng, lambda e: e.copy(out=wt.ap[:, kc, c0:c0 + w], in_=stg[:, 0:w]), reads=[stg], writes=[wt.k[kc]])
                else:
                    S.op(eng, lambda e: e.tensor_copy(out=wt.ap[:, kc, c0:c0 + w], in_=stg[:, 0:w]),
                         reads=[stg], writes=[wt.k[kc]])

    class WT:
        def __init__(self, name, nk, ncols):
            self.h = sb(name, [128, nk, ncols], BF16)
            self.ap = self.h.ap
            self.k = [Tile(f"{name}_{i}") for i in range(nk)]

    def run(gen):
        for _ in gen:
            pass

    def interleave(gens):
        state = [[g, 0, float(tot)] for g, tot in gens]
        while state:
            st = min(state, key=lambda z: z[1] / z[2])
            try:
                next(st[0])
                st[1] += 1
            except StopIteration:
                state.remove(st)

    def norm_tile(xt_ap, xt_tile, gTs, hTs, tph, tmp):
        junk, ssq, rs, rs2, xn = tmp
        S.op("pool", lambda e: e.memset(ssq[:], 0.0), writes=[ssq])
        yield
        S.op("act", lambda e: e.activation(out=junk[:], in_=xt_ap, func=AF.Square, accum_out=ssq[:, 0:1]),
             reads=[xt_tile, ssq], writes=[junk, ssq])
        yield
        S.op("act", lambda e: e.activation(out=rs[:], in_=ssq[:], func=AF.Sqrt, scale=1.0 / DM, bias=EPS),
             reads=[ssq], writes=[rs])
        yield
        S.op("dve", lambda e: e.reciprocal(out=rs2[:], in_=rs[:]), reads=[rs], writes=[rs2])
        yield
        S.op("dve", lambda e: e.tensor_scalar(out=xn[:], in0=xt_ap, scalar1=rs2[:, 0:1], scalar2=None, op0=ALU.mult),
             reads=[xt_tile, rs2], writes=[xn])
        yield
        for kc in range(8):
            S.op("pe", lambda e: e.transpose(out=tph[:, kc, :], in_=xn[:, kc * 128:(kc + 1) * 128], identity=ident[:]),
                 reads=[xn, ident], writes=[tph])
            yield
        for gT, hT in zip(gTs, hTs):
            S.op("dve", lambda e: e.tensor_tensor(out=hT[:], in0=tph[:], in1=gT[:].unsqueeze(2).to_broadcast([128, 8, 128]), op=ALU.mult),
                 reads=[tph, gT], writes=[hT])
            yield

    def head_norm(YA, nh, nrope, GA, CS, YB, tmp):
        SQ, SS, RS, RS2, YN, T1, T2 = tmp
        S.op("act", lambda e: e.activation(out=SQ[:, 0:nh, :], in_=YA[:, 0:nh, :], func=AF.Square),
             reads=[YA], writes=[SQ])
        yield
        S.op("dve", lambda e: e.tensor_reduce(out=SS[:, 0:nh], in_=SQ[:, 0:nh, :], axis=AX.X, op=ALU.add),
             reads=[SQ], writes=[SS])
        yield
        S.op("act", lambda e: e.activation(out=RS[:, 0:nh], in_=SS[:, 0:nh], func=AF.Sqrt, scale=1.0 / 64, bias=EPS),
             reads=[SS], writes=[RS])
        yield
        S.op("dve", lambda e: e.reciprocal(out=RS2[:, 0:nh], in_=RS[:, 0:nh]), reads=[RS], writes=[RS2])
        yield
        S.op("dve", lambda e: e.tensor_tensor(out=YN[:, 0:nh, :], in0=YA[:, 0:nh, :],
                                              in1=RS2[:, 0:nh].unsqueeze(2).to_broadcast([128, nh, 64]), op=ALU.mult),
             reads=[YA, RS2], writes=[YN])
        yield
        S.op("pool", lambda e: e.tensor_tensor(out=YN[:, 0:nh, :], in0=YN[:, 0:nh, :], in1=GA[:, 0:nh, :], op=ALU.mult),
             reads=[YN, GA], writes=[YN])
        yield
        if nrope:
            n = nrope
            cosb = CS[:, 0:1, :].to_broadcast([128, n, 32])
            sinb = CS[:, 1:2, :].to_broadcast([128, n, 32])
            x1v = YN[:, 0:n, 0:32]
            x2v = YN[:, 0:n, 32:64]
            S.op("dve", lambda e: e.tensor_tensor(out=T1[:, 0:n, 0:32], in0=x1v, in1=cosb, op=ALU.mult), reads=[YN, CS], writes=[T1])
            yield
            S.op("pool", lambda e: e.tensor_tensor(out=T2[:, 0:n, 0:32], in0=x2v, in1=sinb, op=ALU.mult), reads=[YN, CS], writes=[T2])
            yield
            S.op("dve", lambda e: e.tensor_tensor(out=YB[:, 0:n, 0:32], in0=T1[:, 0:n, 0:32], in1=T2[:, 0:n, 0:32], op=ALU.subtract),
                 reads=[T1, T2], writes=[YB])
            yield
            S.op("pool", lambda e: e.tensor_tensor(out=T1[:, 0:n, 32:64], in0=x1v, in1=sinb, op=ALU.mult), reads=[YN, CS], writes=[T1])
            yield
            S.op("dve", lambda e: e.tensor_tensor(out=T2[:, 0:n, 32:64], in0=x2v, in1=cosb, op=ALU.mult), reads=[YN, CS], writes=[T2])
            yield
            S.op("pool", lambda e: e.tensor_tensor(out=YB[:, 0:n, 32:64], in0=T1[:, 0:n, 32:64], in1=T2[:, 0:n, 32:64], op=ALU.add),
                 reads=[T1, T2], writes=[YB])
            yield
        if nh > nrope:
            S.op("act", lambda e: e.copy(out=YB[:, nrope:nh, :], in_=YN[:, nrope:nh, :]), reads=[YN], writes=[YB])
            yield

    def feat_transposes(YB, npairs, tpf, TT):
        YBf = YB[:].rearrange("p h d -> p (h d)")
        for b0 in range(0, npairs, 8):
            nb = min(8, npairs - b0)
            for j in range(nb):
                S.op("pe", lambda e: e.transpose(out=tpf[:, j, :], in_=YBf[:, (b0 + j) * 128:(b0 + j + 1) * 128], identity=ident[:]),
                     reads=[YB, ident], writes=[tpf])
                yield
            S.op("act", lambda e: e.copy(out=TT[:, b0:b0 + nb, :], in_=tpf[:, 0:nb, :]), reads=[tpf], writes=[TT])
            yield

    def build_gains(name, nh, specs):
        GA = sb(name, [128, nh, 64], F32)
        G1 = sb(name + "_raw", [128, len(specs), 64], F32)
        for i, (src, h0, n, scale) in enumerate(specs):
            bcast_row(G1[:, i, :], src, G1)
        for i, (src, h0, n, scale) in enumerate(specs):
            S.op("dve", lambda e: e.tensor_scalar(out=GA[:, h0:h0 + n, :], in0=G1[:, i:i + 1, :].to_broadcast([128, n, 64]),
                                                  scalar1=float(scale), scalar2=None, op0=ALU.mult),
                 reads=[G1], writes=[GA])
        return GA

    def proj_phase(x_src, ntiles, projs, nh, nrope, GA, raw_specs, featT, vtok, vtok_n, gate_cols=None,
                   seg_map=None, pos_tab=True):
        nraw_t = sum((c1 - c0) // 64 for (_, c0, c1, k, _) in seg_map if k == "T")
        ntot = nh + nraw_t
        npairs = ntot // 2
        xr = ring("xt", [128, DM], F32, 3)
        junkr = ring("junk", [128, DM], BF16, 2)
        ssqr = ring("ssq", [128, 1], F32, 2); rsr_ = ring("rs", [128, 1], F32, 2); rs2r = ring("rs2", [128, 1], F32, 2)
        xnr = ring("xn", [128, DM], BF16, 2)
        tph = ps("tph", [128, 8, 128], BF16)
        tpf = ps("tpf", [128, 8, 128], BF16)
        hTr = [ring(f"hT{i}", [128, 8, 128], BF16, 2) for i in range(len(projs))]
        banks = []
        for pi, (gT, W, ncols) in enumerate(projs):
            for c0 in range(0, ncols, 512):
                banks.append((pi, c0, min(512, ncols - c0), ps(f"pb{pi}_{c0}", [128, 512], F32)))
        YAr = ring("YA", [128, ntot, 64], F32, 2)
        YBr = ring("YB", [128, ntot, 64], BF16, 2)
        SQ = sb("SQ", [128, nh, 64], F32); SS = sb("SS", [128, nh]); RS = sb("RS", [128, nh]); RS2 = sb("RS2", [128, nh])
        YN = sb("YN", [128, nh, 64], F32); T1 = sb("T1", [128, max(nrope, 1), 64], F32); T2 = sb("T2", [128, max(nrope, 1), 64], F32)
        CSr = ring("CS", [128, 2, 32], F32, 6)
        TTr = ring("TT", [128, npairs, 128], BF16, 2)
        YVr = ring("YV", [128, vtok_n, 65], BF16, 2) if vtok is not None else None
        if YVr is not None:
            for t in YVr.tiles:
                S.op("pool", lambda e: e.memset(t[:], 1.0), writes=[t])
        YGr = ring("YG", [128, 36], F32, 2) if gate_cols else None
        ctx = {}

        def s0(ti):
            t0 = ti * 128
            c = ctx[ti] = {}
            xt = c["xt"] = xr()
            S.dma("sp", xt[:], x_src[t0:t0 + 128, :], writes=[xt], dram_r=[x_src])
            CS = CSr()
            c["CS"] = CS
            if nrope:
                S.dma("sp", CS[:, 0, :], C["c_cos"][t0:t0 + 128, :], writes=[CS])
                S.dma("sp", CS[:, 1, :], C["c_sin"][t0:t0 + 128, :], writes=[CS])

        def s1(ti):
            c = ctx[ti]
            xt = c["xt"]
            c["hT"] = [r() for r in hTr]
            yield from norm_tile(xt[:], xt, [p[0] for p in projs], c["hT"], tph, (junkr(), ssqr(), rsr_(), rs2r(), xnr()))

        def s2(ti):
            c = ctx[ti]
            hTs = c["hT"]
            for (pi, c0, w, pb) in banks:
                W = projs[pi][1]
                for kc in range(8):
                    S.op("pe", lambda e: e.matmul(pb[:, 0:w], lhsT=hTs[pi][:, kc, :], rhs=W.ap[:, kc, c0:c0 + w],
                                                  start=(kc == 0), stop=(kc == 7)),
                         reads=[hTs[pi], W.k[kc]], writes=[pb])
                    yield
            YA = c["YA"] = YAr()
            YB = c["YB"] = YBr()
            YV = c["YV"] = YVr() if YVr is not None else None
            YG = c["YG"] = YGr() if YGr is not None else None
            for (pi, c0, c1, kind, di) in seg_map:
                cc = c0
                while cc < c1:
                    bk = [b for b in banks if b[0] == pi and b[1] <= cc < b[1] + b[2]][0]
                    ce = min(c1, bk[1] + bk[2])
                    src = bk[3][:, cc - bk[1]:ce - bk[1]]
                    off = cc - c0
                    n = ce - cc
                    if kind in ("A", "T"):
                        dst = YA if kind == "A" else YB
                        dflat = dst[:].rearrange("p h d -> p (h d)")
                        S.op("act", lambda e: e.copy(out=dflat[:, di * 64 + off: di * 64 + off + n], in_=src),
                             reads=[bk[3]], writes=[dst])
                        yield
                    elif kind == "V":
                        h0 = di + off // 64
                        S.op("act", lambda e: e.copy(out=YV[:, h0:h0 + n // 64, 0:64],
                                                     in_=src.rearrange("p (h d) -> p h d", d=64)),
                             reads=[bk[3]], writes=[YV])
                        yield
                    elif kind == "G":
                        S.op("act", lambda e: e.copy(out=YG[:, off:off + n], in_=src), reads=[bk[3]], writes=[YG])
                        yield
                    cc = ce

        def s3(ti):
            t0 = ti * 128
            c = ctx.pop(ti)
            yield from head_norm(c["YA"], nh, nrope, GA, c["CS"], c["YB"], (SQ, SS, RS, RS2, YN, T1, T2))
            TT = TTr()
            yield from feat_transposes(c["YB"], npairs, tpf, TT)
            S.dma("sp", featT.ap.rearrange("(j h2) d t -> (h2 d) j t", h2=2)[:, :, t0:t0 + 128], TT[:],
                  reads=[TT], dram_w=[featT])
            yield
            if vtok is not None:
                S.dma("sp", vtok[t0:t0 + 128, :, :], c["YV"][:], reads=[c["YV"]], dram_w=[vtok])
                yield
            if c["YG"] is not None:
                S.dma("sp", gates[t0:t0 + 128, :], c["YG"][:], reads=[c["YG"]], dram_w=[gates])
                yield

        s0(0)
        s0(1)
        for step in range(ntiles + 2):
            if step + 2 < ntiles:
                s0(step + 2)
            gens = []
            if step < ntiles:
                gens.append((s1(step), 10))
            if 0 <= step - 1 < ntiles:
                gens.append((s2(step - 1), 56))
            if 0 <= step - 2 < ntiles:
                gens.append((s3(step - 2), 28))
            interleave(gens)

    def mem_phase(layer, kmT, VM):
        with Phase():
            stg = ring("stg", [128, 2048], F32, 2)
            W = WT("wmem", 8, 512)
            load_weight(W, I["w_mem_kv"][layer], 8, 512, stg)
            GA = build_gains(f"GAm{layer}", 4, [(I["mem_k_norm"][layer:layer + 1, :], 0, 4, 1.0)])
            xr = ring("xt", [128, DM], F32, 2)
            junk = sb("junk", [128, DM], BF16)
            ssq = sb("ssq", [128, 1]); rs = sb("rs", [128, 1]); rs2 = sb("rs2", [128, 1])
            xn = sb("xn", [128, DM], BF16)
            tph = ps("tph", [128, 8, 128], BF16)
            tpf = ps("tpf", [128, 8, 128], BF16)
            hT = sb("hT", [128, 8, 128], BF16)
            pb = ps("pb", [128, 512], F32)
            YA = sb("YA", [128, 4, 64], F32); YB = sb("YB", [128, 4, 64], BF16)
            SQ = sb("SQ", [128, 4, 64], F32); SS = sb("SS", [128, 4]); RS = sb("RS", [128, 4]); RS2 = sb("RS2", [128, 4])
            YN = sb("YN", [128, 4, 64], F32); T1 = sb("T1", [128, 1, 64], F32); T2 = sb("T2", [128, 1, 64], F32)
            TT = sb("TT", [128, 2, 128], BF16)
            S.op("pool", lambda e: e.memset(VM[:], 1.0), writes=[VM])
            memT = DramT(I["mem"], "mem")
            for mt in range(2):
                xt = xr()
                S.dma("sp", xt[:], memT[mt * 128:(mt + 1) * 128, :], writes=[xt])
                run(norm_tile(xt[:], xt, [gT_mem[layer]], [hT], tph, (junk, ssq, rs, rs2, xn)))
                for kc in range(8):
                    S.op("pe", lambda e: e.matmul(pb[:, :], lhsT=hT[:, kc, :], rhs=W.ap[:, kc, :], start=(kc == 0), stop=(kc == 7)),
                         reads=[hT, W.k[kc]], writes=[pb])
                S.op("act", lambda e: e.copy(out=YA[:].rearrange("p h d -> p (h d)"), in_=pb[:, 0:256]), reads=[pb], writes=[YA])
                S.op("act", lambda e: e.copy(out=VM[:, mt, :, 0:64], in_=pb[:, 256:512].rearrange("p (h d) -> p h d", d=64)),
                     reads=[pb], writes=[VM])
                run(head_norm(YA, 4, 0, GA, None, YB, (SQ, SS, RS, RS2, YN, T1, T2)))
                run(feat_transposes(YB, 2, tpf, TT))
                for mh in range(4):
                    S.op("dve", lambda e: e.tensor_copy(out=kmT[:, mh, mt * 128:(mt + 1) * 128],
                                                        in_=TT[(mh % 2) * 64:(mh % 2) * 64 + 64, mh // 2, :]),
                         reads=[TT], writes=[kmT])

    LOOK = 3

    def tile_job(lhsT_ap, rhs_fn, reads, mask_tile, mask_off, acc, v_ap, v_tile, subs, first):
        return ["tile", dict(lhsT=lhsT_ap, rhs_fn=rhs_fn, reads=reads, mt=mask_tile, mo=mask_off, acc=acc, v=v_ap,
                             vt=v_tile, subs=subs, first=first)]

    def run_jobs(jobs, ST, PTr):
        tiles = [j[1] for j in jobs if j[0] == "tile"]
        pos = {"fi": 0, "ti": 0}

        def front(j):
            c0 = min(j["subs"]) * 128
            c1 = (max(j["subs"]) + 1) * 128
            st = ST()
            pt = PTr()
            j["pt"] = pt
            S.op("pe", lambda e: e.matmul(st[:, c0:c1], lhsT=j["lhsT"], rhs=j["rhs_fn"](c0, c1), start=True, stop=True),
                 reads=j["reads"], writes=[st])
            S.op("act", lambda e: e.activation(out=pt[:, c0:c1], in_=st[:, c0:c1], func=AF.Exp), reads=[st], writes=[pt])
            if j["mt"] is not None:
                mo = j["mo"]
                S.op("dve", lambda e: e.tensor_tensor(out=pt[:, c0:c1], in0=pt[:, c0:c1], in1=j["mt"][:, mo + c0:mo + c1], op=ALU.mult),
                     reads=[pt, j["mt"]], writes=[pt])

        def back(j):
            pt = j["pt"]
            for i, sidx in enumerate(j["subs"]):
                S.op("pe", lambda e: e.matmul(j["acc"][:, sidx, :], lhsT=pt[:, sidx * 128:(sidx + 1) * 128], rhs=j["v"],
                                              start=(j["first"] and i == 0), stop=False, skip_group_check=True),
                     reads=[pt, j["vt"]], writes=[j["acc"]])

        for j in jobs:
            if j[0] == "tile":
                while pos["fi"] < len(tiles) and pos["fi"] <= pos["ti"] + LOOK:
                    front(tiles[pos["fi"]])
                    pos["fi"] += 1
                back(j[1])
                pos["ti"] += 1
            else:
                j[1]()

    def finalize(acc, OM, col0, fac_fn, tmp, accumulate):
        R, F, T = tmp
        S.op("dve", lambda e: e.reciprocal(out=R[:], in_=acc[:, :, 64]), reads=[acc], writes=[R])
        fac = R
        if fac_fn is not None:
            gap, gtile = fac_fn
            S.op("dve", lambda e: e.tensor_tensor(out=F[:], in0=R[:], in1=gap, op=ALU.mult), reads=[R, gtile], writes=[F])
            fac = F
        facb = fac[:].unsqueeze(2).to_broadcast([128, 4, 64])
        if not accumulate:
            S.op("dve", lambda e: e.tensor_tensor(out=OM[:, :, col0:col0 + 64], in0=acc[:, :, 0:64], in1=facb, op=ALU.mult),
                 reads=[acc, fac], writes=[OM])
        else:
            S.op("dve", lambda e: e.tensor_tensor(out=T[:], in0=acc[:, :, 0:64], in1=facb, op=ALU.mult), reads=[acc, fac], writes=[T])
            S.op("pool", lambda e: e.tensor_tensor(out=OM[:, :, col0:col0 + 64], in0=OM[:, :, col0:col0 + 64], in1=T[:], op=ALU.add),
                 reads=[OM, T], writes=[OM])

    def out_proj(OM, ncols_in, WO, Xres, dst, Q0, tp, po_ring, OBr, OTr, XOr):
        nk = ncols_in // 128
        for s in range(4):
            OB = OBr()
            S.op("dve", lambda e: e.tensor_copy(out=OB[:, 0:ncols_in], in_=OM[:, s, :]), reads=[OM], writes=[OB])
            for kc in range(nk):
                S.op("pe", lambda e: e.transpose(out=tp[:, kc, :], in_=OB[:, kc * 128:(kc + 1) * 128], identity=ident[:]),
                     reads=[OB, ident], writes=[tp])
            OT = OTr()
            S.op("act", lambda e: e.copy(out=OT[:, 0:nk, :], in_=tp[:, 0:nk, :]), reads=[tp], writes=[OT])
            XO = XOr()
            for half in range(2):
                po = po_ring()
                for kc in range(nk):
                    S.op("pe", lambda e: e.matmul(po[:, :], lhsT=OT[:, kc, :], rhs=WO.ap[:, kc, half * 512:(half + 1) * 512],
                                                  start=(kc == 0), stop=(kc == nk - 1)),
                         reads=[OT, WO.k[kc]], writes=[po])
                S.op("dve", lambda e: e.tensor_tensor(out=XO[:, half * 512:(half + 1) * 512], in0=po[:, :],
                                                      in1=Xres[:, s, half * 512:(half + 1) * 512], op=ALU.add),
                     reads=[po, Xres], writes=[XO])
            S.dma("sp", dst[Q0 + s * 128:Q0 + (s + 1) * 128, :], XO[:], reads=[XO], dram_w=[dst])

    def fin_job(acc, OM, col0, fac_fn, tmp, accumulate):
        return ["fin", lambda: finalize(acc, OM, col0, fac_fn, tmp, accumulate)]

    def mem_jobs(QM, qm_tile, kmT, VM, ACC, OM, col0, tmp):
        jobs = []
        for mh in range(4):
            acc = ACC()
            for mt in range(2):
                jobs.append(tile_job(kmT[:, mh, mt * 128:(mt + 1) * 128], (lambda c0, c1, mh=mh: QM[0:64, mh, c0:c1]), [kmT, qm_tile],
                                     None, None, acc, VM[:, mt, mh, :], VM, [0, 1, 2, 3], mt == 0))
            jobs.append(fin_job(acc, OM, col0 + mh * 64, None, tmp, False))
        return jobs

    def bank_ring(tiles):
        st = {"i": -1}

        def nxt():
            st["i"] = (st["i"] + 1) % len(tiles)
            return tiles[st["i"]]
        return nxt

    def acc_view(b):
        return b.view(b.ap[:, 0:260].rearrange("p (s c) -> p s c", c=65))

    def mlp_phase(layer, src, dst):
        with Phase():
            stg = ring("stg", [128, 1024], F32, 2)
            WU = WT("WU", 8, 4096)
            WD = WT("WD", 32, 1024)
            load_weight(WU, I["w_up"][layer], 8, 4096, stg, col_chunk=1024)
            xr = ring("xt", [128, 2, DM], F32, 2)
            junk = sb("junk", [128, DM], BF16)
            ssq = sb("ssq", [128, 1]); rs = sb("rs", [128, 1]); rs2 = sb("rs2", [128, 1])
            xn = sb("xn", [128, DM], BF16)
            tph = ps("tph", [128, 8, 128], BF16)
            hT = sb("hT", [128, 8, 256], BF16)
            hTa = sb("hTa", [128, 8, 128], BF16)
            uT = sb("uT", [128, 32, 256], BF16)
            rl = ring("rl", [128, 256], BF16, 3)
            XO = ring("XO", [128, DM], F32, 2)
            pu = [ps(f"pu{i}", [128, 512], F32) for i in range(3)]
            pd = [ps(f"pd{i}", [128, 512], F32) for i in range(3)]
            cnt = 0
            xts = {}

            def ldx(tb):
                xts[tb] = xr()
                S.dma("sp", xts[tb][:], src.ap[tb * 256:tb * 256 + 256, :].rearrange("(s p) d -> p s d", p=128),
                      writes=[xts[tb]], dram_r=[src])
            ldx(0)
            for tb in range(SEQ // 256):
                t0 = tb * 256
                if tb + 1 < SEQ // 256:
                    ldx(tb + 1)
                xt = xts.pop(tb)
                for s in range(2):
                    run(norm_tile(xt[:, s, :], xt, [gT_mlp[layer]], [hTa], tph, (junk, ssq, rs, rs2, xn)))
                    S.op(alt(), lambda e: e.tensor_copy(out=hT[:, :, s * 128:(s + 1) * 128], in_=hTa[:]), reads=[hTa], writes=[hT])
                for fc in range(32):
                    p = pu[fc % 3]
                    for kc in range(8):
                        S.op("pe", lambda e: e.matmul(p[:, 0:256], lhsT=WU.ap[:, kc, fc * 128:(fc + 1) * 128], rhs=hT[:, kc, :],
                                                      start=(kc == 0), stop=(kc == 7)),
                             reads=[WU.k[kc], hT], writes=[p])
                    r = rl()
                    S.op("act", lambda e: e.activation(out=r[:], in_=p[:, 0:256], func=AF.Relu), reads=[p], writes=[r])
                    S.op(alt(), lambda e: e.tensor_tensor(out=uT[:, fc, :], in0=r[:], in1=r[:], op=ALU.mult), reads=[r], writes=[uT])
                if tb == 0:
                    load_weight(WD, I["w_down"][layer], 32, 1024, stg, col_chunk=1024, engs=("dve", "act"))
                for s in range(2):
                    xo = XO()
                    for half in range(2):
                        p = pd[cnt % 3]
                        cnt += 1
                        for kc in range(32):
                            S.op("pe", lambda e: e.matmul(p[:, :], lhsT=uT[:, kc, s * 128:(s + 1) * 128],
                                                          rhs=WD.ap[:, kc, half * 512:(half + 1) * 512],
                                                          start=(kc == 0), stop=(kc == 31)),
                                 reads=[uT, WD.k[kc]], writes=[p])
                        S.op("dve", lambda e: e.tensor_tensor(out=xo[:, half * 512:(half + 1) * 512], in0=p[:, :],
                                                              in1=xt[:, s, half * 512:(half + 1) * 512], op=ALU.add),
                             reads=[p, xt], writes=[xo])
                    S.dma("sp", dst[t0 + s * 128:t0 + (s + 1) * 128, :], xo[:], reads=[xo], dram_w=[dst])

    with Phase():
        stg = ring("stg", [128, 2212], F32, 2)
        WI = WT("WI", 8, 2212)
        load_weight(WI, I["a_w_in"][0], 8, 2212, stg, col_chunk=2212)
        GA0 = build_gains("GA0", 22, [(I["a_q_norm"][0:1, :], 0, 12, 0.125), (I["a_k_norm"][0, 1:2, :], 12, 3, 1.0),
                                     (I["a_k_norm"][0, 2:3, :], 15, 3, 1.0), (I["mem_q_norm"][0:1, :], 18, 4, 0.125)])
        seg0 = [(0, 0, 768, "A", 0), (0, 1152, 1344, "A", 12), (0, 1536, 1728, "A", 15), (0, 1920, 2176, "A", 18),
                (0, 768, 960, "T", 22), (0, 960, 1152, "T", 25),
                (0, 1344, 1536, "V", 0), (0, 1728, 1920, "V", 3), (0, 2176, 2212, "G", 0)]
        proj_phase(xin, SEQ // 128, [(gT_attn[0], WI, 2212)], 22, 18, GA0, None, featT0, vtok0, 6, gate_cols=True, seg_map=seg0)

    kcT = sb("kcT", [64, 3, 256], BF16, persist=True)
    VCO = sb("VCO", [128, 2, 3, 128], BF16, persist=True)
    with Phase():
        GAc = build_gains("GAc", 1, [(I["a_k_norm"][0, 0:1, :], 0, 1, 1.0)])
        for ct in range(2):
            for g in range(3):
                S.dma("sp", VCO[:, ct, g, 64:128], C["c_ovl"][ct * 128:(ct + 1) * 128, :], writes=[VCO])
        XCr = ring("XC", [64, SEQ], BF16, 2)
        W1s = sb("W1s", [64, 32, 256], F32)
        W1 = sb("W1", [64, 32, 256], BF16)
        W2s = sb("W2s", [128, 2, 64], F32)
        W2 = sb("W2", [128, 2, 64], BF16)
        posS = sb("posS", [64, 32], F32); posT = sb("posT", [64, 32], BF16)
        b1c = sb("b1c", [128, 2], F32); BT = sb("BT", [128, 2], F32)
        b2s = sb("b2s", [1, 64], F32); b2r = sb("b2r", [1, 64], BF16)
        HG = sb("HG", [128, 2, 256], BF16)
        S.op("pool", lambda e: e.memset(HG[:], 0.0), writes=[HG])
        ph = [ps(f"ph{i}", [128, 512], F32) for i in range(2)]
        pbias = ps("pbias", [128, 2], F32)
        pout = ps("pout", [128, 64], F32)
        tpf = ps("tpf", [128, 8, 128], BF16)
        CSc = sb("CSc", [128, 2, 2, 32], F32)
        S.op("pool", lambda e: e.memset(CSc[:], 0.0), writes=[CSc])
        for ct in range(2):
            n = 128 if ct == 0 else 127
            S.dma("sp", CSc[0:n, ct, 0, :], C["c_cos"].rearrange("(c s) f -> c s f", s=16)[ct * 128 + 1:ct * 128 + 1 + n, 15, :], writes=[CSc])
            S.dma("sp", CSc[0:n, ct, 1, :], C["c_sin"].rearrange("(c s) f -> c s f", s=16)[ct * 128 + 1:ct * 128 + 1 + n, 15, :], writes=[CSc])
        YA = sb("YAc", [128, 1, 64], F32); YB = sb("YBc", [128, 2, 64], BF16)
        S.op("pool", lambda e: e.memset(YB[:], 0.0), writes=[YB])
        SQ = sb("SQ", [128, 1, 64], F32); SS = sb("SS", [128, 1]); RS = sb("RS", [128, 1]); RS2 = sb("RS2", [128, 1])
        YN = sb("YN", [128, 1, 64], F32); T1 = sb("T1", [128, 1, 64], F32); T2 = sb("T2", [128, 1, 64], F32)
        CS1 = sb("CS1", [128, 2, 32], F32)
        TTc = sb("TTc", [128, 1, 128], BF16)
        for i in range(2):
            S.dma("sp", W1s[:], I["a_cmp_w1"][0, i].rearrange("(j d) h -> d j h", d=64), writes=[W1s])
            S.op("dve", lambda e: e.tensor_copy(out=W1[:, 0:16, :], in_=W1s[:, 0:16, :]), reads=[W1s], writes=[W1])
            S.op("pool", lambda e: e.tensor_copy(out=W1[:, 16:32, :], in_=W1s[:, 16:32, :]), reads=[W1s], writes=[W1])
            S.dma("sp", W2s[:], I["a_cmp_w2"][0, i].rearrange("(hf p) d -> p hf d", p=128), writes=[W2s])
            S.op("dve", lambda e: e.tensor_copy(out=W2[:], in_=W2s[:]), reads=[W2s], writes=[W2])
            S.dma("sp", posS[:], I["a_cmp_pos"][0, i].rearrange("j d -> d j"), writes=[posS], allow_slow_non_contiguous=True)
            S.op("dve", lambda e: e.tensor_copy(out=posT[:], in_=posS[:]), reads=[posS], writes=[posT])
            S.dma("sp", b1c[:], I["a_cmp_b1"][0, i:i + 1, :].rearrange("o (hf p) -> p (o hf)", p=128), writes=[b1c],
                  allow_slow_non_contiguous=True)
            S.dma("sp", b2s[:], I["a_cmp_b2"][0, i:i + 1, :], writes=[b2s])
            S.op("dve", lambda e: e.tensor_copy(out=b2r[:], in_=b2s[:]), reads=[b2s], writes=[b2r])
            for hf in range(2):
                for j in range(32):
                    S.op("pe", lambda e: e.matmul(pbias[:, hf:hf + 1], lhsT=W1[:, j, hf * 128:(hf + 1) * 128], rhs=posT[:, j:j + 1],
                                                  start=(j == 0 and hf == 0), stop=(j == 31), skip_group_check=True),
                         reads=[W1, posT], writes=[pbias])
            S.op("dve", lambda e: e.tensor_tensor(out=BT[:], in0=pbias[:], in1=b1c[:], op=ALU.add), reads=[pbias, b1c], writes=[BT])
            for g in range(3):
                XC = XCr()
                S.dma("sp", XC[:], featT0[22 + 3 * i + g, :, :], writes=[XC], dram_r=[featT0])
                XCv = XC[:].rearrange("d (c s) -> d c s", s=16)
                for hf in range(2):
                    for j in range(32):
                        S.op("pe", lambda e: e.matmul(ph[hf][:, 0:255], lhsT=W1[:, j, hf * 128:(hf + 1) * 128],
                                                      rhs=XCv[:, (j // 16):(j // 16) + 255, j % 16], start=(j == 0), stop=(j == 31)),
                             reads=[W1, XC], writes=[ph[hf]])
                    S.op("act", lambda e: e.activation(out=HG[:, hf, 0:255], in_=ph[hf][:, 0:255], func=AF.Gelu_apprx_tanh,
                                                       bias=BT[:, hf:hf + 1]),
                         reads=[ph[hf], BT], writes=[HG])
                for ct in range(2):
                    for hf in range(2):
                        S.op("pe", lambda e: e.matmul(pout[:, :], lhsT=HG[:, hf, ct * 128:(ct + 1) * 128], rhs=W2[:, hf, :],
                                                      start=(hf == 0), stop=False),
                             reads=[HG, W2], writes=[pout])
                    S.op("pe", lambda e: e.matmul(pout[:, :], lhsT=ones_bf[0:1, :], rhs=b2r[0:1, :], start=False, stop=True),
                         reads=[ones_bf, b2r], writes=[pout])
                    if i == 0:
                        S.op("act", lambda e: e.copy(out=YA[:, 0, :], in_=pout[:, :]), reads=[pout], writes=[YA])
                        S.op("dve", lambda e: e.tensor_copy(out=CS1[:], in_=CSc[:, ct, :, :]), reads=[CSc], writes=[CS1])
                        run(head_norm(YA, 1, 1, GAc, CS1, YB, (SQ, SS, RS, RS2, YN, T1, T2)))
                        run(feat_transposes(YB, 1, tpf, TTc))
                        S.op("dve", lambda e: e.tensor_copy(out=kcT[:, g, ct * 128:(ct + 1) * 128], in_=TTc[0:64, 0, :]),
                             reads=[TTc], writes=[kcT])
                    else:
                        S.op("act", lambda e: e.copy(out=VCO[:, ct, g, 0:64], in_=pout[:, :]), reads=[pout], writes=[VCO])

    kmT0 = sb("kmT0", [64, 4, 256], BF16, persist=True)
    VM0 = sb("VM0", [128, 2, 4, 65], BF16, persist=True)
    mem_phase(0, kmT0, VM0)

    with Phase():
        stg = ring("stg", [128, 1024], F32, 1)
        WO = WT("WO", 8, 1024)
        load_weight(WO, I["a_w_out"][0], 8, 1024, stg)
        KA = sb("KA", [128, 3, SEQ], BF16)
        KW = sb("KW", [64, 3, SEQ], BF16)
        VSW = sb("VSW", [128, 32, 6, 65], BF16)
        S.dma("sp", KA[0:64, :, :], featT0.ap[12:15, :, :].rearrange("h d t -> d h t"), writes=[KA], dram_r=[featT0])
        for g in range(3):
            S.dma("sp", KA[64:128, g, :], C["c_onehot"][:, :], writes=[KA])
        S.dma("sp", KW[:, :, :], featT0.ap[15:18, :, :].rearrange("h d t -> d h t"), writes=[KW], dram_r=[featT0])
        for kt0 in range(0, 32, 8):
            S.dma("sp", VSW[:, kt0:kt0 + 8, :, :], vtok0.ap[kt0 * 128:(kt0 + 8) * 128].rearrange("(kt p) h d -> p kt h d", p=128),
                  writes=[VSW], dram_r=[vtok0])
        msel = sb("msel", [128, 896], BF16); mwin = sb("mwin", [128, 1408], BF16)
        mcmp = sb("mcmp", [128, 512], F32); tkeep = sb("tkeep", [128, 128], F32); tadd = sb("tadd", [128, 128], F32)
        for t, k in ((msel, "m_sel"), (mwin, "m_win"), (mcmp, "m_cmp"), (tkeep, "m_tkeep"), (tadd, "m_tadd")):
            S.dma("sp", t[:], C[k][:, :], writes=[t])
        QAr = ring("QA", [128, 16, 512], BF16, 1)
        GTr = ring("GT", [128, 4, 36], F32, 1)
        XRr = ring("XR", [128, 4, DM], F32, 1)
        OM = sb("OM", [128, 4, DM], F32)
        Bk = [ps(f"bk{i}", [128, 512], F32) for i in range(7)]
        TP = ps("tp", [128, 8, 128], BF16)
        ST = bank_ring(Bk[0:4])
        ACC = bank_ring([acc_view(b) for b in Bk[4:7]])
        PO = bank_ring(Bk[0:2])
        SCsets = [(Bk[0], Bk[1]), (Bk[2], Bk[3])]
        OCI = [b.view(b.ap[:, 0:512].rearrange("p (r d) -> p r d", d=64)) for b in Bk[4:7]]
        PTr = ring("PT", [128, 512], BF16, LOOK + 2)
        SCm = ring("SCm", [128, 256], F32, 4)
        PCr = ring("PC", [128, 256], F32, 4)
        for t in PCr.tiles:
            S.op("pool", lambda e: e.memset(t[:], 0.0), writes=[t])
        PNr = ring("PN", [128, 256], BF16, 24)
        PTc = ring("PTc", [128, 8, 128], BF16, 3)
        rsr = ring("rsr", [128, 4, 2], F32, 2)
        TK = [dict(I1=sb("I1", [128, 64], F32), I2=sb("I2", [128, 64], F32), I3=sb("I3", [128, 64], F32),
                   M8=sb("M8", [128, 16], F32), SEL=sb("SEL", [128, 64], F32), VAL=sb("VAL", [128, 64], F32)) for _ in range(3)]
        MTr = ring("MT", [128, 128], BF16, 4)
        for t in MTr.tiles:
            S.op("pool", lambda e: e.memset(t[:], 0.0), writes=[t])
        Rt = sb("Rt", [128, 4], F32); Ft = sb("Ft", [128, 4], F32); Tt = sb("Tt", [128, 4, 64], F32)
        OBr = ring("OB", [128, DM], BF16, 1); OTr = ring("OT", [128, 8, 128], BF16, 1); XOr = ring("XO", [128, DM], F32, 1)
        for qi in range(SEQ // 512):
            Q0 = qi * 512
            QA = QAr(); GT = GTr(); XR = XRr()
            S.dma("sp", QA[0:64, 0:12, :], featT0.ap[0:12, :, Q0:Q0 + 512].rearrange("h d t -> d h t"), writes=[QA], dram_r=[featT0])
            S.dma("sp", QA[0:64, 12:16, :], featT0.ap[18:22, :, Q0:Q0 + 512].rearrange("h d t -> d h t"), writes=[QA], dram_r=[featT0])
            S.dma("sp", GT[:], gates.ap[Q0:Q0 + 512, :].rearrange("(s p) c -> p s c", p=128), writes=[GT], dram_r=[gates])
            S.op("act", lambda e: e.activation(out=GT[:], in_=GT[:], func=AF.Sigmoid), reads=[GT], writes=[GT])
            S.dma("sp", XR[:], xin.ap[Q0:Q0 + 512, :].rearrange("(s p) d -> p s d", p=128), writes=[XR])

            def stepA_front(n, s, g):
                i128 = qi * 4 + s
                bA, bB = SCsets[n % 2]
                scv = [(bA if r < 2 else bB) for r in range(4)]
                off = 248 - 8 * i128
                pns = []
                for r in range(4):
                    h = g * 4 + r
                    c0 = (r % 2) * 256
                    S.op("pe", lambda e: e.matmul(scv[r][:, c0:c0 + 255], lhsT=QA[0:64, h, s * 128:(s + 1) * 128], rhs=kcT[:, g, 0:255],
                                                  start=True, stop=True), reads=[QA, kcT], writes=[scv[r]])
                    yield
                scms = []
                for r in range(4):
                    c0 = (r % 2) * 256
                    scm = SCm()
                    scms.append(scm)
                    S.op("dve", lambda e: e.tensor_tensor(out=scm[:, 0:255], in0=scv[r][:, c0:c0 + 255], in1=mcmp[:, off:off + 255], op=ALU.add),
                         reads=[scv[r], mcmp], writes=[scm])
                    yield
                rs4 = rsr()
                S.op("pool", lambda e: e.memset(rs4[:], 0.0), writes=[rs4])
                yield
                pcs = []
                for r in range(4):
                    pc = PCr()
                    pcs.append(pc)
                    S.op("act", lambda e: e.activation(out=pc[:, 0:255], in_=scms[r][:, 0:255], func=AF.Exp, accum_out=rs4[:, r, 0:1]),
                         reads=[scms[r], rs4], writes=[pc, rs4])
                    yield
                S.op("dve", lambda e: e.tensor_scalar(out=rs4[:, :, 1], in0=rs4[:, :, 0], scalar1=1e-30, scalar2=None, op0=ALU.add),
                     reads=[rs4], writes=[rs4])
                yield
                S.op("dve", lambda e: e.reciprocal(out=rs4[:, :, 0], in_=rs4[:, :, 1]), reads=[rs4], writes=[rs4])
                yield
                for r in range(4):
                    pn = PNr()
                    pns.append(pn)
                    S.op("act", lambda e: e.activation(out=pn[:], in_=pcs[r][:], func=AF.Copy, scale=rs4[:, r, 0:1]),
                         reads=[pcs[r], rs4], writes=[pn])
                    yield
                ctxs[(s, g)] = pns

            def stepA_back(ci, s, g):
                pns = ctxs.pop((s, g))
                i128 = qi * 4 + s
                oci = OCI[ci]
                tk = TK[ci]
                I1, I2, I3, M8, SEL, VAL = tk["I1"], tk["I2"], tk["I3"], tk["M8"], tk["SEL"], tk["VAL"]
                for r in range(4):
                    for ct in range(2):
                        S.op("pe", lambda e: e.transpose(out=TP[:, r * 2 + ct, :], in_=pns[r][:, ct * 128:(ct + 1) * 128], identity=ident[:]),
                             reads=[pns[r], ident], writes=[TP])
                ptc = PTc()
                S.op("act", lambda e: e.copy(out=ptc[:], in_=TP[:, 0:8, :]), reads=[TP], writes=[ptc])
                yield
                for r in range(4):
                    for ct in range(2):
                        S.op("pe", lambda e: e.matmul(oci[:, r, :], lhsT=ptc[:, r * 2 + ct, :], rhs=VCO[:, ct, g, 0:64], start=(ct == 0), stop=(ct == 1),
                                                      skip_group_check=True),
                             reads=[ptc, VCO], writes=[oci])
                    for ct in range(2):
                        S.op("pe", lambda e: e.matmul(oci[:, 4 + r, :], lhsT=ptc[:, r * 2 + ct, :], rhs=VCO[:, ct, g, 64:128], start=(ct == 0), stop=(ct == 1),
                                                      skip_group_check=True),
                             reads=[ptc, VCO], writes=[oci])
                    yield
                gc = GT[:, s, 12 * g:12 * g + 12].rearrange("p (r b) -> p r b", b=3)[:, :, 0]
                S.op("dve", lambda e: e.tensor_tensor(out=OM[:, s, 256 * g:256 * g + 256].rearrange("p (r d) -> p r d", d=64), in0=oci[:, 0:4, :],
                                                      in1=gc.unsqueeze(2).to_broadcast([128, 4, 64]), op=ALU.mult),
                     reads=[oci, GT], writes=[OM])
                yield
                toff = 62 - 2 * i128
                S.op("dve", lambda e: e.tensor_reduce(out=I3[:], in_=oci[:, 4:8, :].rearrange("p r j -> p j r"), axis=AX.X, op=ALU.add),
                     reads=[oci], writes=[I3])
                yield
                S.op("dve", lambda e: e.tensor_tensor(out=I1[:], in0=I3[:], in1=tkeep[:, toff:toff + 64], op=ALU.mult),
                     reads=[I3, tkeep], writes=[I1])
                yield
                S.op("dve", lambda e: e.tensor_tensor(out=I2[:], in0=I1[:], in1=tadd[:, toff:toff + 64], op=ALU.add),
                     reads=[I1, tadd], writes=[I2])
                yield
                S.op("dve", lambda e: e.memset(I2[:, 0:1], 3e9), reads=[I2], writes=[I2])
                yield
                S.op("dve", lambda e: e.max(out=M8[:, 0:8], in_=I2[:]), reads=[I2], writes=[M8])
                yield
                S.op("dve", lambda e: e.match_replace(out=I3[:], in_to_replace=M8[:, 0:8], in_values=I2[:], imm_value=-1e30),
                     reads=[I2, M8], writes=[I3])
                yield
                S.op("dve", lambda e: e.max(out=M8[:, 8:16], in_=I3[:]), reads=[I3], writes=[M8])
                yield
                S.op("dve", lambda e: e.tensor_scalar(out=SEL[:], in0=I2[:], scalar1=M8[:, 15:16], scalar2=None, op0=ALU.is_ge),
                     reads=[I2, M8], writes=[SEL])
                yield
                S.op("dve", lambda e: e.tensor_scalar(out=VAL[:], in0=I2[:], scalar1=-1e29, scalar2=None, op0=ALU.is_gt),
                     reads=[I2], writes=[VAL])
                yield
                S.op("dve", lambda e: e.tensor_tensor(out=SEL[:], in0=SEL[:], in1=VAL[:], op=ALU.mult), reads=[SEL, VAL], writes=[SEL])
                yield
                MT = MTr()
                S.op("dve", lambda e: e.tensor_scalar(out=MT[:, 64:128], in0=SEL[:], scalar1=-1.0, scalar2=-NBIG, op0=ALU.add, op1=ALU.mult),
                     reads=[SEL], writes=[MT])
                yield
                S.op("pe", lambda e: e.transpose(out=TP[:, 0, :], in_=MT[:], identity=ident[:]), reads=[MT, ident], writes=[TP])
                S.op("act", lambda e: e.copy(out=QA[64:128, 4 * g:4 * g + 4, s * 128:(s + 1) * 128],
                                             in_=TP[64:128, 0:1, :].to_broadcast([64, 4, 128])), reads=[TP], writes=[QA])
                yield

            def fronts(s):
                for g in range(3):
                    yield from stepA_front(s * 3 + g, s, g)

            ctxs = {}
            run(fronts(0))
            for s in range(4):
                gens = [(stepA_back(g, s, g), 24) for g in range(3)]
                if s + 1 < 4:
                    gens.append((fronts(s + 1), 60))
                interleave(gens)

            nkt = qi * 4 + 4
            jobs = []
            for h in range(12):
                g = h // 4
                accS = ACC(); accW = ACC()
                for kt in range(nkt):
                    D = Q0 - kt * 128
                    subs = [sx for sx in range(4) if D + 128 * sx >= 0]
                    mt_, mo_ = (msel, D + 384) if D < 128 else (None, None)
                    jobs.append(tile_job(KA[:, g, kt * 128:(kt + 1) * 128], (lambda c0, c1, h=h: QA[:, h, c0:c1]), [KA, QA], mt_, mo_,
                                         accS, VSW[:, kt, g, :], VSW, subs, kt == 0))
                kts = list(range(max(0, qi * 4 - 4), nkt))
                for kt in kts:
                    D = Q0 - kt * 128
                    subs = [sx for sx in range(4) if 0 <= D + 128 * sx <= 512]
                    jobs.append(tile_job(KW[:, g, kt * 128:(kt + 1) * 128], (lambda c0, c1, h=h: QA[0:64, h, c0:c1]), [KW, QA], mwin, D + 384,
                                         accW, VSW[:, kt, 3 + g, :], VSW, subs, kt == kts[0]))
                jobs.append(fin_job(accS, OM, h * 64, (GT[:, :, 3 * h + 1], GT), (Rt, Ft, Tt), True))
                jobs.append(fin_job(accW, OM, h * 64, (GT[:, :, 3 * h + 2], GT), (Rt, Ft, Tt), True))
            jobs += mem_jobs(QA[:, 12:16, :], QA, kmT0, VM0, ACC, OM, 768, (Rt, Ft, Tt))
            run_jobs(jobs, ST, PTr)
            out_proj(OM, 1024, WO, XR, x1, Q0, TP, PO, OBr, OTr, XOr)

    mlp_phase(0, x1, x2)

    with Phase():
        stg = ring("stg", [128, 1792], F32, 2)
        WB = WT("WB", 8, 1792)
        WK = WT("WK", 8, 1024)
        load_weight(WB, I["b_w_in"][0], 8, 1792, stg, col_chunk=1792)
        load_weight(WK, I["w_kv_shared"], 8, 1024, stg, col_chunk=1024)
        GA1 = build_gains("GA1", 36, [(I["b_q_norm"][0, 0:1, :], 0, 8, 0.125), (I["b_q_norm"][0, 1:2, :], 8, 8, 0.125),
                                     (I["b_q_norm"][0, 2:3, :], 16, 8, 0.125), (I["kv_k_norm"][0:1, :], 24, 8, 1.0),
                                     (I["mem_q_norm"][1:2, :], 32, 4, 0.125)])
        seg1 = [(0, 0, 1536, "A", 0), (1, 0, 512, "A", 24), (0, 1536, 1792, "A", 32), (1, 512, 1024, "V", 0)]
        proj_phase(x2, SEQ // 128, [(gT_attn[1], WB, 1792), (gT_kv, WK, 1024)], 36, 32, GA1, None, featT1, vtok1, 8, seg_map=seg1)

    kmT1 = sb("kmT1", [64, 4, 256], BF16, persist=True)
    VM1 = sb("VM1", [128, 2, 4, 65], BF16, persist=True)
    mem_phase(1, kmT1, VM1)

    with Phase():
        stg = ring("stg", [128, 1024], F32, 2)
        WO1 = WT("WO1", 6, 1024)
        load_weight(WO1, I["b_w_out"][0], 6, 1024, stg)
        KT = sb("KT", [128, 4, SEQ], BF16)
        V1 = sb("V1", [128, 32, 8, 65], BF16)
        S.dma("sp", KT[:, :, :], featT1.ap[24:32, :, :].rearrange("(j h2) d t -> (h2 d) j t", h2=2), writes=[KT], dram_r=[featT1])
        for kt0 in range(0, 32, 8):
            S.dma("sp", V1[:, kt0:kt0 + 8, :, :], vtok1.ap[kt0 * 128:(kt0 + 8) * 128].rearrange("(kt p) h d -> p kt h d", p=128),
                  writes=[V1], dram_r=[vtok1])
        md = []
        for k, w in (("m_d0", 1024), ("m_d1", 1408), ("m_d2", 2944)):
            t = sb(k, [128, w], BF16)
            S.dma("sp", t[:], C[k][:, :], writes=[t])
            md.append(t)
        QTr = ring("QT", [128, 12, 512], BF16, 2)
        QMr = ring("QM", [64, 4, 512], BF16, 2)
        XRr = ring("XR", [128, 4, DM], F32, 1)
        OM = sb("OM", [128, 4, 768], F32)
        Bk = [ps(f"bk{i}", [128, 512], F32) for i in range(7)]
        TP = ps("tp", [128, 8, 128], BF16)
        ST = bank_ring(Bk[0:4])
        ACC = bank_ring([acc_view(b) for b in Bk[4:7]])
        PO = bank_ring(Bk[0:2])
        PTr = ring("PT", [128, 512], BF16, LOOK + 2)
        Rt = sb("Rt", [128, 4], F32); Ft = sb("Ft", [128, 4], F32); Tt = sb("Tt", [128, 4, 64], F32)
        OBr = ring("OB", [128, DM], BF16, 2); OTr = ring("OT", [128, 8, 128], BF16, 2); XOr = ring("XO", [128, DM], F32, 2)
        pats = ((128, 1), (512, 4), (2048, 16))
        for qi in range(SEQ // 512):
            Q0 = qi * 512
            QT = QTr(); QM = QMr(); XR = XRr()
            S.dma("sp", QT[:, :, :], featT1.ap[0:24, :, Q0:Q0 + 512].rearrange("(j h2) d t -> (h2 d) j t", h2=2), writes=[QT], dram_r=[featT1])
            S.dma("sp", QM[:, :, :], featT1.ap[32:36, :, Q0:Q0 + 512].rearrange("h d t -> d h t"), writes=[QM], dram_r=[featT1])
            S.dma("sp", XR[:], x2.ap[Q0:Q0 + 512, :].rearrange("(s p) d -> p s d", p=128), writes=[XR], dram_r=[x2])
            jobs = []
            for hh in range(8):
                acc = ACC()
                p0 = (hh % 2) * 64
                first = True
                for gi, (window, dil) in enumerate(pats):
                    qh = gi * 8 + hh
                    for kt in range(max(0, (Q0 - window) // 128), qi * 4 + 4):
                        D = Q0 - kt * 128
                        subs = [sx for sx in range(4) if 0 <= D + 128 * sx <= window]
                        jobs.append(tile_job(KT[p0:p0 + 64, hh // 2, kt * 128:(kt + 1) * 128],
                                             (lambda c0, c1, p0=p0, qh=qh: QT[p0:p0 + 64, qh // 2, c0:c1]), [KT, QT],
                                             md[gi], D + 384, acc, V1[:, kt, hh, :], V1, subs, first))
                        first = False
                jobs.append(fin_job(acc, OM, hh * 64, None, (Rt, Ft, Tt), False))
            jobs += mem_jobs(QM, QM, kmT1, VM1, ACC, OM, 512, (Rt, Ft, Tt))
            run_jobs(jobs, ST, PTr)
            out_proj(OM, 768, WO1, XR, x3, Q0, TP, PO, OBr, OTr, XOr)

    mlp_phase(1, x3, outT)
    S.barrier()
    gst.close()
    return nc


_CACHE = {}


def kernel(**inputs):
    n = 8
    consts = host_consts()
    if "nc" not in _CACHE:
        _CACHE["nc"] = build_program()
    nc = _CACHE["nc"]
    in_maps = []
    for b in range(n):
        m = {}
        for k, shp in IN_SHAPES.items():
            a = np.asarray(inputs[k], dtype=np.float32)
            if k in ("x", "mem"):
                a = a[b]
            m[k] = np.ascontiguousarray(a.reshape(shp))
        m.update(consts)
        in_maps.append(m)
    res = run_bass_kernel_spmd(nc, in_maps, core_ids=list(range(n)))
    return np.stack([np.asarray(r["out"], dtype=np.float32) for r in res.results], axis=0)
```

```python
import numpy as np
import ml_dtypes
import concourse.bass as bass
import concourse.mybir as mybir
from concourse.bass_utils import run_bass_kernel_spmd
from contextlib import ExitStack

F32 = mybir.dt.float32
BF16 = mybir.dt.bfloat16
AF = mybir.ActivationFunctionType
ALU = mybir.AluOpType
AX = mybir.AxisListType
SEM_ROT = 30000
SEQ = 4096
DM = 1024
EPS = 1e-6
NBIG = -30000.0


class Tile:
    __slots__ = ("name", "w", "r", "dsem", "ap", "base")

    def __init__(self, name, ap=None, base=None):
        self.name = name
        self.w = None
        self.r = []
        self.dsem = None
        self.ap = ap
        self.base = base

    def view(self, ap):
        return Tile(self.name + "_v", ap, base=(self.base or self))

    def __getitem__(self, k):
        return self.ap[k]


class DramT:
    def __init__(self, ap, name="d"):
        self.ap = ap
        self.name = name
        self.pend = set()

    def __getitem__(self, k):
        return self.ap[k]


class Sched:
    def __init__(self, nc):
        self.nc = nc
        self.eng = {"pe": nc.tensor, "act": nc.scalar, "dve": nc.vector,
                    "pool": nc.gpsimd, "sp": nc.sync}
        self.esem = {}
        self.ecnt = {}
        self.seen = {e: {} for e in self.eng}
        self.latest = {}
        self.nsem = 0
        self.free_dsems = []
        for e in ("pe", "act", "dve", "pool"):
            self._new_esem(e)

    def _alloc(self, name):
        self.nsem += 1
        return self.nc.alloc_semaphore(name=f"{name}_{self.nsem}")

    def _new_esem(self, e):
        self.esem[e] = self._alloc("e" + e)
        self.ecnt[e] = 0

    def tile(self, ap, name="t"):
        return Tile(name, ap)

    def _need(self, engine, reads, writes):
        need = {}
        reads = [t.base or t for t in reads]
        writes = [t.base or t for t in writes]

        def add(ev, raw):
            if ev is None:
                return
            sem, val, kind, eng = ev
            if kind == "c" and eng == engine:
                if engine == "pe":
                    return
            if kind == "d":
                val = self.latest[sem]
            if need.get(sem, 0) < val:
                need[sem] = val

        for t in reads:
            add(t.w, True)
        for t in writes:
            add(t.w, False)
            for ev in t.r:
                add(ev, False)
        seen = self.seen[engine]
        out = []
        for sem, val in need.items():
            if seen.get(sem, 0) < val:
                seen[sem] = val
                out.append((sem, val))
        return out

    def _record(self, ev, reads, writes):
        reads = [t.base or t for t in reads]
        writes = [t.base or t for t in writes]
        for t in reads:
            t.r = [x for x in t.r if x[0] is not ev[0]]
            t.r.append(ev)
        for t in writes:
            t.w = ev
            t.r = []

    def op(self, engine, fn, reads=(), writes=()):
        e = self.eng[engine]
        for sem, val in self._need(engine, reads, writes):
            e.wait_ge(sem, val)
        if self.ecnt[engine] >= SEM_ROT:
            self._new_esem(engine)
        inst = fn(e)
        self.ecnt[engine] += 1
        inst.then_inc(self.esem[engine], 1)
        ev = (self.esem[engine], self.ecnt[engine], "c", engine)
        self._record(ev, reads, writes)
        return ev

    def free_dsem(self, tiles):
        for t in tiles:
            if t.dsem is not None:
                self.free_dsems.append(t.dsem)
                t.dsem = None

    def dma(self, queue, out, in_, reads=(), writes=(), dram_r=(), dram_w=(), **kw):
        e = self.eng[queue]
        stile = writes[0] if writes else reads[0]
        stile = stile.base or stile
        if stile.dsem is None or self.latest[stile.dsem] >= SEM_ROT:
            stile.dsem = None
            while self.free_dsems and stile.dsem is None:
                c = self.free_dsems.pop()
                if self.latest[c] < SEM_ROT:
                    stile.dsem = c
            if stile.dsem is None:
                stile.dsem = self._alloc("d")
                self.latest[stile.dsem] = 0
        waits = self._need(queue, reads, writes)
        seen = self.seen[queue]
        for d in dram_r:
            for sem in d.pend:
                val = self.latest[sem]
                if seen.get(sem, 0) < val:
                    seen[sem] = val
                    waits.append((sem, val))
        for sem, val in waits:
            e.wait_ge(sem, val)
        inst = e.dma_start(out=out, in_=in_, **kw)
        sem = stile.dsem
        self.latest[sem] += 16
        inst.then_inc(sem, 16)
        ev = (sem, self.latest[sem], "d", None)
        self._record(ev, reads, writes)
        for d in dram_w:
            d.pend.add(sem)
        return ev

    def barrier(self):
        for en, e in self.eng.items():
            seen = self.seen[en]
            for f in ("pe", "act", "dve", "pool"):
                if f == en:
                    continue
                sem, val = self.esem[f], self.ecnt[f]
                if val > 0 and seen.get(sem, 0) < val:
                    seen[sem] = val
                    e.wait_ge(sem, val)
            for sem, val in self.latest.items():
                if val > 0 and seen.get(sem, 0) < val:
                    seen[sem] = val
                    e.wait_ge(sem, val)


def host_consts():
    bf = ml_dtypes.bfloat16
    c = {}
    half = 32
    inv = (10000.0 ** (-np.arange(half, dtype=np.float32) / half)).astype(np.float32)
    ang = np.arange(SEQ + 32, dtype=np.float32)[:, None] * inv[None, :]
    c["c_cos"] = np.cos(ang).astype(np.float32)
    c["c_sin"] = np.sin(ang).astype(np.float32)
    c["c_ident"] = np.eye(128, dtype=np.float32).astype(bf)
    kk = np.arange(128)[:, None]

    def toep(width, f):
        j = np.arange(width)[None, :]
        return f(j - 384 - kk).astype(np.float32).astype(bf)

    c["m_sel"] = toep(896, lambda d: d >= 0)
    c["m_win"] = toep(1408, lambda d: (d >= 0) & (d < 512))
    c["m_d0"] = toep(1024, lambda d: (d >= 0) & (d <= 128))
    c["m_d1"] = toep(1408, lambda d: (d >= 0) & (d <= 512) & (d % 4 == 0))
    c["m_d2"] = toep(2944, lambda d: (d >= 0) & (d <= 2048) & (d % 16 == 0))
    u = np.arange(512)[None, :]
    c["m_cmp"] = np.where(16 * (u - 248) + 31 <= kk, 0.0, NBIG).astype(np.float32)
    u = np.arange(128)[None, :]
    rel = u - 62 - (kk // 64)
    keep = (rel < -1).astype(np.float32)
    add = np.where(rel == 0, 2e9, np.where(rel == -1, 1e9, np.where(rel > 0, -1e30, 0.0)))
    c["m_tkeep"] = keep.astype(np.float32)
    c["m_tadd"] = add.astype(np.float32)
    n_cmp = SEQ // 16 - 1
    cs = np.arange(256)[:, None] * 16
    bs = np.arange(64)[None, :] * 64
    ov = ((cs < bs + 64) & (cs + 32 > bs)).astype(np.float32)
    ov[n_cmp:] = 0
    c["c_ovl"] = ov.astype(bf)
    c["c_onehot"] = (np.arange(SEQ)[None, :] // 64 == np.arange(64)[:, None]).astype(np.float32).astype(bf)
    return c


CONST_SHAPES = {
    "c_cos": ([SEQ + 32, 32], F32), "c_sin": ([SEQ + 32, 32], F32), "c_ident": ([128, 128], BF16),
    "m_sel": ([128, 896], BF16), "m_win": ([128, 1408], BF16), "m_d0": ([128, 1024], BF16),
    "m_d1": ([128, 1408], BF16), "m_d2": ([128, 2944], BF16), "m_cmp": ([128, 512], F32),
    "m_tkeep": ([128, 128], F32), "m_tadd": ([128, 128], F32), "c_ovl": ([256, 64], BF16),
    "c_onehot": ([64, SEQ], BF16),
}

IN_SHAPES = {
    "x": [SEQ, DM], "mem": [256, DM], "attn_norm": [2, DM], "mlp_norm": [2, DM],
    "w_up": [2, DM, 4096], "w_down": [2, 4096, DM], "mem_norm": [2, DM], "w_mem_kv": [2, DM, 512],
    "mem_q_norm": [2, 64], "mem_k_norm": [2, 64], "a_w_in": [1, DM, 2212], "a_w_out": [1, DM, DM],
    "a_q_norm": [1, 64], "a_k_norm": [1, 3, 64], "a_cmp_pos": [1, 2, 32, 64],
    "a_cmp_w1": [1, 2, 2048, 256], "a_cmp_b1": [1, 2, 256], "a_cmp_w2": [1, 2, 256, 64],
    "a_cmp_b2": [1, 2, 64], "kv_norm": [1, DM], "w_kv_shared": [DM, DM], "kv_k_norm": [1, 64],
    "b_w_in": [1, DM, 1792], "b_w_out": [1, 768, DM], "b_q_norm": [1, 3, 64],
}


def build_program(debug=()):
    nc = bass.Bass("TRN2", target_bir_lowering=False)
    S = Sched(nc)
    I = {}
    for k, shp in IN_SHAPES.items():
        I[k] = nc.dram_tensor(k, shp, F32, kind="ExternalInput").ap()
    C = {}
    for k, (shp, dt) in CONST_SHAPES.items():
        C[k] = nc.dram_tensor(k, shp, dt, kind="ExternalInput").ap()
    out_ap = nc.dram_tensor("out", [SEQ, DM], F32, kind="ExternalOutput").ap()

    def scratch(name, shape, dt):
        kind = "ExternalOutput" if name in debug else "Internal"
        return DramT(nc.dram_tensor(name, shape, dt, kind=kind).ap(), name)

    featT0 = scratch("featT0", [28, 64, SEQ], BF16)
    vtok0 = scratch("vtok0", [SEQ, 6, 65], BF16)
    gates = scratch("gates", [SEQ, 36], F32)
    x1 = scratch("x1", [SEQ, DM], F32)
    x2 = scratch("x2", [SEQ, DM], F32)
    x3 = scratch("x3", [SEQ, DM], F32)
    featT1 = scratch("featT1", [36, 64, SEQ], BF16)
    vtok1 = scratch("vtok1", [SEQ, 8, 65], BF16)
    outT = DramT(out_ap, "out")
    xin = DramT(I["x"], "x")

    gst = ExitStack()
    cur = {"st": gst}

    uid = {"i": 0}
    local_tiles = []

    def sb(name, shape, dt=F32, persist=False):
        st = gst if persist else cur["st"]
        uid["i"] += 1
        t = S.tile(st.enter_context(nc.sbuf_tensor(f"{name}_{uid['i']}", list(shape), dt)), name)
        if not persist:
            local_tiles.append(t)
        return t

    def ps(name, shape, dt=F32):
        uid["i"] += 1
        full = [128, 512] if dt == F32 else [128, 1024]
        h = cur["st"].enter_context(nc.psum_tensor(f"{name}_{uid['i']}", full, dt))
        n = 1
        for d in shape[1:]:
            n *= d
        v = h[:, 0:n]
        if len(shape) == 3:
            v = v.rearrange("p (a b) -> p a b", b=shape[2])
        return S.tile(v, name)

    class Phase:
        def __enter__(self):
            self.st = ExitStack()
            cur["st"] = self.st
            return self

        def __exit__(self, *a):
            S.barrier()
            S.free_dsem(local_tiles)
            del local_tiles[:]
            self.st.close()
            cur["st"] = gst
            return False

    rr = {"i": 0}

    def alt(engs=("dve", "pool")):
        rr["i"] += 1
        return engs[rr["i"] % len(engs)]

    def ring(name, shape, dt, n):
        tiles = [sb(f"{name}{i}", shape, dt) for i in range(n)]
        st = {"i": -1}

        def nxt():
            st["i"] = (st["i"] + 1) % n
            return tiles[st["i"]]
        nxt.tiles = tiles
        return nxt

    ident = sb("ident", [128, 128], BF16, persist=True)
    S.dma("sp", ident[:], C["c_ident"][:, :], writes=[ident])
    ones_bf = sb("ones_bf", [128, 128], BF16, persist=True)
    S.op("dve", lambda e: e.memset(ones_bf[:], 1.0), writes=[ones_bf])

    def bcast_row(dst_ap, src_row_ap, tile):
        S.dma("sp", dst_ap, src_row_ap.partition_broadcast(128), writes=[tile])

    def load_gT(name, src_row):
        t = sb(name, [128, 8], F32, persist=True)
        S.dma("sp", t[:], src_row.rearrange("o (kc p) -> p (o kc)", p=128), writes=[t],
              allow_slow_non_contiguous=True)
        return t

    gT_attn = [load_gT(f"gT_attn{l}", I["attn_norm"][l:l + 1, :]) for l in range(2)]
    gT_mlp = [load_gT(f"gT_mlp{l}", I["mlp_norm"][l:l + 1, :]) for l in range(2)]
    gT_mem = [load_gT(f"gT_mem{l}", I["mem_norm"][l:l + 1, :]) for l in range(2)]
    gT_kv = load_gT("gT_kv", I["kv_norm"][0:1, :])

    def load_weight(wt, src, nk, ncols, stg_ring, col_chunk=2048, engs=("dve", "act", "pool")):
        for kc in range(nk):
            eng = engs[kc % len(engs)]
            for c0 in range(0, ncols, col_chunk):
                w = min(col_chunk, ncols - c0)
                stg = stg_ring()
                S.dma("sp", stg[:, 0:w], src[kc * 128:(kc + 1) * 128, c0:c0 + w], writes=[stg])
                if eng == "act":
                    S.op(eng, lambda e: e.copy(out=wt.ap[:, kc, c0:c0 + w], in_=stg[:, 0:w]), reads=[stg], writes=[wt.k[kc]])
                else:
                    S.op(eng, lambda e: e.tensor_copy(out=wt.ap[:, kc, c0:c0 + w], in_=stg[:, 0:w]),
                         reads=[stg], writes=[wt.k[kc]])

    class WT:
        def __init__(self, name, nk, ncols):
            self.h = sb(name, [128, nk, ncols], BF16)
            self.ap = self.h.ap
            self.k = [Tile(f"{name}_{i}") for i in range(nk)]

    def run(gen):
        for _ in gen:
            pass

    def interleave(gens):
        state = [[g, 0, float(tot)] for g, tot in gens]
        while state:
            st = min(state, key=lambda z: z[1] / z[2])
            try:
                next(st[0])
                st[1] += 1
            except StopIteration:
                state.remove(st)

    def norm_tile(xt_ap, xt_tile, gTs, hTs, tph, tmp):
        junk, ssq, rs, rs2, xn = tmp
        S.op("pool", lambda e: e.memset(ssq[:], 0.0), writes=[ssq])
        yield
        S.op("act", lambda e: e.activation(out=junk[:], in_=xt_ap, func=AF.Square, accum_out=ssq[:, 0:1]),
             reads=[xt_tile, ssq], writes=[junk, ssq])
        yield
        S.op("act", lambda e: e.activation(out=rs[:], in_=ssq[:], func=AF.Sqrt, scale=1.0 / DM, bias=EPS),
             reads=[ssq], writes=[rs])
        yield
        S.op("dve", lambda e: e.reciprocal(out=rs2[:], in_=rs[:]), reads=[rs], writes=[rs2])
        yield
        S.op("dve", lambda e: e.tensor_scalar(out=xn[:], in0=xt_ap, scalar1=rs2[:, 0:1], scalar2=None, op0=ALU.mult),
             reads=[xt_tile, rs2], writes=[xn])
        yield
        for kc in range(8):
            S.op("pe", lambda e: e.transpose(out=tph[:, kc, :], in_=xn[:, kc * 128:(kc + 1) * 128], identity=ident[:]),
                 reads=[xn, ident], writes=[tph])
            yield
        for gT, hT in zip(gTs, hTs):
            S.op("dve", lambda e: e.tensor_tensor(out=hT[:], in0=tph[:], in1=gT[:].unsqueeze(2).to_broadcast([128, 8, 128]), op=ALU.mult),
                 reads=[tph, gT], writes=[hT])
            yield

    def head_norm(YA, nh, nrope, GA, CS, YB, tmp):
        SQ, SS, RS, RS2, YN, T1, T2 = tmp
        S.op("act", lambda e: e.activation(out=SQ[:, 0:nh, :], in_=YA[:, 0:nh, :], func=AF.Square),
             reads=[YA], writes=[SQ])
        yield
        S.op("dve", lambda e: e.tensor_reduce(out=SS[:, 0:nh], in_=SQ[:, 0:nh, :], axis=AX.X, op=ALU.add),
             reads=[SQ], writes=[SS])
        yield
        S.op("act", lambda e: e.activation(out=RS[:, 0:nh], in_=SS[:, 0:nh], func=AF.Sqrt, scale=1.0 / 64, bias=EPS),
             reads=[SS], writes=[RS])
        yield
        S.op("dve", lambda e: e.reciprocal(out=RS2[:, 0:nh], in_=RS[:, 0:nh]), reads=[RS], writes=[RS2])
        yield
        S.op("dve", lambda e: e.tensor_tensor(out=YN[:, 0:nh, :], in0=YA[:, 0:nh, :],
                                              in1=RS2[:, 0:nh].unsqueeze(2).to_broadcast([128, nh, 64]), op=ALU.mult),
             reads=[YA, RS2], writes=[YN])
        yield
        S.op("pool", lambda e: e.tensor_tensor(out=YN[:, 0:nh, :], in0=YN[:, 0:nh, :], in1=GA[:, 0:nh, :], op=ALU.mult),
             reads=[YN, GA], writes=[YN])
        yield
        if nrope:
            n = nrope
            cosb = CS[:, 0:1, :].to_broadcast([128, n, 32])
            sinb = CS[:, 1:2, :].to_broadcast([128, n, 32])
            x1v = YN[:, 0:n, 0:32]
            x2v = YN[:, 0:n, 32:64]
            S.op("dve", lambda e: e.tensor_tensor(out=T1[:, 0:n, 0:32], in0=x1v, in1=cosb, op=ALU.mult), reads=[YN, CS], writes=[T1])
            yield
            S.op("pool", lambda e: e.tensor_tensor(out=T2[:, 0:n, 0:32], in0=x2v, in1=sinb, op=ALU.mult), reads=[YN, CS], writes=[T2])
            yield
            S.op("dve", lambda e: e.tensor_tensor(out=YB[:, 0:n, 0:32], in0=T1[:, 0:n, 0:32], in1=T2[:, 0:n, 0:32], op=ALU.subtract),
                 reads=[T1, T2], writes=[YB])
            yield
            S.op("pool", lambda e: e.tensor_tensor(out=T1[:, 0:n, 32:64], in0=x1v, in1=sinb, op=ALU.mult), reads=[YN, CS], writes=[T1])
            yield
            S.op("dve", lambda e: e.tensor_tensor(out=T2[:, 0:n, 32:64], in0=x2v, in1=cosb, op=ALU.mult), reads=[YN, CS], writes=[T2])
            yield
            S.op("pool", lambda e: e.tensor_tensor(out=YB[:, 0:n, 32:64], in0=T1[:, 0:n, 32:64], in1=T2[:, 0:n, 32:64], op=ALU.add),
                 reads=[T1, T2], writes=[YB])
            yield
        if nh > nrope:
            S.op("act", lambda e: e.copy(out=YB[:, nrope:nh, :], in_=YN[:, nrope:nh, :]), reads=[YN], writes=[YB])
            yield

    def feat_transposes(YB, npairs, tpf, TT):
        YBf = YB[:].rearrange("p h d -> p (h d)")
        for b0 in range(0, npairs, 8):
            nb = min(8, npairs - b0)
            for j in range(nb):
                S.op("pe", lambda e: e.transpose(out=tpf[:, j, :], in_=YBf[:, (b0 + j) * 128:(b0 + j + 1) * 128], identity=ident[:]),
                     reads=[YB, ident], writes=[tpf])
                yield
            S.op("act", lambda e: e.copy(out=TT[:, b0:b0 + nb, :], in_=tpf[:, 0:nb, :]), reads=[tpf], writes=[TT])
            yield

    def build_gains(name, nh, specs):
        GA = sb(name, [128, nh, 64], F32)
        G1 = sb(name + "_raw", [128, len(specs), 64], F32)
        for i, (src, h0, n, scale) in enumerate(specs):
            bcast_row(G1[:, i, :], src, G1)
        for i, (src, h0, n, scale) in enumerate(specs):
            S.op("dve", lambda e: e.tensor_scalar(out=GA[:, h0:h0 + n, :], in0=G1[:, i:i + 1, :].to_broadcast([128, n, 64]),
                                                  scalar1=float(scale), scalar2=None, op0=ALU.mult),
                 reads=[G1], writes=[GA])
        return GA

    def proj_phase(x_src, ntiles, projs, nh, nrope, GA, raw_specs, featT, vtok, vtok_n, gate_cols=None,
                   seg_map=None, pos_tab=True):
        nraw_t = sum((c1 - c0) // 64 for (_, c0, c1, k, _) in seg_map if k == "T")
        ntot = nh + nraw_t
        npairs = ntot // 2
        xr = ring("xt", [128, DM], F32, 3)
        junkr = ring("junk", [128, DM], BF16, 2)
        ssqr = ring("ssq", [128, 1], F32, 2); rsr_ = ring("rs", [128, 1], F32, 2); rs2r = ring("rs2", [128, 1], F32, 2)
        xnr = ring("xn", [128, DM], BF16, 2)
        tph = ps("tph", [128, 8, 128], BF16)
        tpf = ps("tpf", [128, 8, 128], BF16)
        hTr = [ring(f"hT{i}", [128, 8, 128], BF16, 2) for i in range(len(projs))]
        banks = []
        for pi, (gT, W, ncols) in enumerate(projs):
            for c0 in range(0, ncols, 512):
                banks.append((pi, c0, min(512, ncols - c0), ps(f"pb{pi}_{c0}", [128, 512], F32)))
        YAr = ring("YA", [128, ntot, 64], F32, 2)
        YBr = ring("YB", [128, ntot, 64], BF16, 2)
        SQ = sb("SQ", [128, nh, 64], F32); SS = sb("SS", [128, nh]); RS = sb("RS", [128, nh]); RS2 = sb("RS2", [128, nh])
        YN = sb("YN", [128, nh, 64], F32); T1 = sb("T1", [128, max(nrope, 1), 64], F32); T2 = sb("T2", [128, max(nrope, 1), 64], F32)
        CSr = ring("CS", [128, 2, 32], F32, 6)
        TTr = ring("TT", [128, npairs, 128], BF16, 2)
        YVr = ring("YV", [128, vtok_n, 65], BF16, 2) if vtok is not None else None
        if YVr is not None:
            for t in YVr.tiles:
                S.op("pool", lambda e: e.memset(t[:], 1.0), writes=[t])
        YGr = ring("YG", [128, 36], F32, 2) if gate_cols else None
        ctx = {}

        def s0(ti):
            t0 = ti * 128
            c = ctx[ti] = {}
            xt = c["xt"] = xr()
            S.dma("sp", xt[:], x_src[t0:t0 + 128, :], writes=[xt], dram_r=[x_src])
            CS = CSr()
            c["CS"] = CS
            if nrope:
                S.dma("sp", CS[:, 0, :], C["c_cos"][t0:t0 + 128, :], writes=[CS])
                S.dma("sp", CS[:, 1, :], C["c_sin"][t0:t0 + 128, :], writes=[CS])

        def s1(ti):
            c = ctx[ti]
            xt = c["xt"]
            c["hT"] = [r() for r in hTr]
            yield from norm_tile(xt[:], xt, [p[0] for p in projs], c["hT"], tph, (junkr(), ssqr(), rsr_(), rs2r(), xnr()))

        def s2(ti):
            c = ctx[ti]
            hTs = c["hT"]
            for (pi, c0, w, pb) in banks:
                W = projs[pi][1]
                for kc in range(8):
                    S.op("pe", lambda e: e.matmul(pb[:, 0:w], lhsT=hTs[pi][:, kc, :], rhs=W.ap[:, kc, c0:c0 + w],
                                                  start=(kc == 0), stop=(kc == 7)),
                         reads=[hTs[pi], W.k[kc]], writes=[pb])
                    yield
            YA = c["YA"] = YAr()
            YB = c["YB"] = YBr()
            YV = c["YV"] = YVr() if YVr is not None else None
            YG = c["YG"] = YGr() if YGr is not None else None
            for (pi, c0, c1, kind, di) in seg_map:
                cc = c0
                while cc < c1:
                    bk = [b for b in banks if b[0] == pi and b[1] <= cc < b[1] + b[2]][0]
                    ce = min(c1, bk[1] + bk[2])
                    src = bk[3][:, cc - bk[1]:ce - bk[1]]
                    off = cc - c0
                    n = ce - cc
                    if kind in ("A", "T"):
                        dst = YA if kind == "A" else YB
                        dflat = dst[:].rearrange("p h d -> p (h d)")
                        S.op("act", lambda e: e.copy(out=dflat[:, di * 64 + off: di * 64 + off + n], in_=src),
                             reads=[bk[3]], writes=[dst])
                        yield
                    elif kind == "V":
                        h0 = di + off // 64
                        S.op("act", lambda e: e.copy(out=YV[:, h0:h0 + n // 64, 0:64],
                                                     in_=src.rearrange("p (h d) -> p h d", d=64)),
                             reads=[bk[3]], writes=[YV])
                        yield
                    elif kind == "G":
                        S.op("act", lambda e: e.copy(out=YG[:, off:off + n], in_=src), reads=[bk[3]], writes=[YG])
                        yield
                    cc = ce

        def s3(ti):
            t0 = ti * 128
            c = ctx.pop(ti)
            yield from head_norm(c["YA"], nh, nrope, GA, c["CS"], c["YB"], (SQ, SS, RS, RS2, YN, T1, T2))
            TT = TTr()
            yield from feat_transposes(c["YB"], npairs, tpf, TT)
            S.dma("sp", featT.ap.rearrange("(j h2) d t -> (h2 d) j t", h2=2)[:, :, t0:t0 + 128], TT[:],
                  reads=[TT], dram_w=[featT])
            yield
            if vtok is not None:
                S.dma("sp", vtok[t0:t0 + 128, :, :], c["YV"][:], reads=[c["YV"]], dram_w=[vtok])
                yield
            if c["YG"] is not None:
                S.dma("sp", gates[t0:t0 + 128, :], c["YG"][:], reads=[c["YG"]], dram_w=[gates])
                yield

        s0(0)
        s0(1)
        for step in range(ntiles + 2):
            if step + 2 < ntiles:
                s0(step + 2)
            gens = []
            if step < ntiles:
                gens.append((s1(step), 10))
            if 0 <= step - 1 < ntiles:
                gens.append((s2(step - 1), 56))
            if 0 <= step - 2 < ntiles:
                gens.append((s3(step - 2), 28))
            interleave(gens)

    def mem_phase(layer, kmT, VM):
        with Phase():
            stg = ring("stg", [128, 2048], F32, 2)
            W = WT("wmem", 8, 512)
            load_weight(W, I["w_mem_kv"][layer], 8, 512, stg)
            GA = build_gains(f"GAm{layer}", 4, [(I["mem_k_norm"][layer:layer + 1, :], 0, 4, 1.0)])
            xr = ring("xt", [128, DM], F32, 2)
            junk = sb("junk", [128, DM], BF16)
            ssq = sb("ssq", [128, 1]); rs = sb("rs", [128, 1]); rs2 = sb("rs2", [128, 1])
            xn = sb("xn", [128, DM], BF16)
            tph = ps("tph", [128, 8, 128], BF16)
            tpf = ps("tpf", [128, 8, 128], BF16)
            hT = sb("hT", [128, 8, 128], BF16)
            pb = ps("pb", [128, 512], F32)
            YA = sb("YA", [128, 4, 64], F32); YB = sb("YB", [128, 4, 64], BF16)
            SQ = sb("SQ", [128, 4, 64], F32); SS = sb("SS", [128, 4]); RS = sb("RS", [128, 4]); RS2 = sb("RS2", [128, 4])
            YN = sb("YN", [128, 4, 64], F32); T1 = sb("T1", [128, 1, 64], F32); T2 = sb("T2", [128, 1, 64], F32)
            TT = sb("TT", [128, 2, 128], BF16)
            S.op("pool", lambda e: e.memset(VM[:], 1.0), writes=[VM])
            memT = DramT(I["mem"], "mem")
            for mt in range(2):
                xt = xr()
                S.dma("sp", xt[:], memT[mt * 128:(mt + 1) * 128, :], writes=[xt])
                run(norm_tile(xt[:], xt, [gT_mem[layer]], [hT], tph, (junk, ssq, rs, rs2, xn)))
                for kc in range(8):
                    S.op("pe", lambda e: e.matmul(pb[:, :], lhsT=hT[:, kc, :], rhs=W.ap[:, kc, :], start=(kc == 0), stop=(kc == 7)),
                         reads=[hT, W.k[kc]], writes=[pb])
                S.op("act", lambda e: e.copy(out=YA[:].rearrange("p h d -> p (h d)"), in_=pb[:, 0:256]), reads=[pb], writes=[YA])
                S.op("act", lambda e: e.copy(out=VM[:, mt, :, 0:64], in_=pb[:, 256:512].rearrange("p (h d) -> p h d", d=64)),
                     reads=[pb], writes=[VM])
                run(head_norm(YA, 4, 0, GA, None, YB, (SQ, SS, RS, RS2, YN, T1, T2)))
                run(feat_transposes(YB, 2, tpf, TT))
                for mh in range(4):
                    S.op("dve", lambda e: e.tensor_copy(out=kmT[:, mh, mt * 128:(mt + 1) * 128],
                                                        in_=TT[(mh % 2) * 64:(mh % 2) * 64 + 64, mh // 2, :]),
                         reads=[TT], writes=[kmT])

    LOOK = 3

    def tile_job(lhsT_ap, rhs_fn, reads, mask_tile, mask_off, acc, v_ap, v_tile, subs, first):
        return ["tile", dict(lhsT=lhsT_ap, rhs_fn=rhs_fn, reads=reads, mt=mask_tile, mo=mask_off, acc=acc, v=v_ap,
                             vt=v_tile, subs=subs, first=first)]

    def run_jobs(jobs, ST, PTr):
        tiles = [j[1] for j in jobs if j[0] == "tile"]
        pos = {"fi": 0, "ti": 0}

        def front(j):
            c0 = min(j["subs"]) * 128
            c1 = (max(j["subs"]) + 1) * 128
            st = ST()
            pt = PTr()
            j["pt"] = pt
            S.op("pe", lambda e: e.matmul(st[:, c0:c1], lhsT=j["lhsT"], rhs=j["rhs_fn"](c0, c1), start=True, stop=True),
                 reads=j["reads"], writes=[st])
            S.op("act", lambda e: e.activation(out=pt[:, c0:c1], in_=st[:, c0:c1], func=AF.Exp), reads=[st], writes=[pt])
            if j["mt"] is not None:
                mo = j["mo"]
                S.op("dve", lambda e: e.tensor_tensor(out=pt[:, c0:c1], in0=pt[:, c0:c1], in1=j["mt"][:, mo + c0:mo + c1], op=ALU.mult),
                     reads=[pt, j["mt"]], writes=[pt])

        def back(j):
            pt = j["pt"]
            for i, sidx in enumerate(j["subs"]):
                S.op("pe", lambda e: e.matmul(j["acc"][:, sidx, :], lhsT=pt[:, sidx * 128:(sidx + 1) * 128], rhs=j["v"],
                                              start=(j["first"] and i == 0), stop=False, skip_group_check=True),
                     reads=[pt, j["vt"]], writes=[j["acc"]])

        for j in jobs:
            if j[0] == "tile":
                while pos["fi"] < len(tiles) and pos["fi"] <= pos["ti"] + LOOK:
                    front(tiles[pos["fi"]])
                    pos["fi"] += 1
                back(j[1])
                pos["ti"] += 1
            else:
                j[1]()

    def finalize(acc, OM, col0, fac_fn, tmp, accumulate):
        R, F, T = tmp
        S.op("dve", lambda e: e.reciprocal(out=R[:], in_=acc[:, :, 64]), reads=[acc], writes=[R])
        fac = R
        if fac_fn is not None:
            gap, gtile = fac_fn
            S.op("dve", lambda e: e.tensor_tensor(out=F[:], in0=R[:], in1=gap, op=ALU.mult), reads=[R, gtile], writes=[F])
            fac = F
        facb = fac[:].unsqueeze(2).to_broadcast([128, 4, 64])
        if not accumulate:
            S.op("dve", lambda e: e.tensor_tensor(out=OM[:, :, col0:col0 + 64], in0=acc[:, :, 0:64], in1=facb, op=ALU.mult),
                 reads=[acc, fac], writes=[OM])
        else:
            S.op("dve", lambda e: e.tensor_tensor(out=T[:], in0=acc[:, :, 0:64], in1=facb, op=ALU.mult), reads=[acc, fac], writes=[T])
            S.op("pool", lambda e: e.tensor_tensor(out=OM[:, :, col0:col0 + 64], in0=OM[:, :, col0:col0 + 64], in1=T[:], op=ALU.add),
                 reads=[OM, T], writes=[OM])

    def out_proj(OM, ncols_in, WO, Xres, dst, Q0, tp, po_ring, OBr, OTr, XOr):
        nk = ncols_in // 128
        for s in range(4):
            OB = OBr()
            S.op("dve", lambda e: e.tensor_copy(out=OB[:, 0:ncols_in], in_=OM[:, s, :]), reads=[OM], writes=[OB])
            for kc in range(nk):
                S.op("pe", lambda e: e.transpose(out=tp[:, kc, :], in_=OB[:, kc * 128:(kc + 1) * 128], identity=ident[:]),
                     reads=[OB, ident], writes=[tp])
            OT = OTr()
            S.op("act", lambda e: e.copy(out=OT[:, 0:nk, :], in_=tp[:, 0:nk, :]), reads=[tp], writes=[OT])
            XO = XOr()
            for half in range(2):
                po = po_ring()
                for kc in range(nk):
                    S.op("pe", lambda e: e.matmul(po[:, :], lhsT=OT[:, kc, :], rhs=WO.ap[:, kc, half * 512:(half + 1) * 512],
                                                  start=(kc == 0), stop=(kc == nk - 1)),
                         reads=[OT, WO.k[kc]], writes=[po])
                S.op("dve", lambda e: e.tensor_tensor(out=XO[:, half * 512:(half + 1) * 512], in0=po[:, :],
                                                      in1=Xres[:, s, half * 512:(half + 1) * 512], op=ALU.add),
                     reads=[po, Xres], writes=[XO])
            S.dma("sp", dst[Q0 + s * 128:Q0 + (s + 1) * 128, :], XO[:], reads=[XO], dram_w=[dst])

    def fin_job(acc, OM, col0, fac_fn, tmp, accumulate):
        return ["fin", lambda: finalize(acc, OM, col0, fac_fn, tmp, accumulate)]

    def mem_jobs(QM, qm_tile, kmT, VM, ACC, OM, col0, tmp):
        jobs = []
        for mh in range(4):
            acc = ACC()
            for mt in range(2):
                jobs.append(tile_job(kmT[:, mh, mt * 128:(mt + 1) * 128], (lambda c0, c1, mh=mh: QM[0:64, mh, c0:c1]), [kmT, qm_tile],
                                     None, None, acc, VM[:, mt, mh, :], VM, [0, 1, 2, 3], mt == 0))
            jobs.append(fin_job(acc, OM, col0 + mh * 64, None, tmp, False))
        return jobs

    def bank_ring(tiles):
        st = {"i": -1}

        def nxt():
            st["i"] = (st["i"] + 1) % len(tiles)
            return tiles[st["i"]]
        return nxt

    def acc_view(b):
        return b.view(b.ap[:, 0:260].rearrange("p (s c) -> p s c", c=65))

    def mlp_phase(layer, src, dst):
        with Phase():
            stg = ring("stg", [128, 1024], F32, 2)
            WU = WT("WU", 8, 4096)
            WD = WT("WD", 32, 1024)
            load_weight(WU, I["w_up"][layer], 8, 4096, stg, col_chunk=1024)
            xr = ring("xt", [128, 2, DM], F32, 3)
            junk = sb("junk", [128, DM], BF16)
            ssq = sb("ssq", [128, 1]); rs = sb("rs", [128, 1]); rs2 = sb("rs2", [128, 1])
            xn = sb("xn", [128, DM], BF16)
            tph = ps("tph", [128, 8, 128], BF16)
            uT = sb("uT", [128, 32, 256], BF16)
            rl = ring("rl", [128, 256], BF16, 3)
            XO = ring("XO", [128, DM], F32, 2)
            pu = [ps(f"pu{i}", [128, 512], F32) for i in range(3)]
            pd = [ps(f"pd{i}", [128, 512], F32) for i in range(3)]
            cnt = 0
            xts = {}

            def ldx(tb):
                xts[tb] = xr()
                S.dma("sp", xts[tb][:], src.ap[tb * 256:tb * 256 + 256, :].rearrange("(s p) d -> p s d", p=128),
                      writes=[xts[tb]], dram_r=[src])
            ldx(0)
            hTr2 = ring("hTm", [128, 8, 256], BF16, 2)
            hTs = {}

            def normgen(tb):
                xt = xts[tb]
                hTt = hTr2()
                hTs[tb] = hTt
                for s in range(2):
                    hv = hTt.view(hTt.ap[:, :, s * 128:(s + 1) * 128])
                    yield from norm_tile(xt[:, s, :], xt, [gT_mlp[layer]], [hv], tph, (junk, ssq, rs, rs2, xn))

            def blockgen(tb):
                nonlocal_cnt = cntbox
                t0 = tb * 256
                xt = xts[tb]
                hTt = hTs.pop(tb)
                for fc in range(32):
                    p = pu[fc % 3]
                    for kc in range(8):
                        S.op("pe", lambda e: e.matmul(p[:, 0:256], lhsT=WU.ap[:, kc, fc * 128:(fc + 1) * 128], rhs=hTt[:, kc, :],
                                                      start=(kc == 0), stop=(kc == 7)),
                             reads=[WU.k[kc], hTt], writes=[p])
                        yield
                    r = rl()
                    S.op("act", lambda e: e.activation(out=r[:], in_=p[:, 0:256], func=AF.Relu), reads=[p], writes=[r])
                    yield
                    S.op(alt(), lambda e: e.tensor_tensor(out=uT[:, fc, :], in0=r[:], in1=r[:], op=ALU.mult), reads=[r], writes=[uT])
                    yield
                if tb == 0:
                    load_weight(WD, I["w_down"][layer], 32, 1024, stg, col_chunk=1024, engs=("dve", "act"))
                for s in range(2):
                    xo = XO()
                    for half in range(2):
                        p = pd[nonlocal_cnt[0] % 3]
                        nonlocal_cnt[0] += 1
                        for kc in range(32):
                            S.op("pe", lambda e: e.matmul(p[:, :], lhsT=uT[:, kc, s * 128:(s + 1) * 128],
                                                          rhs=WD.ap[:, kc, half * 512:(half + 1) * 512],
                                                          start=(kc == 0), stop=(kc == 31)),
                                 reads=[uT, WD.k[kc]], writes=[p])
                            yield
                        S.op("dve", lambda e: e.tensor_tensor(out=xo[:, half * 512:(half + 1) * 512], in0=p[:, :],
                                                              in1=xt[:, s, half * 512:(half + 1) * 512], op=ALU.add),
                             reads=[p, xt], writes=[xo])
                        yield
                    S.dma("sp", dst[t0 + s * 128:t0 + (s + 1) * 128, :], xo[:], reads=[xo], dram_w=[dst])
                    yield
                xts.pop(tb)

            cntbox = [0]
            nblk = SEQ // 256
            ldx(1)
            run(normgen(0))
            for tb in range(nblk):
                gens = [(blockgen(tb), 460)]
                if tb + 1 < nblk:
                    gens.append((normgen(tb + 1), 32))
                interleave(gens)
                if tb + 2 < nblk:
                    ldx(tb + 2)

    with Phase():
        stg = ring("stg", [128, 2212], F32, 2)
        WI = WT("WI", 8, 2212)
        load_weight(WI, I["a_w_in"][0], 8, 2212, stg, col_chunk=2212)
        GA0 = build_gains("GA0", 22, [(I["a_q_norm"][0:1, :], 0, 12, 0.125), (I["a_k_norm"][0, 1:2, :], 12, 3, 1.0),
                                     (I["a_k_norm"][0, 2:3, :], 15, 3, 1.0), (I["mem_q_norm"][0:1, :], 18, 4, 0.125)])
        seg0 = [(0, 0, 768, "A", 0), (0, 1152, 1344, "A", 12), (0, 1536, 1728, "A", 15), (0, 1920, 2176, "A", 18),
                (0, 768, 960, "T", 22), (0, 960, 1152, "T", 25),
                (0, 1344, 1536, "V", 0), (0, 1728, 1920, "V", 3), (0, 2176, 2212, "G", 0)]
        proj_phase(xin, SEQ // 128, [(gT_attn[0], WI, 2212)], 22, 18, GA0, None, featT0, vtok0, 6, gate_cols=True, seg_map=seg0)

    kcT = sb("kcT", [64, 3, 256], BF16, persist=True)
    VCO = sb("VCO", [128, 2, 3, 128], BF16, persist=True)
    with Phase():
        GAc = build_gains("GAc", 1, [(I["a_k_norm"][0, 0:1, :], 0, 1, 1.0)])
        for ct in range(2):
            for g in range(3):
                S.dma("sp", VCO[:, ct, g, 64:128], C["c_ovl"][ct * 128:(ct + 1) * 128, :], writes=[VCO])
        XCr = ring("XC", [64, SEQ], BF16, 2)
        W1s = sb("W1s", [64, 32, 256], F32)
        W1 = sb("W1", [64, 32, 256], BF16)
        W2s = sb("W2s", [128, 2, 64], F32)
        W2 = sb("W2", [128, 2, 64], BF16)
        posS = sb("posS", [64, 32], F32); posT = sb("posT", [64, 32], BF16)
        b1c = sb("b1c", [128, 2], F32); BT = sb("BT", [128, 2], F32)
        b2s = sb("b2s", [1, 64], F32); b2r = sb("b2r", [1, 64], BF16)
        HG = sb("HG", [128, 2, 256], BF16)
        S.op("pool", lambda e: e.memset(HG[:], 0.0), writes=[HG])
        ph = [ps(f"ph{i}", [128, 512], F32) for i in range(2)]
        pbias = ps("pbias", [128, 2], F32)
        pout = ps("pout", [128, 64], F32)
        tpf = ps("tpf", [128, 8, 128], BF16)
        CSc = sb("CSc", [128, 2, 2, 32], F32)
        S.op("pool", lambda e: e.memset(CSc[:], 0.0), writes=[CSc])
        for ct in range(2):
            n = 128 if ct == 0 else 127
            S.dma("sp", CSc[0:n, ct, 0, :], C["c_cos"].rearrange("(c s) f -> c s f", s=16)[ct * 128 + 1:ct * 128 + 1 + n, 15, :], writes=[CSc])
            S.dma("sp", CSc[0:n, ct, 1, :], C["c_sin"].rearrange("(c s) f -> c s f", s=16)[ct * 128 + 1:ct * 128 + 1 + n, 15, :], writes=[CSc])
        YA = sb("YAc", [128, 1, 64], F32); YB = sb("YBc", [128, 2, 64], BF16)
        S.op("pool", lambda e: e.memset(YB[:], 0.0), writes=[YB])
        SQ = sb("SQ", [128, 1, 64], F32); SS = sb("SS", [128, 1]); RS = sb("RS", [128, 1]); RS2 = sb("RS2", [128, 1])
        YN = sb("YN", [128, 1, 64], F32); T1 = sb("T1", [128, 1, 64], F32); T2 = sb("T2", [128, 1, 64], F32)
        CS1 = sb("CS1", [128, 2, 32], F32)
        TTc = sb("TTc", [128, 1, 128], BF16)
        for i in range(2):
            S.dma("sp", W1s[:], I["a_cmp_w1"][0, i].rearrange("(j d) h -> d j h", d=64), writes=[W1s])
            S.op("dve", lambda e: e.tensor_copy(out=W1[:, 0:16, :], in_=W1s[:, 0:16, :]), reads=[W1s], writes=[W1])
            S.op("pool", lambda e: e.tensor_copy(out=W1[:, 16:32, :], in_=W1s[:, 16:32, :]), reads=[W1s], writes=[W1])
            S.dma("sp", W2s[:], I["a_cmp_w2"][0, i].rearrange("(hf p) d -> p hf d", p=128), writes=[W2s])
            S.op("dve", lambda e: e.tensor_copy(out=W2[:], in_=W2s[:]), reads=[W2s], writes=[W2])
            S.dma("sp", posS[:], I["a_cmp_pos"][0, i].rearrange("j d -> d j"), writes=[posS], allow_slow_non_contiguous=True)
            S.op("dve", lambda e: e.tensor_copy(out=posT[:], in_=posS[:]), reads=[posS], writes=[posT])
            S.dma("sp", b1c[:], I["a_cmp_b1"][0, i:i + 1, :].rearrange("o (hf p) -> p (o hf)", p=128), writes=[b1c],
                  allow_slow_non_contiguous=True)
            S.dma("sp", b2s[:], I["a_cmp_b2"][0, i:i + 1, :], writes=[b2s])
            S.op("dve", lambda e: e.tensor_copy(out=b2r[:], in_=b2s[:]), reads=[b2s], writes=[b2r])
            for hf in range(2):
                for j in range(32):
                    S.op("pe", lambda e: e.matmul(pbias[:, hf:hf + 1], lhsT=W1[:, j, hf * 128:(hf + 1) * 128], rhs=posT[:, j:j + 1],
                                                  start=(j == 0 and hf == 0), stop=(j == 31), skip_group_check=True),
                         reads=[W1, posT], writes=[pbias])
            S.op("dve", lambda e: e.tensor_tensor(out=BT[:], in0=pbias[:], in1=b1c[:], op=ALU.add), reads=[pbias, b1c], writes=[BT])
            for g in range(3):
                XC = XCr()
                S.dma("sp", XC[:], featT0[22 + 3 * i + g, :, :], writes=[XC], dram_r=[featT0])
                XCv = XC[:].rearrange("d (c s) -> d c s", s=16)
                for hf in range(2):
                    for j in range(32):
                        S.op("pe", lambda e: e.matmul(ph[hf][:, 0:255], lhsT=W1[:, j, hf * 128:(hf + 1) * 128],
                                                      rhs=XCv[:, (j // 16):(j // 16) + 255, j % 16], start=(j == 0), stop=(j == 31)),
                             reads=[W1, XC], writes=[ph[hf]])
                    S.op("act", lambda e: e.activation(out=HG[:, hf, 0:255], in_=ph[hf][:, 0:255], func=AF.Gelu_apprx_tanh,
                                                       bias=BT[:, hf:hf + 1]),
                         reads=[ph[hf], BT], writes=[HG])
                for ct in range(2):
                    for hf in range(2):
                        S.op("pe", lambda e: e.matmul(pout[:, :], lhsT=HG[:, hf, ct * 128:(ct + 1) * 128], rhs=W2[:, hf, :],
                                                      start=(hf == 0), stop=False),
                             reads=[HG, W2], writes=[pout])
                    S.op("pe", lambda e: e.matmul(pout[:, :], lhsT=ones_bf[0:1, :], rhs=b2r[0:1, :], start=False, stop=True),
                         reads=[ones_bf, b2r], writes=[pout])
                    if i == 0:
                        S.op("act", lambda e: e.copy(out=YA[:, 0, :], in_=pout[:, :]), reads=[pout], writes=[YA])
                        S.op("dve", lambda e: e.tensor_copy(out=CS1[:], in_=CSc[:, ct, :, :]), reads=[CSc], writes=[CS1])
                        run(head_norm(YA, 1, 1, GAc, CS1, YB, (SQ, SS, RS, RS2, YN, T1, T2)))
                        run(feat_transposes(YB, 1, tpf, TTc))
                        S.op("dve", lambda e: e.tensor_copy(out=kcT[:, g, ct * 128:(ct + 1) * 128], in_=TTc[0:64, 0, :]),
                             reads=[TTc], writes=[kcT])
                    else:
                        S.op("act", lambda e: e.copy(out=VCO[:, ct, g, 0:64], in_=pout[:, :]), reads=[pout], writes=[VCO])

    kmT0 = sb("kmT0", [64, 4, 256], BF16, persist=True)
    VM0 = sb("VM0", [128, 2, 4, 65], BF16, persist=True)
    mem_phase(0, kmT0, VM0)

    with Phase():
        stg = ring("stg", [128, 1024], F32, 1)
        WO = WT("WO", 8, 1024)
        load_weight(WO, I["a_w_out"][0], 8, 1024, stg)
        KA = sb("KA", [128, 3, SEQ], BF16)
        KW = sb("KW", [64, 3, SEQ], BF16)
        VSW = sb("VSW", [128, 32, 6, 65], BF16)
        S.dma("sp", KA[0:64, :, :], featT0.ap[12:15, :, :].rearrange("h d t -> d h t"), writes=[KA], dram_r=[featT0])
        for g in range(3):
            S.dma("sp", KA[64:128, g, :], C["c_onehot"][:, :], writes=[KA])
        S.dma("sp", KW[:, :, :], featT0.ap[15:18, :, :].rearrange("h d t -> d h t"), writes=[KW], dram_r=[featT0])
        for kt0 in range(0, 32, 8):
            S.dma("sp", VSW[:, kt0:kt0 + 8, :, :], vtok0.ap[kt0 * 128:(kt0 + 8) * 128].rearrange("(kt p) h d -> p kt h d", p=128),
                  writes=[VSW], dram_r=[vtok0])
        msel = sb("msel", [128, 896], BF16); mwin = sb("mwin", [128, 1408], BF16)
        mcmp = sb("mcmp", [128, 512], F32); tkeep = sb("tkeep", [128, 128], F32); tadd = sb("tadd", [128, 128], F32)
        for t, k in ((msel, "m_sel"), (mwin, "m_win"), (mcmp, "m_cmp"), (tkeep, "m_tkeep"), (tadd, "m_tadd")):
            S.dma("sp", t[:], C[k][:, :], writes=[t])
        QAr = ring("QA", [128, 16, 512], BF16, 1)
        GTr = ring("GT", [128, 4, 36], F32, 1)
        XRr = ring("XR", [128, 4, DM], F32, 1)
        OM = sb("OM", [128, 4, DM], F32)
        Bk = [ps(f"bk{i}", [128, 512], F32) for i in range(7)]
        TP = ps("tp", [128, 8, 128], BF16)
        ST = bank_ring(Bk[0:4])
        ACC = bank_ring([acc_view(b) for b in Bk[4:7]])
        PO = bank_ring(Bk[0:2])
        SCsets = [(Bk[0], Bk[1]), (Bk[2], Bk[3])]
        OCI = [b.view(b.ap[:, 0:512].rearrange("p (r d) -> p r d", d=64)) for b in Bk[4:7]]
        PTr = ring("PT", [128, 512], BF16, LOOK + 2)
        SCm = ring("SCm", [128, 256], F32, 4)
        PCr = ring("PC", [128, 256], F32, 4)
        for t in PCr.tiles:
            S.op("pool", lambda e: e.memset(t[:], 0.0), writes=[t])
        PNr = ring("PN", [128, 256], BF16, 24)
        PTc = ring("PTc", [128, 8, 128], BF16, 3)
        rsr = ring("rsr", [128, 4, 2], F32, 2)
        TK = [dict(I1=sb("I1", [128, 64], F32), I2=sb("I2", [128, 64], F32), I3=sb("I3", [128, 64], F32),
                   M8=sb("M8", [128, 16], F32), SEL=sb("SEL", [128, 64], F32), VAL=sb("VAL", [128, 64], F32)) for _ in range(3)]
        MTr = ring("MT", [128, 128], BF16, 4)
        for t in MTr.tiles:
            S.op("pool", lambda e: e.memset(t[:], 0.0), writes=[t])
        Rt = sb("Rt", [128, 4], F32); Ft = sb("Ft", [128, 4], F32); Tt = sb("Tt", [128, 4, 64], F32)
        OBr = ring("OB", [128, DM], BF16, 1); OTr = ring("OT", [128, 8, 128], BF16, 1); XOr = ring("XO", [128, DM], F32, 1)
        for qi in range(SEQ // 512):
            Q0 = qi * 512
            QA = QAr(); GT = GTr(); XR = XRr()
            S.dma("sp", QA[0:64, 0:12, :], featT0.ap[0:12, :, Q0:Q0 + 512].rearrange("h d t -> d h t"), writes=[QA], dram_r=[featT0])
            S.dma("sp", QA[0:64, 12:16, :], featT0.ap[18:22, :, Q0:Q0 + 512].rearrange("h d t -> d h t"), writes=[QA], dram_r=[featT0])
            S.dma("sp", GT[:], gates.ap[Q0:Q0 + 512, :].rearrange("(s p) c -> p s c", p=128), writes=[GT], dram_r=[gates])
            S.op("act", lambda e: e.activation(out=GT[:], in_=GT[:], func=AF.Sigmoid), reads=[GT], writes=[GT])
            S.dma("sp", XR[:], xin.ap[Q0:Q0 + 512, :].rearrange("(s p) d -> p s d", p=128), writes=[XR])

            def stepA_front(n, s, g):
                i128 = qi * 4 + s
                bA, bB = SCsets[n % 2]
                scv = [(bA if r < 2 else bB) for r in range(4)]
                off = 248 - 8 * i128
                pns = []
                for r in range(4):
                    h = g * 4 + r
                    c0 = (r % 2) * 256
                    S.op("pe", lambda e: e.matmul(scv[r][:, c0:c0 + 255], lhsT=QA[0:64, h, s * 128:(s + 1) * 128], rhs=kcT[:, g, 0:255],
                                                  start=True, stop=True), reads=[QA, kcT], writes=[scv[r]])
                    yield
                scms = []
                for r in range(4):
                    c0 = (r % 2) * 256
                    scm = SCm()
                    scms.append(scm)
                    S.op("dve", lambda e: e.tensor_tensor(out=scm[:, 0:255], in0=scv[r][:, c0:c0 + 255], in1=mcmp[:, off:off + 255], op=ALU.add),
                         reads=[scv[r], mcmp], writes=[scm])
                    yield
                rs4 = rsr()
                S.op("pool", lambda e: e.memset(rs4[:], 0.0), writes=[rs4])
                yield
                pcs = []
                for r in range(4):
                    pc = PCr()
                    pcs.append(pc)
                    S.op("act", lambda e: e.activation(out=pc[:, 0:255], in_=scms[r][:, 0:255], func=AF.Exp, accum_out=rs4[:, r, 0:1]),
                         reads=[scms[r], rs4], writes=[pc, rs4])
                    yield
                S.op("dve", lambda e: e.tensor_scalar(out=rs4[:, :, 1], in0=rs4[:, :, 0], scalar1=1e-30, scalar2=None, op0=ALU.add),
                     reads=[rs4], writes=[rs4])
                yield
                S.op("dve", lambda e: e.reciprocal(out=rs4[:, :, 0], in_=rs4[:, :, 1]), reads=[rs4], writes=[rs4])
                yield
                for r in range(4):
                    pn = PNr()
                    pns.append(pn)
                    S.op("act", lambda e: e.activation(out=pn[:], in_=pcs[r][:], func=AF.Copy, scale=rs4[:, r, 0:1]),
                         reads=[pcs[r], rs4], writes=[pn])
                    yield
                ctxs[(s, g)] = pns

            def stepA_back(ci, s, g):
                pns = ctxs.pop((s, g))
                i128 = qi * 4 + s
                oci = OCI[ci]
                tk = TK[ci]
                I1, I2, I3, M8, SEL, VAL = tk["I1"], tk["I2"], tk["I3"], tk["M8"], tk["SEL"], tk["VAL"]
                for r in range(4):
                    for ct in range(2):
                        S.op("pe", lambda e: e.transpose(out=TP[:, r * 2 + ct, :], in_=pns[r][:, ct * 128:(ct + 1) * 128], identity=ident[:]),
                             reads=[pns[r], ident], writes=[TP])
                ptc = PTc()
                S.op("act", lambda e: e.copy(out=ptc[:], in_=TP[:, 0:8, :]), reads=[TP], writes=[ptc])
                yield
                for r in range(4):
                    for ct in range(2):
                        S.op("pe", lambda e: e.matmul(oci[:, r, :], lhsT=ptc[:, r * 2 + ct, :], rhs=VCO[:, ct, g, 0:64], start=(ct == 0), stop=(ct == 1),
                                                      skip_group_check=True),
                             reads=[ptc, VCO], writes=[oci])
                    for ct in range(2):
                        S.op("pe", lambda e: e.matmul(oci[:, 4 + r, :], lhsT=ptc[:, r * 2 + ct, :], rhs=VCO[:, ct, g, 64:128], start=(ct == 0), stop=(ct == 1),
                                                      skip_group_check=True),
                             reads=[ptc, VCO], writes=[oci])
                    yield
                gc = GT[:, s, 12 * g:12 * g + 12].rearrange("p (r b) -> p r b", b=3)[:, :, 0]
                S.op("dve", lambda e: e.tensor_tensor(out=OM[:, s, 256 * g:256 * g + 256].rearrange("p (r d) -> p r d", d=64), in0=oci[:, 0:4, :],
                                                      in1=gc.unsqueeze(2).to_broadcast([128, 4, 64]), op=ALU.mult),
                     reads=[oci, GT], writes=[OM])
                yield
                toff = 62 - 2 * i128
                S.op("dve", lambda e: e.tensor_reduce(out=I3[:], in_=oci[:, 4:8, :].rearrange("p r j -> p j r"), axis=AX.X, op=ALU.add),
                     reads=[oci], writes=[I3])
                yield
                S.op("dve", lambda e: e.tensor_tensor(out=I1[:], in0=I3[:], in1=tkeep[:, toff:toff + 64], op=ALU.mult),
                     reads=[I3, tkeep], writes=[I1])
                yield
                S.op("dve", lambda e: e.tensor_tensor(out=I2[:], in0=I1[:], in1=tadd[:, toff:toff + 64], op=ALU.add),
                     reads=[I1, tadd], writes=[I2])
                yield
                S.op("dve", lambda e: e.memset(I2[:, 0:1], 3e9), reads=[I2], writes=[I2])
                yield
                S.op("dve", lambda e: e.max(out=M8[:, 0:8], in_=I2[:]), reads=[I2], writes=[M8])
                yield
                S.op("dve", lambda e: e.match_replace(out=I3[:], in_to_replace=M8[:, 0:8], in_values=I2[:], imm_value=-1e30),
                     reads=[I2, M8], writes=[I3])
                yield
                S.op("dve", lambda e: e.max(out=M8[:, 8:16], in_=I3[:]), reads=[I3], writes=[M8])
                yield
                S.op("dve", lambda e: e.tensor_scalar(out=SEL[:], in0=I2[:], scalar1=M8[:, 15:16], scalar2=None, op0=ALU.is_ge),
                     reads=[I2, M8], writes=[SEL])
                yield
                S.op("dve", lambda e: e.tensor_scalar(out=VAL[:], in0=I2[:], scalar1=-1e29, scalar2=None, op0=ALU.is_gt),
                     reads=[I2], writes=[VAL])
                yield
                S.op("dve", lambda e: e.tensor_tensor(out=SEL[:], in0=SEL[:], in1=VAL[:], op=ALU.mult), reads=[SEL, VAL], writes=[SEL])
                yield
                MT = MTr()
                S.op("dve", lambda e: e.tensor_scalar(out=MT[:, 64:128], in0=SEL[:], scalar1=-1.0, scalar2=-NBIG, op0=ALU.add, op1=ALU.mult),
                     reads=[SEL], writes=[MT])
                yield
                S.op("pe", lambda e: e.transpose(out=TP[:, 0, :], in_=MT[:], identity=ident[:]), reads=[MT, ident], writes=[TP])
                S.op("act", lambda e: e.copy(out=QA[64:128, 4 * g:4 * g + 4, s * 128:(s + 1) * 128],
                                             in_=TP[64:128, 0:1, :].to_broadcast([64, 4, 128])), reads=[TP], writes=[QA])
                yield

            def fronts(s):
                for g in range(3):
                    yield from stepA_front(s * 3 + g, s, g)

            ctxs = {}
            run(fronts(0))
            for s in range(4):
                gens = [(stepA_back(g, s, g), 24) for g in range(3)]
                if s + 1 < 4:
                    gens.append((fronts(s + 1), 60))
                interleave(gens)

            nkt = qi * 4 + 4
            jobs = []
            for h in range(12):
                g = h // 4
                accS = ACC(); accW = ACC()
                for kt in range(nkt):
                    D = Q0 - kt * 128
                    subs = [sx for sx in range(4) if D + 128 * sx >= 0]
                    mt_, mo_ = (msel, D + 384) if D < 128 else (None, None)
                    jobs.append(tile_job(KA[:, g, kt * 128:(kt + 1) * 128], (lambda c0, c1, h=h: QA[:, h, c0:c1]), [KA, QA], mt_, mo_,
                                         accS, VSW[:, kt, g, :], VSW, subs, kt == 0))
                kts = list(range(max(0, qi * 4 - 4), nkt))
                for kt in kts:
                    D = Q0 - kt * 128
                    subs = [sx for sx in range(4) if 0 <= D + 128 * sx <= 512]
                    jobs.append(tile_job(KW[:, g, kt * 128:(kt + 1) * 128], (lambda c0, c1, h=h: QA[0:64, h, c0:c1]), [KW, QA], mwin, D + 384,
                                         accW, VSW[:, kt, 3 + g, :], VSW, subs, kt == kts[0]))
                jobs.append(fin_job(accS, OM, h * 64, (GT[:, :, 3 * h + 1], GT), (Rt, Ft, Tt), True))
                jobs.append(fin_job(accW, OM, h * 64, (GT[:, :, 3 * h + 2], GT), (Rt, Ft, Tt), True))
            jobs += mem_jobs(QA[:, 12:16, :], QA, kmT0, VM0, ACC, OM, 768, (Rt, Ft, Tt))
            run_jobs(jobs, ST, PTr)
            out_proj(OM, 1024, WO, XR, x1, Q0, TP, PO, OBr, OTr, XOr)

    mlp_phase(0, x1, x2)

    with Phase():
        stg = ring("stg", [128, 1792], F32, 2)
        WB = WT("WB", 8, 1792)
        WK = WT("WK", 8, 1024)
        load_weight(WB, I["b_w_in"][0], 8, 1792, stg, col_chunk=1792)
        load_weight(WK, I["w_kv_shared"], 8, 1024, stg, col_chunk=1024)
        GA1 = build_gains("GA1", 36, [(I["b_q_norm"][0, 0:1, :], 0, 8, 0.125), (I["b_q_norm"][0, 1:2, :], 8, 8, 0.125),
                                     (I["b_q_norm"][0, 2:3, :], 16, 8, 0.125), (I["kv_k_norm"][0:1, :], 24, 8, 1.0),
                                     (I["mem_q_norm"][1:2, :], 32, 4, 0.125)])
        seg1 = [(0, 0, 1536, "A", 0), (1, 0, 512, "A", 24), (0, 1536, 1792, "A", 32), (1, 512, 1024, "V", 0)]
        proj_phase(x2, SEQ // 128, [(gT_attn[1], WB, 1792), (gT_kv, WK, 1024)], 36, 32, GA1, None, featT1, vtok1, 8, seg_map=seg1)

    kmT1 = sb("kmT1", [64, 4, 256], BF16, persist=True)
    VM1 = sb("VM1", [128, 2, 4, 65], BF16, persist=True)
    mem_phase(1, kmT1, VM1)

    with Phase():
        stg = ring("stg", [128, 1024], F32, 2)
        WO1 = WT("WO1", 6, 1024)
        load_weight(WO1, I["b_w_out"][0], 6, 1024, stg)
        KT = sb("KT", [128, 4, SEQ], BF16)
        V1 = sb("V1", [128, 32, 8, 65], BF16)
        S.dma("sp", KT[:, :, :], featT1.ap[24:32, :, :].rearrange("(j h2) d t -> (h2 d) j t", h2=2), writes=[KT], dram_r=[featT1])
        for kt0 in range(0, 32, 8):
            S.dma("sp", V1[:, kt0:kt0 + 8, :, :], vtok1.ap[kt0 * 128:(kt0 + 8) * 128].rearrange("(kt p) h d -> p kt h d", p=128),
                  writes=[V1], dram_r=[vtok1])
        md = []
        for k, w in (("m_d0", 1024), ("m_d1", 1408), ("m_d2", 2944)):
            t = sb(k, [128, w], BF16)
            S.dma("sp", t[:], C[k][:, :], writes=[t])
            md.append(t)
        QTr = ring("QT", [128, 12, 512], BF16, 2)
        QMr = ring("QM", [64, 4, 512], BF16, 2)
        XRr = ring("XR", [128, 4, DM], F32, 1)
        OM = sb("OM", [128, 4, 768], F32)
        Bk = [ps(f"bk{i}", [128, 512], F32) for i in range(7)]
        TP = ps("tp", [128, 8, 128], BF16)
        ST = bank_ring(Bk[0:4])
        ACC = bank_ring([acc_view(b) for b in Bk[4:7]])
        PO = bank_ring(Bk[0:2])
        PTr = ring("PT", [128, 512], BF16, LOOK + 2)
        Rt = sb("Rt", [128, 4], F32); Ft = sb("Ft", [128, 4], F32); Tt = sb("Tt", [128, 4, 64], F32)
        OBr = ring("OB", [128, DM], BF16, 2); OTr = ring("OT", [128, 8, 128], BF16, 2); XOr = ring("XO", [128, DM], F32, 2)
        pats = ((128, 1), (512, 4), (2048, 16))
        for qi in range(SEQ // 512):
            Q0 = qi * 512
            QT = QTr(); QM = QMr(); XR = XRr()
            S.dma("sp", QT[:, :, :], featT1.ap[0:24, :, Q0:Q0 + 512].rearrange("(j h2) d t -> (h2 d) j t", h2=2), writes=[QT], dram_r=[featT1])
            S.dma("sp", QM[:, :, :], featT1.ap[32:36, :, Q0:Q0 + 512].rearrange("h d t -> d h t"), writes=[QM], dram_r=[featT1])
            S.dma("sp", XR[:], x2.ap[Q0:Q0 + 512, :].rearrange("(s p) d -> p s d", p=128), writes=[XR], dram_r=[x2])
            jobs = []
            for hh in range(8):
                acc = ACC()
                p0 = (hh % 2) * 64
                first = True
                for gi, (window, dil) in enumerate(pats):
                    qh = gi * 8 + hh
                    for kt in range(max(0, (Q0 - window) // 128), qi * 4 + 4):
                        D = Q0 - kt * 128
                        subs = [sx for sx in range(4) if 0 <= D + 128 * sx <= window]
                        jobs.append(tile_job(KT[p0:p0 + 64, hh // 2, kt * 128:(kt + 1) * 128],
                                             (lambda c0, c1, p0=p0, qh=qh: QT[p0:p0 + 64, qh // 2, c0:c1]), [KT, QT],
                                             md[gi], D + 384, acc, V1[:, kt, hh, :], V1, subs, first))
                        first = False
                jobs.append(fin_job(acc, OM, hh * 64, None, (Rt, Ft, Tt), False))
            jobs += mem_jobs(QM, QM, kmT1, VM1, ACC, OM, 512, (Rt, Ft, Tt))
            run_jobs(jobs, ST, PTr)
            out_proj(OM, 768, WO1, XR, x3, Q0, TP, PO, OBr, OTr, XOr)

    mlp_phase(1, x3, outT)
    S.barrier()
    gst.close()
    return nc


_CACHE = {}


def kernel(**inputs):
    n = 8
    consts = host_consts()
    if "nc" not in _CACHE:
        _CACHE["nc"] = build_program()
    nc = _CACHE["nc"]
    in_maps = []
    for b in range(n):
        m = {}
        for k, shp in IN_SHAPES.items():
            a = np.asarray(inputs[k], dtype=np.float32)
            if k in ("x", "mem"):
                a = a[b]
            m[k] = np.ascontiguousarray(a.reshape(shp))
        m.update(consts)
        in_maps.append(m)
    res = run_bass_kernel_spmd(nc, in_maps, core_ids=list(range(n)))
    return np.stack([np.asarray(r["out"], dtype=np.float32) for r in res.results], axis=0)
```

```python
import numpy as np
import ml_dtypes
import concourse.bass as bass
import concourse.mybir as mybir
from concourse.bass_utils import run_bass_kernel_spmd
from contextlib import ExitStack

F32 = mybir.dt.float32
BF16 = mybir.dt.bfloat16
AF = mybir.ActivationFunctionType
ALU = mybir.AluOpType
AX = mybir.AxisListType
SEM_ROT = 30000
SEQ = 4096
DM = 1024
EPS = 1e-6
NBIG = -30000.0


class Tile:
    __slots__ = ("name", "w", "r", "dsem", "ap", "base")

    def __init__(self, name, ap=None, base=None):
        self.name = name
        self.w = None
        self.r = []
        self.dsem = None
        self.ap = ap
        self.base = base

    def view(self, ap):
        return Tile(self.name + "_v", ap, base=(self.base or self))

    def __getitem__(self, k):
        return self.ap[k]


class DramT:
    def __init__(self, ap, name="d"):
        self.ap = ap
        self.name = name
        self.pend = set()

    def __getitem__(self, k):
        return self.ap[k]


class Sched:
    def __init__(self, nc):
        self.nc = nc
        self.eng = {"pe": nc.tensor, "act": nc.scalar, "dve": nc.vector,
                    "pool": nc.gpsimd, "sp": nc.sync}
        self.esem = {}
        self.ecnt = {}
        self.seen = {e: {} for e in self.eng}
        self.latest = {}
        self.nsem = 0
        self.free_dsems = []
        for e in ("pe", "act", "dve", "pool"):
            self._new_esem(e)

    def _alloc(self, name):
        self.nsem += 1
        return self.nc.alloc_semaphore(name=f"{name}_{self.nsem}")

    def _new_esem(self, e):
        self.esem[e] = self._alloc("e" + e)
        self.ecnt[e] = 0

    def tile(self, ap, name="t"):
        return Tile(name, ap)

    def _need(self, engine, reads, writes):
        need = {}
        reads = [t.base or t for t in reads]
        writes = [t.base or t for t in writes]

        def add(ev, raw):
            if ev is None:
                return
            sem, val, kind, eng = ev
            if kind == "c" and eng == engine:
                if engine == "pe":
                    return
            if kind == "d":
                val = self.latest[sem]
            if need.get(sem, 0) < val:
                need[sem] = val

        for t in reads:
            add(t.w, True)
        for t in writes:
            add(t.w, False)
            for ev in t.r:
                add(ev, False)
        seen = self.seen[engine]
        out = []
        for sem, val in need.items():
            if seen.get(sem, 0) < val:
                seen[sem] = val
                out.append((sem, val))
        return out

    def _record(self, ev, reads, writes):
        reads = [t.base or t for t in reads]
        writes = [t.base or t for t in writes]
        for t in reads:
            t.r = [x for x in t.r if x[0] is not ev[0]]
            t.r.append(ev)
        for t in writes:
            t.w = ev
            t.r = []

    def op(self, engine, fn, reads=(), writes=()):
        e = self.eng[engine]
        for sem, val in self._need(engine, reads, writes):
            e.wait_ge(sem, val)
        if self.ecnt[engine] >= SEM_ROT:
            self._new_esem(engine)
        inst = fn(e)
        self.ecnt[engine] += 1
        inst.then_inc(self.esem[engine], 1)
        ev = (self.esem[engine], self.ecnt[engine], "c", engine)
        self._record(ev, reads, writes)
        return ev

    def free_dsem(self, tiles):
        for t in tiles:
            if t.dsem is not None:
                self.free_dsems.append(t.dsem)
                t.dsem = None

    def dma(self, queue, out, in_, reads=(), writes=(), dram_r=(), dram_w=(), **kw):
        e = self.eng[queue]
        stile = writes[0] if writes else reads[0]
        stile = stile.base or stile
        if stile.dsem is None or self.latest[stile.dsem] >= SEM_ROT:
            stile.dsem = None
            while self.free_dsems and stile.dsem is None:
                c = self.free_dsems.pop()
                if self.latest[c] < SEM_ROT:
                    stile.dsem = c
            if stile.dsem is None:
                stile.dsem = self._alloc("d")
                self.latest[stile.dsem] = 0
        waits = self._need(queue, reads, writes)
        seen = self.seen[queue]
        for d in dram_r:
            for sem in d.pend:
                val = self.latest[sem]
                if seen.get(sem, 0) < val:
                    seen[sem] = val
                    waits.append((sem, val))
        for sem, val in waits:
            e.wait_ge(sem, val)
        inst = e.dma_start(out=out, in_=in_, **kw)
        sem = stile.dsem
        self.latest[sem] += 16
        inst.then_inc(sem, 16)
        ev = (sem, self.latest[sem], "d", None)
        self._record(ev, reads, writes)
        for d in dram_w:
            d.pend.add(sem)
        return ev

    def barrier(self):
        for en, e in self.eng.items():
            seen = self.seen[en]
            for f in ("pe", "act", "dve", "pool"):
                if f == en:
                    continue
                sem, val = self.esem[f], self.ecnt[f]
                if val > 0 and seen.get(sem, 0) < val:
                    seen[sem] = val
                    e.wait_ge(sem, val)
            for sem, val in self.latest.items():
                if val > 0 and seen.get(sem, 0) < val:
                    seen[sem] = val
                    e.wait_ge(sem, val)


def host_consts():
    bf = ml_dtypes.bfloat16
    c = {}
    half = 32
    inv = (10000.0 ** (-np.arange(half, dtype=np.float32) / half)).astype(np.float32)
    ang = np.arange(SEQ + 32, dtype=np.float32)[:, None] * inv[None, :]
    c["c_cos"] = np.cos(ang).astype(np.float32)
    c["c_sin"] = np.sin(ang).astype(np.float32)
    c["c_ident"] = np.eye(128, dtype=np.float32).astype(bf)
    kk = np.arange(128)[:, None]

    def toep(width, f):
        j = np.arange(width)[None, :]
        return f(j - 384 - kk).astype(np.float32).astype(bf)

    c["m_sel"] = toep(896, lambda d: d >= 0)
    c["m_win"] = toep(1408, lambda d: (d >= 0) & (d < 512))
    c["m_d0"] = toep(1024, lambda d: (d >= 0) & (d <= 128))
    c["m_d1"] = toep(1408, lambda d: (d >= 0) & (d <= 512) & (d % 4 == 0))
    c["m_d2"] = toep(2944, lambda d: (d >= 0) & (d <= 2048) & (d % 16 == 0))
    u = np.arange(512)[None, :]
    c["m_cmp"] = np.where(16 * (u - 248) + 31 <= kk, 0.0, NBIG).astype(np.float32)
    u = np.arange(128)[None, :]
    rel = u - 62 - (kk // 64)
    keep = (rel < -1).astype(np.float32)
    add = np.where(rel == 0, 2e9, np.where(rel == -1, 1e9, np.where(rel > 0, -1e30, 0.0)))
    c["m_tkeep"] = keep.astype(np.float32)
    c["m_tadd"] = add.astype(np.float32)
    n_cmp = SEQ // 16 - 1
    cs = np.arange(256)[:, None] * 16
    bs = np.arange(64)[None, :] * 64
    ov = ((cs < bs + 64) & (cs + 32 > bs)).astype(np.float32)
    ov[n_cmp:] = 0
    c["c_ovl"] = ov.astype(bf)
    c["c_onehot"] = (np.arange(SEQ)[None, :] // 64 == np.arange(64)[:, None]).astype(np.float32).astype(bf)
    return c


CONST_SHAPES = {
    "c_cos": ([SEQ + 32, 32], F32), "c_sin": ([SEQ + 32, 32], F32), "c_ident": ([128, 128], BF16),
    "m_sel": ([128, 896], BF16), "m_win": ([128, 1408], BF16), "m_d0": ([128, 1024], BF16),
    "m_d1": ([128, 1408], BF16), "m_d2": ([128, 2944], BF16), "m_cmp": ([128, 512], F32),
    "m_tkeep": ([128, 128], F32), "m_tadd": ([128, 128], F32), "c_ovl": ([256, 64], BF16),
    "c_onehot": ([64, SEQ], BF16),
}

IN_SHAPES = {
    "x": [SEQ, DM], "mem": [256, DM], "attn_norm": [2, DM], "mlp_norm": [2, DM],
    "w_up": [2, DM, 4096], "w_down": [2, 4096, DM], "mem_norm": [2, DM], "w_mem_kv": [2, DM, 512],
    "mem_q_norm": [2, 64], "mem_k_norm": [2, 64], "a_w_in": [1, DM, 2212], "a_w_out": [1, DM, DM],
    "a_q_norm": [1, 64], "a_k_norm": [1, 3, 64], "a_cmp_pos": [1, 2, 32, 64],
    "a_cmp_w1": [1, 2, 2048, 256], "a_cmp_b1": [1, 2, 256], "a_cmp_w2": [1, 2, 256, 64],
    "a_cmp_b2": [1, 2, 64], "kv_norm": [1, DM], "w_kv_shared": [DM, DM], "kv_k_norm": [1, 64],
    "b_w_in": [1, DM, 1792], "b_w_out": [1, 768, DM], "b_q_norm": [1, 3, 64],
}


def build_program(debug=()):
    nc = bass.Bass("TRN2", target_bir_lowering=False)
    S = Sched(nc)
    I = {}
    for k, shp in IN_SHAPES.items():
        I[k] = nc.dram_tensor(k, shp, F32, kind="ExternalInput").ap()
    C = {}
    for k, (shp, dt) in CONST_SHAPES.items():
        C[k] = nc.dram_tensor(k, shp, dt, kind="ExternalInput").ap()
    out_ap = nc.dram_tensor("out", [SEQ, DM], F32, kind="ExternalOutput").ap()

    def scratch(name, shape, dt):
        kind = "ExternalOutput" if name in debug else "Internal"
        return DramT(nc.dram_tensor(name, shape, dt, kind=kind).ap(), name)

    featT0 = scratch("featT0", [28, 64, SEQ], BF16)
    vtok0 = scratch("vtok0", [SEQ, 6, 65], BF16)
    gates = scratch("gates", [SEQ, 36], F32)
    x1 = scratch("x1", [SEQ, DM], F32)
    x2 = scratch("x2", [SEQ, DM], F32)
    x3 = scratch("x3", [SEQ, DM], F32)
    featT1 = scratch("featT1", [36, 64, SEQ], BF16)
    vtok1 = scratch("vtok1", [SEQ, 8, 65], BF16)
    outT = DramT(out_ap, "out")
    xin = DramT(I["x"], "x")

    gst = ExitStack()
    cur = {"st": gst}

    uid = {"i": 0}
    local_tiles = []

    def sb(name, shape, dt=F32, persist=False):
        st = gst if persist else cur["st"]
        uid["i"] += 1
        t = S.tile(st.enter_context(nc.sbuf_tensor(f"{name}_{uid['i']}", list(shape), dt)), name)
        if not persist:
            local_tiles.append(t)
        return t

    def ps(name, shape, dt=F32):
        uid["i"] += 1
        full = [128, 512] if dt == F32 else [128, 1024]
        h = cur["st"].enter_context(nc.psum_tensor(f"{name}_{uid['i']}", full, dt))
        n = 1
        for d in shape[1:]:
            n *= d
        v = h[:, 0:n]
        if len(shape) == 3:
            v = v.rearrange("p (a b) -> p a b", b=shape[2])
        return S.tile(v, name)

    class Phase:
        def __enter__(self):
            self.st = ExitStack()
            cur["st"] = self.st
            return self

        def __exit__(self, *a):
            S.barrier()
            S.free_dsem(local_tiles)
            del local_tiles[:]
            self.st.close()
            cur["st"] = gst
            return False

    rr = {"i": 0}

    def alt(engs=("dve", "pool")):
        rr["i"] += 1
        return engs[rr["i"] % len(engs)]

    def ring(name, shape, dt, n):
        tiles = [sb(f"{name}{i}", shape, dt) for i in range(n)]
        st = {"i": -1}

        def nxt():
            st["i"] = (st["i"] + 1) % n
            return tiles[st["i"]]
        nxt.tiles = tiles
        return nxt

    ident = sb("ident", [128, 128], BF16, persist=True)
    S.dma("sp", ident[:], C["c_ident"][:, :], writes=[ident])
    ones_bf = sb("ones_bf", [128, 128], BF16, persist=True)
    S.op("dve", lambda e: e.memset(ones_bf[:], 1.0), writes=[ones_bf])

    def bcast_row(dst_ap, src_row_ap, tile):
        S.dma("sp", dst_ap, src_row_ap.partition_broadcast(128), writes=[tile])

    def load_gT(name, src_row):
        t = sb(name, [128, 8], F32, persist=True)
        S.dma("sp", t[:], src_row.rearrange("o (kc p) -> p (o kc)", p=128), writes=[t],
              allow_slow_non_contiguous=True)
        return t

    gT_attn = [load_gT(f"gT_attn{l}", I["attn_norm"][l:l + 1, :]) for l in range(2)]
    gT_mlp = [load_gT(f"gT_mlp{l}", I["mlp_norm"][l:l + 1, :]) for l in range(2)]
    gT_mem = [load_gT(f"gT_mem{l}", I["mem_norm"][l:l + 1, :]) for l in range(2)]
    gT_kv = load_gT("gT_kv", I["kv_norm"][0:1, :])

    def load_weight(wt, src, nk, ncols, stg_ring, col_chunk=2048, engs=("dve", "act", "pool")):
        for kc in range(nk):
            eng = engs[kc % len(engs)]
            for c0 in range(0, ncols, col_chunk):
                w = min(col_chunk, ncols - c0)
                stg = stg_ring()
                S.dma("sp", stg[:, 0:w], src[kc * 128:(kc + 1) * 128, c0:c0 + w], writes=[stg])
                if eng == "act":
                    S.op(eng, lambda e: e.copy(out=wt.ap[:, kc, c0:c0 + w], in_=stg[:, 0:w]), reads=[stg], writes=[wt.k[kc]])
                else:
                    S.op(eng, lambda e: e.tensor_copy(out=wt.ap[:, kc, c0:c0 + w], in_=stg[:, 0:w]),
                         reads=[stg], writes=[wt.k[kc]])

    class WT:
        def __init__(self, name, nk, ncols):
            self.h = sb(name, [128, nk, ncols], BF16)
            self.ap = self.h.ap
            self.k = [Tile(f"{name}_{i}") for i in range(nk)]

    def run(gen):
        for _ in gen:
            pass

    def interleave(gens):
        state = [[g, 0, float(tot)] for g, tot in gens]
        while state:
            st = min(state, key=lambda z: z[1] / z[2])
            try:
                next(st[0])
                st[1] += 1
            except StopIteration:
                state.remove(st)

    def norm_tile(xt_ap, xt_tile, gTs, hTs, tph, tmp):
        junk, ssq, rs, rs2, xn = tmp
        S.op("pool", lambda e: e.memset(ssq[:], 0.0), writes=[ssq])
        yield
        S.op("act", lambda e: e.activation(out=junk[:], in_=xt_ap, func=AF.Square, accum_out=ssq[:, 0:1]),
             reads=[xt_tile, ssq], writes=[junk, ssq])
        yield
        S.op("act", lambda e: e.activation(out=rs[:], in_=ssq[:], func=AF.Sqrt, scale=1.0 / DM, bias=EPS),
             reads=[ssq], writes=[rs])
        yield
        S.op("dve", lambda e: e.reciprocal(out=rs2[:], in_=rs[:]), reads=[rs], writes=[rs2])
        yield
        S.op("dve", lambda e: e.tensor_scalar(out=xn[:], in0=xt_ap, scalar1=rs2[:, 0:1], scalar2=None, op0=ALU.mult),
             reads=[xt_tile, rs2], writes=[xn])
        yield
        for kc in range(8):
            S.op("pe", lambda e: e.transpose(out=tph[:, kc, :], in_=xn[:, kc * 128:(kc + 1) * 128], identity=ident[:]),
                 reads=[xn, ident], writes=[tph])
            yield
        for gT, hT in zip(gTs, hTs):
            S.op("dve", lambda e: e.tensor_tensor(out=hT[:], in0=tph[:], in1=gT[:].unsqueeze(2).to_broadcast([128, 8, 128]), op=ALU.mult),
                 reads=[tph, gT], writes=[hT])
            yield

    def head_norm(YA, nh, nrope, GA, CS, YB, tmp):
        SQ, SS, RS, RS2, YN, T1, T2 = tmp
        S.op("act", lambda e: e.activation(out=SQ[:, 0:nh, :], in_=YA[:, 0:nh, :], func=AF.Square),
             reads=[YA], writes=[SQ])
        yield
        S.op("dve", lambda e: e.tensor_reduce(out=SS[:, 0:nh], in_=SQ[:, 0:nh, :], axis=AX.X, op=ALU.add),
             reads=[SQ], writes=[SS])
        yield
        S.op("act", lambda e: e.activation(out=RS[:, 0:nh], in_=SS[:, 0:nh], func=AF.Sqrt, scale=1.0 / 64, bias=EPS),
             reads=[SS], writes=[RS])
        yield
        S.op("dve", lambda e: e.reciprocal(out=RS2[:, 0:nh], in_=RS[:, 0:nh]), reads=[RS], writes=[RS2])
        yield
        S.op("dve", lambda e: e.tensor_tensor(out=YN[:, 0:nh, :], in0=YA[:, 0:nh, :],
                                              in1=RS2[:, 0:nh].unsqueeze(2).to_broadcast([128, nh, 64]), op=ALU.mult),
             reads=[YA, RS2], writes=[YN])
        yield
        S.op("dve", lambda e: e.tensor_tensor(out=YN[:, 0:nh, :], in0=YN[:, 0:nh, :], in1=GA[:, 0:nh, :], op=ALU.mult),
             reads=[YN, GA], writes=[YN])
        yield
        if nrope:
            n = nrope
            cosb = CS[:, 0:1, :].to_broadcast([128, n, 32])
            sinb = CS[:, 1:2, :].to_broadcast([128, n, 32])
            x1v = YN[:, 0:n, 0:32]
            x2v = YN[:, 0:n, 32:64]
            S.op("dve", lambda e: e.tensor_tensor(out=T1[:, 0:n, 0:32], in0=x1v, in1=cosb, op=ALU.mult), reads=[YN, CS], writes=[T1])
            yield
            S.op("dve", lambda e: e.tensor_tensor(out=T2[:, 0:n, 0:32], in0=x2v, in1=sinb, op=ALU.mult), reads=[YN, CS], writes=[T2])
            yield
            S.op("dve", lambda e: e.tensor_tensor(out=YB[:, 0:n, 0:32], in0=T1[:, 0:n, 0:32], in1=T2[:, 0:n, 0:32], op=ALU.subtract),
                 reads=[T1, T2], writes=[YB])
            yield
            S.op("dve", lambda e: e.tensor_tensor(out=T1[:, 0:n, 32:64], in0=x1v, in1=sinb, op=ALU.mult), reads=[YN, CS], writes=[T1])
            yield
            S.op("dve", lambda e: e.tensor_tensor(out=T2[:, 0:n, 32:64], in0=x2v, in1=cosb, op=ALU.mult), reads=[YN, CS], writes=[T2])
            yield
            S.op("dve", lambda e: e.tensor_tensor(out=YB[:, 0:n, 32:64], in0=T1[:, 0:n, 32:64], in1=T2[:, 0:n, 32:64], op=ALU.add),
                 reads=[T1, T2], writes=[YB])
            yield
        if nh > nrope:
            S.op("act", lambda e: e.copy(out=YB[:, nrope:nh, :], in_=YN[:, nrope:nh, :]), reads=[YN], writes=[YB])
            yield

    def feat_transposes(YB, npairs, tpf, TT):
        YBf = YB[:].rearrange("p h d -> p (h d)")
        for b0 in range(0, npairs, 8):
            nb = min(8, npairs - b0)
            for j in range(nb):
                S.op("pe", lambda e: e.transpose(out=tpf[:, j, :], in_=YBf[:, (b0 + j) * 128:(b0 + j + 1) * 128], identity=ident[:]),
                     reads=[YB, ident], writes=[tpf])
                yield
            S.op("act", lambda e: e.copy(out=TT[:, b0:b0 + nb, :], in_=tpf[:, 0:nb, :]), reads=[tpf], writes=[TT])
            yield

    def build_gains(name, nh, specs):
        GA = sb(name, [128, nh, 64], F32)
        G1 = sb(name + "_raw", [128, len(specs), 64], F32)
        for i, (src, h0, n, scale) in enumerate(specs):
            bcast_row(G1[:, i, :], src, G1)
        for i, (src, h0, n, scale) in enumerate(specs):
            S.op("dve", lambda e: e.tensor_scalar(out=GA[:, h0:h0 + n, :], in0=G1[:, i:i + 1, :].to_broadcast([128, n, 64]),
                                                  scalar1=float(scale), scalar2=None, op0=ALU.mult),
                 reads=[G1], writes=[GA])
        return GA

    def proj_phase(x_src, ntiles, projs, nh, nrope, GA, raw_specs, featT, vtok, vtok_n, gate_cols=None,
                   seg_map=None, pos_tab=True):
        nraw_t = sum((c1 - c0) // 64 for (_, c0, c1, k, _) in seg_map if k == "T")
        ntot = nh + nraw_t
        npairs = ntot // 2
        xr = ring("xt", [128, DM], F32, 3)
        junkr = ring("junk", [128, DM], BF16, 2)
        ssqr = ring("ssq", [128, 1], F32, 2); rsr_ = ring("rs", [128, 1], F32, 2); rs2r = ring("rs2", [128, 1], F32, 2)
        xnr = ring("xn", [128, DM], BF16, 2)
        tph = ps("tph", [128, 8, 128], BF16)
        tpf = ps("tpf", [128, 8, 128], BF16)
        hTr = [ring(f"hT{i}", [128, 8, 128], BF16, 2) for i in range(len(projs))]
        banks = []
        for pi, (gT, W, ncols) in enumerate(projs):
            for c0 in range(0, ncols, 512):
                banks.append((pi, c0, min(512, ncols - c0), ps(f"pb{pi}_{c0}", [128, 512], F32)))
        YAr = ring("YA", [128, ntot, 64], F32, 2)
        YBr = ring("YB", [128, ntot, 64], BF16, 2)
        SQ = sb("SQ", [128, nh, 64], F32); SS = sb("SS", [128, nh]); RS = sb("RS", [128, nh]); RS2 = sb("RS2", [128, nh])
        YN = sb("YN", [128, nh, 64], F32); T1 = sb("T1", [128, max(nrope, 1), 64], F32); T2 = sb("T2", [128, max(nrope, 1), 64], F32)
        CSr = ring("CS", [128, 2, 32], F32, 6)
        TTr = ring("TT", [128, npairs, 128], BF16, 2)
        YVr = ring("YV", [128, vtok_n, 65], BF16, 2) if vtok is not None else None
        if YVr is not None:
            for t in YVr.tiles:
                S.op("pool", lambda e: e.memset(t[:], 1.0), writes=[t])
        YGr = ring("YG", [128, 36], F32, 2) if gate_cols else None
        ctx = {}

        def s0(ti):
            t0 = ti * 128
            c = ctx[ti] = {}
            xt = c["xt"] = xr()
            S.dma("sp", xt[:], x_src[t0:t0 + 128, :], writes=[xt], dram_r=[x_src])
            CS = CSr()
            c["CS"] = CS
            if nrope:
                S.dma("sp", CS[:, 0, :], C["c_cos"][t0:t0 + 128, :], writes=[CS])
                S.dma("sp", CS[:, 1, :], C["c_sin"][t0:t0 + 128, :], writes=[CS])

        def s1(ti):
            c = ctx[ti]
            xt = c["xt"]
            c["hT"] = [r() for r in hTr]
            yield from norm_tile(xt[:], xt, [p[0] for p in projs], c["hT"], tph, (junkr(), ssqr(), rsr_(), rs2r(), xnr()))

        def s2(ti):
            c = ctx[ti]
            hTs = c["hT"]
            for (pi, c0, w, pb) in banks:
                W = projs[pi][1]
                for kc in range(8):
                    S.op("pe", lambda e: e.matmul(pb[:, 0:w], lhsT=hTs[pi][:, kc, :], rhs=W.ap[:, kc, c0:c0 + w],
                                                  start=(kc == 0), stop=(kc == 7)),
                         reads=[hTs[pi], W.k[kc]], writes=[pb])
                    yield
            YA = c["YA"] = YAr()
            YB = c["YB"] = YBr()
            YV = c["YV"] = YVr() if YVr is not None else None
            YG = c["YG"] = YGr() if YGr is not None else None
            for (pi, c0, c1, kind, di) in seg_map:
                cc = c0
                while cc < c1:
                    bk = [b for b in banks if b[0] == pi and b[1] <= cc < b[1] + b[2]][0]
                    ce = min(c1, bk[1] + bk[2])
                    src = bk[3][:, cc - bk[1]:ce - bk[1]]
                    off = cc - c0
                    n = ce - cc
                    if kind in ("A", "T"):
                        dst = YA if kind == "A" else YB
                        dflat = dst[:].rearrange("p h d -> p (h d)")
                        S.op("act", lambda e: e.copy(out=dflat[:, di * 64 + off: di * 64 + off + n], in_=src),
                             reads=[bk[3]], writes=[dst])
                        yield
                    elif kind == "V":
                        h0 = di + off // 64
                        S.op("act", lambda e: e.copy(out=YV[:, h0:h0 + n // 64, 0:64],
                                                     in_=src.rearrange("p (h d) -> p h d", d=64)),
                             reads=[bk[3]], writes=[YV])
                        yield
                    elif kind == "G":
                        S.op("act", lambda e: e.copy(out=YG[:, off:off + n], in_=src), reads=[bk[3]], writes=[YG])
                        yield
                    cc = ce

        def s3(ti):
            t0 = ti * 128
            c = ctx.pop(ti)
            yield from head_norm(c["YA"], nh, nrope, GA, c["CS"], c["YB"], (SQ, SS, RS, RS2, YN, T1, T2))
            TT = TTr()
            yield from feat_transposes(c["YB"], npairs, tpf, TT)
            S.dma("sp", featT.ap.rearrange("(j h2) d t -> (h2 d) j t", h2=2)[:, :, t0:t0 + 128], TT[:],
                  reads=[TT], dram_w=[featT])
            yield
            if vtok is not None:
                S.dma("sp", vtok[t0:t0 + 128, :, :], c["YV"][:], reads=[c["YV"]], dram_w=[vtok])
                yield
            if c["YG"] is not None:
                S.dma("sp", gates[t0:t0 + 128, :], c["YG"][:], reads=[c["YG"]], dram_w=[gates])
                yield

        s0(0)
        s0(1)
        for step in range(ntiles + 2):
            if step + 2 < ntiles:
                s0(step + 2)
            gens = []
            if step < ntiles:
                gens.append((s1(step), 10))
            if 0 <= step - 1 < ntiles:
                gens.append((s2(step - 1), 56))
            if 0 <= step - 2 < ntiles:
                gens.append((s3(step - 2), 28))
            interleave(gens)

    def mem_phase(layer, kmT, VM):
        with Phase():
            stg = ring("stg", [128, 2048], F32, 2)
            W = WT("wmem", 8, 512)
            load_weight(W, I["w_mem_kv"][layer], 8, 512, stg)
            GA = build_gains(f"GAm{layer}", 4, [(I["mem_k_norm"][layer:layer + 1, :], 0, 4, 1.0)])
            xr = ring("xt", [128, DM], F32, 2)
            junk = sb("junk", [128, DM], BF16)
            ssq = sb("ssq", [128, 1]); rs = sb("rs", [128, 1]); rs2 = sb("rs2", [128, 1])
            xn = sb("xn", [128, DM], BF16)
            tph = ps("tph", [128, 8, 128], BF16)
            tpf = ps("tpf", [128, 8, 128], BF16)
            hT = sb("hT", [128, 8, 128], BF16)
            pb = ps("pb", [128, 512], F32)
            YA = sb("YA", [128, 4, 64], F32); YB = sb("YB", [128, 4, 64], BF16)
            SQ = sb("SQ", [128, 4, 64], F32); SS = sb("SS", [128, 4]); RS = sb("RS", [128, 4]); RS2 = sb("RS2", [128, 4])
            YN = sb("YN", [128, 4, 64], F32); T1 = sb("T1", [128, 1, 64], F32); T2 = sb("T2", [128, 1, 64], F32)
            TT = sb("TT", [128, 2, 128], BF16)
            S.op("pool", lambda e: e.memset(VM[:], 1.0), writes=[VM])
            memT = DramT(I["mem"], "mem")
            for mt in range(2):
                xt = xr()
                S.dma("sp", xt[:], memT[mt * 128:(mt + 1) * 128, :], writes=[xt])
                run(norm_tile(xt[:], xt, [gT_mem[layer]], [hT], tph, (junk, ssq, rs, rs2, xn)))
                for kc in range(8):
                    S.op("pe", lambda e: e.matmul(pb[:, :], lhsT=hT[:, kc, :], rhs=W.ap[:, kc, :], start=(kc == 0), stop=(kc == 7)),
                         reads=[hT, W.k[kc]], writes=[pb])
                S.op("act", lambda e: e.copy(out=YA[:].rearrange("p h d -> p (h d)"), in_=pb[:, 0:256]), reads=[pb], writes=[YA])
                S.op("act", lambda e: e.copy(out=VM[:, mt, :, 0:64], in_=pb[:, 256:512].rearrange("p (h d) -> p h d", d=64)),
                     reads=[pb], writes=[VM])
                run(head_norm(YA, 4, 0, GA, None, YB, (SQ, SS, RS, RS2, YN, T1, T2)))
                run(feat_transposes(YB, 2, tpf, TT))
                for mh in range(4):
                    S.op("dve", lambda e: e.tensor_copy(out=kmT[:, mh, mt * 128:(mt + 1) * 128],
                                                        in_=TT[(mh % 2) * 64:(mh % 2) * 64 + 64, mh // 2, :]),
                         reads=[TT], writes=[kmT])

    LOOK = 3

    def tile_job(lhsT_ap, rhs_fn, reads, mask_tile, mask_off, acc, v_ap, v_tile, subs, first):
        return ["tile", dict(lhsT=lhsT_ap, rhs_fn=rhs_fn, reads=reads, mt=mask_tile, mo=mask_off, acc=acc, v=v_ap,
                             vt=v_tile, subs=subs, first=first)]

    def run_jobs(jobs, ST, PTr):
        tiles = [j[1] for j in jobs if j[0] == "tile"]
        pos = {"fi": 0, "ti": 0}

        def front(j):
            c0 = min(j["subs"]) * 128
            c1 = (max(j["subs"]) + 1) * 128
            st = ST()
            pt = PTr()
            j["pt"] = pt
            S.op("pe", lambda e: e.matmul(st[:, c0:c1], lhsT=j["lhsT"], rhs=j["rhs_fn"](c0, c1), start=True, stop=True),
                 reads=j["reads"], writes=[st])
            S.op("act", lambda e: e.activation(out=pt[:, c0:c1], in_=st[:, c0:c1], func=AF.Exp), reads=[st], writes=[pt])
            if j["mt"] is not None:
                mo = j["mo"]
                S.op("dve", lambda e: e.tensor_tensor(out=pt[:, c0:c1], in0=pt[:, c0:c1], in1=j["mt"][:, mo + c0:mo + c1], op=ALU.mult),
                     reads=[pt, j["mt"]], writes=[pt])

        def back(j):
            pt = j["pt"]
            for i, sidx in enumerate(j["subs"]):
                S.op("pe", lambda e: e.matmul(j["acc"][:, sidx, :], lhsT=pt[:, sidx * 128:(sidx + 1) * 128], rhs=j["v"],
                                              start=(j["first"] and i == 0), stop=False, skip_group_check=True),
                     reads=[pt, j["vt"]], writes=[j["acc"]])

        for j in jobs:
            if j[0] == "tile":
                while pos["fi"] < len(tiles) and pos["fi"] <= pos["ti"] + LOOK:
                    front(tiles[pos["fi"]])
                    pos["fi"] += 1
                back(j[1])
                pos["ti"] += 1
            else:
                j[1]()

    def finalize(acc, OM, col0, fac_fn, tmp, accumulate):
        R, F, T = tmp
        S.op("dve", lambda e: e.reciprocal(out=R[:], in_=acc[:, :, 64]), reads=[acc], writes=[R])
        fac = R
        if fac_fn is not None:
            gap, gtile = fac_fn
            S.op("dve", lambda e: e.tensor_tensor(out=F[:], in0=R[:], in1=gap, op=ALU.mult), reads=[R, gtile], writes=[F])
            fac = F
        facb = fac[:].unsqueeze(2).to_broadcast([128, 4, 64])
        if not accumulate:
            S.op("dve", lambda e: e.tensor_tensor(out=OM[:, :, col0:col0 + 64], in0=acc[:, :, 0:64], in1=facb, op=ALU.mult),
                 reads=[acc, fac], writes=[OM])
        else:
            S.op("dve", lambda e: e.tensor_tensor(out=T[:], in0=acc[:, :, 0:64], in1=facb, op=ALU.mult), reads=[acc, fac], writes=[T])
            S.op("pool", lambda e: e.tensor_tensor(out=OM[:, :, col0:col0 + 64], in0=OM[:, :, col0:col0 + 64], in1=T[:], op=ALU.add),
                 reads=[OM, T], writes=[OM])

    def out_proj(OM, ncols_in, WO, Xres, dst, Q0, tp, po_ring, OBr, OTr, XOr):
        nk = ncols_in // 128
        for s in range(4):
            OB = OBr()
            S.op("dve", lambda e: e.tensor_copy(out=OB[:, 0:ncols_in], in_=OM[:, s, :]), reads=[OM], writes=[OB])
            for kc in range(nk):
                S.op("pe", lambda e: e.transpose(out=tp[:, kc, :], in_=OB[:, kc * 128:(kc + 1) * 128], identity=ident[:]),
                     reads=[OB, ident], writes=[tp])
            OT = OTr()
            S.op("act", lambda e: e.copy(out=OT[:, 0:nk, :], in_=tp[:, 0:nk, :]), reads=[tp], writes=[OT])
            XO = XOr()
            for half in range(2):
                po = po_ring()
                for kc in range(nk):
                    S.op("pe", lambda e: e.matmul(po[:, :], lhsT=OT[:, kc, :], rhs=WO.ap[:, kc, half * 512:(half + 1) * 512],
                                                  start=(kc == 0), stop=(kc == nk - 1)),
                         reads=[OT, WO.k[kc]], writes=[po])
                S.op("dve", lambda e: e.tensor_tensor(out=XO[:, half * 512:(half + 1) * 512], in0=po[:, :],
                                                      in1=Xres[:, s, half * 512:(half + 1) * 512], op=ALU.add),
                     reads=[po, Xres], writes=[XO])
            S.dma("sp", dst[Q0 + s * 128:Q0 + (s + 1) * 128, :], XO[:], reads=[XO], dram_w=[dst])

    def fin_job(acc, OM, col0, fac_fn, tmp, accumulate):
        return ["fin", lambda: finalize(acc, OM, col0, fac_fn, tmp, accumulate)]

    def mem_jobs(QM, qm_tile, kmT, VM, ACC, OM, col0, tmp):
        jobs = []
        for mh in range(4):
            acc = ACC()
            for mt in range(2):
                jobs.append(tile_job(kmT[:, mh, mt * 128:(mt + 1) * 128], (lambda c0, c1, mh=mh: QM[0:64, mh, c0:c1]), [kmT, qm_tile],
                                     None, None, acc, VM[:, mt, mh, :], VM, [0, 1, 2, 3], mt == 0))
            jobs.append(fin_job(acc, OM, col0 + mh * 64, None, tmp, False))
        return jobs

    def bank_ring(tiles):
        st = {"i": -1}

        def nxt():
            st["i"] = (st["i"] + 1) % len(tiles)
            return tiles[st["i"]]
        return nxt

    def acc_view(b):
        return b.view(b.ap[:, 0:260].rearrange("p (s c) -> p s c", c=65))

    def mlp_phase(layer, src, dst):
        with Phase():
            stg = ring("stg", [128, 1024], F32, 2)
            WU = WT("WU", 8, 4096)
            WD = WT("WD", 32, 1024)
            load_weight(WU, I["w_up"][layer], 8, 4096, stg, col_chunk=1024)
            xr = ring("xt", [128, 2, DM], F32, 3)
            junk = sb("junk", [128, DM], BF16)
            ssq = sb("ssq", [128, 1]); rs = sb("rs", [128, 1]); rs2 = sb("rs2", [128, 1])
            xn = sb("xn", [128, DM], BF16)
            tph = ps("tph", [128, 8, 128], BF16)
            uT = sb("uT", [128, 32, 256], BF16)
            rl = ring("rl", [128, 256], BF16, 3)
            XO = ring("XO", [128, DM], F32, 2)
            pu = [ps(f"pu{i}", [128, 512], F32) for i in range(3)]
            pd = [ps(f"pd{i}", [128, 512], F32) for i in range(3)]
            cnt = 0
            xts = {}

            def ldx(tb):
                xts[tb] = xr()
                S.dma("sp", xts[tb][:], src.ap[tb * 256:tb * 256 + 256, :].rearrange("(s p) d -> p s d", p=128),
                      writes=[xts[tb]], dram_r=[src])
            ldx(0)
            hTr2 = ring("hTm", [128, 8, 256], BF16, 2)
            hTs = {}

            def normgen(tb):
                xt = xts[tb]
                hTt = hTr2()
                hTs[tb] = hTt
                for s in range(2):
                    hv = hTt.view(hTt.ap[:, :, s * 128:(s + 1) * 128])
                    yield from norm_tile(xt[:, s, :], xt, [gT_mlp[layer]], [hv], tph, (junk, ssq, rs, rs2, xn))

            def blockgen(tb):
                nonlocal_cnt = cntbox
                t0 = tb * 256
                xt = xts[tb]
                hTt = hTs.pop(tb)
                for fc in range(32):
                    p = pu[fc % 3]
                    for kc in range(8):
                        S.op("pe", lambda e: e.matmul(p[:, 0:256], lhsT=WU.ap[:, kc, fc * 128:(fc + 1) * 128], rhs=hTt[:, kc, :],
                                                      start=(kc == 0), stop=(kc == 7)),
                             reads=[WU.k[kc], hTt], writes=[p])
                        yield
                    r = rl()
                    S.op("act", lambda e: e.activation(out=r[:], in_=p[:, 0:256], func=AF.Relu), reads=[p], writes=[r])
                    yield
                    S.op("dve", lambda e: e.tensor_tensor(out=uT[:, fc, :], in0=r[:], in1=r[:], op=ALU.mult), reads=[r], writes=[uT])
                    yield
                if tb == 0:
                    load_weight(WD, I["w_down"][layer], 32, 1024, stg, col_chunk=1024, engs=("dve", "act"))
                for s in range(2):
                    xo = XO()
                    for half in range(2):
                        p = pd[nonlocal_cnt[0] % 3]
                        nonlocal_cnt[0] += 1
                        for kc in range(32):
                            S.op("pe", lambda e: e.matmul(p[:, :], lhsT=uT[:, kc, s * 128:(s + 1) * 128],
                                                          rhs=WD.ap[:, kc, half * 512:(half + 1) * 512],
                                                          start=(kc == 0), stop=(kc == 31)),
                                 reads=[uT, WD.k[kc]], writes=[p])
                            yield
                        S.op("dve", lambda e: e.tensor_tensor(out=xo[:, half * 512:(half + 1) * 512], in0=p[:, :],
                                                              in1=xt[:, s, half * 512:(half + 1) * 512], op=ALU.add),
                             reads=[p, xt], writes=[xo])
                        yield
                    S.dma("sp", dst[t0 + s * 128:t0 + (s + 1) * 128, :], xo[:], reads=[xo], dram_w=[dst])
                    yield
                xts.pop(tb)

            cntbox = [0]
            nblk = SEQ // 256
            ldx(1)
            run(normgen(0))
            for tb in range(nblk):
                gens = [(blockgen(tb), 460)]
                if tb + 1 < nblk:
                    gens.append((normgen(tb + 1), 32))
                interleave(gens)
                if tb + 2 < nblk:
                    ldx(tb + 2)

    with Phase():
        stg = ring("stg", [128, 2212], F32, 2)
        WI = WT("WI", 8, 2212)
        load_weight(WI, I["a_w_in"][0], 8, 2212, stg, col_chunk=2212)
        GA0 = build_gains("GA0", 22, [(I["a_q_norm"][0:1, :], 0, 12, 0.125), (I["a_k_norm"][0, 1:2, :], 12, 3, 1.0),
                                     (I["a_k_norm"][0, 2:3, :], 15, 3, 1.0), (I["mem_q_norm"][0:1, :], 18, 4, 0.125)])
        seg0 = [(0, 0, 768, "A", 0), (0, 1152, 1344, "A", 12), (0, 1536, 1728, "A", 15), (0, 1920, 2176, "A", 18),
                (0, 768, 960, "T", 22), (0, 960, 1152, "T", 25),
                (0, 1344, 1536, "V", 0), (0, 1728, 1920, "V", 3), (0, 2176, 2212, "G", 0)]
        proj_phase(xin, SEQ // 128, [(gT_attn[0], WI, 2212)], 22, 18, GA0, None, featT0, vtok0, 6, gate_cols=True, seg_map=seg0)

    kcT = sb("kcT", [64, 3, 256], BF16, persist=True)
    VCO = sb("VCO", [128, 2, 3, 128], BF16, persist=True)
    with Phase():
        GAc = build_gains("GAc", 1, [(I["a_k_norm"][0, 0:1, :], 0, 1, 1.0)])
        for ct in range(2):
            for g in range(3):
                S.dma("sp", VCO[:, ct, g, 64:128], C["c_ovl"][ct * 128:(ct + 1) * 128, :], writes=[VCO])
        XCr = ring("XC", [64, SEQ], BF16, 2)
        W1s = sb("W1s", [64, 32, 256], F32)
        W1 = sb("W1", [64, 32, 256], BF16)
        W2s = sb("W2s", [128, 2, 64], F32)
        W2 = sb("W2", [128, 2, 64], BF16)
        posS = sb("posS", [64, 32], F32); posT = sb("posT", [64, 32], BF16)
        b1c = sb("b1c", [128, 2], F32); BT = sb("BT", [128, 2], F32)
        b2s = sb("b2s", [1, 64], F32); b2r = sb("b2r", [1, 64], BF16)
        HG = sb("HG", [128, 2, 256], BF16)
        S.op("pool", lambda e: e.memset(HG[:], 0.0), writes=[HG])
        ph = [ps(f"ph{i}", [128, 512], F32) for i in range(2)]
        pbias = ps("pbias", [128, 2], F32)
        pout = ps("pout", [128, 64], F32)
        tpf = ps("tpf", [128, 8, 128], BF16)
        CSc = sb("CSc", [128, 2, 2, 32], F32)
        S.op("pool", lambda e: e.memset(CSc[:], 0.0), writes=[CSc])
        for ct in range(2):
            n = 128 if ct == 0 else 127
            S.dma("sp", CSc[0:n, ct, 0, :], C["c_cos"].rearrange("(c s) f -> c s f", s=16)[ct * 128 + 1:ct * 128 + 1 + n, 15, :], writes=[CSc])
            S.dma("sp", CSc[0:n, ct, 1, :], C["c_sin"].rearrange("(c s) f -> c s f", s=16)[ct * 128 + 1:ct * 128 + 1 + n, 15, :], writes=[CSc])
        YA = sb("YAc", [128, 1, 64], F32); YB = sb("YBc", [128, 2, 64], BF16)
        S.op("pool", lambda e: e.memset(YB[:], 0.0), writes=[YB])
        SQ = sb("SQ", [128, 1, 64], F32); SS = sb("SS", [128, 1]); RS = sb("RS", [128, 1]); RS2 = sb("RS2", [128, 1])
        YN = sb("YN", [128, 1, 64], F32); T1 = sb("T1", [128, 1, 64], F32); T2 = sb("T2", [128, 1, 64], F32)
        CS1 = sb("CS1", [128, 2, 32], F32)
        TTc = sb("TTc", [128, 1, 128], BF16)
        for i in range(2):
            S.dma("sp", W1s[:], I["a_cmp_w1"][0, i].rearrange("(j d) h -> d j h", d=64), writes=[W1s])
            S.op("dve", lambda e: e.tensor_copy(out=W1[:, 0:16, :], in_=W1s[:, 0:16, :]), reads=[W1s], writes=[W1])
            S.op("pool", lambda e: e.tensor_copy(out=W1[:, 16:32, :], in_=W1s[:, 16:32, :]), reads=[W1s], writes=[W1])
            S.dma("sp", W2s[:], I["a_cmp_w2"][0, i].rearrange("(hf p) d -> p hf d", p=128), writes=[W2s])
            S.op("dve", lambda e: e.tensor_copy(out=W2[:], in_=W2s[:]), reads=[W2s], writes=[W2])
            S.dma("sp", posS[:], I["a_cmp_pos"][0, i].rearrange("j d -> d j"), writes=[posS], allow_slow_non_contiguous=True)
            S.op("dve", lambda e: e.tensor_copy(out=posT[:], in_=posS[:]), reads=[posS], writes=[posT])
            S.dma("sp", b1c[:], I["a_cmp_b1"][0, i:i + 1, :].rearrange("o (hf p) -> p (o hf)", p=128), writes=[b1c],
                  allow_slow_non_contiguous=True)
            S.dma("sp", b2s[:], I["a_cmp_b2"][0, i:i + 1, :], writes=[b2s])
            S.op("dve", lambda e: e.tensor_copy(out=b2r[:], in_=b2s[:]), reads=[b2s], writes=[b2r])
            for hf in range(2):
                for j in range(32):
                    S.op("pe", lambda e: e.matmul(pbias[:, hf:hf + 1], lhsT=W1[:, j, hf * 128:(hf + 1) * 128], rhs=posT[:, j:j + 1],
                                                  start=(j == 0 and hf == 0), stop=(j == 31), skip_group_check=True),
                         reads=[W1, posT], writes=[pbias])
            S.op("dve", lambda e: e.tensor_tensor(out=BT[:], in0=pbias[:], in1=b1c[:], op=ALU.add), reads=[pbias, b1c], writes=[BT])
            for g in range(3):
                XC = XCr()
                S.dma("sp", XC[:], featT0[22 + 3 * i + g, :, :], writes=[XC], dram_r=[featT0])
                XCv = XC[:].rearrange("d (c s) -> d c s", s=16)
                for hf in range(2):
                    for j in range(32):
                        S.op("pe", lambda e: e.matmul(ph[hf][:, 0:255], lhsT=W1[:, j, hf * 128:(hf + 1) * 128],
                                                      rhs=XCv[:, (j // 16):(j // 16) + 255, j % 16], start=(j == 0), stop=(j == 31)),
                             reads=[W1, XC], writes=[ph[hf]])
                    S.op("act", lambda e: e.activation(out=HG[:, hf, 0:255], in_=ph[hf][:, 0:255], func=AF.Gelu_apprx_tanh,
                                                       bias=BT[:, hf:hf + 1]),
                         reads=[ph[hf], BT], writes=[HG])
                for ct in range(2):
                    for hf in range(2):
                        S.op("pe", lambda e: e.matmul(pout[:, :], lhsT=HG[:, hf, ct * 128:(ct + 1) * 128], rhs=W2[:, hf, :],
                                                      start=(hf == 0), stop=False),
                             reads=[HG, W2], writes=[pout])
                    S.op("pe", lambda e: e.matmul(pout[:, :], lhsT=ones_bf[0:1, :], rhs=b2r[0:1, :], start=False, stop=True),
                         reads=[ones_bf, b2r], writes=[pout])
                    if i == 0:
                        S.op("act", lambda e: e.copy(out=YA[:, 0, :], in_=pout[:, :]), reads=[pout], writes=[YA])
                        S.op("dve", lambda e: e.tensor_copy(out=CS1[:], in_=CSc[:, ct, :, :]), reads=[CSc], writes=[CS1])
                        run(head_norm(YA, 1, 1, GAc, CS1, YB, (SQ, SS, RS, RS2, YN, T1, T2)))
                        run(feat_transposes(YB, 1, tpf, TTc))
                        S.op("dve", lambda e: e.tensor_copy(out=kcT[:, g, ct * 128:(ct + 1) * 128], in_=TTc[0:64, 0, :]),
                             reads=[TTc], writes=[kcT])
                    else:
                        S.op("act", lambda e: e.copy(out=VCO[:, ct, g, 0:64], in_=pout[:, :]), reads=[pout], writes=[VCO])

    kmT0 = sb("kmT0", [64, 4, 256], BF16, persist=True)
    VM0 = sb("VM0", [128, 2, 4, 65], BF16, persist=True)
    mem_phase(0, kmT0, VM0)

    with Phase():
        stg = ring("stg", [128, 1024], F32, 1)
        WO = WT("WO", 8, 1024)
        load_weight(WO, I["a_w_out"][0], 8, 1024, stg)
        KA = sb("KA", [128, 3, SEQ], BF16)
        KW = sb("KW", [64, 3, SEQ], BF16)
        VSW = sb("VSW", [128, 32, 6, 65], BF16)
        S.dma("sp", KA[0:64, :, :], featT0.ap[12:15, :, :].rearrange("h d t -> d h t"), writes=[KA], dram_r=[featT0])
        for g in range(3):
            S.dma("sp", KA[64:128, g, :], C["c_onehot"][:, :], writes=[KA])
        S.dma("sp", KW[:, :, :], featT0.ap[15:18, :, :].rearrange("h d t -> d h t"), writes=[KW], dram_r=[featT0])
        for kt0 in range(0, 32, 8):
            S.dma("sp", VSW[:, kt0:kt0 + 8, :, :], vtok0.ap[kt0 * 128:(kt0 + 8) * 128].rearrange("(kt p) h d -> p kt h d", p=128),
                  writes=[VSW], dram_r=[vtok0])
        msel = sb("msel", [128, 896], BF16); mwin = sb("mwin", [128, 1408], BF16)
        mcmp = sb("mcmp", [128, 512], F32); tkeep = sb("tkeep", [128, 128], F32); tadd = sb("tadd", [128, 128], F32)
        for t, k in ((msel, "m_sel"), (mwin, "m_win"), (mcmp, "m_cmp"), (tkeep, "m_tkeep"), (tadd, "m_tadd")):
            S.dma("sp", t[:], C[k][:, :], writes=[t])
        QAr = ring("QA", [128, 16, 512], BF16, 1)
        GTr = ring("GT", [128, 4, 36], F32, 1)
        XRr = ring("XR", [128, 4, DM], F32, 1)
        OM = sb("OM", [128, 4, DM], F32)
        Bk = [ps(f"bk{i}", [128, 512], F32) for i in range(7)]
        TP = ps("tp", [128, 8, 128], BF16)
        ST = bank_ring(Bk[0:4])
        ACC = bank_ring([acc_view(b) for b in Bk[4:7]])
        PO = bank_ring(Bk[0:2])
        SCsets = [(Bk[0], Bk[1]), (Bk[2], Bk[3])]
        OCI = [b.view(b.ap[:, 0:512].rearrange("p (r d) -> p r d", d=64)) for b in Bk[4:7]]
        PTr = ring("PT", [128, 512], BF16, LOOK + 2)
        SCm = ring("SCm", [128, 256], F32, 4)
        PCr = ring("PC", [128, 256], F32, 4)
        for t in PCr.tiles:
            S.op("pool", lambda e: e.memset(t[:], 0.0), writes=[t])
        PNr = ring("PN", [128, 256], BF16, 24)
        PTc = ring("PTc", [128, 8, 128], BF16, 3)
        rsr = ring("rsr", [128, 4, 2], F32, 2)
        TK = [dict(I1=sb("I1", [128, 64], F32), I2=sb("I2", [128, 64], F32), I3=sb("I3", [128, 64], F32),
                   M8=sb("M8", [128, 16], F32), SEL=sb("SEL", [128, 64], F32), VAL=sb("VAL", [128, 64], F32)) for _ in range(3)]
        MTr = ring("MT", [128, 128], BF16, 4)
        for t in MTr.tiles:
            S.op("pool", lambda e: e.memset(t[:], 0.0), writes=[t])
        Rt = sb("Rt", [128, 4], F32); Ft = sb("Ft", [128, 4], F32); Tt = sb("Tt", [128, 4, 64], F32)
        OBr = ring("OB", [128, DM], BF16, 1); OTr = ring("OT", [128, 8, 128], BF16, 1); XOr = ring("XO", [128, DM], F32, 1)
        for qi in range(SEQ // 512):
            Q0 = qi * 512
            QA = QAr(); GT = GTr(); XR = XRr()
            S.dma("sp", QA[0:64, 0:12, :], featT0.ap[0:12, :, Q0:Q0 + 512].rearrange("h d t -> d h t"), writes=[QA], dram_r=[featT0])
            S.dma("sp", QA[0:64, 12:16, :], featT0.ap[18:22, :, Q0:Q0 + 512].rearrange("h d t -> d h t"), writes=[QA], dram_r=[featT0])
            S.dma("sp", GT[:], gates.ap[Q0:Q0 + 512, :].rearrange("(s p) c -> p s c", p=128), writes=[GT], dram_r=[gates])
            S.op("act", lambda e: e.activation(out=GT[:], in_=GT[:], func=AF.Sigmoid), reads=[GT], writes=[GT])
            S.dma("sp", XR[:], xin.ap[Q0:Q0 + 512, :].rearrange("(s p) d -> p s d", p=128), writes=[XR])

            def stepA_front(n, s, g):
                i128 = qi * 4 + s
                bA, bB = SCsets[n % 2]
                scv = [(bA if r < 2 else bB) for r in range(4)]
                off = 248 - 8 * i128
                pns = []
                for r in range(4):
                    h = g * 4 + r
                    c0 = (r % 2) * 256
                    S.op("pe", lambda e: e.matmul(scv[r][:, c0:c0 + 255], lhsT=QA[0:64, h, s * 128:(s + 1) * 128], rhs=kcT[:, g, 0:255],
                                                  start=True, stop=True), reads=[QA, kcT], writes=[scv[r]])
                    yield
                scms = []
                for r in range(4):
                    c0 = (r % 2) * 256
                    scm = SCm()
                    scms.append(scm)
                    S.op("dve", lambda e: e.tensor_tensor(out=scm[:, 0:255], in0=scv[r][:, c0:c0 + 255], in1=mcmp[:, off:off + 255], op=ALU.add),
                         reads=[scv[r], mcmp], writes=[scm])
                    yield
                rs4 = rsr()
                S.op("pool", lambda e: e.memset(rs4[:], 0.0), writes=[rs4])
                yield
                pcs = []
                for r in range(4):
                    pc = PCr()
                    pcs.append(pc)
                    S.op("act", lambda e: e.activation(out=pc[:, 0:255], in_=scms[r][:, 0:255], func=AF.Exp, accum_out=rs4[:, r, 0:1]),
                         reads=[scms[r], rs4], writes=[pc, rs4])
                    yield
                S.op("dve", lambda e: e.tensor_scalar(out=rs4[:, :, 1], in0=rs4[:, :, 0], scalar1=1e-30, scalar2=None, op0=ALU.add),
                     reads=[rs4], writes=[rs4])
                yield
                S.op("dve", lambda e: e.reciprocal(out=rs4[:, :, 0], in_=rs4[:, :, 1]), reads=[rs4], writes=[rs4])
                yield
                for r in range(4):
                    pn = PNr()
                    pns.append(pn)
                    S.op("act", lambda e: e.activation(out=pn[:], in_=pcs[r][:], func=AF.Copy, scale=rs4[:, r, 0:1]),
                         reads=[pcs[r], rs4], writes=[pn])
                    yield
                ctxs[(s, g)] = pns

            def stepA_back(ci, s, g):
                pns = ctxs.pop((s, g))
                i128 = qi * 4 + s
                oci = OCI[ci]
                tk = TK[ci]
                I1, I2, I3, M8, SEL, VAL = tk["I1"], tk["I2"], tk["I3"], tk["M8"], tk["SEL"], tk["VAL"]
                for r in range(4):
                    for ct in range(2):
                        S.op("pe", lambda e: e.transpose(out=TP[:, r * 2 + ct, :], in_=pns[r][:, ct * 128:(ct + 1) * 128], identity=ident[:]),
                             reads=[pns[r], ident], writes=[TP])
                ptc = PTc()
                S.op("act", lambda e: e.copy(out=ptc[:], in_=TP[:, 0:8, :]), reads=[TP], writes=[ptc])
                yield
                for r in range(4):
                    for ct in range(2):
                        S.op("pe", lambda e: e.matmul(oci[:, r, :], lhsT=ptc[:, r * 2 + ct, :], rhs=VCO[:, ct, g, 0:64], start=(ct == 0), stop=(ct == 1),
                                                      skip_group_check=True),
                             reads=[ptc, VCO], writes=[oci])
                    for ct in range(2):
                        S.op("pe", lambda e: e.matmul(oci[:, 4 + r, :], lhsT=ptc[:, r * 2 + ct, :], rhs=VCO[:, ct, g, 64:128], start=(ct == 0), stop=(ct == 1),
                                                      skip_group_check=True),
                             reads=[ptc, VCO], writes=[oci])
                    yield
                gc = GT[:, s, 12 * g:12 * g + 12].rearrange("p (r b) -> p r b", b=3)[:, :, 0]
                S.op("dve", lambda e: e.tensor_tensor(out=OM[:, s, 256 * g:256 * g + 256].rearrange("p (r d) -> p r d", d=64), in0=oci[:, 0:4, :],
                                                      in1=gc.unsqueeze(2).to_broadcast([128, 4, 64]), op=ALU.mult),
                     reads=[oci, GT], writes=[OM])
                yield
                toff = 62 - 2 * i128
                S.op("dve", lambda e: e.tensor_reduce(out=I3[:], in_=oci[:, 4:8, :].rearrange("p r j -> p j r"), axis=AX.X, op=ALU.add),
                     reads=[oci], writes=[I3])
                yield
                S.op("dve", lambda e: e.tensor_tensor(out=I1[:], in0=I3[:], in1=tkeep[:, toff:toff + 64], op=ALU.mult),
                     reads=[I3, tkeep], writes=[I1])
                yield
                S.op("dve", lambda e: e.tensor_tensor(out=I2[:], in0=I1[:], in1=tadd[:, toff:toff + 64], op=ALU.add),
                     reads=[I1, tadd], writes=[I2])
                yield
                S.op("dve", lambda e: e.memset(I2[:, 0:1], 3e9), reads=[I2], writes=[I2])
                yield
                S.op("dve", lambda e: e.max(out=M8[:, 0:8], in_=I2[:]), reads=[I2], writes=[M8])
                yield
                S.op("dve", lambda e: e.match_replace(out=I3[:], in_to_replace=M8[:, 0:8], in_values=I2[:], imm_value=-1e30),
                     reads=[I2, M8], writes=[I3])
                yield
                S.op("dve", lambda e: e.max(out=M8[:, 8:16], in_=I3[:]), reads=[I3], writes=[M8])
                yield
                S.op("dve", lambda e: e.tensor_scalar(out=SEL[:], in0=I2[:], scalar1=M8[:, 15:16], scalar2=None, op0=ALU.is_ge),
                     reads=[I2, M8], writes=[SEL])
                yield
                S.op("dve", lambda e: e.tensor_scalar(out=VAL[:], in0=I2[:], scalar1=-1e29, scalar2=None, op0=ALU.is_gt),
                     reads=[I2], writes=[VAL])
                yield
                S.op("dve", lambda e: e.tensor_tensor(out=SEL[:], in0=SEL[:], in1=VAL[:], op=ALU.mult), reads=[SEL, VAL], writes=[SEL])
                yield
                MT = MTr()
                S.op("dve", lambda e: e.tensor_scalar(out=MT[:, 64:128], in0=SEL[:], scalar1=-1.0, scalar2=-NBIG, op0=ALU.add, op1=ALU.mult),
                     reads=[SEL], writes=[MT])
                yield
                S.op("pe", lambda e: e.transpose(out=TP[:, 0, :], in_=MT[:], identity=ident[:]), reads=[MT, ident], writes=[TP])
                S.op("act", lambda e: e.copy(out=QA[64:128, 4 * g:4 * g + 4, s * 128:(s + 1) * 128],
                                             in_=TP[64:128, 0:1, :].to_broadcast([64, 4, 128])), reads=[TP], writes=[QA])
                yield

            def fronts(s):
                for g in range(3):
                    yield from stepA_front(s * 3 + g, s, g)

            ctxs = {}
            run(fronts(0))
            for s in range(4):
                gens = [(stepA_back(g, s, g), 24) for g in range(3)]
                if s + 1 < 4:
                    gens.append((fronts(s + 1), 60))
                interleave(gens)

            nkt = qi * 4 + 4
            jobs = []
            for h in range(12):
                g = h // 4
                accS = ACC(); accW = ACC()
                for kt in range(nkt):
                    D = Q0 - kt * 128
                    subs = [sx for sx in range(4) if D + 128 * sx >= 0]
                    mt_, mo_ = (msel, D + 384) if D < 128 else (None, None)
                    jobs.append(tile_job(KA[:, g, kt * 128:(kt + 1) * 128], (lambda c0, c1, h=h: QA[:, h, c0:c1]), [KA, QA], mt_, mo_,
                                         accS, VSW[:, kt, g, :], VSW, subs, kt == 0))
                kts = list(range(max(0, qi * 4 - 4), nkt))
                for kt in kts:
                    D = Q0 - kt * 128
                    subs = [sx for sx in range(4) if 0 <= D + 128 * sx <= 512]
                    jobs.append(tile_job(KW[:, g, kt * 128:(kt + 1) * 128], (lambda c0, c1, h=h: QA[0:64, h, c0:c1]), [KW, QA], mwin, D + 384,
                                         accW, VSW[:, kt, 3 + g, :], VSW, subs, kt == kts[0]))
                jobs.append(fin_job(accS, OM, h * 64, (GT[:, :, 3 * h + 1], GT), (Rt, Ft, Tt), True))
                jobs.append(fin_job(accW, OM, h * 64, (GT[:, :, 3 * h + 2], GT), (Rt, Ft, Tt), True))
            jobs += mem_jobs(QA[:, 12:16, :], QA, kmT0, VM0, ACC, OM, 768, (Rt, Ft, Tt))
            run_jobs(jobs, ST, PTr)
            out_proj(OM, 1024, WO, XR, x1, Q0, TP, PO, OBr, OTr, XOr)

    mlp_phase(0, x1, x2)

    with Phase():
        stg = ring("stg", [128, 1792], F32, 2)
        WB = WT("WB", 8, 1792)
        WK = WT("WK", 8, 1024)
        load_weight(WB, I["b_w_in"][0], 8, 1792, stg, col_chunk=1792)
        load_weight(WK, I["w_kv_shared"], 8, 1024, stg, col_chunk=1024)
        GA1 = build_gains("GA1", 36, [(I["b_q_norm"][0, 0:1, :], 0, 8, 0.125), (I["b_q_norm"][0, 1:2, :], 8, 8, 0.125),
                                     (I["b_q_norm"][0, 2:3, :], 16, 8, 0.125), (I["kv_k_norm"][0:1, :], 24, 8, 1.0),
                                     (I["mem_q_norm"][1:2, :], 32, 4, 0.125)])
        seg1 = [(0, 0, 1536, "A", 0), (1, 0, 512, "A", 24), (0, 1536, 1792, "A", 32), (1, 512, 1024, "V", 0)]
        proj_phase(x2, SEQ // 128, [(gT_attn[1], WB, 1792), (gT_kv, WK, 1024)], 36, 32, GA1, None, featT1, vtok1, 8, seg_map=seg1)

    kmT1 = sb("kmT1", [64, 4, 256], BF16, persist=True)
    VM1 = sb("VM1", [128, 2, 4, 65], BF16, persist=True)
    mem_phase(1, kmT1, VM1)

    with Phase():
        stg = ring("stg", [128, 1024], F32, 2)
        WO1 = WT("WO1", 6, 1024)
        load_weight(WO1, I["b_w_out"][0], 6, 1024, stg)
        KT = sb("KT", [128, 4, SEQ], BF16)
        V1 = sb("V1", [128, 32, 8, 65], BF16)
        S.dma("sp", KT[:, :, :], featT1.ap[24:32, :, :].rearrange("(j h2) d t -> (h2 d) j t", h2=2), writes=[KT], dram_r=[featT1])
        for kt0 in range(0, 32, 8):
            S.dma("sp", V1[:, kt0:kt0 + 8, :, :], vtok1.ap[kt0 * 128:(kt0 + 8) * 128].rearrange("(kt p) h d -> p kt h d", p=128),
                  writes=[V1], dram_r=[vtok1])
        md = []
        for k, w in (("m_d0", 1024), ("m_d1", 1408), ("m_d2", 2944)):
            t = sb(k, [128, w], BF16)
            S.dma("sp", t[:], C[k][:, :], writes=[t])
            md.append(t)
        QTr = ring("QT", [128, 12, 512], BF16, 2)
        QMr = ring("QM", [64, 4, 512], BF16, 2)
        XRr = ring("XR", [128, 4, DM], F32, 1)
        OM = sb("OM", [128, 4, 768], F32)
        Bk = [ps(f"bk{i}", [128, 512], F32) for i in range(7)]
        TP = ps("tp", [128, 8, 128], BF16)
        ST = bank_ring(Bk[0:4])
        ACC = bank_ring([acc_view(b) for b in Bk[4:7]])
        PO = bank_ring(Bk[0:2])
        PTr = ring("PT", [128, 512], BF16, LOOK + 2)
        Rt = sb("Rt", [128, 4], F32); Ft = sb("Ft", [128, 4], F32); Tt = sb("Tt", [128, 4, 64], F32)
        OBr = ring("OB", [128, DM], BF16, 2); OTr = ring("OT", [128, 8, 128], BF16, 2); XOr = ring("XO", [128, DM], F32, 2)
        pats = ((128, 1), (512, 4), (2048, 16))
        for qi in range(SEQ // 512):
            Q0 = qi * 512
            QT = QTr(); QM = QMr(); XR = XRr()
            S.dma("sp", QT[:, :, :], featT1.ap[0:24, :, Q0:Q0 + 512].rearrange("(j h2) d t -> (h2 d) j t", h2=2), writes=[QT], dram_r=[featT1])
            S.dma("sp", QM[:, :, :], featT1.ap[32:36, :, Q0:Q0 + 512].rearrange("h d t -> d h t"), writes=[QM], dram_r=[featT1])
            S.dma("sp", XR[:], x2.ap[Q0:Q0 + 512, :].rearrange("(s p) d -> p s d", p=128), writes=[XR], dram_r=[x2])
            jobs = []
            for hh in range(8):
                acc = ACC()
                p0 = (hh % 2) * 64
                first = True
                for gi, (window, dil) in enumerate(pats):
                    qh = gi * 8 + hh
                    for kt in range(max(0, (Q0 - window) // 128), qi * 4 + 4):
                        D = Q0 - kt * 128
                        subs = [sx for sx in range(4) if 0 <= D + 128 * sx <= window]
                        jobs.append(tile_job(KT[p0:p0 + 64, hh // 2, kt * 128:(kt + 1) * 128],
                                             (lambda c0, c1, p0=p0, qh=qh: QT[p0:p0 + 64, qh // 2, c0:c1]), [KT, QT],
                                             md[gi], D + 384, acc, V1[:, kt, hh, :], V1, subs, first))
                        first = False
                jobs.append(fin_job(acc, OM, hh * 64, None, (Rt, Ft, Tt), False))
            jobs += mem_jobs(QM, QM, kmT1, VM1, ACC, OM, 512, (Rt, Ft, Tt))
            run_jobs(jobs, ST, PTr)
            out_proj(OM, 768, WO1, XR, x3, Q0, TP, PO, OBr, OTr, XOr)

    mlp_phase(1, x3, outT)
    S.barrier()
    gst.close()
    return nc


_CACHE = {}


def kernel(**inputs):
    n = 8
    consts = host_consts()
    if "nc" not in _CACHE:
        _CACHE["nc"] = build_program()
    nc = _CACHE["nc"]
    in_maps = []
    for b in range(n):
        m = {}
        for k, shp in IN_SHAPES.items():
            a = np.asarray(inputs[k], dtype=np.float32)
            if k in ("x", "mem"):
                a = a[b]
            m[k] = np.ascontiguousarray(a.reshape(shp))
        m.update(consts)
        in_maps.append(m)
    res = run_bass_kernel_spmd(nc, in_maps, core_ids=list(range(n)))
    return np.stack([np.asarray(r["out"], dtype=np.float32) for r in res.results], axis=0)
```

```python
import numpy as np
import ml_dtypes
import concourse.bass as bass
import concourse.mybir as mybir
from concourse.bass_utils import run_bass_kernel_spmd
from contextlib import ExitStack

F32 = mybir.dt.float32
BF16 = mybir.dt.bfloat16
AF = mybir.ActivationFunctionType
ALU = mybir.AluOpType
AX = mybir.AxisListType
SEM_ROT = 30000
SEQ = 4096
DM = 1024
EPS = 1e-6
NBIG = -30000.0


class Tile:
    __slots__ = ("name", "w", "r", "dsem", "ap", "base")

    def __init__(self, name, ap=None, base=None):
        self.name = name
        self.w = None
        self.r = []
        self.dsem = None
        self.ap = ap
        self.base = base

    def view(self, ap):
        return Tile(self.name + "_v", ap, base=(self.base or self))

    def __getitem__(self, k):
        return self.ap[k]


class DramT:
    def __init__(self, ap, name="d"):
        self.ap = ap
        self.name = name
        self.pend = set()

    def __getitem__(self, k):
        return self.ap[k]


class Sched:
    def __init__(self, nc):
        self.nc = nc
        self.eng = {"pe": nc.tensor, "act": nc.scalar, "dve": nc.vector,
                    "pool": nc.gpsimd, "sp": nc.sync}
        self.esem = {}
        self.ecnt = {}
        self.seen = {e: {} for e in self.eng}
        self.latest = {}
        self.nsem = 0
        self.free_dsems = []
        for e in ("pe", "act", "dve", "pool"):
            self._new_esem(e)

    def _alloc(self, name):
        self.nsem += 1
        return self.nc.alloc_semaphore(name=f"{name}_{self.nsem}")

    def _new_esem(self, e):
        self.esem[e] = self._alloc("e" + e)
        self.ecnt[e] = 0

    def tile(self, ap, name="t"):
        return Tile(name, ap)

    def _need(self, engine, reads, writes):
        need = {}
        reads = [t.base or t for t in reads]
        writes = [t.base or t for t in writes]

        def add(ev, raw):
            if ev is None:
                return
            sem, val, kind, eng = ev
            if kind == "c" and eng == engine:
                if engine == "pe":
                    return
            if kind == "d":
                val = self.latest[sem]
            if need.get(sem, 0) < val:
                need[sem] = val

        for t in reads:
            add(t.w, True)
        for t in writes:
            add(t.w, False)
            for ev in t.r:
                add(ev, False)
        seen = self.seen[engine]
        out = []
        for sem, val in need.items():
            if seen.get(sem, 0) < val:
                seen[sem] = val
                out.append((sem, val))
        return out

    def _record(self, ev, reads, writes):
        reads = [t.base or t for t in reads]
        writes = [t.base or t for t in writes]
        for t in reads:
            t.r = [x for x in t.r if x[0] is not ev[0]]
            t.r.append(ev)
        for t in writes:
            t.w = ev
            t.r = []

    def op(self, engine, fn, reads=(), writes=()):
        e = self.eng[engine]
        for sem, val in self._need(engine, reads, writes):
            e.wait_ge(sem, val)
        if self.ecnt[engine] >= SEM_ROT:
            self._new_esem(engine)
        inst = fn(e)
        self.ecnt[engine] += 1
        inst.then_inc(self.esem[engine], 1)
        ev = (self.esem[engine], self.ecnt[engine], "c", engine)
        self._record(ev, reads, writes)
        return ev

    def free_dsem(self, tiles):
        for t in tiles:
            if t.dsem is not None:
                self.free_dsems.append(t.dsem)
                t.dsem = None

    def dma(self, queue, out, in_, reads=(), writes=(), dram_r=(), dram_w=(), **kw):
        e = self.eng[queue]
        stile = writes[0] if writes else reads[0]
        stile = stile.base or stile
        if stile.dsem is None or self.latest[stile.dsem] >= SEM_ROT:
            stile.dsem = None
            while self.free_dsems and stile.dsem is None:
                c = self.free_dsems.pop()
                if self.latest[c] < SEM_ROT:
                    stile.dsem = c
            if stile.dsem is None:
                stile.dsem = self._alloc("d")
                self.latest[stile.dsem] = 0
        waits = self._need(queue, reads, writes)
        seen = self.seen[queue]
        for d in dram_r:
            for sem in d.pend:
                val = self.latest[sem]
                if seen.get(sem, 0) < val:
                    seen[sem] = val
                    waits.append((sem, val))
        for sem, val in waits:
            e.wait_ge(sem, val)
        inst = e.dma_start(out=out, in_=in_, **kw)
        sem = stile.dsem
        self.latest[sem] += 16
        inst.then_inc(sem, 16)
        ev = (sem, self.latest[sem], "d", None)
        self._record(ev, reads, writes)
        for d in dram_w:
            d.pend.add(sem)
        return ev

    def barrier(self):
        for en, e in self.eng.items():
            seen = self.seen[en]
            for f in ("pe", "act", "dve", "pool"):
                if f == en:
                    continue
                sem, val = self.esem[f], self.ecnt[f]
                if val > 0 and seen.get(sem, 0) < val:
                    seen[sem] = val
                    e.wait_ge(sem, val)
            for sem, val in self.latest.items():
                if val > 0 and seen.get(sem, 0) < val:
                    seen[sem] = val
                    e.wait_ge(sem, val)


def host_consts():
    bf = ml_dtypes.bfloat16
    c = {}
    half = 32
    inv = (10000.0 ** (-np.arange(half, dtype=np.float32) / half)).astype(np.float32)
    ang = np.arange(SEQ + 32, dtype=np.float32)[:, None] * inv[None, :]
    c["c_cos"] = np.cos(ang).astype(np.float32)
    c["c_sin"] = np.sin(ang).astype(np.float32)
    c["c_ident"] = np.eye(128, dtype=np.float32).astype(bf)
    kk = np.arange(128)[:, None]

    def toep(width, f):
        j = np.arange(width)[None, :]
        return f(j - 384 - kk).astype(np.float32).astype(bf)

    c["m_sel"] = toep(896, lambda d: d >= 0)
    c["m_win"] = toep(1408, lambda d: (d >= 0) & (d < 512))
    c["m_d0"] = toep(1024, lambda d: (d >= 0) & (d <= 128))
    c["m_d1"] = toep(1408, lambda d: (d >= 0) & (d <= 512) & (d % 4 == 0))
    c["m_d2"] = toep(2944, lambda d: (d >= 0) & (d <= 2048) & (d % 16 == 0))
    u = np.arange(512)[None, :]
    c["m_cmp"] = np.where(16 * (u - 248) + 31 <= kk, 0.0, NBIG).astype(np.float32)
    u = np.arange(128)[None, :]
    rel = u - 62 - (kk // 64)
    keep = (rel < -1).astype(np.float32)
    add = np.where(rel == 0, 2e9, np.where(rel == -1, 1e9, np.where(rel > 0, -1e30, 0.0)))
    c["m_tkeep"] = keep.astype(np.float32)
    c["m_tadd"] = add.astype(np.float32)
    n_cmp = SEQ // 16 - 1
    cs = np.arange(256)[:, None] * 16
    bs = np.arange(64)[None, :] * 64
    ov = ((cs < bs + 64) & (cs + 32 > bs)).astype(np.float32)
    ov[n_cmp:] = 0
    c["c_ovl"] = ov.astype(bf)
    c["c_onehot"] = (np.arange(SEQ)[None, :] // 64 == np.arange(64)[:, None]).astype(np.float32).astype(bf)
    return c


CONST_SHAPES = {
    "c_cos": ([SEQ + 32, 32], F32), "c_sin": ([SEQ + 32, 32], F32), "c_ident": ([128, 128], BF16),
    "m_sel": ([128, 896], BF16), "m_win": ([128, 1408], BF16), "m_d0": ([128, 1024], BF16),
    "m_d1": ([128, 1408], BF16), "m_d2": ([128, 2944], BF16), "m_cmp": ([128, 512], F32),
    "m_tkeep": ([128, 128], F32), "m_tadd": ([128, 128], F32), "c_ovl": ([256, 64], BF16),
    "c_onehot": ([64, SEQ], BF16),
}

IN_SHAPES = {
    "x": [SEQ, DM], "mem": [256, DM], "attn_norm": [2, DM], "mlp_norm": [2, DM],
    "w_up": [2, DM, 4096], "w_down": [2, 4096, DM], "mem_norm": [2, DM], "w_mem_kv": [2, DM, 512],
    "mem_q_norm": [2, 64], "mem_k_norm": [2, 64], "a_w_in": [1, DM, 2212], "a_w_out": [1, DM, DM],
    "a_q_norm": [1, 64], "a_k_norm": [1, 3, 64], "a_cmp_pos": [1, 2, 32, 64],
    "a_cmp_w1": [1, 2, 2048, 256], "a_cmp_b1": [1, 2, 256], "a_cmp_w2": [1, 2, 256, 64],
    "a_cmp_b2": [1, 2, 64], "kv_norm": [1, DM], "w_kv_shared": [DM, DM], "kv_k_norm": [1, 64],
    "b_w_in": [1, DM, 1792], "b_w_out": [1, 768, DM], "b_q_norm": [1, 3, 64],
}


def build_program(debug=()):
    nc = bass.Bass("TRN2", target_bir_lowering=False)
    S = Sched(nc)
    I = {}
    for k, shp in IN_SHAPES.items():
        I[k] = nc.dram_tensor(k, shp, F32, kind="ExternalInput").ap()
    C = {}
    for k, (shp, dt) in CONST_SHAPES.items():
        C[k] = nc.dram_tensor(k, shp, dt, kind="ExternalInput").ap()
    out_ap = nc.dram_tensor("out", [SEQ, DM], F32, kind="ExternalOutput").ap()

    def scratch(name, shape, dt):
        kind = "ExternalOutput" if name in debug else "Internal"
        return DramT(nc.dram_tensor(name, shape, dt, kind=kind).ap(), name)

    featT0 = scratch("featT0", [28, 64, SEQ], BF16)
    vtok0 = scratch("vtok0", [SEQ, 6, 65], BF16)
    gates = scratch("gates", [SEQ, 36], F32)
    x1 = scratch("x1", [SEQ, DM], F32)
    x2 = scratch("x2", [SEQ, DM], F32)
    x3 = scratch("x3", [SEQ, DM], F32)
    featT1 = scratch("featT1", [36, 64, SEQ], BF16)
    vtok1 = scratch("vtok1", [SEQ, 8, 65], BF16)
    outT = DramT(out_ap, "out")
    xin = DramT(I["x"], "x")

    gst = ExitStack()
    cur = {"st": gst}

    uid = {"i": 0}
    local_tiles = []

    def sb(name, shape, dt=F32, persist=False):
        st = gst if persist else cur["st"]
        uid["i"] += 1
        t = S.tile(st.enter_context(nc.sbuf_tensor(f"{name}_{uid['i']}", list(shape), dt)), name)
        if not persist:
            local_tiles.append(t)
        return t

    def ps(name, shape, dt=F32):
        uid["i"] += 1
        full = [128, 512] if dt == F32 else [128, 1024]
        h = cur["st"].enter_context(nc.psum_tensor(f"{name}_{uid['i']}", full, dt))
        n = 1
        for d in shape[1:]:
            n *= d
        v = h[:, 0:n]
        if len(shape) == 3:
            v = v.rearrange("p (a b) -> p a b", b=shape[2])
        return S.tile(v, name)

    class Phase:
        def __enter__(self):
            self.st = ExitStack()
            cur["st"] = self.st
            return self

        def __exit__(self, *a):
            S.barrier()
            S.free_dsem(local_tiles)
            del local_tiles[:]
            self.st.close()
            cur["st"] = gst
            return False

    rr = {"i": 0}

    def alt(engs=("dve", "pool")):
        rr["i"] += 1
        return engs[rr["i"] % len(engs)]

    def ring(name, shape, dt, n):
        tiles = [sb(f"{name}{i}", shape, dt) for i in range(n)]
        st = {"i": -1}

        def nxt():
            st["i"] = (st["i"] + 1) % n
            return tiles[st["i"]]
        nxt.tiles = tiles
        return nxt

    ident = sb("ident", [128, 128], BF16, persist=True)
    S.dma("sp", ident[:], C["c_ident"][:, :], writes=[ident])
    ones_bf = sb("ones_bf", [128, 128], BF16, persist=True)
    S.op("dve", lambda e: e.memset(ones_bf[:], 1.0), writes=[ones_bf])

    def bcast_row(dst_ap, src_row_ap, tile):
        S.dma("sp", dst_ap, src_row_ap.partition_broadcast(128), writes=[tile])

    def load_gT(name, src_row):
        t = sb(name, [128, 8], F32, persist=True)
        S.dma("sp", t[:], src_row.rearrange("o (kc p) -> p (o kc)", p=128), writes=[t],
              allow_slow_non_contiguous=True)
        return t

    gT_attn = [load_gT(f"gT_attn{l}", I["attn_norm"][l:l + 1, :]) for l in range(2)]
    gT_mlp = [load_gT(f"gT_mlp{l}", I["mlp_norm"][l:l + 1, :]) for l in range(2)]
    gT_mem = [load_gT(f"gT_mem{l}", I["mem_norm"][l:l + 1, :]) for l in range(2)]
    gT_kv = load_gT("gT_kv", I["kv_norm"][0:1, :])

    def load_weight(wt, src, nk, ncols, stg_ring, col_chunk=2048, engs=("dve", "act", "pool")):
        for kc in range(nk):
            eng = engs[kc % len(engs)]
            for c0 in range(0, ncols, col_chunk):
                w = min(col_chunk, ncols - c0)
                stg = stg_ring()
                S.dma("sp", stg[:, 0:w], src[kc * 128:(kc + 1) * 128, c0:c0 + w], writes=[stg])
                if eng == "act":
                    S.op(eng, lambda e: e.copy(out=wt.ap[:, kc, c0:c0 + w], in_=stg[:, 0:w]), reads=[stg], writes=[wt.k[kc]])
                else:
                    S.op(eng, lambda e: e.tensor_copy(out=wt.ap[:, kc, c0:c0 + w], in_=stg[:, 0:w]),
                         reads=[stg], writes=[wt.k[kc]])

    class WT:
        def __init__(self, name, nk, ncols):
            self.h = sb(name, [128, nk, ncols], BF16)
            self.ap = self.h.ap
            self.k = [Tile(f"{name}_{i}") for i in range(nk)]

    def run(gen):
        for _ in gen:
            pass

    def interleave(gens):
        state = [[g, 0, float(tot)] for g, tot in gens]
        while state:
            st = min(state, key=lambda z: z[1] / z[2])
            try:
                next(st[0])
                st[1] += 1
            except StopIteration:
                state.remove(st)

    def norm_tile(xt_ap, xt_tile, gTs, hTs, tph, tmp):
        junk, ssq, rs, rs2, xn = tmp
        S.op("pool", lambda e: e.memset(ssq[:], 0.0), writes=[ssq])
        yield
        S.op("act", lambda e: e.activation(out=junk[:], in_=xt_ap, func=AF.Square, accum_out=ssq[:, 0:1]),
             reads=[xt_tile, ssq], writes=[junk, ssq])
        yield
        S.op("act", lambda e: e.activation(out=rs[:], in_=ssq[:], func=AF.Sqrt, scale=1.0 / DM, bias=EPS),
             reads=[ssq], writes=[rs])
        yield
        S.op("dve", lambda e: e.reciprocal(out=rs2[:], in_=rs[:]), reads=[rs], writes=[rs2])
        yield
        S.op("dve", lambda e: e.tensor_scalar(out=xn[:], in0=xt_ap, scalar1=rs2[:, 0:1], scalar2=None, op0=ALU.mult),
             reads=[xt_tile, rs2], writes=[xn])
        yield
        for kc in range(8):
            S.op("pe", lambda e: e.transpose(out=tph[:, kc, :], in_=xn[:, kc * 128:(kc + 1) * 128], identity=ident[:]),
                 reads=[xn, ident], writes=[tph])
            yield
        for gT, hT in zip(gTs, hTs):
            S.op("dve", lambda e: e.tensor_tensor(out=hT[:], in0=tph[:], in1=gT[:].unsqueeze(2).to_broadcast([128, 8, 128]), op=ALU.mult),
                 reads=[tph, gT], writes=[hT])
            yield

    def head_norm(YA, nh, nrope, GA, CS, YB, tmp):
        SQ, SS, RS, RS2, YN, T1, T2 = tmp
        S.op("act", lambda e: e.activation(out=SQ[:, 0:nh, :], in_=YA[:, 0:nh, :], func=AF.Square),
             reads=[YA], writes=[SQ])
        yield
        S.op("dve", lambda e: e.tensor_reduce(out=SS[:, 0:nh], in_=SQ[:, 0:nh, :], axis=AX.X, op=ALU.add),
             reads=[SQ], writes=[SS])
        yield
        S.op("act", lambda e: e.activation(out=RS[:, 0:nh], in_=SS[:, 0:nh], func=AF.Sqrt, scale=1.0 / 64, bias=EPS),
             reads=[SS], writes=[RS])
        yield
        S.op("dve", lambda e: e.reciprocal(out=RS2[:, 0:nh], in_=RS[:, 0:nh]), reads=[RS], writes=[RS2])
        yield
        S.op("dve", lambda e: e.tensor_tensor(out=YN[:, 0:nh, :], in0=YA[:, 0:nh, :],
                                              in1=RS2[:, 0:nh].unsqueeze(2).to_broadcast([128, nh, 64]), op=ALU.mult),
             reads=[YA, RS2], writes=[YN])
        yield
        S.op("dve", lambda e: e.tensor_tensor(out=YN[:, 0:nh, :], in0=YN[:, 0:nh, :], in1=GA[:, 0:nh, :], op=ALU.mult),
             reads=[YN, GA], writes=[YN])
        yield
        if nrope:
            n = nrope
            cosb = CS[:, 0:1, :].to_broadcast([128, n, 32])
            sinb = CS[:, 1:2, :].to_broadcast([128, n, 32])
            x1v = YN[:, 0:n, 0:32]
            x2v = YN[:, 0:n, 32:64]
            S.op("dve", lambda e: e.tensor_tensor(out=T1[:, 0:n, 0:32], in0=x1v, in1=cosb, op=ALU.mult), reads=[YN, CS], writes=[T1])
            yield
            S.op("dve", lambda e: e.tensor_tensor(out=T2[:, 0:n, 0:32], in0=x2v, in1=sinb, op=ALU.mult), reads=[YN, CS], writes=[T2])
            yield
            S.op("dve", lambda e: e.tensor_tensor(out=YB[:, 0:n, 0:32], in0=T1[:, 0:n, 0:32], in1=T2[:, 0:n, 0:32], op=ALU.subtract),
                 reads=[T1, T2], writes=[YB])
            yield
            S.op("dve", lambda e: e.tensor_tensor(out=T1[:, 0:n, 32:64], in0=x1v, in1=sinb, op=ALU.mult), reads=[YN, CS], writes=[T1])
            yield
            S.op("dve", lambda e: e.tensor_tensor(out=T2[:, 0:n, 32:64], in0=x2v, in1=cosb, op=ALU.mult), reads=[YN, CS], writes=[T2])
            yield
            S.op("dve", lambda e: e.tensor_tensor(out=YB[:, 0:n, 32:64], in0=T1[:, 0:n, 32:64], in1=T2[:, 0:n, 32:64], op=ALU.add),
                 reads=[T1, T2], writes=[YB])
            yield
        if nh > nrope:
            S.op("act", lambda e: e.copy(out=YB[:, nrope:nh, :], in_=YN[:, nrope:nh, :]), reads=[YN], writes=[YB])
            yield

    def feat_transposes(YB, npairs, tpf, TT):
        YBf = YB[:].rearrange("p h d -> p (h d)")
        for b0 in range(0, npairs, 8):
            nb = min(8, npairs - b0)
            for j in range(nb):
                S.op("pe", lambda e: e.transpose(out=tpf[:, j, :], in_=YBf[:, (b0 + j) * 128:(b0 + j + 1) * 128], identity=ident[:]),
                     reads=[YB, ident], writes=[tpf])
                yield
            S.op("act", lambda e: e.copy(out=TT[:, b0:b0 + nb, :], in_=tpf[:, 0:nb, :]), reads=[tpf], writes=[TT])
            yield

    def build_gains(name, nh, specs):
        GA = sb(name, [128, nh, 64], F32)
        G1 = sb(name + "_raw", [128, len(specs), 64], F32)
        for i, (src, h0, n, scale) in enumerate(specs):
            bcast_row(G1[:, i, :], src, G1)
        for i, (src, h0, n, scale) in enumerate(specs):
            S.op("dve", lambda e: e.tensor_scalar(out=GA[:, h0:h0 + n, :], in0=G1[:, i:i + 1, :].to_broadcast([128, n, 64]),
                                                  scalar1=float(scale), scalar2=None, op0=ALU.mult),
                 reads=[G1], writes=[GA])
        return GA

    def proj_phase(x_src, ntiles, projs, nh, nrope, GA, raw_specs, featT, vtok, vtok_n, gate_cols=None,
                   seg_map=None, pos_tab=True):
        nraw_t = sum((c1 - c0) // 64 for (_, c0, c1, k, _) in seg_map if k == "T")
        ntot = nh + nraw_t
        npairs = ntot // 2
        xr = ring("xt", [128, DM], F32, 3)
        junkr = ring("junk", [128, DM], BF16, 2)
        ssqr = ring("ssq", [128, 1], F32, 2); rsr_ = ring("rs", [128, 1], F32, 2); rs2r = ring("rs2", [128, 1], F32, 2)
        xnr = ring("xn", [128, DM], BF16, 2)
        tph = ps("tph", [128, 8, 128], BF16)
        tpf = ps("tpf", [128, 8, 128], BF16)
        hTr = [ring(f"hT{i}", [128, 8, 128], BF16, 2) for i in range(len(projs))]
        banks = []
        for pi, (gT, W, ncols) in enumerate(projs):
            for c0 in range(0, ncols, 512):
                banks.append((pi, c0, min(512, ncols - c0), ps(f"pb{pi}_{c0}", [128, 512], F32)))
        YAr = ring("YA", [128, ntot, 64], F32, 2)
        YBr = ring("YB", [128, ntot, 64], BF16, 2)
        SQ = sb("SQ", [128, nh, 64], F32); SS = sb("SS", [128, nh]); RS = sb("RS", [128, nh]); RS2 = sb("RS2", [128, nh])
        YN = sb("YN", [128, nh, 64], F32); T1 = sb("T1", [128, max(nrope, 1), 64], F32); T2 = sb("T2", [128, max(nrope, 1), 64], F32)
        CSr = ring("CS", [128, 2, 32], F32, 6)
        TTr = ring("TT", [128, npairs, 128], BF16, 2)
        YVr = ring("YV", [128, vtok_n, 65], BF16, 2) if vtok is not None else None
        if YVr is not None:
            for t in YVr.tiles:
                S.op("pool", lambda e: e.memset(t[:], 1.0), writes=[t])
        YGr = ring("YG", [128, 36], F32, 2) if gate_cols else None
        ctx = {}

        def s0(ti):
            t0 = ti * 128
            c = ctx[ti] = {}
            xt = c["xt"] = xr()
            S.dma("sp", xt[:], x_src[t0:t0 + 128, :], writes=[xt], dram_r=[x_src])
            CS = CSr()
            c["CS"] = CS
            if nrope:
                S.dma("sp", CS[:, 0, :], C["c_cos"][t0:t0 + 128, :], writes=[CS])
                S.dma("sp", CS[:, 1, :], C["c_sin"][t0:t0 + 128, :], writes=[CS])

        def s1(ti):
            c = ctx[ti]
            xt = c["xt"]
            c["hT"] = [r() for r in hTr]
            yield from norm_tile(xt[:], xt, [p[0] for p in projs], c["hT"], tph, (junkr(), ssqr(), rsr_(), rs2r(), xnr()))

        def s2(ti):
            c = ctx[ti]
            hTs = c["hT"]
            for (pi, c0, w, pb) in banks:
                W = projs[pi][1]
                for kc in range(8):
                    S.op("pe", lambda e: e.matmul(pb[:, 0:w], lhsT=hTs[pi][:, kc, :], rhs=W.ap[:, kc, c0:c0 + w],
                                                  start=(kc == 0), stop=(kc == 7)),
                         reads=[hTs[pi], W.k[kc]], writes=[pb])
                    yield
            YA = c["YA"] = YAr()
            YB = c["YB"] = YBr()
            YV = c["YV"] = YVr() if YVr is not None else None
            YG = c["YG"] = YGr() if YGr is not None else None
            for (pi, c0, c1, kind, di) in seg_map:
                cc = c0
                while cc < c1:
                    bk = [b for b in banks if b[0] == pi and b[1] <= cc < b[1] + b[2]][0]
                    ce = min(c1, bk[1] + bk[2])
                    src = bk[3][:, cc - bk[1]:ce - bk[1]]
                    off = cc - c0
                    n = ce - cc
                    if kind in ("A", "T"):
                        dst = YA if kind == "A" else YB
                        dflat = dst[:].rearrange("p h d -> p (h d)")
                        S.op("act", lambda e: e.copy(out=dflat[:, di * 64 + off: di * 64 + off + n], in_=src),
                             reads=[bk[3]], writes=[dst])
                        yield
                    elif kind == "V":
                        h0 = di + off // 64
                        S.op("act", lambda e: e.copy(out=YV[:, h0:h0 + n // 64, 0:64],
                                                     in_=src.rearrange("p (h d) -> p h d", d=64)),
                             reads=[bk[3]], writes=[YV])
                        yield
                    elif kind == "G":
                        S.op("act", lambda e: e.copy(out=YG[:, off:off + n], in_=src), reads=[bk[3]], writes=[YG])
                        yield
                    cc = ce

        def s3(ti):
            t0 = ti * 128
            c = ctx.pop(ti)
            yield from head_norm(c["YA"], nh, nrope, GA, c["CS"], c["YB"], (SQ, SS, RS, RS2, YN, T1, T2))
            TT = TTr()
            yield from feat_transposes(c["YB"], npairs, tpf, TT)
            S.dma("sp", featT.ap.rearrange("(j h2) d t -> (h2 d) j t", h2=2)[:, :, t0:t0 + 128], TT[:],
                  reads=[TT], dram_w=[featT])
            yield
            if vtok is not None:
                S.dma("sp", vtok[t0:t0 + 128, :, :], c["YV"][:], reads=[c["YV"]], dram_w=[vtok])
                yield
            if c["YG"] is not None:
                S.dma("sp", gates[t0:t0 + 128, :], c["YG"][:], reads=[c["YG"]], dram_w=[gates])
                yield

        s0(0)
        s0(1)
        for step in range(ntiles + 2):
            if step + 2 < ntiles:
                s0(step + 2)
            gens = []
            if step < ntiles:
                gens.append((s1(step), 10))
            if 0 <= step - 1 < ntiles:
                gens.append((s2(step - 1), 56))
            if 0 <= step - 2 < ntiles:
                gens.append((s3(step - 2), 28))
            interleave(gens)

    def mem_phase(layer, kmT, VM):
        with Phase():
            stg = ring("stg", [128, 2048], F32, 2)
            W = WT("wmem", 8, 512)
            load_weight(W, I["w_mem_kv"][layer], 8, 512, stg)
            GA = build_gains(f"GAm{layer}", 4, [(I["mem_k_norm"][layer:layer + 1, :], 0, 4, 1.0)])
            xr = ring("xt", [128, DM], F32, 2)
            junk = sb("junk", [128, DM], BF16)
            ssq = sb("ssq", [128, 1]); rs = sb("rs", [128, 1]); rs2 = sb("rs2", [128, 1])
            xn = sb("xn", [128, DM], BF16)
            tph = ps("tph", [128, 8, 128], BF16)
            tpf = ps("tpf", [128, 8, 128], BF16)
            hT = sb("hT", [128, 8, 128], BF16)
            pb = ps("pb", [128, 512], F32)
            YA = sb("YA", [128, 4, 64], F32); YB = sb("YB", [128, 4, 64], BF16)
            SQ = sb("SQ", [128, 4, 64], F32); SS = sb("SS", [128, 4]); RS = sb("RS", [128, 4]); RS2 = sb("RS2", [128, 4])
            YN = sb("YN", [128, 4, 64], F32); T1 = sb("T1", [128, 1, 64], F32); T2 = sb("T2", [128, 1, 64], F32)
            TT = sb("TT", [128, 2, 128], BF16)
            S.op("pool", lambda e: e.memset(VM[:], 1.0), writes=[VM])
            memT = DramT(I["mem"], "mem")
            for mt in range(2):
                xt = xr()
                S.dma("sp", xt[:], memT[mt * 128:(mt + 1) * 128, :], writes=[xt])
                run(norm_tile(xt[:], xt, [gT_mem[layer]], [hT], tph, (junk, ssq, rs, rs2, xn)))
                for kc in range(8):
                    S.op("pe", lambda e: e.matmul(pb[:, :], lhsT=hT[:, kc, :], rhs=W.ap[:, kc, :], start=(kc == 0), stop=(kc == 7)),
                         reads=[hT, W.k[kc]], writes=[pb])
                S.op("act", lambda e: e.copy(out=YA[:].rearrange("p h d -> p (h d)"), in_=pb[:, 0:256]), reads=[pb], writes=[YA])
                S.op("act", lambda e: e.copy(out=VM[:, mt, :, 0:64], in_=pb[:, 256:512].rearrange("p (h d) -> p h d", d=64)),
                     reads=[pb], writes=[VM])
                run(head_norm(YA, 4, 0, GA, None, YB, (SQ, SS, RS, RS2, YN, T1, T2)))
                run(feat_transposes(YB, 2, tpf, TT))
                for mh in range(4):
                    S.op("dve", lambda e: e.tensor_copy(out=kmT[0:64, mh, mt * 128:(mt + 1) * 128],
                                                        in_=TT[(mh % 2) * 64:(mh % 2) * 64 + 64, mh // 2, :]),
                         reads=[TT], writes=[kmT])

    LOOK = 3

    def tile_job(lhsT_ap, rhs_fn, reads, mask_tile, mask_off, acc, v_ap, v_tile, subs, first):
        return ["tile", dict(lhsT=lhsT_ap, rhs_fn=rhs_fn, reads=reads, mt=mask_tile, mo=mask_off, acc=acc, v=v_ap,
                             vt=v_tile, subs=subs, first=first)]

    def run_jobs(jobs, ST, PTr):
        tiles = [j[1] for j in jobs if j[0] == "tile"]
        pos = {"fi": 0, "ti": 0}

        def front(j):
            c0 = min(j["subs"]) * 128
            c1 = (max(j["subs"]) + 1) * 128
            st = ST()
            pt = PTr()
            j["pt"] = pt
            S.op("pe", lambda e: e.matmul(st[:, c0:c1], lhsT=j["lhsT"], rhs=j["rhs_fn"](c0, c1), start=True, stop=True),
                 reads=j["reads"], writes=[st])
            S.op("act", lambda e: e.activation(out=pt[:, c0:c1], in_=st[:, c0:c1], func=AF.Exp), reads=[st], writes=[pt])
            if j["mt"] is not None:
                mo = j["mo"]
                S.op("dve", lambda e: e.tensor_tensor(out=pt[:, c0:c1], in0=pt[:, c0:c1], in1=j["mt"][:, mo + c0:mo + c1], op=ALU.mult),
                     reads=[pt, j["mt"]], writes=[pt])

        def back(j):
            pt = j["pt"]
            for i, sidx in enumerate(j["subs"]):
                S.op("pe", lambda e: e.matmul(j["acc"][:, sidx, :], lhsT=pt[:, sidx * 128:(sidx + 1) * 128], rhs=j["v"],
                                              start=(j["first"] and i == 0), stop=False, skip_group_check=True),
                     reads=[pt, j["vt"]], writes=[j["acc"]])

        for j in jobs:
            if j[0] == "tile":
                while pos["fi"] < len(tiles) and pos["fi"] <= pos["ti"] + LOOK:
                    front(tiles[pos["fi"]])
                    pos["fi"] += 1
                back(j[1])
                pos["ti"] += 1
            else:
                j[1]()

    def finalize(acc, OM, col0, fac_fn, tmp, accumulate):
        R, F, T = tmp
        S.op("dve", lambda e: e.reciprocal(out=R[:], in_=acc[:, :, 64]), reads=[acc], writes=[R])
        fac = R
        if fac_fn is not None:
            gap, gtile = fac_fn
            S.op("dve", lambda e: e.tensor_tensor(out=F[:], in0=R[:], in1=gap, op=ALU.mult), reads=[R, gtile], writes=[F])
            fac = F
        facb = fac[:].unsqueeze(2).to_broadcast([128, 4, 64])
        if not accumulate:
            S.op("dve", lambda e: e.tensor_tensor(out=OM[:, :, col0:col0 + 64], in0=acc[:, :, 0:64], in1=facb, op=ALU.mult),
                 reads=[acc, fac], writes=[OM])
        else:
            S.op("dve", lambda e: e.tensor_tensor(out=T[:], in0=acc[:, :, 0:64], in1=facb, op=ALU.mult), reads=[acc, fac], writes=[T])
            S.op("pool", lambda e: e.tensor_tensor(out=OM[:, :, col0:col0 + 64], in0=OM[:, :, col0:col0 + 64], in1=T[:], op=ALU.add),
                 reads=[OM, T], writes=[OM])

    def out_proj(OM, ncols_in, WO, Xres, dst, Q0, tp, po_ring, OBr, OTr, XOr):
        nk = ncols_in // 128
        for s in range(4):
            OB = OBr()
            S.op("dve", lambda e: e.tensor_copy(out=OB[:, 0:ncols_in], in_=OM[:, s, :]), reads=[OM], writes=[OB])
            for kc in range(nk):
                S.op("pe", lambda e: e.transpose(out=tp[:, kc, :], in_=OB[:, kc * 128:(kc + 1) * 128], identity=ident[:]),
                     reads=[OB, ident], writes=[tp])
            OT = OTr()
            S.op("act", lambda e: e.copy(out=OT[:, 0:nk, :], in_=tp[:, 0:nk, :]), reads=[tp], writes=[OT])
            XO = XOr()
            for half in range(2):
                po = po_ring()
                for kc in range(nk):
                    S.op("pe", lambda e: e.matmul(po[:, :], lhsT=OT[:, kc, :], rhs=WO.ap[:, kc, half * 512:(half + 1) * 512],
                                                  start=(kc == 0), stop=(kc == nk - 1)),
                         reads=[OT, WO.k[kc]], writes=[po])
                S.op("dve", lambda e: e.tensor_tensor(out=XO[:, half * 512:(half + 1) * 512], in0=po[:, :],
                                                      in1=Xres[:, s, half * 512:(half + 1) * 512], op=ALU.add),
                     reads=[po, Xres], writes=[XO])
            S.dma("sp", dst[Q0 + s * 128:Q0 + (s + 1) * 128, :], XO[:], reads=[XO], dram_w=[dst])

    def fin_job(acc, OM, col0, fac_fn, tmp, accumulate):
        return ["fin", lambda: finalize(acc, OM, col0, fac_fn, tmp, accumulate)]

    def mem_jobs(QM, qm_tile, kmT, VM, ACC, OM, col0, tmp):
        jobs = []
        for mh in range(4):
            acc = ACC()
            for mt in range(2):
                jobs.append(tile_job(kmT[:, mh, mt * 128:(mt + 1) * 128], (lambda c0, c1, mh=mh: QM[:, mh, c0:c1]), [kmT, qm_tile],
                                     None, None, acc, VM[:, mt, mh, :], VM, [0, 1, 2, 3], mt == 0))
            jobs.append(fin_job(acc, OM, col0 + mh * 64, None, tmp, False))
        return jobs

    def bank_ring(tiles):
        st = {"i": -1}

        def nxt():
            st["i"] = (st["i"] + 1) % len(tiles)
            return tiles[st["i"]]
        return nxt

    def acc_view(b):
        return b.view(b.ap[:, 0:260].rearrange("p (s c) -> p s c", c=65))

    def mlp_phase(layer, src, dst):
        with Phase():
            stg = ring("stg", [128, 1024], F32, 2)
            WU = WT("WU", 8, 4096)
            WD = WT("WD", 32, 1024)
            load_weight(WU, I["w_up"][layer], 8, 4096, stg, col_chunk=1024, engs=("dve", "act"))
            xr = ring("xt", [128, 2, DM], F32, 3)
            ssq = sb("ssq", [128, 1]); rs = sb("rs", [128, 1]); rs2 = sb("rs2", [128, 1])
            xn = sb("xn", [128, DM], BF16)
            junk = xn
            tph = ps("tph", [128, 8, 128], BF16)
            uT = sb("uT", [128, 32, 256], BF16)
            rl = ring("rl", [128, 256], BF16, 3)
            XO = ring("XO", [128, DM], F32, 2)
            pu = [ps(f"pu{i}", [128, 512], F32) for i in range(3)]
            pd = [ps(f"pd{i}", [128, 512], F32) for i in range(3)]
            cnt = 0
            xts = {}

            def ldx(tb):
                xts[tb] = xr()
                S.dma("sp", xts[tb][:], src.ap[tb * 256:tb * 256 + 256, :].rearrange("(s p) d -> p s d", p=128),
                      writes=[xts[tb]], dram_r=[src])
            ldx(0)
            hTr2 = ring("hTm", [128, 8, 256], BF16, 2)
            hTs = {}

            def normgen(tb):
                xt = xts[tb]
                hTt = hTr2()
                hTs[tb] = hTt
                for s in range(2):
                    hv = hTt.view(hTt.ap[:, :, s * 128:(s + 1) * 128])
                    yield from norm_tile(xt[:, s, :], xt, [gT_mlp[layer]], [hv], tph, (junk, ssq, rs, rs2, xn))

            def blockgen(tb):
                nonlocal_cnt = cntbox
                t0 = tb * 256
                xt = xts[tb]
                hTt = hTs.pop(tb)
                for fc in range(32):
                    p = pu[fc % 3]
                    for kc in range(8):
                        S.op("pe", lambda e: e.matmul(p[:, 0:256], lhsT=WU.ap[:, kc, fc * 128:(fc + 1) * 128], rhs=hTt[:, kc, :],
                                                      start=(kc == 0), stop=(kc == 7)),
                             reads=[WU.k[kc], hTt], writes=[p])
                        yield
                    r = rl()
                    S.op("act", lambda e: e.activation(out=r[:], in_=p[:, 0:256], func=AF.Relu), reads=[p], writes=[r])
                    yield
                    S.op("dve", lambda e: e.tensor_tensor(out=uT[:, fc, :], in0=r[:], in1=r[:], op=ALU.mult), reads=[r], writes=[uT])
                    yield
                if tb == 0:
                    load_weight(WD, I["w_down"][layer], 32, 1024, stg, col_chunk=1024, engs=("dve", "act"))
                for s in range(2):
                    xo = XO()
                    for half in range(2):
                        p = pd[nonlocal_cnt[0] % 3]
                        nonlocal_cnt[0] += 1
                        for kc in range(32):
                            S.op("pe", lambda e: e.matmul(p[:, :], lhsT=uT[:, kc, s * 128:(s + 1) * 128],
                                                          rhs=WD.ap[:, kc, half * 512:(half + 1) * 512],
                                                          start=(kc == 0), stop=(kc == 31)),
                                 reads=[uT, WD.k[kc]], writes=[p])
                            yield
                        S.op("dve", lambda e: e.tensor_tensor(out=xo[:, half * 512:(half + 1) * 512], in0=p[:, :],
                                                              in1=xt[:, s, half * 512:(half + 1) * 512], op=ALU.add),
                             reads=[p, xt], writes=[xo])
                        yield
                    S.dma("sp", dst[t0 + s * 128:t0 + (s + 1) * 128, :], xo[:], reads=[xo], dram_w=[dst])
                    yield
                xts.pop(tb)

            cntbox = [0]
            nblk = SEQ // 256
            ldx(1)
            run(normgen(0))
            for tb in range(nblk):
                gens = [(blockgen(tb), 460)]
                if tb + 1 < nblk:
                    gens.append((normgen(tb + 1), 32))
                interleave(gens)
                if tb + 2 < nblk:
                    ldx(tb + 2)

    with Phase():
        stg = ring("stg", [128, 1106], F32, 4)
        WI = WT("WI", 8, 2212)
        load_weight(WI, I["a_w_in"][0], 8, 2212, stg, col_chunk=1106, engs=("dve", "act"))
        GA0 = build_gains("GA0", 22, [(I["a_q_norm"][0:1, :], 0, 12, 0.125), (I["a_k_norm"][0, 1:2, :], 12, 3, 1.0),
                                     (I["a_k_norm"][0, 2:3, :], 15, 3, 1.0), (I["mem_q_norm"][0:1, :], 18, 4, 0.125)])
        seg0 = [(0, 0, 768, "A", 0), (0, 1152, 1344, "A", 12), (0, 1536, 1728, "A", 15), (0, 1920, 2176, "A", 18),
                (0, 768, 960, "T", 22), (0, 960, 1152, "T", 25),
                (0, 1344, 1536, "V", 0), (0, 1728, 1920, "V", 3), (0, 2176, 2212, "G", 0)]
        proj_phase(xin, SEQ // 128, [(gT_attn[0], WI, 2212)], 22, 18, GA0, None, featT0, vtok0, 6, gate_cols=True, seg_map=seg0)

    kcT = sb("kcT", [128, 3, 256], BF16, persist=True)
    S.op("pool", lambda e: e.memset(kcT[:], 0.0), writes=[kcT])
    VCO = sb("VCO", [128, 2, 3, 128], BF16, persist=True)
    with Phase():
        GAc = build_gains("GAc", 1, [(I["a_k_norm"][0, 0:1, :], 0, 1, 1.0)])
        for ct in range(2):
            for g in range(3):
                S.dma("sp", VCO[:, ct, g, 64:128], C["c_ovl"][ct * 128:(ct + 1) * 128, :], writes=[VCO])
        XCr = ring("XC", [64, SEQ], BF16, 2)
        W1s = sb("W1s", [64, 32, 256], F32)
        W1 = sb("W1", [64, 32, 256], BF16)
        W2s = sb("W2s", [128, 2, 64], F32)
        W2 = sb("W2", [128, 2, 64], BF16)
        posS = sb("posS", [64, 32], F32); posT = sb("posT", [64, 32], BF16)
        b1c = sb("b1c", [128, 2], F32); BT = sb("BT", [128, 2], F32)
        b2s = sb("b2s", [1, 64], F32); b2r = sb("b2r", [1, 64], BF16)
        HG = sb("HG", [128, 2, 256], BF16)
        S.op("pool", lambda e: e.memset(HG[:], 0.0), writes=[HG])
        ph = [ps(f"ph{i}", [128, 512], F32) for i in range(2)]
        pbias = ps("pbias", [128, 2], F32)
        pout = ps("pout", [128, 64], F32)
        tpf = ps("tpf", [128, 8, 128], BF16)
        CSc = sb("CSc", [128, 2, 2, 32], F32)
        S.op("pool", lambda e: e.memset(CSc[:], 0.0), writes=[CSc])
        for ct in range(2):
            n = 128 if ct == 0 else 127
            S.dma("sp", CSc[0:n, ct, 0, :], C["c_cos"].rearrange("(c s) f -> c s f", s=16)[ct * 128 + 1:ct * 128 + 1 + n, 15, :], writes=[CSc])
            S.dma("sp", CSc[0:n, ct, 1, :], C["c_sin"].rearrange("(c s) f -> c s f", s=16)[ct * 128 + 1:ct * 128 + 1 + n, 15, :], writes=[CSc])
        YA = sb("YAc", [128, 1, 64], F32); YB = sb("YBc", [128, 2, 64], BF16)
        S.op("pool", lambda e: e.memset(YB[:], 0.0), writes=[YB])
        SQ = sb("SQ", [128, 1, 64], F32); SS = sb("SS", [128, 1]); RS = sb("RS", [128, 1]); RS2 = sb("RS2", [128, 1])
        YN = sb("YN", [128, 1, 64], F32); T1 = sb("T1", [128, 1, 64], F32); T2 = sb("T2", [128, 1, 64], F32)
        CS1 = sb("CS1", [128, 2, 32], F32)
        TTc = sb("TTc", [128, 1, 128], BF16)
        for i in range(2):
            S.dma("sp", W1s[:], I["a_cmp_w1"][0, i].rearrange("(j d) h -> d j h", d=64), writes=[W1s])
            S.op("dve", lambda e: e.tensor_copy(out=W1[:, 0:16, :], in_=W1s[:, 0:16, :]), reads=[W1s], writes=[W1])
            S.op("pool", lambda e: e.tensor_copy(out=W1[:, 16:32, :], in_=W1s[:, 16:32, :]), reads=[W1s], writes=[W1])
            S.dma("sp", W2s[:], I["a_cmp_w2"][0, i].rearrange("(hf p) d -> p hf d", p=128), writes=[W2s])
            S.op("dve", lambda e: e.tensor_copy(out=W2[:], in_=W2s[:]), reads=[W2s], writes=[W2])
            S.dma("sp", posS[:], I["a_cmp_pos"][0, i].rearrange("j d -> d j"), writes=[posS], allow_slow_non_contiguous=True)
            S.op("dve", lambda e: e.tensor_copy(out=posT[:], in_=posS[:]), reads=[posS], writes=[posT])
            S.dma("sp", b1c[:], I["a_cmp_b1"][0, i:i + 1, :].rearrange("o (hf p) -> p (o hf)", p=128), writes=[b1c],
                  allow_slow_non_contiguous=True)
            S.dma("sp", b2s[:], I["a_cmp_b2"][0, i:i + 1, :], writes=[b2s])
            S.op("dve", lambda e: e.tensor_copy(out=b2r[:], in_=b2s[:]), reads=[b2s], writes=[b2r])
            for hf in range(2):
                for j in range(32):
                    S.op("pe", lambda e: e.matmul(pbias[:, hf:hf + 1], lhsT=W1[:, j, hf * 128:(hf + 1) * 128], rhs=posT[:, j:j + 1],
                                                  start=(j == 0 and hf == 0), stop=(j == 31), skip_group_check=True),
                         reads=[W1, posT], writes=[pbias])
            S.op("dve", lambda e: e.tensor_tensor(out=BT[:], in0=pbias[:], in1=b1c[:], op=ALU.add), reads=[pbias, b1c], writes=[BT])
            for g in range(3):
                XC = XCr()
                S.dma("sp", XC[:], featT0[22 + 3 * i + g, :, :], writes=[XC], dram_r=[featT0])
                XCv = XC[:].rearrange("d (c s) -> d c s", s=16)
                for hf in range(2):
                    for j in range(32):
                        S.op("pe", lambda e: e.matmul(ph[hf][:, 0:255], lhsT=W1[:, j, hf * 128:(hf + 1) * 128],
                                                      rhs=XCv[:, (j // 16):(j // 16) + 255, j % 16], start=(j == 0), stop=(j == 31)),
                             reads=[W1, XC], writes=[ph[hf]])
                    S.op("act", lambda e: e.activation(out=HG[:, hf, 0:255], in_=ph[hf][:, 0:255], func=AF.Gelu_apprx_tanh,
                                                       bias=BT[:, hf:hf + 1]),
                         reads=[ph[hf], BT], writes=[HG])
                for ct in range(2):
                    for hf in range(2):
                        S.op("pe", lambda e: e.matmul(pout[:, :], lhsT=HG[:, hf, ct * 128:(ct + 1) * 128], rhs=W2[:, hf, :],
                                                      start=(hf == 0), stop=False),
                             reads=[HG, W2], writes=[pout])
                    S.op("pe", lambda e: e.matmul(pout[:, :], lhsT=ones_bf[0:1, :], rhs=b2r[0:1, :], start=False, stop=True),
                         reads=[ones_bf, b2r], writes=[pout])
                    if i == 0:
                        S.op("act", lambda e: e.copy(out=YA[:, 0, :], in_=pout[:, :]), reads=[pout], writes=[YA])
                        S.op("dve", lambda e: e.tensor_copy(out=CS1[:], in_=CSc[:, ct, :, :]), reads=[CSc], writes=[CS1])
                        run(head_norm(YA, 1, 1, GAc, CS1, YB, (SQ, SS, RS, RS2, YN, T1, T2)))
                        run(feat_transposes(YB, 1, tpf, TTc))
                        S.op("dve", lambda e: e.tensor_copy(out=kcT[0:64, g, ct * 128:(ct + 1) * 128], in_=TTc[0:64, 0, :]),
                             reads=[TTc], writes=[kcT])
                    else:
                        S.op("act", lambda e: e.copy(out=VCO[:, ct, g, 0:64], in_=pout[:, :]), reads=[pout], writes=[VCO])

    kmT0 = sb("kmT0", [128, 4, 256], BF16, persist=True)
    S.op("pool", lambda e: e.memset(kmT0[:], 0.0), writes=[kmT0])
    VM0 = sb("VM0", [128, 2, 4, 65], BF16, persist=True)
    mem_phase(0, kmT0, VM0)

    with Phase():
        stg = ring("stg", [128, 1024], F32, 1)
        WO = WT("WO", 8, 1024)
        load_weight(WO, I["a_w_out"][0], 8, 1024, stg)
        KA = sb("KA", [128, 3, SEQ], BF16)
        KW = sb("KW", [128, 3, SEQ], BF16)
        S.op("pool", lambda e: e.memset(KW[64:128, :, :], 0.0), writes=[KW])
        VSW = sb("VSW", [128, 32, 6, 65], BF16)
        S.dma("sp", KA[0:64, :, :], featT0.ap[12:15, :, :].rearrange("h d t -> d h t"), writes=[KA], dram_r=[featT0])
        for g in range(3):
            S.dma("sp", KA[64:128, g, :], C["c_onehot"][:, :], writes=[KA])
        S.dma("sp", KW[0:64, :, :], featT0.ap[15:18, :, :].rearrange("h d t -> d h t"), writes=[KW], dram_r=[featT0])
        for kt0 in range(0, 32, 8):
            S.dma("sp", VSW[:, kt0:kt0 + 8, :, :], vtok0.ap[kt0 * 128:(kt0 + 8) * 128].rearrange("(kt p) h d -> p kt h d", p=128),
                  writes=[VSW], dram_r=[vtok0])
        msel = sb("msel", [128, 896], BF16); mwin = sb("mwin", [128, 1408], BF16)
        mcmp = sb("mcmp", [128, 512], F32); tkeep = sb("tkeep", [128, 128], F32); tadd = sb("tadd", [128, 128], F32)
        for t, k in ((msel, "m_sel"), (mwin, "m_win"), (mcmp, "m_cmp"), (tkeep, "m_tkeep"), (tadd, "m_tadd")):
            S.dma("sp", t[:], C[k][:, :], writes=[t])
        QAr = ring("QA", [128, 16, 512], BF16, 1)
        for t in QAr.tiles:
            S.op("pool", lambda e: e.memset(t[:], 0.0), writes=[t])
        GTr = ring("GT", [128, 4, 36], F32, 1)
        XRr = ring("XR", [128, 4, DM], F32, 1)
        OM = sb("OM", [128, 4, DM], F32)
        Bk = [ps(f"bk{i}", [128, 512], F32) for i in range(7)]
        TP = ps("tp", [128, 8, 128], BF16)
        ST = bank_ring(Bk[0:4])
        ACC = bank_ring([acc_view(b) for b in Bk[4:7]])
        PO = bank_ring(Bk[0:2])
        SCsets = [(Bk[0], Bk[1]), (Bk[2], Bk[3])]
        OCI = [b.view(b.ap[:, 0:512].rearrange("p (r d) -> p r d", d=64)) for b in Bk[4:7]]
        PTr = ring("PT", [128, 512], BF16, LOOK + 2)
        SCm = ring("SCm", [128, 256], F32, 4)
        PCr = ring("PC", [128, 256], F32, 4)
        for t in PCr.tiles:
            S.op("pool", lambda e: e.memset(t[:], 0.0), writes=[t])
        PNr = ring("PN", [128, 256], BF16, 24)
        PTc = ring("PTc", [128, 8, 128], BF16, 3)
        rsr = ring("rsr", [128, 4, 2], F32, 2)
        TK = [dict(I1=sb("I1", [128, 64], F32), I2=sb("I2", [128, 64], F32), I3=sb("I3", [128, 64], F32),
                   M8=sb("M8", [128, 16], F32), SEL=sb("SEL", [128, 64], F32), VAL=sb("VAL", [128, 64], F32)) for _ in range(3)]
        MTr = ring("MT", [128, 128], BF16, 4)
        for t in MTr.tiles:
            S.op("pool", lambda e: e.memset(t[:], 0.0), writes=[t])
        Rt = sb("Rt", [128, 4], F32); Ft = sb("Ft", [128, 4], F32); Tt = sb("Tt", [128, 4, 64], F32)
        OBr = ring("OB", [128, DM], BF16, 1); OTr = ring("OT", [128, 8, 128], BF16, 1); XOr = ring("XO", [128, DM], F32, 1)
        for qi in range(SEQ // 512):
            Q0 = qi * 512
            QA = QAr(); GT = GTr(); XR = XRr()
            S.dma("sp", QA[0:64, 0:12, :], featT0.ap[0:12, :, Q0:Q0 + 512].rearrange("h d t -> d h t"), writes=[QA], dram_r=[featT0])
            S.dma("sp", QA[0:64, 12:16, :], featT0.ap[18:22, :, Q0:Q0 + 512].rearrange("h d t -> d h t"), writes=[QA], dram_r=[featT0])
            S.dma("sp", GT[:], gates.ap[Q0:Q0 + 512, :].rearrange("(s p) c -> p s c", p=128), writes=[GT], dram_r=[gates])
            S.op("act", lambda e: e.activation(out=GT[:], in_=GT[:], func=AF.Sigmoid), reads=[GT], writes=[GT])
            S.dma("sp", XR[:], xin.ap[Q0:Q0 + 512, :].rearrange("(s p) d -> p s d", p=128), writes=[XR])

            def stepA_front(n, s, g):
                i128 = qi * 4 + s
                bA, bB = SCsets[n % 2]
                scv = [(bA if r < 2 else bB) for r in range(4)]
                off = 248 - 8 * i128
                pns = []
                for r in range(4):
                    h = g * 4 + r
                    c0 = (r % 2) * 256
                    S.op("pe", lambda e: e.matmul(scv[r][:, c0:c0 + 255], lhsT=QA[:, h, s * 128:(s + 1) * 128], rhs=kcT[:, g, 0:255],
                                                  start=True, stop=True), reads=[QA, kcT], writes=[scv[r]])
                    yield
                scms = []
                for r in range(4):
                    c0 = (r % 2) * 256
                    scm = SCm()
                    scms.append(scm)
                    S.op("dve", lambda e: e.tensor_tensor(out=scm[:, 0:255], in0=scv[r][:, c0:c0 + 255], in1=mcmp[:, off:off + 255], op=ALU.add),
                         reads=[scv[r], mcmp], writes=[scm])
                    yield
                rs4 = rsr()
                S.op("pool", lambda e: e.memset(rs4[:], 0.0), writes=[rs4])
                yield
                pcs = []
                for r in range(4):
                    pc = PCr()
                    pcs.append(pc)
                    S.op("act", lambda e: e.activation(out=pc[:, 0:255], in_=scms[r][:, 0:255], func=AF.Exp, accum_out=rs4[:, r, 0:1]),
                         reads=[scms[r], rs4], writes=[pc, rs4])
                    yield
                S.op("dve", lambda e: e.tensor_scalar(out=rs4[:, :, 1], in0=rs4[:, :, 0], scalar1=1e-30, scalar2=None, op0=ALU.add),
                     reads=[rs4], writes=[rs4])
                yield
                S.op("dve", lambda e: e.reciprocal(out=rs4[:, :, 0], in_=rs4[:, :, 1]), reads=[rs4], writes=[rs4])
                yield
                for r in range(4):
                    pn = PNr()
                    pns.append(pn)
                    S.op("act", lambda e: e.activation(out=pn[:], in_=pcs[r][:], func=AF.Copy, scale=rs4[:, r, 0:1]),
                         reads=[pcs[r], rs4], writes=[pn])
                    yield
                ctxs[(s, g)] = pns

            def stepA_back(ci, s, g):
                pns = ctxs.pop((s, g))
                i128 = qi * 4 + s
                oci = OCI[ci]
                tk = TK[ci]
                I1, I2, I3, M8, SEL, VAL = tk["I1"], tk["I2"], tk["I3"], tk["M8"], tk["SEL"], tk["VAL"]
                for r in range(4):
                    for ct in range(2):
                        S.op("pe", lambda e: e.transpose(out=TP[:, r * 2 + ct, :], in_=pns[r][:, ct * 128:(ct + 1) * 128], identity=ident[:]),
                             reads=[pns[r], ident], writes=[TP])
                ptc = PTc()
                S.op("act", lambda e: e.copy(out=ptc[:], in_=TP[:, 0:8, :]), reads=[TP], writes=[ptc])
                yield
                for r in range(4):
                    for ct in range(2):
                        S.op("pe", lambda e: e.matmul(oci[:, r, :], lhsT=ptc[:, r * 2 + ct, :], rhs=VCO[:, ct, g, 0:64], start=(ct == 0), stop=(ct == 1),
                                                      skip_group_check=True),
                             reads=[ptc, VCO], writes=[oci])
                    for ct in range(2):
                        S.op("pe", lambda e: e.matmul(oci[:, 4 + r, :], lhsT=ptc[:, r * 2 + ct, :], rhs=VCO[:, ct, g, 64:128], start=(ct == 0), stop=(ct == 1),
                                                      skip_group_check=True),
                             reads=[ptc, VCO], writes=[oci])
                    yield
                gc = GT[:, s, 12 * g:12 * g + 12].rearrange("p (r b) -> p r b", b=3)[:, :, 0]
                S.op("dve", lambda e: e.tensor_tensor(out=OM[:, s, 256 * g:256 * g + 256].rearrange("p (r d) -> p r d", d=64), in0=oci[:, 0:4, :],
                                                      in1=gc.unsqueeze(2).to_broadcast([128, 4, 64]), op=ALU.mult),
                     reads=[oci, GT], writes=[OM])
                yield
                toff = 62 - 2 * i128
                S.op("dve", lambda e: e.tensor_reduce(out=I3[:], in_=oci[:, 4:8, :].rearrange("p r j -> p j r"), axis=AX.X, op=ALU.add),
                     reads=[oci], writes=[I3])
                yield
                S.op("dve", lambda e: e.tensor_tensor(out=I1[:], in0=I3[:], in1=tkeep[:, toff:toff + 64], op=ALU.mult),
                     reads=[I3, tkeep], writes=[I1])
                yield
                S.op("dve", lambda e: e.tensor_tensor(out=I2[:], in0=I1[:], in1=tadd[:, toff:toff + 64], op=ALU.add),
                     reads=[I1, tadd], writes=[I2])
                yield
                S.op("dve", lambda e: e.memset(I2[:, 0:1], 3e9), reads=[I2], writes=[I2])
                yield
                S.op("dve", lambda e: e.max(out=M8[:, 0:8], in_=I2[:]), reads=[I2], writes=[M8])
                yield
                S.op("dve", lambda e: e.match_replace(out=I3[:], in_to_replace=M8[:, 0:8], in_values=I2[:], imm_value=-1e30),
                     reads=[I2, M8], writes=[I3])
                yield
                S.op("dve", lambda e: e.max(out=M8[:, 8:16], in_=I3[:]), reads=[I3], writes=[M8])
                yield
                S.op("dve", lambda e: e.tensor_scalar(out=SEL[:], in0=I2[:], scalar1=M8[:, 15:16], scalar2=None, op0=ALU.is_ge),
                     reads=[I2, M8], writes=[SEL])
                yield
                S.op("dve", lambda e: e.tensor_scalar(out=VAL[:], in0=I2[:], scalar1=-1e29, scalar2=None, op0=ALU.is_gt),
                     reads=[I2], writes=[VAL])
                yield
                S.op("dve", lambda e: e.tensor_tensor(out=SEL[:], in0=SEL[:], in1=VAL[:], op=ALU.mult), reads=[SEL, VAL], writes=[SEL])
                yield
                MT = MTr()
                S.op("dve", lambda e: e.tensor_scalar(out=MT[:, 64:128], in0=SEL[:], scalar1=-1.0, scalar2=-NBIG, op0=ALU.add, op1=ALU.mult),
                     reads=[SEL], writes=[MT])
                yield
                S.op("pe", lambda e: e.transpose(out=TP[:, 0, :], in_=MT[:], identity=ident[:]), reads=[MT, ident], writes=[TP])
                S.op("act", lambda e: e.copy(out=QA[64:128, 4 * g:4 * g + 4, s * 128:(s + 1) * 128],
                                             in_=TP[64:128, 0:1, :].to_broadcast([64, 4, 128])), reads=[TP], writes=[QA])
                yield

            def fronts(s):
                for g in range(3):
                    yield from stepA_front(s * 3 + g, s, g)

            ctxs = {}
            run(fronts(0))
            for s in range(4):
                gens = [(stepA_back(g, s, g), 24) for g in range(3)]
                if s + 1 < 4:
                    gens.append((fronts(s + 1), 60))
                interleave(gens)

            nkt = qi * 4 + 4
            jobs = []
            for h in range(12):
                g = h // 4
                accS = ACC(); accW = ACC()
                for kt in range(nkt):
                    D = Q0 - kt * 128
                    subs = [sx for sx in range(4) if D + 128 * sx >= 0]
                    mt_, mo_ = (msel, D + 384) if D < 128 else (None, None)
                    jobs.append(tile_job(KA[:, g, kt * 128:(kt + 1) * 128], (lambda c0, c1, h=h: QA[:, h, c0:c1]), [KA, QA], mt_, mo_,
                                         accS, VSW[:, kt, g, :], VSW, subs, kt == 0))
                kts = list(range(max(0, qi * 4 - 4), nkt))
                for kt in kts:
                    D = Q0 - kt * 128
                    subs = [sx for sx in range(4) if 0 <= D + 128 * sx <= 512]
                    jobs.append(tile_job(KW[:, g, kt * 128:(kt + 1) * 128], (lambda c0, c1, h=h: QA[:, h, c0:c1]), [KW, QA], mwin, D + 384,
                                         accW, VSW[:, kt, 3 + g, :], VSW, subs, kt == kts[0]))
                jobs.append(fin_job(accS, OM, h * 64, (GT[:, :, 3 * h + 1], GT), (Rt, Ft, Tt), True))
                jobs.append(fin_job(accW, OM, h * 64, (GT[:, :, 3 * h + 2], GT), (Rt, Ft, Tt), True))
            jobs += mem_jobs(QA[:, 12:16, :], QA, kmT0, VM0, ACC, OM, 768, (Rt, Ft, Tt))
            run_jobs(jobs, ST, PTr)
            out_proj(OM, 1024, WO, XR, x1, Q0, TP, PO, OBr, OTr, XOr)

    mlp_phase(0, x1, x2)

    with Phase():
        stg = ring("stg", [128, 1024], F32, 4)
        WB = WT("WB", 8, 1792)
        WK = WT("WK", 8, 1024)
        load_weight(WB, I["b_w_in"][0], 8, 1792, stg, col_chunk=896, engs=("dve", "act"))
        load_weight(WK, I["w_kv_shared"], 8, 1024, stg, col_chunk=1024, engs=("dve", "act"))
        GA1 = build_gains("GA1", 36, [(I["b_q_norm"][0, 0:1, :], 0, 8, 0.125), (I["b_q_norm"][0, 1:2, :], 8, 8, 0.125),
                                     (I["b_q_norm"][0, 2:3, :], 16, 8, 0.125), (I["kv_k_norm"][0:1, :], 24, 8, 1.0),
                                     (I["mem_q_norm"][1:2, :], 32, 4, 0.125)])
        seg1 = [(0, 0, 1536, "A", 0), (1, 0, 512, "A", 24), (0, 1536, 1792, "A", 32), (1, 512, 1024, "V", 0)]
        proj_phase(x2, SEQ // 128, [(gT_attn[1], WB, 1792), (gT_kv, WK, 1024)], 36, 32, GA1, None, featT1, vtok1, 8, seg_map=seg1)

    kmT1 = sb("kmT1", [128, 4, 256], BF16, persist=True)
    S.op("pool", lambda e: e.memset(kmT1[:], 0.0), writes=[kmT1])
    VM1 = sb("VM1", [128, 2, 4, 65], BF16, persist=True)
    mem_phase(1, kmT1, VM1)

    with Phase():
        stg = ring("stg", [128, 1024], F32, 1)
        WO1 = WT("WO1", 6, 1024)
        load_weight(WO1, I["b_w_out"][0], 6, 1024, stg)
        KT = sb("KT", [128, 4, SEQ], BF16)
        V1 = sb("V1", [128, 32, 8, 65], BF16)
        S.dma("sp", KT[:, :, :], featT1.ap[24:32, :, :].rearrange("(j h2) d t -> (h2 d) j t", h2=2), writes=[KT], dram_r=[featT1])
        for kt0 in range(0, 32, 8):
            S.dma("sp", V1[:, kt0:kt0 + 8, :, :], vtok1.ap[kt0 * 128:(kt0 + 8) * 128].rearrange("(kt p) h d -> p kt h d", p=128),
                  writes=[V1], dram_r=[vtok1])
        md = []
        for k, w in (("m_d0", 1024), ("m_d1", 1408), ("m_d2", 2944)):
            t = sb(k, [128, w], BF16)
            S.dma("sp", t[:], C[k][:, :], writes=[t])
            md.append(t)
        QTer = ring("QTe", [128, 12, 512], BF16, 2)
        QTor = ring("QTo", [128, 12, 512], BF16, 2)
        QMr = ring("QM", [128, 4, 512], BF16, 2)
        for t in QTer.tiles + QTor.tiles + QMr.tiles:
            S.op("pool", lambda e: e.memset(t[:], 0.0), writes=[t])
        XRr = ring("XR", [128, 4, DM], F32, 1)
        OM = sb("OM", [128, 4, 768], F32)
        Bk = [ps(f"bk{i}", [128, 512], F32) for i in range(7)]
        TP = ps("tp", [128, 8, 128], BF16)
        ST = bank_ring(Bk[0:4])
        ACC = bank_ring([acc_view(b) for b in Bk[4:7]])
        PO = bank_ring(Bk[0:2])
        PTr = ring("PT", [128, 512], BF16, LOOK + 2)
        Rt = sb("Rt", [128, 4], F32); Ft = sb("Ft", [128, 4], F32); Tt = sb("Tt", [128, 4, 64], F32)
        OBr = ring("OB", [128, DM], BF16, 2); OTr = ring("OT", [128, 8, 128], BF16, 2); XOr = ring("XO", [128, DM], F32, 2)
        pats = ((128, 1), (512, 4), (2048, 16))
        for qi in range(SEQ // 512):
            Q0 = qi * 512
            QTe = QTer(); QTo = QTor(); QM = QMr(); XR = XRr()
            qsrc = featT1.ap[0:24, :, Q0:Q0 + 512].rearrange("(j h2) d t -> h2 d j t", h2=2)
            S.dma("sp", QTe[0:64, :, :], qsrc[0], writes=[QTe], dram_r=[featT1])
            S.dma("sp", QTo[64:128, :, :], qsrc[1], writes=[QTo], dram_r=[featT1])
            S.dma("sp", QM[0:64, :, :], featT1.ap[32:36, :, Q0:Q0 + 512].rearrange("h d t -> d h t"), writes=[QM], dram_r=[featT1])
            S.dma("sp", XR[:], x2.ap[Q0:Q0 + 512, :].rearrange("(s p) d -> p s d", p=128), writes=[XR], dram_r=[x2])
            jobs = []
            for hh in range(8):
                acc = ACC()
                p0 = (hh % 2) * 64
                first = True
                for gi, (window, dil) in enumerate(pats):
                    qh = gi * 8 + hh
                    for kt in range(max(0, (Q0 - window) // 128), qi * 4 + 4):
                        D = Q0 - kt * 128
                        subs = [sx for sx in range(4) if 0 <= D + 128 * sx <= window]
                        QTx = QTe if hh % 2 == 0 else QTo
                        jobs.append(tile_job(KT[:, hh // 2, kt * 128:(kt + 1) * 128],
                                             (lambda c0, c1, QTx=QTx, qh=qh: QTx[:, qh // 2, c0:c1]), [KT, QTx],
                                             md[gi], D + 384, acc, V1[:, kt, hh, :], V1, subs, first))
                        first = False
                jobs.append(fin_job(acc, OM, hh * 64, None, (Rt, Ft, Tt), False))
            jobs += mem_jobs(QM, QM, kmT1, VM1, ACC, OM, 512, (Rt, Ft, Tt))
            run_jobs(jobs, ST, PTr)
            out_proj(OM, 768, WO1, XR, x3, Q0, TP, PO, OBr, OTr, XOr)

    mlp_phase(1, x3, outT)
    S.barrier()
    gst.close()
    return nc


_CACHE = {}


def kernel(**inputs):
    n = 8
    consts = host_consts()
    if "nc" not in _CACHE:
        _CACHE["nc"] = build_program()
    nc = _CACHE["nc"]
    in_maps = []
    for b in range(n):
        m = {}
        for k, shp in IN_SHAPES.items():
            a = np.asarray(inputs[k], dtype=np.float32)
            if k in ("x", "mem"):
                a = a[b]
            m[k] = np.ascontiguousarray(a.reshape(shp))
        m.update(consts)
        in_maps.append(m)
    res = run_bass_kernel_spmd(nc, in_maps, core_ids=list(range(n)))
    return np.stack([np.asarray(r["out"], dtype=np.float32) for r in res.results], axis=0)
```

```python
import numpy as np
import ml_dtypes
import concourse.bass as bass
import concourse.mybir as mybir
from concourse.bass_utils import run_bass_kernel_spmd
from contextlib import ExitStack

F32 = mybir.dt.float32
BF16 = mybir.dt.bfloat16
AF = mybir.ActivationFunctionType
ALU = mybir.AluOpType
AX = mybir.AxisListType
SEM_ROT = 30000
SEQ = 4096
DM = 1024
EPS = 1e-6
NBIG = -30000.0


class Tile:
    __slots__ = ("name", "w", "r", "dsem", "ap", "base")

    def __init__(self, name, ap=None, base=None):
        self.name = name
        self.w = None
        self.r = []
        self.dsem = None
        self.ap = ap
        self.base = base

    def view(self, ap):
        return Tile(self.name + "_v", ap, base=(self.base or self))

    def __getitem__(self, k):
        return self.ap[k]


class DramT:
    def __init__(self, ap, name="d"):
        self.ap = ap
        self.name = name
        self.pend = set()

    def __getitem__(self, k):
        return self.ap[k]


class Sched:
    def __init__(self, nc):
        self.nc = nc
        self.eng = {"pe": nc.tensor, "act": nc.scalar, "dve": nc.vector,
                    "pool": nc.gpsimd, "sp": nc.sync}
        self.esem = {}
        self.ecnt = {}
        self.seen = {e: {} for e in self.eng}
        self.latest = {}
        self.nsem = 0
        self.free_dsems = []
        for e in ("pe", "act", "dve", "pool"):
            self._new_esem(e)

    def _alloc(self, name):
        self.nsem += 1
        return self.nc.alloc_semaphore(name=f"{name}_{self.nsem}")

    def _new_esem(self, e):
        self.esem[e] = self._alloc("e" + e)
        self.ecnt[e] = 0

    def tile(self, ap, name="t"):
        return Tile(name, ap)

    def _need(self, engine, reads, writes):
        need = {}
        reads = [t.base or t for t in reads]
        writes = [t.base or t for t in writes]

        def add(ev, raw):
            if ev is None:
                return
            sem, val, kind, eng = ev
            if kind == "c" and eng == engine:
                if engine == "pe":
                    return
            if kind == "d":
                val = self.latest[sem]
            if need.get(sem, 0) < val:
                need[sem] = val

        for t in reads:
            add(t.w, True)
        for t in writes:
            add(t.w, False)
            for ev in t.r:
                add(ev, False)
        seen = self.seen[engine]
        out = []
        for sem, val in need.items():
            if seen.get(sem, 0) < val:
                seen[sem] = val
                out.append((sem, val))
        return out

    def _record(self, ev, reads, writes):
        reads = [t.base or t for t in reads]
        writes = [t.base or t for t in writes]
        for t in reads:
            t.r = [x for x in t.r if x[0] is not ev[0]]
            t.r.append(ev)
        for t in writes:
            t.w = ev
            t.r = []

    def op(self, engine, fn, reads=(), writes=()):
        e = self.eng[engine]
        for sem, val in self._need(engine, reads, writes):
            e.wait_ge(sem, val)
        if self.ecnt[engine] >= SEM_ROT:
            self._new_esem(engine)
        inst = fn(e)
        self.ecnt[engine] += 1
        inst.then_inc(self.esem[engine], 1)
        ev = (self.esem[engine], self.ecnt[engine], "c", engine)
        self._record(ev, reads, writes)
        return ev

    def free_dsem(self, tiles):
        for t in tiles:
            if t.dsem is not None:
                self.free_dsems.append(t.dsem)
                t.dsem = None

    def dma(self, queue, out, in_, reads=(), writes=(), dram_r=(), dram_w=(), **kw):
        e = self.eng[queue]
        stile = writes[0] if writes else reads[0]
        stile = stile.base or stile
        if stile.dsem is None or self.latest[stile.dsem] >= SEM_ROT:
            stile.dsem = None
            while self.free_dsems and stile.dsem is None:
                c = self.free_dsems.pop()
                if self.latest[c] < SEM_ROT:
                    stile.dsem = c
            if stile.dsem is None:
                stile.dsem = self._alloc("d")
                self.latest[stile.dsem] = 0
        waits = self._need(queue, reads, writes)
        seen = self.seen[queue]
        for d in dram_r:
            for sem in d.pend:
                val = self.latest[sem]
                if seen.get(sem, 0) < val:
                    seen[sem] = val
                    waits.append((sem, val))
        for sem, val in waits:
            e.wait_ge(sem, val)
        inst = e.dma_start(out=out, in_=in_, **kw)
        sem = stile.dsem
        self.latest[sem] += 16
        inst.then_inc(sem, 16)
        ev = (sem, self.latest[sem], "d", None)
        self._record(ev, reads, writes)
        for d in dram_w:
            d.pend.add(sem)
        return ev

    def barrier(self):
        for en, e in self.eng.items():
            seen = self.seen[en]
            for f in ("pe", "act", "dve", "pool"):
                if f == en:
                    continue
                sem, val = self.esem[f], self.ecnt[f]
                if val > 0 and seen.get(sem, 0) < val:
                    seen[sem] = val
                    e.wait_ge(sem, val)
            for sem, val in self.latest.items():
                if val > 0 and seen.get(sem, 0) < val:
                    seen[sem] = val
                    e.wait_ge(sem, val)


def host_consts():
    bf = ml_dtypes.bfloat16
    c = {}
    half = 32
    inv = (10000.0 ** (-np.arange(half, dtype=np.float32) / half)).astype(np.float32)
    ang = np.arange(SEQ + 32, dtype=np.float32)[:, None] * inv[None, :]
    c["c_cos"] = np.cos(ang).astype(np.float32)
    c["c_sin"] = np.sin(ang).astype(np.float32)
    c["c_ident"] = np.eye(128, dtype=np.float32).astype(bf)
    kk = np.arange(128)[:, None]

    def toep(width, f):
        j = np.arange(width)[None, :]
        return f(j - 384 - kk).astype(np.float32).astype(bf)

    c["m_sel"] = toep(896, lambda d: d >= 0)
    c["m_win"] = toep(1408, lambda d: (d >= 0) & (d < 512))
    c["m_d0"] = toep(1024, lambda d: (d >= 0) & (d <= 128))
    c["m_d1"] = toep(1408, lambda d: (d >= 0) & (d <= 512) & (d % 4 == 0))
    c["m_d2"] = toep(2944, lambda d: (d >= 0) & (d <= 2048) & (d % 16 == 0))
    u = np.arange(512)[None, :]
    c["m_cmp"] = np.where(16 * (u - 248) + 31 <= kk, 0.0, NBIG).astype(np.float32)
    u = np.arange(128)[None, :]
    rel = u - 62 - (kk // 64)
    keep = (rel < -1).astype(np.float32)
    add = np.where(rel == 0, 2e9, np.where(rel == -1, 1e9, np.where(rel > 0, -1e30, 0.0)))
    c["m_tkeep"] = keep.astype(np.float32)
    c["m_tadd"] = add.astype(np.float32)
    n_cmp = SEQ // 16 - 1
    cs = np.arange(256)[:, None] * 16
    bs = np.arange(64)[None, :] * 64
    ov = ((cs < bs + 64) & (cs + 32 > bs)).astype(np.float32)
    ov[n_cmp:] = 0
    c["c_ovl"] = ov.astype(bf)
    c["c_onehot"] = (np.arange(SEQ)[None, :] // 64 == np.arange(64)[:, None]).astype(np.float32).astype(bf)
    return c


CONST_SHAPES = {
    "c_cos": ([SEQ + 32, 32], F32), "c_sin": ([SEQ + 32, 32], F32), "c_ident": ([128, 128], BF16),
    "m_sel": ([128, 896], BF16), "m_win": ([128, 1408], BF16), "m_d0": ([128, 1024], BF16),
    "m_d1": ([128, 1408], BF16), "m_d2": ([128, 2944], BF16), "m_cmp": ([128, 512], F32),
    "m_tkeep": ([128, 128], F32), "m_tadd": ([128, 128], F32), "c_ovl": ([256, 64], BF16),
    "c_onehot": ([64, SEQ], BF16),
}

IN_SHAPES = {
    "x": [SEQ, DM], "mem": [256, DM], "attn_norm": [2, DM], "mlp_norm": [2, DM],
    "w_up": [2, DM, 4096], "w_down": [2, 4096, DM], "mem_norm": [2, DM], "w_mem_kv": [2, DM, 512],
    "mem_q_norm": [2, 64], "mem_k_norm": [2, 64], "a_w_in": [1, DM, 2212], "a_w_out": [1, DM, DM],
    "a_q_norm": [1, 64], "a_k_norm": [1, 3, 64], "a_cmp_pos": [1, 2, 32, 64],
    "a_cmp_w1": [1, 2, 2048, 256], "a_cmp_b1": [1, 2, 256], "a_cmp_w2": [1, 2, 256, 64],
    "a_cmp_b2": [1, 2, 64], "kv_norm": [1, DM], "w_kv_shared": [DM, DM], "kv_k_norm": [1, 64],
    "b_w_in": [1, DM, 1792], "b_w_out": [1, 768, DM], "b_q_norm": [1, 3, 64],
}


def build_program(debug=()):
    nc = bass.Bass("TRN2", target_bir_lowering=False)
    S = Sched(nc)
    I = {}
    for k, shp in IN_SHAPES.items():
        I[k] = nc.dram_tensor(k, shp, F32, kind="ExternalInput").ap()
    C = {}
    for k, (shp, dt) in CONST_SHAPES.items():
        C[k] = nc.dram_tensor(k, shp, dt, kind="ExternalInput").ap()
    out_ap = nc.dram_tensor("out", [SEQ, DM], F32, kind="ExternalOutput").ap()

    def scratch(name, shape, dt):
        kind = "ExternalOutput" if name in debug else "Internal"
        return DramT(nc.dram_tensor(name, shape, dt, kind=kind).ap(), name)

    featT0 = scratch("featT0", [28, 64, SEQ], BF16)
    vtok0 = scratch("vtok0", [SEQ, 6, 65], BF16)
    gates = scratch("gates", [SEQ, 36], F32)
    x1 = scratch("x1", [SEQ, DM], F32)
    x2 = scratch("x2", [SEQ, DM], F32)
    x3 = scratch("x3", [SEQ, DM], F32)
    featT1 = scratch("featT1", [36, 64, SEQ], BF16)
    vtok1 = scratch("vtok1", [SEQ, 8, 65], BF16)
    outT = DramT(out_ap, "out")
    xin = DramT(I["x"], "x")

    gst = ExitStack()
    cur = {"st": gst}

    uid = {"i": 0}
    local_tiles = []

    def sb(name, shape, dt=F32, persist=False):
        st = gst if persist else cur["st"]
        uid["i"] += 1
        t = S.tile(st.enter_context(nc.sbuf_tensor(f"{name}_{uid['i']}", list(shape), dt)), name)
        if not persist:
            local_tiles.append(t)
        return t

    def ps(name, shape, dt=F32):
        uid["i"] += 1
        full = [128, 512] if dt == F32 else [128, 1024]
        h = cur["st"].enter_context(nc.psum_tensor(f"{name}_{uid['i']}", full, dt))
        n = 1
        for d in shape[1:]:
            n *= d
        v = h[:, 0:n]
        if len(shape) == 3:
            v = v.rearrange("p (a b) -> p a b", b=shape[2])
        return S.tile(v, name)

    class Phase:
        def __enter__(self):
            self.st = ExitStack()
            cur["st"] = self.st
            return self

        def __exit__(self, *a):
            S.barrier()
            S.free_dsem(local_tiles)
            del local_tiles[:]
            self.st.close()
            cur["st"] = gst
            return False

    rr = {"i": 0}

    def alt(engs=("dve", "pool")):
        rr["i"] += 1
        return engs[rr["i"] % len(engs)]

    def ring(name, shape, dt, n):
        tiles = [sb(f"{name}{i}", shape, dt) for i in range(n)]
        st = {"i": -1}

        def nxt():
            st["i"] = (st["i"] + 1) % n
            return tiles[st["i"]]
        nxt.tiles = tiles
        return nxt

    ident = sb("ident", [128, 128], BF16, persist=True)
    S.dma("sp", ident[:], C["c_ident"][:, :], writes=[ident])
    ones_bf = sb("ones_bf", [128, 128], BF16, persist=True)
    S.op("dve", lambda e: e.memset(ones_bf[:], 1.0), writes=[ones_bf])

    def bcast_row(dst_ap, src_row_ap, tile):
        S.dma("sp", dst_ap, src_row_ap.partition_broadcast(128), writes=[tile])

    def load_gT(name, src_row):
        t = sb(name, [128, 8], F32, persist=True)
        S.dma("sp", t[:], src_row.rearrange("o (kc p) -> p (o kc)", p=128), writes=[t],
              allow_slow_non_contiguous=True)
        return t

    gT_attn = [load_gT(f"gT_attn{l}", I["attn_norm"][l:l + 1, :]) for l in range(2)]
    gT_mlp = [load_gT(f"gT_mlp{l}", I["mlp_norm"][l:l + 1, :]) for l in range(2)]
    gT_mem = [load_gT(f"gT_mem{l}", I["mem_norm"][l:l + 1, :]) for l in range(2)]
    gT_kv = load_gT("gT_kv", I["kv_norm"][0:1, :])

    def load_weight(wt, src, nk, ncols, stg_ring, col_chunk=2048, engs=("dve", "act", "pool")):
        g = wt.grp
        for k0 in range(0, nk, g):
            S.dma("pool", wt.ap[:, k0:k0 + g, :], src[k0 * 128:(k0 + g) * 128, :].rearrange("(kc p) n -> p kc n", p=128),
                  writes=[wt.k[k0]])
        return
        for kc in range(nk):
            eng = engs[kc % len(engs)]
            for c0 in range(0, ncols, col_chunk):
                w = min(col_chunk, ncols - c0)
                stg = stg_ring()
                S.dma("sp", stg[:, 0:w], src[kc * 128:(kc + 1) * 128, c0:c0 + w], writes=[stg])
                if eng == "act":
                    S.op(eng, lambda e: e.copy(out=wt.ap[:, kc, c0:c0 + w], in_=stg[:, 0:w]), reads=[stg], writes=[wt.k[kc]])
                else:
                    S.op(eng, lambda e: e.tensor_copy(out=wt.ap[:, kc, c0:c0 + w], in_=stg[:, 0:w]),
                         reads=[stg], writes=[wt.k[kc]])

    class WT:
        def __init__(self, name, nk, ncols, grp=1):
            self.h = sb(name, [128, nk, ncols], BF16)
            self.ap = self.h.ap
            self.grp = grp
            gt = [Tile(f"{name}_{i}") for i in range(0, nk, grp)]
            local_tiles.extend(gt)
            self.k = [gt[i // grp] for i in range(nk)]

    def run(gen):
        for _ in gen:
            pass

    def interleave(gens):
        state = [[g, 0, float(tot)] for g, tot in gens]
        while state:
            st = min(state, key=lambda z: z[1] / z[2])
            try:
                next(st[0])
                st[1] += 1
            except StopIteration:
                state.remove(st)

    def norm_tile(xt_ap, xt_tile, gTs, hTs, tph, tmp):
        junk, ssq, rs, rs2, xn = tmp
        S.op("pool", lambda e: e.memset(ssq[:], 0.0), writes=[ssq])
        yield
        S.op("act", lambda e: e.activation(out=junk[:], in_=xt_ap, func=AF.Square, accum_out=ssq[:, 0:1]),
             reads=[xt_tile, ssq], writes=[junk, ssq])
        yield
        S.op("act", lambda e: e.activation(out=rs[:], in_=ssq[:], func=AF.Sqrt, scale=1.0 / DM, bias=EPS),
             reads=[ssq], writes=[rs])
        yield
        S.op("dve", lambda e: e.reciprocal(out=rs2[:], in_=rs[:]), reads=[rs], writes=[rs2])
        yield
        S.op("dve", lambda e: e.tensor_scalar(out=xn[:], in0=xt_ap, scalar1=rs2[:, 0:1], scalar2=None, op0=ALU.mult),
             reads=[xt_tile, rs2], writes=[xn])
        yield
        for kc in range(8):
            S.op("pe", lambda e: e.transpose(out=tph[:, kc, :], in_=xn[:, kc * 128:(kc + 1) * 128], identity=ident[:]),
                 reads=[xn, ident], writes=[tph])
            yield
        for gT, hT in zip(gTs, hTs):
            S.op("dve", lambda e: e.tensor_tensor(out=hT[:], in0=tph[:], in1=gT[:].unsqueeze(2).to_broadcast([128, 8, 128]), op=ALU.mult),
                 reads=[tph, gT], writes=[hT])
            yield

    def head_norm(YA, nh, nrope, GA, CS, YB, tmp):
        SQ, SS, RS, RS2, YN, T1, T2 = tmp
        S.op("act", lambda e: e.activation(out=SQ[:, 0:nh, :], in_=YA[:, 0:nh, :], func=AF.Square),
             reads=[YA], writes=[SQ])
        yield
        S.op("dve", lambda e: e.tensor_reduce(out=SS[:, 0:nh], in_=SQ[:, 0:nh, :], axis=AX.X, op=ALU.add),
             reads=[SQ], writes=[SS])
        yield
        S.op("act", lambda e: e.activation(out=RS[:, 0:nh], in_=SS[:, 0:nh], func=AF.Sqrt, scale=1.0 / 64, bias=EPS),
             reads=[SS], writes=[RS])
        yield
        S.op("dve", lambda e: e.reciprocal(out=RS2[:, 0:nh], in_=RS[:, 0:nh]), reads=[RS], writes=[RS2])
        yield
        S.op("dve", lambda e: e.tensor_tensor(out=YN[:, 0:nh, :], in0=YA[:, 0:nh, :],
                                              in1=RS2[:, 0:nh].unsqueeze(2).to_broadcast([128, nh, 64]), op=ALU.mult),
             reads=[YA, RS2], writes=[YN])
        yield
        S.op("dve", lambda e: e.tensor_tensor(out=YN[:, 0:nh, :], in0=YN[:, 0:nh, :], in1=GA[:, 0:nh, :], op=ALU.mult),
             reads=[YN, GA], writes=[YN])
        yield
        if nrope:
            n = nrope
            cosb = CS[:, 0:1, :].to_broadcast([128, n, 32])
            sinb = CS[:, 1:2, :].to_broadcast([128, n, 32])
            x1v = YN[:, 0:n, 0:32]
            x2v = YN[:, 0:n, 32:64]
            S.op("dve", lambda e: e.tensor_tensor(out=T1[:, 0:n, 0:32], in0=x1v, in1=cosb, op=ALU.mult), reads=[YN, CS], writes=[T1])
            yield
            S.op("dve", lambda e: e.tensor_tensor(out=T2[:, 0:n, 0:32], in0=x2v, in1=sinb, op=ALU.mult), reads=[YN, CS], writes=[T2])
            yield
            S.op("dve", lambda e: e.tensor_tensor(out=YB[:, 0:n, 0:32], in0=T1[:, 0:n, 0:32], in1=T2[:, 0:n, 0:32], op=ALU.subtract),
                 reads=[T1, T2], writes=[YB])
            yield
            S.op("dve", lambda e: e.tensor_tensor(out=T1[:, 0:n, 32:64], in0=x1v, in1=sinb, op=ALU.mult), reads=[YN, CS], writes=[T1])
            yield
            S.op("dve", lambda e: e.tensor_tensor(out=T2[:, 0:n, 32:64], in0=x2v, in1=cosb, op=ALU.mult), reads=[YN, CS], writes=[T2])
            yield
            S.op("dve", lambda e: e.tensor_tensor(out=YB[:, 0:n, 32:64], in0=T1[:, 0:n, 32:64], in1=T2[:, 0:n, 32:64], op=ALU.add),
                 reads=[T1, T2], writes=[YB])
            yield
        if nh > nrope:
            S.op("act", lambda e: e.copy(out=YB[:, nrope:nh, :], in_=YN[:, nrope:nh, :]), reads=[YN], writes=[YB])
            yield

    def feat_transposes(YB, npairs, tpf, TT):
        YBf = YB[:].rearrange("p h d -> p (h d)")
        for b0 in range(0, npairs, 8):
            nb = min(8, npairs - b0)
            for j in range(nb):
                S.op("pe", lambda e: e.transpose(out=tpf[:, j, :], in_=YBf[:, (b0 + j) * 128:(b0 + j + 1) * 128], identity=ident[:]),
                     reads=[YB, ident], writes=[tpf])
                yield
            S.op("act", lambda e: e.copy(out=TT[:, b0:b0 + nb, :], in_=tpf[:, 0:nb, :]), reads=[tpf], writes=[TT])
            yield

    def build_gains(name, nh, specs):
        GA = sb(name, [128, nh, 64], F32)
        G1 = sb(name + "_raw", [128, len(specs), 64], F32)
        for i, (src, h0, n, scale) in enumerate(specs):
            bcast_row(G1[:, i, :], src, G1)
        for i, (src, h0, n, scale) in enumerate(specs):
            S.op("dve", lambda e: e.tensor_scalar(out=GA[:, h0:h0 + n, :], in0=G1[:, i:i + 1, :].to_broadcast([128, n, 64]),
                                                  scalar1=float(scale), scalar2=None, op0=ALU.mult),
                 reads=[G1], writes=[GA])
        return GA

    def proj_phase(x_src, ntiles, projs, nh, nrope, GA, raw_specs, featT, vtok, vtok_n, gate_cols=None,
                   seg_map=None, pos_tab=True):
        nraw_t = sum((c1 - c0) // 64 for (_, c0, c1, k, _) in seg_map if k == "T")
        ntot = nh + nraw_t
        npairs = ntot // 2
        xr = ring("xt", [128, DM], F32, 3)
        junkr = ring("junk", [128, DM], BF16, 2)
        ssqr = ring("ssq", [128, 1], F32, 2); rsr_ = ring("rs", [128, 1], F32, 2); rs2r = ring("rs2", [128, 1], F32, 2)
        xnr = ring("xn", [128, DM], BF16, 2)
        tph = ps("tph", [128, 8, 128], BF16)
        tpf = ps("tpf", [128, 8, 128], BF16)
        hTr = [ring(f"hT{i}", [128, 8, 128], BF16, 2) for i in range(len(projs))]
        banks = []
        for pi, (gT, W, ncols) in enumerate(projs):
            for c0 in range(0, ncols, 512):
                banks.append((pi, c0, min(512, ncols - c0), ps(f"pb{pi}_{c0}", [128, 512], F32)))
        YAr = ring("YA", [128, ntot, 64], F32, 2)
        YBr = ring("YB", [128, ntot, 64], BF16, 2)
        SQ = sb("SQ", [128, nh, 64], F32); SS = sb("SS", [128, nh]); RS = sb("RS", [128, nh]); RS2 = sb("RS2", [128, nh])
        YN = sb("YN", [128, nh, 64], F32); T1 = sb("T1", [128, max(nrope, 1), 64], F32); T2 = sb("T2", [128, max(nrope, 1), 64], F32)
        CSr = ring("CS", [128, 2, 32], F32, 6)
        TTr = ring("TT", [128, npairs, 128], BF16, 2)
        YVr = ring("YV", [128, vtok_n, 65], BF16, 2) if vtok is not None else None
        if YVr is not None:
            for t in YVr.tiles:
                S.op("pool", lambda e: e.memset(t[:], 1.0), writes=[t])
        YGr = ring("YG", [128, 36], F32, 2) if gate_cols else None
        ctx = {}

        def s0(ti):
            t0 = ti * 128
            c = ctx[ti] = {}
            xt = c["xt"] = xr()
            S.dma("sp", xt[:], x_src[t0:t0 + 128, :], writes=[xt], dram_r=[x_src])
            CS = CSr()
            c["CS"] = CS
            if nrope:
                S.dma("sp", CS[:, 0, :], C["c_cos"][t0:t0 + 128, :], writes=[CS])
                S.dma("sp", CS[:, 1, :], C["c_sin"][t0:t0 + 128, :], writes=[CS])

        def s1(ti):
            c = ctx[ti]
            xt = c["xt"]
            c["hT"] = [r() for r in hTr]
            yield from norm_tile(xt[:], xt, [p[0] for p in projs], c["hT"], tph, (junkr(), ssqr(), rsr_(), rs2r(), xnr()))

        def s2(ti):
            c = ctx[ti]
            hTs = c["hT"]
            for (pi, c0, w, pb) in banks:
                W = projs[pi][1]
                for kc in range(8):
                    S.op("pe", lambda e: e.matmul(pb[:, 0:w], lhsT=hTs[pi][:, kc, :], rhs=W.ap[:, kc, c0:c0 + w],
                                                  start=(kc == 0), stop=(kc == 7)),
                         reads=[hTs[pi], W.k[kc]], writes=[pb])
                    yield
            YA = c["YA"] = YAr()
            YB = c["YB"] = YBr()
            YV = c["YV"] = YVr() if YVr is not None else None
            YG = c["YG"] = YGr() if YGr is not None else None
            for (pi, c0, c1, kind, di) in seg_map:
                cc = c0
                while cc < c1:
                    bk = [b for b in banks if b[0] == pi and b[1] <= cc < b[1] + b[2]][0]
                    ce = min(c1, bk[1] + bk[2])
                    src = bk[3][:, cc - bk[1]:ce - bk[1]]
                    off = cc - c0
                    n = ce - cc
                    if kind in ("A", "T"):
                        dst = YA if kind == "A" else YB
                        dflat = dst[:].rearrange("p h d -> p (h d)")
                        S.op("act", lambda e: e.copy(out=dflat[:, di * 64 + off: di * 64 + off + n], in_=src),
                             reads=[bk[3]], writes=[dst])
                        yield
                    elif kind == "V":
                        h0 = di + off // 64
                        S.op("act", lambda e: e.copy(out=YV[:, h0:h0 + n // 64, 0:64],
                                                     in_=src.rearrange("p (h d) -> p h d", d=64)),
                             reads=[bk[3]], writes=[YV])
                        yield
                    elif kind == "G":
                        S.op("act", lambda e: e.copy(out=YG[:, off:off + n], in_=src), reads=[bk[3]], writes=[YG])
                        yield
                    cc = ce

        def s3(ti):
            t0 = ti * 128
            c = ctx.pop(ti)
            yield from head_norm(c["YA"], nh, nrope, GA, c["CS"], c["YB"], (SQ, SS, RS, RS2, YN, T1, T2))
            TT = TTr()
            yield from feat_transposes(c["YB"], npairs, tpf, TT)
            S.dma("sp", featT.ap.rearrange("(j h2) d t -> (h2 d) j t", h2=2)[:, :, t0:t0 + 128], TT[:],
                  reads=[TT], dram_w=[featT])
            yield
            if vtok is not None:
                S.dma("sp", vtok[t0:t0 + 128, :, :], c["YV"][:], reads=[c["YV"]], dram_w=[vtok])
                yield
            if c["YG"] is not None:
                S.dma("sp", gates[t0:t0 + 128, :], c["YG"][:], reads=[c["YG"]], dram_w=[gates])
                yield

        s0(0)
        s0(1)
        for step in range(ntiles + 2):
            if step + 2 < ntiles:
                s0(step + 2)
            gens = []
            if step < ntiles:
                gens.append((s1(step), 10))
            if 0 <= step - 1 < ntiles:
                gens.append((s2(step - 1), 56))
            if 0 <= step - 2 < ntiles:
                gens.append((s3(step - 2), 28))
            interleave(gens)

    def mem_phase(layer, kmT, VM):
        with Phase():
            stg = ring("stg", [128, 2048], F32, 2)
            W = WT("wmem", 8, 512)
            load_weight(W, I["w_mem_kv"][layer], 8, 512, stg)
            GA = build_gains(f"GAm{layer}", 4, [(I["mem_k_norm"][layer:layer + 1, :], 0, 4, 1.0)])
            xr = ring("xt", [128, DM], F32, 2)
            junk = sb("junk", [128, DM], BF16)
            ssq = sb("ssq", [128, 1]); rs = sb("rs", [128, 1]); rs2 = sb("rs2", [128, 1])
            xn = sb("xn", [128, DM], BF16)
            tph = ps("tph", [128, 8, 128], BF16)
            tpf = ps("tpf", [128, 8, 128], BF16)
            hT = sb("hT", [128, 8, 128], BF16)
            pb = ps("pb", [128, 512], F32)
            YA = sb("YA", [128, 4, 64], F32); YB = sb("YB", [128, 4, 64], BF16)
            SQ = sb("SQ", [128, 4, 64], F32); SS = sb("SS", [128, 4]); RS = sb("RS", [128, 4]); RS2 = sb("RS2", [128, 4])
            YN = sb("YN", [128, 4, 64], F32); T1 = sb("T1", [128, 1, 64], F32); T2 = sb("T2", [128, 1, 64], F32)
            TT = sb("TT", [128, 2, 128], BF16)
            S.op("pool", lambda e: e.memset(VM[:], 1.0), writes=[VM])
            memT = DramT(I["mem"], "mem")
            for mt in range(2):
                xt = xr()
                S.dma("sp", xt[:], memT[mt * 128:(mt + 1) * 128, :], writes=[xt])
                run(norm_tile(xt[:], xt, [gT_mem[layer]], [hT], tph, (junk, ssq, rs, rs2, xn)))
                for kc in range(8):
                    S.op("pe", lambda e: e.matmul(pb[:, :], lhsT=hT[:, kc, :], rhs=W.ap[:, kc, :], start=(kc == 0), stop=(kc == 7)),
                         reads=[hT, W.k[kc]], writes=[pb])
                S.op("act", lambda e: e.copy(out=YA[:].rearrange("p h d -> p (h d)"), in_=pb[:, 0:256]), reads=[pb], writes=[YA])
                S.op("act", lambda e: e.copy(out=VM[:, mt, :, 0:64], in_=pb[:, 256:512].rearrange("p (h d) -> p h d", d=64)),
                     reads=[pb], writes=[VM])
                run(head_norm(YA, 4, 0, GA, None, YB, (SQ, SS, RS, RS2, YN, T1, T2)))
                run(feat_transposes(YB, 2, tpf, TT))
                for mh in range(4):
                    S.op("dve", lambda e: e.tensor_copy(out=kmT[0:64, mh, mt * 128:(mt + 1) * 128],
                                                        in_=TT[(mh % 2) * 64:(mh % 2) * 64 + 64, mh // 2, :]),
                         reads=[TT], writes=[kmT])

    LOOK = 3

    def tile_job(lhsT_ap, rhs_fn, reads, mask_tile, mask_off, acc, v_ap, v_tile, subs, first):
        return ["tile", dict(lhsT=lhsT_ap, rhs_fn=rhs_fn, reads=reads, mt=mask_tile, mo=mask_off, acc=acc, v=v_ap,
                             vt=v_tile, subs=subs, first=first)]

    def run_jobs(jobs, ST, PTr):
        tiles = [j[1] for j in jobs if j[0] == "tile"]
        pos = {"fi": 0, "ti": 0}

        def front(j):
            c0 = min(j["subs"]) * 128
            c1 = (max(j["subs"]) + 1) * 128
            st = ST()
            pt = PTr()
            j["pt"] = pt
            S.op("pe", lambda e: e.matmul(st[:, c0:c1], lhsT=j["lhsT"], rhs=j["rhs_fn"](c0, c1), start=True, stop=True),
                 reads=j["reads"], writes=[st])
            S.op("act", lambda e: e.activation(out=pt[:, c0:c1], in_=st[:, c0:c1], func=AF.Exp), reads=[st], writes=[pt])
            if j["mt"] is not None:
                mo = j["mo"]
                S.op("dve", lambda e: e.tensor_tensor(out=pt[:, c0:c1], in0=pt[:, c0:c1], in1=j["mt"][:, mo + c0:mo + c1], op=ALU.mult),
                     reads=[pt, j["mt"]], writes=[pt])

        def back(j):
            pt = j["pt"]
            for i, sidx in enumerate(j["subs"]):
                S.op("pe", lambda e: e.matmul(j["acc"][:, sidx, :], lhsT=pt[:, sidx * 128:(sidx + 1) * 128], rhs=j["v"],
                                              start=(j["first"] and i == 0), stop=False, skip_group_check=True),
                     reads=[pt, j["vt"]], writes=[j["acc"]])

        for j in jobs:
            if j[0] == "tile":
                while pos["fi"] < len(tiles) and pos["fi"] <= pos["ti"] + LOOK:
                    front(tiles[pos["fi"]])
                    pos["fi"] += 1
                back(j[1])
                pos["ti"] += 1
            else:
                j[1]()

    def finalize(acc, OM, col0, fac_fn, tmp, accumulate):
        R, F, T = tmp
        S.op("dve", lambda e: e.reciprocal(out=R[:], in_=acc[:, :, 64]), reads=[acc], writes=[R])
        fac = R
        if fac_fn is not None:
            gap, gtile = fac_fn
            S.op("dve", lambda e: e.tensor_tensor(out=F[:], in0=R[:], in1=gap, op=ALU.mult), reads=[R, gtile], writes=[F])
            fac = F
        facb = fac[:].unsqueeze(2).to_broadcast([128, 4, 64])
        if not accumulate:
            S.op("dve", lambda e: e.tensor_tensor(out=OM[:, :, col0:col0 + 64], in0=acc[:, :, 0:64], in1=facb, op=ALU.mult),
                 reads=[acc, fac], writes=[OM])
        else:
            S.op("dve", lambda e: e.tensor_tensor(out=T[:], in0=acc[:, :, 0:64], in1=facb, op=ALU.mult), reads=[acc, fac], writes=[T])
            S.op("pool", lambda e: e.tensor_tensor(out=OM[:, :, col0:col0 + 64], in0=OM[:, :, col0:col0 + 64], in1=T[:], op=ALU.add),
                 reads=[OM, T], writes=[OM])

    def out_proj(OM, ncols_in, WO, Xres, dst, Q0, tp, po_ring, OBr, OTr, XOr):
        nk = ncols_in // 128
        for s in range(4):
            OB = OBr()
            S.op("dve", lambda e: e.tensor_copy(out=OB[:, 0:ncols_in], in_=OM[:, s, :]), reads=[OM], writes=[OB])
            for kc in range(nk):
                S.op("pe", lambda e: e.transpose(out=tp[:, kc, :], in_=OB[:, kc * 128:(kc + 1) * 128], identity=ident[:]),
                     reads=[OB, ident], writes=[tp])
            OT = OTr()
            S.op("act", lambda e: e.copy(out=OT[:, 0:nk, :], in_=tp[:, 0:nk, :]), reads=[tp], writes=[OT])
            XO = XOr()
            for half in range(2):
                po = po_ring()
                for kc in range(nk):
                    S.op("pe", lambda e: e.matmul(po[:, :], lhsT=OT[:, kc, :], rhs=WO.ap[:, kc, half * 512:(half + 1) * 512],
                                                  start=(kc == 0), stop=(kc == nk - 1)),
                         reads=[OT, WO.k[kc]], writes=[po])
                S.op("dve", lambda e: e.tensor_tensor(out=XO[:, half * 512:(half + 1) * 512], in0=po[:, :],
                                                      in1=Xres[:, s, half * 512:(half + 1) * 512], op=ALU.add),
                     reads=[po, Xres], writes=[XO])
            S.dma("sp", dst[Q0 + s * 128:Q0 + (s + 1) * 128, :], XO[:], reads=[XO], dram_w=[dst])

    def fin_job(acc, OM, col0, fac_fn, tmp, accumulate):
        return ["fin", lambda: finalize(acc, OM, col0, fac_fn, tmp, accumulate)]

    def mem_jobs(QM, qm_tile, kmT, VM, ACC, OM, col0, tmp):
        jobs = []
        for mh in range(4):
            acc = ACC()
            for mt in range(2):
                jobs.append(tile_job(kmT[:, mh, mt * 128:(mt + 1) * 128], (lambda c0, c1, mh=mh: QM[:, mh, c0:c1]), [kmT, qm_tile],
                                     None, None, acc, VM[:, mt, mh, :], VM, [0, 1, 2, 3], mt == 0))
            jobs.append(fin_job(acc, OM, col0 + mh * 64, None, tmp, False))
        return jobs

    def bank_ring(tiles):
        st = {"i": -1}

        def nxt():
            st["i"] = (st["i"] + 1) % len(tiles)
            return tiles[st["i"]]
        return nxt

    def acc_view(b):
        return b.view(b.ap[:, 0:260].rearrange("p (s c) -> p s c", c=65))

    def mlp_phase(layer, src, dst):
        with Phase():
            stg = ring("stg", [128, 1024], F32, 2)
            WU = WT("WU", 8, 4096)
            WD = WT("WD", 32, 1024, grp=4)
            load_weight(WU, I["w_up"][layer], 8, 4096, stg, col_chunk=1024, engs=("dve", "act"))
            xr = ring("xt", [128, 2, DM], F32, 3)
            ssq = sb("ssq", [128, 1]); rs = sb("rs", [128, 1]); rs2 = sb("rs2", [128, 1])
            xn = sb("xn", [128, DM], BF16)
            junk = xn
            tph = ps("tph", [128, 8, 128], BF16)
            uT = sb("uT", [128, 32, 256], BF16)
            rl = ring("rl", [128, 256], BF16, 3)
            XO = ring("XO", [128, DM], F32, 2)
            pu = [ps(f"pu{i}", [128, 512], F32) for i in range(3)]
            pd = [ps(f"pd{i}", [128, 512], F32) for i in range(3)]
            cnt = 0
            xts = {}

            def ldx(tb):
                xts[tb] = xr()
                S.dma("sp", xts[tb][:], src.ap[tb * 256:tb * 256 + 256, :].rearrange("(s p) d -> p s d", p=128),
                      writes=[xts[tb]], dram_r=[src])
            ldx(0)
            hTr2 = ring("hTm", [128, 8, 256], BF16, 2)
            hTs = {}

            def normgen(tb):
                xt = xts[tb]
                hTt = hTr2()
                hTs[tb] = hTt
                for s in range(2):
                    hv = hTt.view(hTt.ap[:, :, s * 128:(s + 1) * 128])
                    yield from norm_tile(xt[:, s, :], xt, [gT_mlp[layer]], [hv], tph, (junk, ssq, rs, rs2, xn))

            def blockgen(tb):
                nonlocal_cnt = cntbox
                t0 = tb * 256
                xt = xts[tb]
                hTt = hTs.pop(tb)
                for fc in range(32):
                    p = pu[fc % 3]
                    for kc in range(8):
                        S.op("pe", lambda e: e.matmul(p[:, 0:256], lhsT=WU.ap[:, kc, fc * 128:(fc + 1) * 128], rhs=hTt[:, kc, :],
                                                      start=(kc == 0), stop=(kc == 7)),
                             reads=[WU.k[kc], hTt], writes=[p])
                        yield
                    r = rl()
                    S.op("act", lambda e: e.activation(out=r[:], in_=p[:, 0:256], func=AF.Relu), reads=[p], writes=[r])
                    yield
                    S.op("dve", lambda e: e.tensor_tensor(out=uT[:, fc, :], in0=r[:], in1=r[:], op=ALU.mult), reads=[r], writes=[uT])
                    yield
                if tb == 0:
                    load_weight(WD, I["w_down"][layer], 32, 1024, stg, col_chunk=1024, engs=("dve", "act"))
                for s in range(2):
                    xo = XO()
                    for half in range(2):
                        p = pd[nonlocal_cnt[0] % 3]
                        nonlocal_cnt[0] += 1
                        for kc in range(32):
                            S.op("pe", lambda e: e.matmul(p[:, :], lhsT=uT[:, kc, s * 128:(s + 1) * 128],
                                                          rhs=WD.ap[:, kc, half * 512:(half + 1) * 512],
                                                          start=(kc == 0), stop=(kc == 31)),
                                 reads=[uT, WD.k[kc]], writes=[p])
                            yield
                        S.op("dve", lambda e: e.tensor_tensor(out=xo[:, half * 512:(half + 1) * 512], in0=p[:, :],
                                                              in1=xt[:, s, half * 512:(half + 1) * 512], op=ALU.add),
                             reads=[p, xt], writes=[xo])
                        yield
                    S.dma("sp", dst[t0 + s * 128:t0 + (s + 1) * 128, :], xo[:], reads=[xo], dram_w=[dst])
                    yield
                xts.pop(tb)

            cntbox = [0]
            nblk = SEQ // 256
            ldx(1)
            run(normgen(0))
            for tb in range(nblk):
                gens = [(blockgen(tb), 460)]
                if tb + 1 < nblk:
                    gens.append((normgen(tb + 1), 32))
                interleave(gens)
                if tb + 2 < nblk:
                    ldx(tb + 2)

    with Phase():
        stg = ring("stg", [128, 1106], F32, 4)
        WI = WT("WI", 8, 2212)
        load_weight(WI, I["a_w_in"][0], 8, 2212, stg, col_chunk=1106, engs=("dve", "act"))
        GA0 = build_gains("GA0", 22, [(I["a_q_norm"][0:1, :], 0, 12, 0.125), (I["a_k_norm"][0, 1:2, :], 12, 3, 1.0),
                                     (I["a_k_norm"][0, 2:3, :], 15, 3, 1.0), (I["mem_q_norm"][0:1, :], 18, 4, 0.125)])
        seg0 = [(0, 0, 768, "A", 0), (0, 1152, 1344, "A", 12), (0, 1536, 1728, "A", 15), (0, 1920, 2176, "A", 18),
                (0, 768, 960, "T", 22), (0, 960, 1152, "T", 25),
                (0, 1344, 1536, "V", 0), (0, 1728, 1920, "V", 3), (0, 2176, 2212, "G", 0)]
        proj_phase(xin, SEQ // 128, [(gT_attn[0], WI, 2212)], 22, 18, GA0, None, featT0, vtok0, 6, gate_cols=True, seg_map=seg0)

    kcT = sb("kcT", [128, 3, 256], BF16, persist=True)
    S.op("pool", lambda e: e.memset(kcT[:], 0.0), writes=[kcT])
    VCO = sb("VCO", [128, 2, 3, 128], BF16, persist=True)
    with Phase():
        GAc = build_gains("GAc", 1, [(I["a_k_norm"][0, 0:1, :], 0, 1, 1.0)])
        for ct in range(2):
            for g in range(3):
                S.dma("sp", VCO[:, ct, g, 64:128], C["c_ovl"][ct * 128:(ct + 1) * 128, :], writes=[VCO])
        XCr = ring("XC", [64, SEQ], BF16, 2)
        W1s = sb("W1s", [64, 32, 256], F32)
        W1 = sb("W1", [64, 32, 256], BF16)
        W2s = sb("W2s", [128, 2, 64], F32)
        W2 = sb("W2", [128, 2, 64], BF16)
        posS = sb("posS", [64, 32], F32); posT = sb("posT", [64, 32], BF16)
        b1c = sb("b1c", [128, 2], F32); BT = sb("BT", [128, 2], F32)
        b2s = sb("b2s", [1, 64], F32); b2r = sb("b2r", [1, 64], BF16)
        HG = sb("HG", [128, 2, 256], BF16)
        S.op("pool", lambda e: e.memset(HG[:], 0.0), writes=[HG])
        ph = [ps(f"ph{i}", [128, 512], F32) for i in range(2)]
        pbias = ps("pbias", [128, 2], F32)
        pout = ps("pout", [128, 64], F32)
        tpf = ps("tpf", [128, 8, 128], BF16)
        CSc = sb("CSc", [128, 2, 2, 32], F32)
        S.op("pool", lambda e: e.memset(CSc[:], 0.0), writes=[CSc])
        for ct in range(2):
            n = 128 if ct == 0 else 127
            S.dma("sp", CSc[0:n, ct, 0, :], C["c_cos"].rearrange("(c s) f -> c s f", s=16)[ct * 128 + 1:ct * 128 + 1 + n, 15, :], writes=[CSc])
            S.dma("sp", CSc[0:n, ct, 1, :], C["c_sin"].rearrange("(c s) f -> c s f", s=16)[ct * 128 + 1:ct * 128 + 1 + n, 15, :], writes=[CSc])
        YA = sb("YAc", [128, 1, 64], F32); YB = sb("YBc", [128, 2, 64], BF16)
        S.op("pool", lambda e: e.memset(YB[:], 0.0), writes=[YB])
        SQ = sb("SQ", [128, 1, 64], F32); SS = sb("SS", [128, 1]); RS = sb("RS", [128, 1]); RS2 = sb("RS2", [128, 1])
        YN = sb("YN", [128, 1, 64], F32); T1 = sb("T1", [128, 1, 64], F32); T2 = sb("T2", [128, 1, 64], F32)
        CS1 = sb("CS1", [128, 2, 32], F32)
        TTc = sb("TTc", [128, 1, 128], BF16)
        for i in range(2):
            S.dma("sp", W1s[:], I["a_cmp_w1"][0, i].rearrange("(j d) h -> d j h", d=64), writes=[W1s])
            S.op("dve", lambda e: e.tensor_copy(out=W1[:, 0:16, :], in_=W1s[:, 0:16, :]), reads=[W1s], writes=[W1])
            S.op("pool", lambda e: e.tensor_copy(out=W1[:, 16:32, :], in_=W1s[:, 16:32, :]), reads=[W1s], writes=[W1])
            S.dma("sp", W2s[:], I["a_cmp_w2"][0, i].rearrange("(hf p) d -> p hf d", p=128), writes=[W2s])
            S.op("dve", lambda e: e.tensor_copy(out=W2[:], in_=W2s[:]), reads=[W2s], writes=[W2])
            S.dma("sp", posS[:], I["a_cmp_pos"][0, i].rearrange("j d -> d j"), writes=[posS], allow_slow_non_contiguous=True)
            S.op("dve", lambda e: e.tensor_copy(out=posT[:], in_=posS[:]), reads=[posS], writes=[posT])
            S.dma("sp", b1c[:], I["a_cmp_b1"][0, i:i + 1, :].rearrange("o (hf p) -> p (o hf)", p=128), writes=[b1c],
                  allow_slow_non_contiguous=True)
            S.dma("sp", b2s[:], I["a_cmp_b2"][0, i:i + 1, :], writes=[b2s])
            S.op("dve", lambda e: e.tensor_copy(out=b2r[:], in_=b2s[:]), reads=[b2s], writes=[b2r])
            for hf in range(2):
                for j in range(32):
                    S.op("pe", lambda e: e.matmul(pbias[:, hf:hf + 1], lhsT=W1[:, j, hf * 128:(hf + 1) * 128], rhs=posT[:, j:j + 1],
                                                  start=(j == 0 and hf == 0), stop=(j == 31), skip_group_check=True),
                         reads=[W1, posT], writes=[pbias])
            S.op("dve", lambda e: e.tensor_tensor(out=BT[:], in0=pbias[:], in1=b1c[:], op=ALU.add), reads=[pbias, b1c], writes=[BT])
            for g in range(3):
                XC = XCr()
                S.dma("sp", XC[:], featT0[22 + 3 * i + g, :, :], writes=[XC], dram_r=[featT0])
                XCv = XC[:].rearrange("d (c s) -> d c s", s=16)
                for hf in range(2):
                    for j in range(32):
                        S.op("pe", lambda e: e.matmul(ph[hf][:, 0:255], lhsT=W1[:, j, hf * 128:(hf + 1) * 128],
                                                      rhs=XCv[:, (j // 16):(j // 16) + 255, j % 16], start=(j == 0), stop=(j == 31)),
                             reads=[W1, XC], writes=[ph[hf]])
                    S.op("act", lambda e: e.activation(out=HG[:, hf, 0:255], in_=ph[hf][:, 0:255], func=AF.Gelu_apprx_tanh,
                                                       bias=BT[:, hf:hf + 1]),
                         reads=[ph[hf], BT], writes=[HG])
                for ct in range(2):
                    for hf in range(2):
                        S.op("pe", lambda e: e.matmul(pout[:, :], lhsT=HG[:, hf, ct * 128:(ct + 1) * 128], rhs=W2[:, hf, :],
                                                      start=(hf == 0), stop=False),
                             reads=[HG, W2], writes=[pout])
                    S.op("pe", lambda e: e.matmul(pout[:, :], lhsT=ones_bf[0:1, :], rhs=b2r[0:1, :], start=False, stop=True),
                         reads=[ones_bf, b2r], writes=[pout])
                    if i == 0:
                        S.op("act", lambda e: e.copy(out=YA[:, 0, :], in_=pout[:, :]), reads=[pout], writes=[YA])
                        S.op("dve", lambda e: e.tensor_copy(out=CS1[:], in_=CSc[:, ct, :, :]), reads=[CSc], writes=[CS1])
                        run(head_norm(YA, 1, 1, GAc, CS1, YB, (SQ, SS, RS, RS2, YN, T1, T2)))
                        run(feat_transposes(YB, 1, tpf, TTc))
                        S.op("dve", lambda e: e.tensor_copy(out=kcT[0:64, g, ct * 128:(ct + 1) * 128], in_=TTc[0:64, 0, :]),
                             reads=[TTc], writes=[kcT])
                    else:
                        S.op("act", lambda e: e.copy(out=VCO[:, ct, g, 0:64], in_=pout[:, :]), reads=[pout], writes=[VCO])

    kmT0 = sb("kmT0", [128, 4, 256], BF16, persist=True)
    S.op("pool", lambda e: e.memset(kmT0[:], 0.0), writes=[kmT0])
    VM0 = sb("VM0", [128, 2, 4, 65], BF16, persist=True)
    mem_phase(0, kmT0, VM0)

    with Phase():
        stg = ring("stg", [128, 1024], F32, 1)
        WO = WT("WO", 8, 1024)
        load_weight(WO, I["a_w_out"][0], 8, 1024, stg)
        KA = sb("KA", [128, 3, SEQ], BF16)
        KW = sb("KW", [128, 3, SEQ], BF16)
        S.op("pool", lambda e: e.memset(KW[64:128, :, :], 0.0), writes=[KW])
        VSW = sb("VSW", [128, 32, 6, 65], BF16)
        S.dma("sp", KA[0:64, :, :], featT0.ap[12:15, :, :].rearrange("h d t -> d h t"), writes=[KA], dram_r=[featT0])
        for g in range(3):
            S.dma("sp", KA[64:128, g, :], C["c_onehot"][:, :], writes=[KA])
        S.dma("sp", KW[0:64, :, :], featT0.ap[15:18, :, :].rearrange("h d t -> d h t"), writes=[KW], dram_r=[featT0])
        for kt0 in range(0, 32, 8):
            S.dma("sp", VSW[:, kt0:kt0 + 8, :, :], vtok0.ap[kt0 * 128:(kt0 + 8) * 128].rearrange("(kt p) h d -> p kt h d", p=128),
                  writes=[VSW], dram_r=[vtok0])
        msel = sb("msel", [128, 896], BF16); mwin = sb("mwin", [128, 1408], BF16)
        mcmp = sb("mcmp", [128, 512], F32); tkeep = sb("tkeep", [128, 128], F32); tadd = sb("tadd", [128, 128], F32)
        for t, k in ((msel, "m_sel"), (mwin, "m_win"), (mcmp, "m_cmp"), (tkeep, "m_tkeep"), (tadd, "m_tadd")):
            S.dma("sp", t[:], C[k][:, :], writes=[t])
        QAr = ring("QA", [128, 16, 512], BF16, 1)
        for t in QAr.tiles:
            S.op("pool", lambda e: e.memset(t[:], 0.0), writes=[t])
        GTr = ring("GT", [128, 4, 36], F32, 1)
        XRr = ring("XR", [128, 4, DM], F32, 1)
        OM = sb("OM", [128, 4, DM], F32)
        Bk = [ps(f"bk{i}", [128, 512], F32) for i in range(7)]
        TP = ps("tp", [128, 8, 128], BF16)
        ST = bank_ring(Bk[0:4])
        ACC = bank_ring([acc_view(b) for b in Bk[4:7]])
        PO = bank_ring(Bk[0:2])
        SCsets = [(Bk[0], Bk[1]), (Bk[2], Bk[3])]
        OCI = [b.view(b.ap[:, 0:512].rearrange("p (r d) -> p r d", d=64)) for b in Bk[4:7]]
        PTr = ring("PT", [128, 512], BF16, LOOK + 2)
        SCm = ring("SCm", [128, 256], F32, 4)
        PCr = ring("PC", [128, 256], F32, 4)
        for t in PCr.tiles:
            S.op("pool", lambda e: e.memset(t[:], 0.0), writes=[t])
        PNr = ring("PN", [128, 256], BF16, 24)
        PTc = ring("PTc", [128, 8, 128], BF16, 3)
        rsr = ring("rsr", [128, 4, 2], F32, 2)
        TK = [dict(I1=sb("I1", [128, 64], F32), I2=sb("I2", [128, 64], F32), I3=sb("I3", [128, 64], F32),
                   M8=sb("M8", [128, 16], F32), SEL=sb("SEL", [128, 64], F32), VAL=sb("VAL", [128, 64], F32)) for _ in range(3)]
        MTr = ring("MT", [128, 128], BF16, 4)
        for t in MTr.tiles:
            S.op("pool", lambda e: e.memset(t[:], 0.0), writes=[t])
        Rt = sb("Rt", [128, 4], F32); Ft = sb("Ft", [128, 4], F32); Tt = sb("Tt", [128, 4, 64], F32)
        OBr = ring("OB", [128, DM], BF16, 1); OTr = ring("OT", [128, 8, 128], BF16, 1); XOr = ring("XO", [128, DM], F32, 1)
        for qi in range(SEQ // 512):
            Q0 = qi * 512
            QA = QAr(); GT = GTr(); XR = XRr()
            S.dma("sp", QA[0:64, 0:12, :], featT0.ap[0:12, :, Q0:Q0 + 512].rearrange("h d t -> d h t"), writes=[QA], dram_r=[featT0])
            S.dma("sp", QA[0:64, 12:16, :], featT0.ap[18:22, :, Q0:Q0 + 512].rearrange("h d t -> d h t"), writes=[QA], dram_r=[featT0])
            S.dma("sp", GT[:], gates.ap[Q0:Q0 + 512, :].rearrange("(s p) c -> p s c", p=128), writes=[GT], dram_r=[gates])
            S.op("act", lambda e: e.activation(out=GT[:], in_=GT[:], func=AF.Sigmoid), reads=[GT], writes=[GT])
            S.dma("sp", XR[:], xin.ap[Q0:Q0 + 512, :].rearrange("(s p) d -> p s d", p=128), writes=[XR])

            def stepA_front(n, s, g):
                i128 = qi * 4 + s
                bA, bB = SCsets[n % 2]
                scv = [(bA if r < 2 else bB) for r in range(4)]
                off = 248 - 8 * i128
                pns = []
                for r in range(4):
                    h = g * 4 + r
                    c0 = (r % 2) * 256
                    S.op("pe", lambda e: e.matmul(scv[r][:, c0:c0 + 255], lhsT=QA[:, h, s * 128:(s + 1) * 128], rhs=kcT[:, g, 0:255],
                                                  start=True, stop=True), reads=[QA, kcT], writes=[scv[r]])
                    yield
                scms = []
                for r in range(4):
                    c0 = (r % 2) * 256
                    scm = SCm()
                    scms.append(scm)
                    S.op("dve", lambda e: e.tensor_tensor(out=scm[:, 0:255], in0=scv[r][:, c0:c0 + 255], in1=mcmp[:, off:off + 255], op=ALU.add),
                         reads=[scv[r], mcmp], writes=[scm])
                    yield
                rs4 = rsr()
                S.op("pool", lambda e: e.memset(rs4[:], 0.0), writes=[rs4])
                yield
                pcs = []
                for r in range(4):
                    pc = PCr()
                    pcs.append(pc)
                    S.op("act", lambda e: e.activation(out=pc[:, 0:255], in_=scms[r][:, 0:255], func=AF.Exp, accum_out=rs4[:, r, 0:1]),
                         reads=[scms[r], rs4], writes=[pc, rs4])
                    yield
                S.op("dve", lambda e: e.tensor_scalar(out=rs4[:, :, 1], in0=rs4[:, :, 0], scalar1=1e-30, scalar2=None, op0=ALU.add),
                     reads=[rs4], writes=[rs4])
                yield
                S.op("dve", lambda e: e.reciprocal(out=rs4[:, :, 0], in_=rs4[:, :, 1]), reads=[rs4], writes=[rs4])
                yield
                for r in range(4):
                    pn = PNr()
                    pns.append(pn)
                    S.op("act", lambda e: e.activation(out=pn[:], in_=pcs[r][:], func=AF.Copy, scale=rs4[:, r, 0:1]),
                         reads=[pcs[r], rs4], writes=[pn])
                    yield
                ctxs[(s, g)] = pns

            def stepA_back(ci, s, g):
                pns = ctxs.pop((s, g))
                i128 = qi * 4 + s
                oci = OCI[ci]
                tk = TK[ci]
                I1, I2, I3, M8, SEL, VAL = tk["I1"], tk["I2"], tk["I3"], tk["M8"], tk["SEL"], tk["VAL"]
                for r in range(4):
                    for ct in range(2):
                        S.op("pe", lambda e: e.transpose(out=TP[:, r * 2 + ct, :], in_=pns[r][:, ct * 128:(ct + 1) * 128], identity=ident[:]),
                             reads=[pns[r], ident], writes=[TP])
                ptc = PTc()
                S.op("act", lambda e: e.copy(out=ptc[:], in_=TP[:, 0:8, :]), reads=[TP], writes=[ptc])
                yield
                for r in range(4):
                    for ct in range(2):
                        S.op("pe", lambda e: e.matmul(oci[:, r, :], lhsT=ptc[:, r * 2 + ct, :], rhs=VCO[:, ct, g, 0:64], start=(ct == 0), stop=(ct == 1),
                                                      skip_group_check=True),
                             reads=[ptc, VCO], writes=[oci])
                    for ct in range(2):
                        S.op("pe", lambda e: e.matmul(oci[:, 4 + r, :], lhsT=ptc[:, r * 2 + ct, :], rhs=VCO[:, ct, g, 64:128], start=(ct == 0), stop=(ct == 1),
                                                      skip_group_check=True),
                             reads=[ptc, VCO], writes=[oci])
                    yield
                gc = GT[:, s, 12 * g:12 * g + 12].rearrange("p (r b) -> p r b", b=3)[:, :, 0]
                S.op("dve", lambda e: e.tensor_tensor(out=OM[:, s, 256 * g:256 * g + 256].rearrange("p (r d) -> p r d", d=64), in0=oci[:, 0:4, :],
                                                      in1=gc.unsqueeze(2).to_broadcast([128, 4, 64]), op=ALU.mult),
                     reads=[oci, GT], writes=[OM])
                yield
                toff = 62 - 2 * i128
                S.op("dve", lambda e: e.tensor_reduce(out=I3[:], in_=oci[:, 4:8, :].rearrange("p r j -> p j r"), axis=AX.X, op=ALU.add),
                     reads=[oci], writes=[I3])
                yield
                S.op("dve", lambda e: e.tensor_tensor(out=I1[:], in0=I3[:], in1=tkeep[:, toff:toff + 64], op=ALU.mult),
                     reads=[I3, tkeep], writes=[I1])
                yield
                S.op("dve", lambda e: e.tensor_tensor(out=I2[:], in0=I1[:], in1=tadd[:, toff:toff + 64], op=ALU.add),
                     reads=[I1, tadd], writes=[I2])
                yield
                S.op("dve", lambda e: e.memset(I2[:, 0:1], 3e9), reads=[I2], writes=[I2])
                yield
                S.op("dve", lambda e: e.max(out=M8[:, 0:8], in_=I2[:]), reads=[I2], writes=[M8])
                yield
                S.op("dve", lambda e: e.match_replace(out=I3[:], in_to_replace=M8[:, 0:8], in_values=I2[:], imm_value=-1e30),
                     reads=[I2, M8], writes=[I3])
                yield
                S.op("dve", lambda e: e.max(out=M8[:, 8:16], in_=I3[:]), reads=[I3], writes=[M8])
                yield
                S.op("dve", lambda e: e.tensor_scalar(out=SEL[:], in0=I2[:], scalar1=M8[:, 15:16], scalar2=None, op0=ALU.is_ge),
                     reads=[I2, M8], writes=[SEL])
                yield
                S.op("dve", lambda e: e.tensor_scalar(out=VAL[:], in0=I2[:], scalar1=-1e29, scalar2=None, op0=ALU.is_gt),
                     reads=[I2], writes=[VAL])
                yield
                S.op("dve", lambda e: e.tensor_tensor(out=SEL[:], in0=SEL[:], in1=VAL[:], op=ALU.mult), reads=[SEL, VAL], writes=[SEL])
                yield
                MT = MTr()
                S.op("dve", lambda e: e.tensor_scalar(out=MT[:, 64:128], in0=SEL[:], scalar1=-1.0, scalar2=-NBIG, op0=ALU.add, op1=ALU.mult),
                     reads=[SEL], writes=[MT])
                yield
                S.op("pe", lambda e: e.transpose(out=TP[:, 0, :], in_=MT[:], identity=ident[:]), reads=[MT, ident], writes=[TP])
                S.op("act", lambda e: e.copy(out=QA[64:128, 4 * g:4 * g + 4, s * 128:(s + 1) * 128],
                                             in_=TP[64:128, 0:1, :].to_broadcast([64, 4, 128])), reads=[TP], writes=[QA])
                yield

            def fronts(s):
                for g in range(3):
                    yield from stepA_front(s * 3 + g, s, g)

            ctxs = {}
            run(fronts(0))
            for s in range(4):
                gens = [(stepA_back(g, s, g), 24) for g in range(3)]
                if s + 1 < 4:
                    gens.append((fronts(s + 1), 60))
                interleave(gens)

            nkt = qi * 4 + 4
            jobs = []
            for h in range(12):
                g = h // 4
                accS = ACC(); accW = ACC()
                for kt in range(nkt):
                    D = Q0 - kt * 128
                    subs = [sx for sx in range(4) if D + 128 * sx >= 0]
                    mt_, mo_ = (msel, D + 384) if D < 128 else (None, None)
                    jobs.append(tile_job(KA[:, g, kt * 128:(kt + 1) * 128], (lambda c0, c1, h=h: QA[:, h, c0:c1]), [KA, QA], mt_, mo_,
                                         accS, VSW[:, kt, g, :], VSW, subs, kt == 0))
                kts = list(range(max(0, qi * 4 - 4), nkt))
                for kt in kts:
                    D = Q0 - kt * 128
                    subs = [sx for sx in range(4) if 0 <= D + 128 * sx <= 512]
                    jobs.append(tile_job(KW[:, g, kt * 128:(kt + 1) * 128], (lambda c0, c1, h=h: QA[:, h, c0:c1]), [KW, QA], mwin, D + 384,
                                         accW, VSW[:, kt, 3 + g, :], VSW, subs, kt == kts[0]))
                jobs.append(fin_job(accS, OM, h * 64, (GT[:, :, 3 * h + 1], GT), (Rt, Ft, Tt), True))
                jobs.append(fin_job(accW, OM, h * 64, (GT[:, :, 3 * h + 2], GT), (Rt, Ft, Tt), True))
            jobs += mem_jobs(QA[:, 12:16, :], QA, kmT0, VM0, ACC, OM, 768, (Rt, Ft, Tt))
            run_jobs(jobs, ST, PTr)
            out_proj(OM, 1024, WO, XR, x1, Q0, TP, PO, OBr, OTr, XOr)

    mlp_phase(0, x1, x2)

    with Phase():
        stg = ring("stg", [128, 1024], F32, 4)
        WB = WT("WB", 8, 1792)
        WK = WT("WK", 8, 1024)
        load_weight(WB, I["b_w_in"][0], 8, 1792, stg, col_chunk=896, engs=("dve", "act"))
        load_weight(WK, I["w_kv_shared"], 8, 1024, stg, col_chunk=1024, engs=("dve", "act"))
        GA1 = build_gains("GA1", 36, [(I["b_q_norm"][0, 0:1, :], 0, 8, 0.125), (I["b_q_norm"][0, 1:2, :], 8, 8, 0.125),
                                     (I["b_q_norm"][0, 2:3, :], 16, 8, 0.125), (I["kv_k_norm"][0:1, :], 24, 8, 1.0),
                                     (I["mem_q_norm"][1:2, :], 32, 4, 0.125)])
        seg1 = [(0, 0, 1536, "A", 0), (1, 0, 512, "A", 24), (0, 1536, 1792, "A", 32), (1, 512, 1024, "V", 0)]
        proj_phase(x2, SEQ // 128, [(gT_attn[1], WB, 1792), (gT_kv, WK, 1024)], 36, 32, GA1, None, featT1, vtok1, 8, seg_map=seg1)

    kmT1 = sb("kmT1", [128, 4, 256], BF16, persist=True)
    S.op("pool", lambda e: e.memset(kmT1[:], 0.0), writes=[kmT1])
    VM1 = sb("VM1", [128, 2, 4, 65], BF16, persist=True)
    mem_phase(1, kmT1, VM1)

    with Phase():
        stg = ring("stg", [128, 1024], F32, 1)
        WO1 = WT("WO1", 6, 1024)
        load_weight(WO1, I["b_w_out"][0], 6, 1024, stg)
        KT = sb("KT", [128, 4, SEQ], BF16)
        V1 = sb("V1", [128, 32, 8, 65], BF16)
        S.dma("sp", KT[:, :, :], featT1.ap[24:32, :, :].rearrange("(j h2) d t -> (h2 d) j t", h2=2), writes=[KT], dram_r=[featT1])
        for kt0 in range(0, 32, 8):
            S.dma("sp", V1[:, kt0:kt0 + 8, :, :], vtok1.ap[kt0 * 128:(kt0 + 8) * 128].rearrange("(kt p) h d -> p kt h d", p=128),
                  writes=[V1], dram_r=[vtok1])
        md = []
        for k, w in (("m_d0", 1024), ("m_d1", 1408), ("m_d2", 2944)):
            t = sb(k, [128, w], BF16)
            S.dma("sp", t[:], C[k][:, :], writes=[t])
            md.append(t)
        QTer = ring("QTe", [128, 12, 512], BF16, 2)
        QTor = ring("QTo", [128, 12, 512], BF16, 2)
        QMr = ring("QM", [128, 4, 512], BF16, 2)
        for t in QTer.tiles + QTor.tiles + QMr.tiles:
            S.op("pool", lambda e: e.memset(t[:], 0.0), writes=[t])
        XRr = ring("XR", [128, 4, DM], F32, 1)
        OM = sb("OM", [128, 4, 768], F32)
        Bk = [ps(f"bk{i}", [128, 512], F32) for i in range(7)]
        TP = ps("tp", [128, 8, 128], BF16)
        ST = bank_ring(Bk[0:4])
        ACC = bank_ring([acc_view(b) for b in Bk[4:7]])
        PO = bank_ring(Bk[0:2])
        PTr = ring("PT", [128, 512], BF16, LOOK + 2)
        Rt = sb("Rt", [128, 4], F32); Ft = sb("Ft", [128, 4], F32); Tt = sb("Tt", [128, 4, 64], F32)
        OBr = ring("OB", [128, DM], BF16, 2); OTr = ring("OT", [128, 8, 128], BF16, 2); XOr = ring("XO", [128, DM], F32, 2)
        pats = ((128, 1), (512, 4), (2048, 16))
        for qi in range(SEQ // 512):
            Q0 = qi * 512
            QTe = QTer(); QTo = QTor(); QM = QMr(); XR = XRr()
            qsrc = featT1.ap[0:24, :, Q0:Q0 + 512].rearrange("(j h2) d t -> h2 d j t", h2=2)
            S.dma("sp", QTe[0:64, :, :], qsrc[0], writes=[QTe], dram_r=[featT1])
            S.dma("sp", QTo[64:128, :, :], qsrc[1], writes=[QTo], dram_r=[featT1])
            S.dma("sp", QM[0:64, :, :], featT1.ap[32:36, :, Q0:Q0 + 512].rearrange("h d t -> d h t"), writes=[QM], dram_r=[featT1])
            S.dma("sp", XR[:], x2.ap[Q0:Q0 + 512, :].rearrange("(s p) d -> p s d", p=128), writes=[XR], dram_r=[x2])
            jobs = []
            for hh in range(8):
                acc = ACC()
                p0 = (hh % 2) * 64
                first = True
                for gi, (window, dil) in enumerate(pats):
                    qh = gi * 8 + hh
                    for kt in range(max(0, (Q0 - window) // 128), qi * 4 + 4):
                        D = Q0 - kt * 128
                        subs = [sx for sx in range(4) if 0 <= D + 128 * sx <= window]
                        QTx = QTe if hh % 2 == 0 else QTo
                        jobs.append(tile_job(KT[:, hh // 2, kt * 128:(kt + 1) * 128],
                                             (lambda c0, c1, QTx=QTx, qh=qh: QTx[:, qh // 2, c0:c1]), [KT, QTx],
                                             md[gi], D + 384, acc, V1[:, kt, hh, :], V1, subs, first))
                        first = False
                jobs.append(fin_job(acc, OM, hh * 64, None, (Rt, Ft, Tt), False))
            jobs += mem_jobs(QM, QM, kmT1, VM1, ACC, OM, 512, (Rt, Ft, Tt))
            run_jobs(jobs, ST, PTr)
            out_proj(OM, 768, WO1, XR, x3, Q0, TP, PO, OBr, OTr, XOr)

    mlp_phase(1, x3, outT)
    S.barrier()
    gst.close()
    nc._nsem_used = S.nsem
    return nc


_CACHE = {}


def kernel(**inputs):
    n = 8
    consts = host_consts()
    if "nc" not in _CACHE:
        _CACHE["nc"] = build_program()
    nc = _CACHE["nc"]
    in_maps = []
    for b in range(n):
        m = {}
        for k, shp in IN_SHAPES.items():
            a = np.asarray(inputs[k], dtype=np.float32)
            if k in ("x", "mem"):
                a = a[b]
            m[k] = np.ascontiguousarray(a.reshape(shp))
        m.update(consts)
        in_maps.append(m)
    res = run_bass_kernel_spmd(nc, in_maps, core_ids=list(range(n)))
    return np.stack([np.asarray(r["out"], dtype=np.float32) for r in res.results], axis=0)
```
